# Optimizing a Trainium2 kernel written in Bass

```python
import jax, jax.numpy as jnp
from jax import lax
import numpy as np

D_MODEL = 2048
BATCH = 1
SEQ = 8192
DEPTH = 4

N_MIXERS = 2
N_ATTN = (DEPTH + 1) // 2
N_REC = DEPTH // 2
HEAD_DIM = 64
N_Q_HEADS = D_MODEL // HEAD_DIM
N_KV_HEADS = 8
GROUP = N_Q_HEADS // N_KV_HEADS
WINDOW = 128
BLOCK = 128
QKV_DIM = (N_Q_HEADS + 2 * N_KV_HEADS) * HEAD_DIM
LRU_WIDTH = D_MODEL
LRU_BLOCKS = 8
LRU_BLOCK_DIM = LRU_WIDTH // LRU_BLOCKS
LRU_CONV = 4
LRU_C = 8.0
D_FF = 3 * D_MODEL
FFN_CONV = 3
EPS = 1e-6

kernel_name = "hybrid_swa_sink_rglru_convffn"


def rmsnorm(x, g):
    xf = x.astype(jnp.float32)
    y = xf * lax.rsqrt(jnp.mean(xf * xf, axis=-1, keepdims=True) + EPS)
    return (y * g.astype(jnp.float32)).astype(x.dtype)


def causal_dwconv(x, w, b):
    K = w.shape[0]
    T = x.shape[1]
    xp = jnp.pad(x, ((0, 0), (K - 1, 0), (0, 0)))
    y = b
    for k in range(K):
        y = y + xp[:, k:k + T] * w[k]
    return y


def sliding_window_attention(h, w_qkv, q_gain, k_gain, sinks, w_o):
    B, T, _ = h.shape
    nb = T // BLOCK
    qkv = h @ w_qkv
    q, k, v = jnp.split(qkv, [N_Q_HEADS * HEAD_DIM, (N_Q_HEADS + N_KV_HEADS) * HEAD_DIM], axis=-1)
    q = rmsnorm(q.reshape(B, T, N_KV_HEADS, GROUP, HEAD_DIM), q_gain)
    k = rmsnorm(k.reshape(B, T, N_KV_HEADS, HEAD_DIM), k_gain)
    v = v.reshape(B, T, N_KV_HEADS, HEAD_DIM)
    q = q.reshape(B, nb, BLOCK, N_KV_HEADS, GROUP, HEAD_DIM)
    kb = k.reshape(B, nb, BLOCK, N_KV_HEADS, HEAD_DIM)
    vb = v.reshape(B, nb, BLOCK, N_KV_HEADS, HEAD_DIM)
    pad = ((0, 0), (1, 0), (0, 0), (0, 0), (0, 0))
    kw = jnp.concatenate([jnp.pad(kb, pad)[:, :-1], kb], axis=2)
    vw = jnp.concatenate([jnp.pad(vb, pad)[:, :-1], vb], axis=2)
    scores = jnp.einsum('bnqhgd,bnkhd->bnhgqk', q, kw).astype(jnp.float32) * (HEAD_DIM ** -0.5)
    qpos = jnp.arange(BLOCK)[:, None] + BLOCK
    kpos = jnp.arange(2 * BLOCK)[None, :]
    rel = qpos - kpos
    band = (rel >= 0) & (rel < WINDOW)
    real_key = (jnp.arange(nb)[:, None] > 0) | (jnp.arange(2 * BLOCK)[None, :] >= BLOCK)
    mask = band[None, :, :] & real_key[:, None, :]
    scores = jnp.where(mask[None, :, None, None], scores, jnp.finfo(jnp.float32).min)
    sink = jnp.broadcast_to(sinks.astype(jnp.float32).reshape(1, 1, N_KV_HEADS, GROUP, 1, 1),
                            scores.shape[:-1] + (1,))
    probs = jax.nn.softmax(jnp.concatenate([scores, sink], axis=-1), axis=-1)[..., :-1]
    out = jnp.einsum('bnhgqk,bnkhd->bnqhgd', probs.astype(vw.dtype), vw)
    return out.reshape(B, T, N_Q_HEADS * HEAD_DIM) @ w_o


def rglru_block(h, w_in, conv_w, conv_b, w_a, b_a, w_i, b_i, lam, w_out):
    B, T, _ = h.shape
    xb, yb = jnp.split(h @ w_in, 2, axis=-1)
    gate = jax.nn.gelu(yb, approximate=True)
    xb = causal_dwconv(xb, conv_w, conv_b)
    xh = xb.reshape(B, T, LRU_BLOCKS, LRU_BLOCK_DIM)
    r = jax.nn.sigmoid(jnp.einsum('bthi,hij->bthj', xh, w_a) + b_a).reshape(B, T, LRU_WIDTH)
    i = jax.nn.sigmoid(jnp.einsum('bthi,hij->bthj', xh, w_i) + b_i).reshape(B, T, LRU_WIDTH)
    log_a = -LRU_C * r.astype(jnp.float32) * jax.nn.softplus(-lam.astype(jnp.float32))
    a = jnp.exp(log_a)
    u = jnp.sqrt(-jnp.expm1(2.0 * log_a)) * (i * xb).astype(jnp.float32)

    def combine(left, right):
        a1, b1 = left
        a2, b2 = right
        return a1 * a2, a2 * b1 + b2

    _, hs = lax.associative_scan(combine, (a, u), axis=1)
    return (hs.astype(h.dtype) * gate) @ w_out


def conv_ffn(h, w_up, conv_w, conv_b, w_down):
    u = causal_dwconv(h @ w_up, conv_w, conv_b)
    g, v = jnp.split(u, 2, axis=-1)
    return (jax.nn.gelu(g, approximate=True) * v) @ w_down


def setup_inputs(seed: int = 0) -> dict:
    key = jax.random.key(seed)
    ks = iter(jax.random.split(key, 32))
    f32 = jnp.float32

    def nrm(shape, scale):
        return jax.random.normal(next(ks), shape, f32) * scale

    def gain(shape):
        return 1.0 + 0.02 * jax.random.normal(next(ks), shape, f32)

    a0 = jax.random.uniform(next(ks), (N_REC, LRU_WIDTH), f32, 0.9, 0.999)
    return {
        "x": jax.random.normal(next(ks), (BATCH, SEQ, D_MODEL), f32),
        "mix_norm": gain((DEPTH, D_MODEL)),
        "ffn_norm": gain((DEPTH, D_MODEL)),
        "attn_w_qkv": nrm((N_ATTN, D_MODEL, QKV_DIM), D_MODEL ** -0.5),
        "attn_q_gain": gain((N_ATTN, HEAD_DIM)),
        "attn_k_gain": gain((N_ATTN, HEAD_DIM)),
        "attn_sinks": nrm((N_ATTN, N_Q_HEADS), 0.5),
        "attn_w_o": nrm((N_ATTN, N_Q_HEADS * HEAD_DIM, D_MODEL), (N_Q_HEADS * HEAD_DIM) ** -0.5),
        "rec_w_in": nrm((N_REC, D_MODEL, 2 * LRU_WIDTH), D_MODEL ** -0.5),
        "rec_conv_w": nrm((N_REC, LRU_CONV, LRU_WIDTH), LRU_CONV ** -0.5),
        "rec_conv_b": nrm((N_REC, LRU_WIDTH), 0.01),
        "rec_w_a": nrm((N_REC, LRU_BLOCKS, LRU_BLOCK_DIM, LRU_BLOCK_DIM), LRU_BLOCK_DIM ** -0.5),
        "rec_b_a": nrm((N_REC, LRU_BLOCKS, LRU_BLOCK_DIM), 0.01),
        "rec_w_i": nrm((N_REC, LRU_BLOCKS, LRU_BLOCK_DIM, LRU_BLOCK_DIM), LRU_BLOCK_DIM ** -0.5),
        "rec_b_i": nrm((N_REC, LRU_BLOCKS, LRU_BLOCK_DIM), 0.01),
        "rec_lambda": jnp.log(a0) - jnp.log1p(-a0),
        "rec_w_out": nrm((N_REC, LRU_WIDTH, D_MODEL), LRU_WIDTH ** -0.5),
        "ffn_w_up": nrm((DEPTH, D_MODEL, 2 * D_FF), D_MODEL ** -0.5),
        "ffn_conv_w": nrm((DEPTH, FFN_CONV, 2 * D_FF), FFN_CONV ** -0.5),
        "ffn_conv_b": nrm((DEPTH, 2 * D_FF), 0.01),
        "ffn_w_down": nrm((DEPTH, D_FF, D_MODEL), D_FF ** -0.5),
    }


def reference(x, mix_norm, ffn_norm, attn_w_qkv, attn_q_gain, attn_k_gain, attn_sinks, attn_w_o,
              rec_w_in, rec_conv_w, rec_conv_b, rec_w_a, rec_b_a, rec_w_i, rec_b_i, rec_lambda,
              rec_w_out, ffn_w_up, ffn_conv_w, ffn_conv_b, ffn_w_down):
    for layer in range(DEPTH):
        h = rmsnorm(x, mix_norm[layer])
        j = layer // N_MIXERS
        if layer % N_MIXERS == 0:
            x = x + sliding_window_attention(h, attn_w_qkv[j], attn_q_gain[j], attn_k_gain[j],
                                             attn_sinks[j], attn_w_o[j])
        else:
            x = x + rglru_block(h, rec_w_in[j], rec_conv_w[j], rec_conv_b[j], rec_w_a[j], rec_b_a[j],
                                rec_w_i[j], rec_b_i[j], rec_lambda[j], rec_w_out[j])
        h = rmsnorm(x, ffn_norm[layer])
        x = x + conv_ffn(h, ffn_w_up[layer], ffn_conv_w[layer], ffn_conv_b[layer], ffn_w_down[layer])
    return x
```

```python
import contextlib
import numpy as np
import concourse.bass as bass
import concourse.mybir as mybir
from concourse.bass_utils import run_bass_kernel_spmd

F32 = mybir.dt.float32
BF16 = mybir.dt.bfloat16
AF = mybir.ActivationFunctionType
ALU = mybir.AluOpType

NCORES = 8
D = 2048
KC = 16
T = 8192
NT = 1024
DFF = 6144
EPS = 1e-6


class Res:
    __slots__ = ("name", "last_w", "readers", "sem_w", "nw", "sem_r", "nr")

    def __init__(self, name):
        self.name = name
        self.last_w = None
        self.readers = {}
        self.sem_w = None
        self.nw = 0
        self.sem_r = None
        self.nr = 0


class Op:
    __slots__ = ("eng", "fn", "deps", "need_inc", "sem", "semval", "dma", "inc")

    def __init__(self, eng, fn, dma):
        self.eng = eng
        self.fn = fn
        self.deps = []
        self.need_inc = False
        self.sem = None
        self.semval = 0
        self.dma = dma
        self.inc = 1


class Prog:
    ENGS = ("pe", "act", "dve", "pool", "sp")

    def __init__(self, nc):
        self.nc = nc
        self.ops = {e: [] for e in self.ENGS}
        self.esem = {e: nc.alloc_semaphore(name=f"sem_{e}") for e in self.ENGS}
        self.nsem = 0
        self.final = []

    def _newsem(self):
        self.nsem += 1
        return self.nc.alloc_semaphore(name=f"dsem{self.nsem}")

    def add(self, eng, fn, reads=(), writes=(), dma=None, relaxed=False):
        op = Op(eng, fn, dma)
        deps = []
        for r in reads:
            if r.last_w is not None:
                deps.append(r.last_w)
        for w in writes:
            if w.last_w is not None:
                if not (relaxed and w.last_w.dma is None and dma is None and w.last_w.eng == eng):
                    deps.append(w.last_w)
            deps.extend(w.readers.values())
        for d in deps:
            if d is op:
                continue
            if d.dma is None and op.dma is None and d.eng == eng == "pe":
                continue
            if d not in op.deps:
                op.deps.append(d)
                d.need_inc = True
        for r in reads:
            key = eng if dma is None else id(op)
            r.readers[key] = op
        for w in writes:
            w.last_w = op
            w.readers = {}
        if dma == "w":
            res = writes[0]
            if res.sem_w is None:
                res.sem_w = self._newsem()
            res.nw += 1
            op.sem, op.semval, op.inc = res.sem_w, 16 * res.nw, 16
            op.need_inc = True
        elif dma == "r":
            res = reads[0]
            if res.sem_r is None:
                res.sem_r = self._newsem()
            res.nr += 1
            op.sem, op.semval, op.inc = res.sem_r, 16 * res.nr, 16
            op.need_inc = True
            self.final.append(op)
        self.ops[eng].append(op)
        return op

    def emit(self):
        nc = self.nc
        for e in self.ENGS:
            cnt = 0
            for op in self.ops[e]:
                if op.dma is None:
                    op.sem = self.esem[e]
                    if op.need_inc:
                        cnt += 1
                        op.semval = cnt
        final = self.final

        def run(e, h):
            waited = {}
            for op in self.ops[e]:
                need = {}
                for d in op.deps:
                    k = id(d.sem)
                    if k not in need or need[k][1] < d.semval:
                        need[k] = (d.sem, d.semval)
                for k, (s, v) in need.items():
                    if waited.get(k, 0) >= v:
                        continue
                    h.wait_ge(s, v)
                    waited[k] = v
                ins = op.fn(h)
                if op.need_inc:
                    ins.then_inc(op.sem, op.inc)
            if e == "sp":
                need = {}
                for d in final:
                    k = id(d.sem)
                    if k not in need or need[k][1] < d.semval:
                        need[k] = (d.sem, d.semval)
                for k, (s, v) in need.items():
                    h.wait_ge(s, v)

        with nc.Block() as block:
            @block.tensor
            def _(h):
                run("pe", h)

            @block.scalar
            def _(h):
                run("act", h)

            @block.vector
            def _(h):
                run("dve", h)

            @block.gpsimd
            def _(h):
                run("pool", h)

            @block.sync
            def _(h):
                run("sp", h)


class Ctx:
    def __init__(self):
        self.nc = bass.Bass("TRN2", target_bir_lowering=False)
        self.p = Prog(self.nc)
        self.stack = contextlib.ExitStack()

    def sb(self, name, shape, dt):
        return self.stack.enter_context(self.nc.sbuf_tensor(name, shape, dt))

    def psum(self):
        return self.stack.enter_context(self.nc.psum_tensor("ps", [128, 4096], F32))

    def din(self, name, shape, dt=F32):
        return self.nc.dram_tensor(name, list(shape), dt, kind="ExternalInput").ap()

    def dout(self, name, shape, dt=F32):
        return self.nc.dram_tensor(name, list(shape), dt, kind="ExternalOutput").ap()

    def finish(self):
        self.p.emit()
        self.stack.close()
        return self.nc


def token_tiles(ncols, first):
    tiles = []
    c = 0
    if first:
        tiles.append((0, first))
        c = first
    while c < ncols:
        tiles.append((c, min(c + 512, ncols)))
        c = tiles[-1][1]
    return tiles


def emit_rmsnorm(cx, ps, ps_res, x, xres, gain, prm_res, h, hres, ncols, ones, ones_res, scr, extra_ps_res=(), use_ln=False):
    p = cx.p
    tiles = token_tiles(ncols, 0)
    for kc in range(KC):
        sq, sqr = scr["sq"][kc % 2], scr["sqr"][kc % 2]
        p.add("act", lambda e, kc=kc, sq=sq: e.activation(out=sq[:, 0:ncols], in_=x[:, kc, 0:ncols], func=AF.Square),
              reads=xres[kc], writes=[sqr])
        for (c0, c1) in tiles:
            p.add("pe", lambda e, kc=kc, sq=sq, c0=c0, c1=c1: e.matmul(
                ps[:, c0:c1], lhsT=ones[:, :], rhs=sq[:, c0:c1], start=(kc == 0), stop=(kc == KC - 1)),
                reads=[sqr, ones_res], writes=[ps_res] + list(extra_ps_res))
    rstd, rres = scr["rstd"], scr["rstdr"]
    sq, sqr = scr["sq"][0], scr["sqr"][0]
    if use_ln:
        p.add("act", lambda e: e.activation(out=sq[:, 0:ncols], in_=ps[:, 0:ncols], func=AF.Ln,
                                            bias=scr["eps"][:, 0:1], scale=1.0 / D),
              reads=[ps_res, prm_res] + list(extra_ps_res), writes=[sqr])
        p.add("act", lambda e: e.activation(out=rstd[:, 0:ncols], in_=sq[:, 0:ncols], func=AF.Exp, scale=-0.5),
              reads=[sqr], writes=[rres])
    else:
        p.add("act", lambda e: e.activation(out=sq[:, 0:ncols], in_=ps[:, 0:ncols], func=AF.Sqrt,
                                            bias=scr["eps"][:, 0:1], scale=1.0 / D),
              reads=[ps_res, prm_res] + list(extra_ps_res), writes=[sqr])
        p.add("dve", lambda e: e.reciprocal(out=rstd[:, 0:ncols], in_=sq[:, 0:ncols]), reads=[sqr], writes=[rres])
    for kc in range(KC):
        p.add("dve", lambda e, kc=kc: e.scalar_tensor_tensor(
            out=h[:, kc, 0:ncols], in0=x[:, kc, 0:ncols], scalar=gain[:, kc:kc + 1], in1=rstd[:, 0:ncols],
            op0=ALU.mult, op1=ALU.mult), reads=list(xres[kc]) + [rres, prm_res], writes=[hres[kc]])


def build_ffn(rec_prologue):
    cx = Ctx()
    nc, p = cx.nc, cx.p
    NCOL = NT + 2
    xT = cx.din("xT", [D, NCOL])
    w_up = cx.din("w_up", [D, 2 * DFF])
    w_down = cx.din("w_down", [DFF, D])
    prm_d = cx.din("prm", [128, 16 + 288 + 96])
    yT = cx.dout("yT", [D, NT])
    if rec_prologue:
        g1T = cx.din("g1T", [D, NCOL])
        zT = cx.din("zT", [D, NCOL])
        carr = cx.din("carr", [128, 2 * KC * 8])
        msk = cx.din("msk", [128, 16])
        w_out = cx.din("w_out", [D, D])

    ps = cx.psum()
    x = cx.sb("x", [128, KC, NCOL], F32)
    h = cx.sb("h", [128, KC, NCOL], BF16)
    sq = [cx.sb(f"sq{i}", [128, NCOL], F32) for i in range(2)]
    rstd = cx.sb("rstd", [128, NCOL], F32)
    prm = cx.sb("prm_sb", [128, 16 + 288 + 96], F32)
    ones = cx.sb("ones", [128, 128], F32)
    epsb = cx.sb("epsb", [128, 1], F32)
    wup = [cx.sb(f"wup{i}", [128, KC, 2, 128], BF16) for i in range(3)]
    wdn = [cx.sb(f"wdn{i}", [128, 4, D], BF16) for i in range(2)]
    act = [cx.sb(f"act{i}", [128, NT], BF16) for i in range(8)]
    cg = cx.sb("cg", [128, NCOL], F32)
    cv = cx.sb("cv", [128, NCOL], F32)
    gg = cx.sb("gg", [128, NT], F32)

    xres = [[Res(f"x{m}_{tt}") for tt in range(3)] for m in range(KC)]
    hres = [Res(f"h{k}") for k in range(KC)]
    sqr = [Res("sq0"), Res("sq1")]
    rstdr = Res("rstd")
    prm_r = Res("prm")
    ones_r = Res("ones")
    wupr = [[Res(f"wup{i}_{hf}") for hf in range(2)] for i in range(3)]
    wdnr = [Res(f"wdn{i}") for i in range(2)]
    actr = [Res(f"act{i}") for i in range(8)]
    cgr, cvr, ggr = Res("cg"), Res("cv"), Res("gg")
    psX, psY = Res("psX"), Res("psY")
    bankr = [Res(f"bank{i}") for i in range(2)]
    OX, OY = 0, 1536
    tilesX = token_tiles(NCOL, 0)
    tilesY = tilesX
    BANK = [3072, 3584]

    gain = prm[:, 0:16]
    cw = prm[:, 16:16 + 288]
    cb = prm[:, 304:400]
    scr = {"sq": sq, "sqr": sqr, "rstd": rstd, "rstdr": rstdr, "eps": epsb}

    p.add("pool", lambda e: e.memset(ones[:, :], 1.0), writes=[ones_r])
    p.add("pool", lambda e: e.memset(epsb[:, :], EPS), writes=[prm_r])
    p.add("sp", lambda e: e.dma_start(out=prm[:, :], in_=prm_d[:, :]), writes=[prm_r], dma="w")
    xv = xT.rearrange("(kc p) n -> p kc n", p=128)
    for kc in range(KC):
        p.add("sp", lambda e, kc=kc: e.dma_start(out=x[:, kc, :], in_=xv[:, kc, :]),
              writes=xres[kc], dma="w")

    if rec_prologue:
        cin = cx.sb("cin", [128, 2 * KC * 8], F32)
        mk = cx.sb("mk", [128, 16], F32)
        ca = cx.sb("ca", [128, 2, KC, 8], F32)
        ch = cx.sb("ch", [128, 2, KC, 8], F32)
        cs = cx.sb("cs", [128, 2, KC, 8], F32)
        cin_r, mk_r, ca_r, ch_r, cs_r = Res("cin"), Res("mk"), Res("ca"), Res("ch"), Res("cs")
        p.add("sp", lambda e: e.dma_start(out=cin[:, :], in_=carr[:, :]), writes=[cin_r], dma="w")
        p.add("sp", lambda e: e.dma_start(out=mk[:, :], in_=msk[:, :]), writes=[mk_r], dma="w")
        for w in range(2):
            for kc in range(KC):
                a_in = cin[:, kc * 8:(kc + 1) * 8]
                h_in = cin[:, KC * 8 + kc * 8: KC * 8 + (kc + 1) * 8]
                m = mk[:, w * 8:(w + 1) * 8]
                p.add("dve", lambda e, a_in=a_in, m=m, w=w, kc=kc: e.scalar_tensor_tensor(
                    out=ca[:, w, kc, :], in0=a_in, scalar=-1.0, in1=m, op0=ALU.add, op1=ALU.mult),
                    reads=[cin_r, mk_r], writes=[ca_r])
                p.add("dve", lambda e, w=w, kc=kc: e.tensor_scalar(
                    out=ca[:, w, kc, :], in0=ca[:, w, kc, :], scalar1=1.0, scalar2=None, op0=ALU.add),
                    reads=[ca_r], writes=[ca_r])
                p.add("dve", lambda e, h_in=h_in, m=m, w=w, kc=kc: e.tensor_tensor(
                    out=ch[:, w, kc, :], in0=h_in, in1=m, op=ALU.mult),
                    reads=[cin_r, mk_r], writes=[ch_r])
                p.add("dve", lambda e, w=w, kc=kc: e.tensor_tensor_scan(
                    out=cs[:, w, kc, :], data0=ca[:, w, kc, :], data1=ch[:, w, kc, :], initial=0.0,
                    op0=ALU.mult, op1=ALU.add), reads=[ca_r, ch_r], writes=[cs_r])
        gz = [(cg, cgr), (cv, cvr), (gg, ggr), (rstd, rstdr)]
        g1v = g1T.rearrange("(kc p) n -> p kc n", p=128)
        zv = zT.rearrange("(kc p) n -> p kc n", p=128)
        zst = [cg, cv]
        zstr = [cgr, cvr]
        for kc in range(KC):
            gb, gr = sq[kc % 2], sqr[kc % 2]
            zb, zr = zst[kc % 2], zstr[kc % 2]
            p.add("sp", lambda e, kc=kc, gb=gb: e.dma_start(out=gb[:, :], in_=g1v[:, kc, :]), writes=[gr], dma="w")
            p.add("sp", lambda e, kc=kc, zb=zb: e.dma_start(out=zb[:, :], in_=zv[:, kc, :]), writes=[zr], dma="w")
            p.add("dve", lambda e, kc=kc, gb=gb, zb=zb: e.scalar_tensor_tensor(
                out=h[:, kc, 0:2], in0=zb[:, 0:2], scalar=cs[:, 1, kc, 7:8], in1=gb[:, 0:2],
                op0=ALU.mult, op1=ALU.add), reads=[gr, zr, cs_r], writes=[hres[kc]])
            p.add("dve", lambda e, kc=kc, gb=gb, zb=zb: e.scalar_tensor_tensor(
                out=h[:, kc, 2:NCOL], in0=zb[:, 2:NCOL], scalar=cs[:, 0, kc, 7:8], in1=gb[:, 2:NCOL],
                op0=ALU.mult, op1=ALU.add), reads=[gr, zr, cs_r], writes=[hres[kc]])
        wo = [wup[i][:, :, 0, :] for i in range(3)]
        wor = [wupr[i][0] for i in range(3)]
        wov = w_out.rearrange("(kc p) n -> p kc n", p=128)
        for m in range(KC):
            s = m % 3
            p.add("pool", lambda e, m=m, s=s: e.dma_start(out=wo[s], in_=wov[:, :, m * 128:(m + 1) * 128]),
                  writes=[wor[s]], dma="w")
            O, tl, pr = (OX, tilesX, psX) if m % 2 == 0 else (OY, tilesY, psY)
            for kc in range(KC):
                for (c0, c1) in tl:
                    p.add("pe", lambda e, s=s, kc=kc, c0=c0, c1=c1, O=O: e.matmul(
                        ps[:, O + c0:O + c1], lhsT=wo[s][:, kc, :], rhs=h[:, kc, c0:c1],
                        start=(kc == 0), stop=(kc == KC - 1)), reads=[wor[s], hres[kc]], writes=[pr])
            p.add("dve", lambda e, m=m, O=O: e.tensor_tensor(
                out=x[:, m, :], in0=ps[:, O:O + NCOL], in1=x[:, m, :], op=ALU.add),
                reads=[pr] + xres[m], writes=xres[m])

    emit_rmsnorm(cx, ps, psX, x, xres, gain, prm_r, h, hres, NCOL, ones, ones_r, scr)

    wupv = w_up.rearrange("(kc p) (two c) -> p kc two c", p=128, two=2)

    def load_up(j):
        s = j % 3
        for half in range(2):
            p.add("pool", lambda e, j=j, s=s, half=half: e.dma_start(
                out=wup[s][:, :, half, :], in_=wupv[:, :, half, j * 128:(j + 1) * 128]),
                writes=[wupr[s][half]], dma="w")

    def load_dn(q):
        s = q % 2
        src = w_down[q * 512:(q + 1) * 512, :].rearrange("(k p) n -> p k n", p=128)
        p.add("pool", lambda e, s=s, src=src: e.dma_start(out=wdn[s][:, :, :], in_=src), writes=[wdnr[s]], dma="w")

    def up_pair(j):
        s = j % 3
        slot = j % 8
        for half in range(2):
            O, tl, pr = (OX, tilesX, psX) if half == 0 else (OY, tilesY, psY)
            ch = half * 48 + j
            for kc in range(KC):
                for (c0, c1) in tl:
                    p.add("pe", lambda e, s=s, kc=kc, half=half, c0=c0, c1=c1, O=O: e.matmul(
                        ps[:, O + c0:O + c1], lhsT=wup[s][:, kc, half, :], rhs=h[:, kc, c0:c1],
                        start=(kc == 0), stop=(kc == KC - 1)), reads=[wupr[s][half], hres[kc]], writes=[pr])
            c, cr = (cg, cgr) if half == 0 else (cv, cvr)
            P = ps[:, O:O + NCOL]
            p.add("act", lambda e, c=c, P=P, ch=ch: e.activation(
                out=c[:, 0:NT], in_=P[:, 2:2 + NT], func=AF.Identity,
                bias=cb[:, ch:ch + 1], scale=cw[:, 192 + ch:192 + ch + 1]), reads=[pr, prm_r], writes=[cr])
            p.add("dve", lambda e, c=c, P=P, ch=ch: e.scalar_tensor_tensor(
                out=c[:, 0:NT], in0=P[:, 1:1 + NT], scalar=cw[:, 96 + ch:96 + ch + 1], in1=c[:, 0:NT],
                op0=ALU.mult, op1=ALU.add), reads=[pr, prm_r, cr], writes=[cr])
            p.add("dve", lambda e, c=c, P=P, ch=ch: e.scalar_tensor_tensor(
                out=c[:, 0:NT], in0=P[:, 0:NT], scalar=cw[:, ch:ch + 1], in1=c[:, 0:NT],
                op0=ALU.mult, op1=ALU.add), reads=[pr, prm_r, cr], writes=[cr])
            if half == 0:
                p.add("act", lambda e: e.activation(out=gg[:, :], in_=cg[:, 0:NT], func=AF.Gelu_apprx_tanh),
                      reads=[cgr], writes=[ggr])
        p.add("pool", lambda e, slot=slot: e.tensor_tensor(out=act[slot][:, :], in0=gg[:, :], in1=cv[:, 0:NT], op=ALU.mult),
              reads=[ggr, cvr], writes=[actr[slot]])

    def down_part(q, part, last):
        s = q % 2
        for m in range(part * 4, part * 4 + 4):
            for tt in range(2):
                b = (m * 2 + tt) % 2
                for k in range(4):
                    slot = (q * 4 + k) % 8
                    p.add("pe", lambda e, s=s, k=k, m=m, tt=tt, b=b, slot=slot: e.matmul(
                        ps[:, BANK[b]:BANK[b] + 512], lhsT=wdn[s][:, k, m * 128:(m + 1) * 128],
                        rhs=act[slot][:, tt * 512:(tt + 1) * 512], start=(k == 0), stop=(k == 3)),
                        reads=[wdnr[s], actr[slot]], writes=[bankr[b]])
                p.add("dve", lambda e, m=m, tt=tt, b=b: e.tensor_tensor(
                    out=x[:, m, 2 + tt * 512:2 + (tt + 1) * 512], in0=ps[:, BANK[b]:BANK[b] + 512],
                    in1=x[:, m, 2 + tt * 512:2 + (tt + 1) * 512], op=ALU.add),
                    reads=[bankr[b], xres[m][1 + tt]], writes=[xres[m][1 + tt]])
            if last:
                yv = yT.rearrange("(kc p) n -> p kc n", p=128)
                p.add("sp", lambda e, m=m: e.dma_start(out=yv[:, m, :], in_=x[:, m, 2:2 + NT]),
                      reads=[xres[m][1], xres[m][2]], dma="r")

    NP = 48
    load_up(0)
    load_up(1)
    load_dn(0)
    for j in range(NP + 4):
        if j < NP:
            if j + 2 < NP:
                load_up(j + 2)
            up_pair(j)
        q = j // 4 - 1
        if q >= 0:
            if j % 4 == 0 and q + 1 < NP // 4:
                load_dn(q + 1)
            down_part(q, j % 4, last=(q == NP // 4 - 1))
    return cx.finish()


def chunked(v):
    v = np.asarray(v, np.float32)
    return np.ascontiguousarray(v.reshape(-1, 128).T)


def ffn_params(g, cw, cb):
    parts = [chunked(g)] + [chunked(cw[k]) for k in range(3)] + [chunked(cb)]
    return np.ascontiguousarray(np.concatenate(parts, axis=1))


def build_rec1():
    cx = Ctx()
    nc, p = cx.nc, cx.p
    HALO = 3
    NCOL = NT + HALO
    xT = cx.din("xT", [D, NCOL])
    w_in = cx.din("w_in", [D, 2 * D])
    w_a = cx.din("w_a", [8, 256, 256])
    w_i = cx.din("w_i", [8, 256, 256])
    NPRM = 16 + 64 + 16 * 4
    prm_d = cx.din("prm", [128, NPRM])
    g1T = cx.dout("g1T", [D, NT])
    zT = cx.dout("zT", [D, NT])
    carr_o = cx.dout("carr", [128, 32])

    ps = cx.psum()
    x = cx.sb("x", [128, KC, NCOL], F32)
    h = cx.sb("h", [128, KC, NCOL], BF16)
    sq = [cx.sb(f"sq{i}", [128, NCOL], F32) for i in range(2)]
    rstd = cx.sb("rstd", [128, NCOL], F32)
    prm = cx.sb("prm_sb", [128, NPRM], F32)
    ones = cx.sb("ones", [128, 128], F32)
    epsb = cx.sb("epsb", [128, 1], F32)
    win = [cx.sb(f"win{i}", [128, KC, 128], BF16) for i in range(4)]
    wg = [[cx.sb(f"wg{i}_{j}", [128, 2, 256], BF16) for j in range(2)] for i in range(2)]
    xc = [cx.sb(f"xc{i}", [128, NT], F32) for i in range(2)]
    xcb = [cx.sb(f"xcb{i}", [128, NT], BF16) for i in range(2)]
    gate = [cx.sb(f"gate{i}", [128, NT], F32) for i in range(2)]
    tr = cx.sb("tr", [128, NT], F32)
    ta = cx.sb("ta", [128, NT], F32)
    tm = cx.sb("tm", [128, NT], F32)
    ti = cx.sb("ti", [128, NT], F32)
    ths = cx.sb("ths", [128, NT], F32)
    tac = cx.sb("tac", [128, NT], F32)
    tg1 = cx.sb("tg1", [128, NT], F32)
    tz = cx.sb("tz", [128, NT], F32)
    zeros = cx.sb("zeros", [128, NT], F32)
    cl = cx.sb("cl", [128, 48], F32)
    carr = cx.sb("carr_sb", [128, 32], F32)

    xres = [[Res(f"x{m}")] for m in range(KC)]
    hres = [Res(f"h{k}") for k in range(KC)]
    sqr = [Res("sq0"), Res("sq1")]
    rstdr, prm_r, ones_r = Res("rstd"), Res("prm"), Res("ones")
    winr = [Res(f"win{i}") for i in range(4)]
    wgr = [[Res(f"wg{i}_{j}") for j in range(2)] for i in range(2)]
    xcr = [Res("xc0"), Res("xc1")]
    xcbr = [Res("xcb0"), Res("xcb1")]
    gater = [Res("gate0"), Res("gate1")]
    trr, tar, tmr, tir, thsr, tacr, tg1r, tzr = (Res(n) for n in ("tr", "ta", "tm", "ti", "ths", "tac", "tg1", "tz"))
    zer_r, cl_r, carr_r = Res("zeros"), Res("cl"), Res("carr")
    psX, psY = Res("psX"), Res("psY")
    bankr = [Res(f"bank{i}") for i in range(3)]
    BANK = [2560, 3072, 3584]
    tilesX = token_tiles(NCOL, 0)
    OY = 1536

    gain = prm[:, 0:16]
    cw = prm[:, 16:80]
    cb = prm[:, 80:96]
    ba = prm[:, 96:112]
    bi = prm[:, 112:128]
    lam = prm[:, 128:144]
    scr = {"sq": sq, "sqr": sqr, "rstd": rstd, "rstdr": rstdr, "eps": epsb}

    p.add("pool", lambda e: e.memset(ones[:, :], 1.0), writes=[ones_r])
    p.add("pool", lambda e: e.memset(epsb[:, :], EPS), writes=[prm_r])
    p.add("pool", lambda e: e.memset(zeros[:, :], 0.0), writes=[zer_r])
    p.add("sp", lambda e: e.dma_start(out=prm[:, :], in_=prm_d[:, :]), writes=[prm_r], dma="w")
    xv = xT.rearrange("(kc p) n -> p kc n", p=128)
    for kc in range(KC):
        p.add("sp", lambda e, kc=kc: e.dma_start(out=x[:, kc, :], in_=xv[:, kc, :]), writes=xres[kc], dma="w")

    winv = w_in.rearrange("(kc p) n -> p kc n", p=128)
    nload = [0]

    def load_in(col0):
        s = nload[0] % 4
        nload[0] += 1
        p.add("pool", lambda e, s=s, col0=col0: e.dma_start(out=win[s][:, :, :], in_=winv[:, :, col0:col0 + 128]),
              writes=[winr[s]], dma="w")
        return s

    def load_g(b):
        s = b % 2
        for j, wsrc in enumerate((w_a, w_i)):
            p.add("pool", lambda e, s=s, j=j, wsrc=wsrc, b=b: e.dma_start(
                out=wg[s][j][:, :, :], in_=wsrc[b].rearrange("(ic p) n -> p ic n", p=128)),
                writes=[wgr[s][j]], dma="w")

    p.add("act", lambda e: e.activation(out=cl[:, 0:16], in_=lam, func=AF.Exp, scale=-1.0), reads=[prm_r], writes=[cl_r])
    p.add("act", lambda e: e.activation(out=cl[:, 0:16], in_=cl[:, 0:16], func=AF.Ln, bias=1.0), reads=[cl_r], writes=[cl_r])
    p.add("dve", lambda e: e.tensor_scalar(out=cl[:, 16:32], in0=cl[:, 0:16], scalar1=-8.0, scalar2=None, op0=ALU.mult),
          reads=[cl_r], writes=[cl_r])
    p.add("dve", lambda e: e.tensor_scalar(out=cl[:, 32:48], in0=cl[:, 0:16], scalar1=-16.0, scalar2=None, op0=ALU.mult),
          reads=[cl_r], writes=[cl_r])

    emit_rmsnorm(cx, ps, psX, x, xres, gain, prm_r, h, hres, NCOL, ones, ones_r, scr)

    pending = [load_in(0), load_in(D)]
    load_g(0)
    order = []
    for b in range(8):
        for c in range(2):
            ch = 2 * b + c
            order.append(ch * 128)
            order.append(D + ch * 128)
    li = 2
    for b in range(8):
        if b + 1 < 8:
            load_g(b + 1)
        for c in range(2):
            ch = 2 * b + c
            s = pending.pop(0)
            if li < len(order):
                pending.append(load_in(order[li])); li += 1
            for kc in range(KC):
                for (c0, c1) in tilesX:
                    p.add("pe", lambda e, s=s, kc=kc, c0=c0, c1=c1: e.matmul(
                        ps[:, c0:c1], lhsT=win[s][:, kc, :], rhs=h[:, kc, c0:c1],
                        start=(kc == 0), stop=(kc == KC - 1)), reads=[winr[s], hres[kc]], writes=[psX])
            P = ps[:, 0:NCOL]
            t = xc[c]
            p.add("act", lambda e, t=t, P=P, ch=ch: e.activation(
                out=t[:, :], in_=P[:, 3:3 + NT], func=AF.Identity, bias=cb[:, ch:ch + 1],
                scale=cw[:, 48 + ch:48 + ch + 1]), reads=[psX, prm_r], writes=[xcr[c]])
            for k in (2, 1, 0):
                p.add("dve", lambda e, t=t, P=P, ch=ch, k=k: e.scalar_tensor_tensor(
                    out=t[:, :], in0=P[:, k:k + NT], scalar=cw[:, k * 16 + ch:k * 16 + ch + 1], in1=t[:, :],
                    op0=ALU.mult, op1=ALU.add), reads=[psX, prm_r, xcr[c]], writes=[xcr[c]])
            p.add("pool", lambda e, c=c: e.tensor_copy(out=xcb[c][:, :], in_=xc[c][:, :]), reads=[xcr[c]], writes=[xcbr[c]])
            s = pending.pop(0)
            if li < len(order):
                pending.append(load_in(order[li])); li += 1
            for kc in range(KC):
                for tt in range(2):
                    p.add("pe", lambda e, s=s, kc=kc, tt=tt: e.matmul(
                        ps[:, OY + tt * 512:OY + (tt + 1) * 512], lhsT=win[s][:, kc, :],
                        rhs=h[:, kc, HALO + tt * 512:HALO + (tt + 1) * 512],
                        start=(kc == 0), stop=(kc == KC - 1)), reads=[winr[s], hres[kc]], writes=[psY])
            p.add("act", lambda e, c=c: e.activation(out=gate[c][:, :], in_=ps[:, OY:OY + NT], func=AF.Gelu_apprx_tanh),
                  reads=[psY], writes=[gater[c]])
        gs = b % 2
        for oc in range(2):
            ch = 2 * b + oc
            for j in range(2):
                dst, dres = (tr, trr) if j == 0 else (ti, tir)
                bias = ba if j == 0 else bi
                for tt in range(2):
                    bk = (oc * 4 + j * 2 + tt) % 3
                    for ic in range(2):
                        p.add("pe", lambda e, gs=gs, j=j, ic=ic, oc=oc, tt=tt, bk=bk: e.matmul(
                            ps[:, BANK[bk]:BANK[bk] + 512], lhsT=wg[gs][j][:, ic, oc * 128:(oc + 1) * 128],
                            rhs=xcb[ic][:, tt * 512:(tt + 1) * 512], start=(ic == 0), stop=(ic == 1)),
                            reads=[wgr[gs][j], xcbr[ic]], writes=[bankr[bk]])
                    p.add("act", lambda e, dst=dst, bias=bias, ch=ch, tt=tt, bk=bk: e.activation(
                        out=dst[:, tt * 512:(tt + 1) * 512], in_=ps[:, BANK[bk]:BANK[bk] + 512], func=AF.Sigmoid,
                        bias=bias[:, ch:ch + 1]), reads=[bankr[bk], prm_r], writes=[dres])
            p.add("act", lambda e, ch=ch: e.activation(out=ta[:, :], in_=tr[:, :], func=AF.Exp, scale=cl[:, 16 + ch:17 + ch]),
                  reads=[trr, cl_r], writes=[tar])
            p.add("act", lambda e, ch=ch: e.activation(out=tm[:, :], in_=tr[:, :], func=AF.Exp, scale=cl[:, 32 + ch:33 + ch]),
                  reads=[trr, cl_r], writes=[tmr])
            p.add("act", lambda e: e.activation(out=tm[:, :], in_=tm[:, :], func=AF.Sqrt, scale=-1.0, bias=1.0),
                  reads=[tmr], writes=[tmr])
            p.add("dve", lambda e, oc=oc: e.tensor_tensor(out=ti[:, :], in0=ti[:, :], in1=xc[oc][:, :], op=ALU.mult),
                  reads=[tir, xcr[oc]], writes=[tir])
            p.add("dve", lambda e: e.tensor_tensor(out=ti[:, :], in0=ti[:, :], in1=tm[:, :], op=ALU.mult),
                  reads=[tir, tmr], writes=[tir])
            p.add("dve", lambda e: e.tensor_tensor_scan(out=ths[:, :], data0=ta[:, :], data1=ti[:, :], initial=0.0,
                                                        op0=ALU.mult, op1=ALU.add), reads=[tar, tir], writes=[thsr])
            p.add("dve", lambda e: e.tensor_tensor_scan(out=tac[:, :], data0=ta[:, :], data1=zeros[:, :], initial=1.0,
                                                        op0=ALU.mult, op1=ALU.add), reads=[tar, zer_r], writes=[tacr])
            p.add("pool", lambda e, oc=oc: e.tensor_tensor(out=tg1[:, :], in0=ths[:, :], in1=gate[oc][:, :], op=ALU.mult),
                  reads=[thsr, gater[oc]], writes=[tg1r])
            p.add("pool", lambda e, oc=oc: e.tensor_tensor(out=tz[:, :], in0=tac[:, :], in1=gate[oc][:, :], op=ALU.mult),
                  reads=[tacr, gater[oc]], writes=[tzr])
            p.add("dve", lambda e, ch=ch: e.tensor_copy(out=carr[:, ch:ch + 1], in_=tac[:, NT - 1:NT]), reads=[tacr], writes=[carr_r])
            p.add("dve", lambda e, ch=ch: e.tensor_copy(out=carr[:, 16 + ch:17 + ch], in_=ths[:, NT - 1:NT]), reads=[thsr], writes=[carr_r])
            p.add("sp", lambda e, ch=ch: e.dma_start(out=g1T[ch * 128:(ch + 1) * 128, :], in_=tg1[:, :]), reads=[tg1r], dma="r")
            p.add("sp", lambda e, ch=ch: e.dma_start(out=zT[ch * 128:(ch + 1) * 128, :], in_=tz[:, :]), reads=[tzr], dma="r")
    p.add("sp", lambda e: e.dma_start(out=carr_o[:, :], in_=carr[:, :]), reads=[carr_r], dma="r")
    return cx.finish()


def rec_params(g, cw, cb, ba, bi, lam):
    parts = [chunked(g)] + [chunked(cw[k]) for k in range(4)] + [chunked(cb), chunked(ba.reshape(-1)), chunked(bi.reshape(-1)), chunked(lam)]
    return np.ascontiguousarray(np.concatenate(parts, axis=1))


def build_attn(stage=99):
    cx = Ctx()
    nc, p = cx.nc, cx.p
    HALO = 128
    NCOL = NT + HALO
    NTB = NCOL // 128
    xT = cx.din("xT", [D, NCOL])
    w_qkv = cx.din("w_qkv", [D, 3072])
    w_o = cx.din("w_o", [D, D])
    NPRM = 16 + 2 + 32 + 1
    prm_d = cx.din("prm", [128, NPRM])
    msk_d = cx.din("msk", [128, 1024])
    yT = cx.dout("yT", [D, NT])

    ps = cx.psum()
    x = cx.sb("x", [128, KC, NCOL], F32)
    h = cx.sb("h", [128, KC, NCOL], BF16)
    sq = [cx.sb(f"sq{i}", [128, NCOL], F32) for i in range(2)]
    rstd = cx.sb("rstd", [128, NCOL], F32)
    prm = cx.sb("prm_sb", [128, NPRM], F32)
    es = cx.sb("es", [128, 32], F32)
    ones = cx.sb("ones", [128, 128], F32)
    bd = cx.sb("bd", [128, 128], F32)
    epsb = cx.sb("epsb", [128, 1], F32)
    msk = cx.sb("msk_sb", [128, 2, 512], BF16)
    Kd = cx.sb("Kd", [128, 8, NCOL], BF16)
    Vaug = cx.sb("Vaug", [128, NTB, 8, 128], BF16)
    wbig = cx.sb("wbig", [128, 8192], BF16)
    wq = [cx.sb(f"wq{i}", [128, KC, 128], BF16) for i in range(2)]
    Qn = cx.sb("Qn", [128, 4, NT], BF16)
    AO = cx.sb("AO", [128, 2, NT], BF16)
    PT = [cx.sb(f"PT{i}", [128, 2, 512], BF16) for i in range(2)]

    xres = [[Res(f"x{m}_h"), Res(f"x{m}_0"), Res(f"x{m}_1")] for m in range(KC)]
    hres = [Res(f"h{k}") for k in range(KC)]
    sqr = [Res("sq0"), Res("sq1")]
    rstdr, prm_r, ones_r, bd_r, es_r, msk_r = (Res(n) for n in ("rstd", "prm", "ones", "bd", "es", "msk"))
    Kdr = [Res(f"Kd{g}") for g in range(8)]
    Vr = Res("Vaug")
    wbr = [Res("wb0"), Res("wb1")]
    wqr = [[Res(f"wq{i}_0"), Res(f"wq{i}_1")] for i in range(2)]
    Qnr = [Res(f"Qn{i}") for i in range(4)]
    AOr = [Res("AO0"), Res("AO1")]
    PTr = [[Res(f"PT{i}_{k}") for k in range(2)] for i in range(2)]
    sqt = [sq[0][:, 0:512], sq[0][:, 512:1024]]
    rsb = [sq[1][:, 0:512], sq[1][:, 512:1024]]
    den = rstd[:, 0:512]
    rec = rstd[:, 512:1024]
    sqtr = [Res("sqt0"), Res("sqt1")]
    rsr = [Res("rs0"), Res("rs1")]
    denr, recr = Res("den"), Res("rec")
    regr = [Res("regA"), Res("regB")]
    REG = [0, 1536]
    statr = Res("stat")
    STAT = 3072
    pvr = Res("pv")
    PVB = 3584
    SC = [1024, 2560]
    scr_ = [Res("sc0"), Res("sc1")]
    OB = [0, 512, 1536, 2048]
    obr = [Res(f"ob{i}") for i in range(4)]

    gain = prm[:, 0:16]
    qg = prm[:, 16:17]
    kg = prm[:, 17:18]
    hb = prm[:, 50:51]
    scr = {"sq": sq, "sqr": sqr, "rstd": rstd, "rstdr": rstdr, "eps": epsb}

    p.add("pool", lambda e: e.memset(ones[:, :], 1.0), writes=[ones_r])
    p.add("pool", lambda e: e.memset(epsb[:, :], EPS), writes=[prm_r])
    p.add("pool", lambda e: e.memset(bd[:, :], 0.0), writes=[bd_r])
    p.add("pool", lambda e: e.memset(bd[0:64, 0:64], 1.0), writes=[bd_r])
    p.add("pool", lambda e: e.memset(bd[64:128, 64:128], 1.0), writes=[bd_r])
    p.add("pool", lambda e: e.memset(Vaug[:, :, :, :], 1.0), writes=[Vr])
    p.add("sp", lambda e: e.dma_start(out=prm[:, :], in_=prm_d[:, :]), writes=[prm_r], dma="w")
    p.add("pool", lambda e: e.dma_start(out=msk[:, :, :], in_=msk_d.rearrange("p (k n) -> p k n", k=2)),
          writes=[msk_r], dma="w")
    xv = xT.rearrange("(kc p) n -> p kc n", p=128)
    for kc in range(KC):
        p.add("sp", lambda e, kc=kc: e.dma_start(out=x[:, kc, :], in_=xv[:, kc, :]), writes=xres[kc], dma="w")
    p.add("act", lambda e: e.activation(out=es[:, :], in_=prm[:, 18:50], func=AF.Exp), reads=[prm_r], writes=[es_r])

    wv = wbig[:, :].rearrange("p (kc n) -> p kc n", kc=KC)
    wqkv_v = w_qkv.rearrange("(kc p) n -> p kc n", p=128)
    p.add("pool", lambda e: e.dma_start(out=wv, in_=wqkv_v[:, :, 2560:3072]), writes=wbr, dma="w")

    nq = [0]

    def load_k(g):
        s = nq[0] % 2
        nq[0] += 1
        for half in range(2):
            p.add("pool", lambda e, s=s, g=g, half=half: e.dma_start(
                out=wq[s][:, :, half * 64:(half + 1) * 64], in_=wqkv_v[:, :, 2048 + g * 64:2048 + (g + 1) * 64]),
                writes=[wqr[s][half]], dma="w")
        return s

    def load_q(hd):
        s = nq[0] % 2
        nq[0] += 1
        for half in range(2):
            p.add("pool", lambda e, s=s, hd=hd, half=half: e.dma_start(
                out=wq[s][:, :, half * 64:(half + 1) * 64], in_=wqkv_v[:, :, hd * 64:(hd + 1) * 64]),
                writes=[wqr[s][half]], dma="w")
        return s

    def load_o(g):
        s = g % 2
        dst = wbig[:, s * 4096:(s + 1) * 4096].rearrange("p (c n) -> p c n", c=2)
        src = w_o[g * 256:(g + 1) * 256, :].rearrange("(c p) n -> p c n", p=128)
        p.add("pool", lambda e, dst=dst, src=src: e.dma_start(out=dst, in_=src), writes=[wbr[s]], dma="w")

    emit_rmsnorm(cx, ps, regr[0], x, xres, gain, prm_r, h, hres, NCOL, ones, ones_r, scr)

    nstat = [0]

    def proj_norm(s, ri, tiles, hoff, gain_ap, dst_fn, dst_res):
        R = REG[ri]
        for kc in range(KC):
            for (c0, c1) in tiles:
                p.add("pe", lambda e, s=s, kc=kc, c0=c0, c1=c1, R=R: e.matmul(
                    ps[:, R + c0:R + c1], lhsT=wq[s][:, kc, :], rhs=h[:, kc, hoff + c0:hoff + c1],
                    start=(kc == 0), stop=(kc == KC - 1)), reads=wqr[s] + [hres[kc]], writes=[regr[ri]])
        for (c0, c1) in tiles:
            n = c1 - c0
            i = nstat[0] % 2
            nstat[0] += 1
            src = ps[:, R + c0:R + c1]
            p.add("act", lambda e, i=i, n=n, src=src: e.activation(out=sqt[i][:, 0:n], in_=src, func=AF.Square),
                  reads=[regr[ri]], writes=[sqtr[i]])
            p.add("pe", lambda e, i=i, n=n: e.matmul(ps[:, STAT:STAT + n], lhsT=bd[:, :], rhs=sqt[i][:, 0:n],
                                                     start=True, stop=True), reads=[bd_r, sqtr[i]], writes=[statr])
            p.add("act", lambda e, i=i, n=n: e.activation(out=sqt[i][:, 0:n], in_=ps[:, STAT:STAT + n], func=AF.Sqrt,
                                                          bias=epsb[:, 0:1], scale=1.0 / 64), reads=[statr, prm_r], writes=[sqtr[i]])
            p.add("dve", lambda e, i=i, n=n: e.reciprocal(out=rsb[i][:, 0:n], in_=sqt[i][:, 0:n]), reads=[sqtr[i]], writes=[rsr[i]])
            dst = dst_fn(c0, c1)
            p.add("dve", lambda e, i=i, n=n, src=src, dst=dst: e.scalar_tensor_tensor(
                out=dst, in0=src, scalar=gain_ap, in1=rsb[i][:, 0:n], op0=ALU.mult, op1=ALU.mult),
                reads=[regr[ri], rsr[i], prm_r], writes=[dst_res], relaxed=True)

    BV = [1536, 2048, 2560]
    bvr = [Res(f"bv{i}") for i in range(3)]
    for tb in range(NTB):
        b = tb % 3
        for kc in range(KC):
            p.add("pe", lambda e, tb=tb, kc=kc, b=b: e.matmul(
                ps[:, BV[b]:BV[b] + 512], lhsT=h[:, kc, tb * 128:(tb + 1) * 128], rhs=wv[:, kc, :],
                start=(kc == 0), stop=(kc == KC - 1)), reads=wbr + [hres[kc]], writes=[bvr[b]])
        p.add("act", lambda e, tb=tb, b=b: e.activation(
            out=Vaug[:, tb, :, 0:64], in_=ps[:, BV[b]:BV[b] + 512].rearrange("p (g d) -> p g d", g=8), func=AF.Identity),
            reads=[bvr[b], regr[1]], writes=[Vr], relaxed=True)

    if stage <= 1:
        return cx.finish()
    tilesK = token_tiles(NCOL, 0)
    ks = load_k(0)
    for g in range(8):
        s = ks
        if g + 1 < 8:
            ks = load_k(g + 1)
        proj_norm(s, g % 2, tilesK, 0, kg, lambda c0, c1, g=g: Kd[:, g, c0:c1], Kdr[g])

    if stage <= 2:
        return cx.finish()
    tilesQ = token_tiles(NT, 0)
    qs = [load_q(0), load_q(1)]
    load_o(0)
    for g in range(8):
        for hh in range(4):
            s = qs.pop(0)
            proj_norm(s, hh % 2, tilesQ, HALO, qg, lambda c0, c1, hh=hh: Qn[:, hh, c0:c1], Qnr[hh])
            nxt = g * 4 + hh + 2
            if nxt < 32:
                qs.append(load_q(nxt))
        if g + 1 < 8:
            load_o(g + 1)
        if stage <= 3:
            return cx.finish()
        for qb in range(8):
            i = qb % 2
            for kb in range(2):
                for hh in range(4):
                    p.add("pe", lambda e, g=g, qb=qb, kb=kb, hh=hh: e.matmul(
                        ps[:, SC[kb] + hh * 128:SC[kb] + (hh + 1) * 128],
                        lhsT=Kd[:, g, (qb + kb) * 128:(qb + kb + 1) * 128],
                        rhs=Qn[:, hh, qb * 128:(qb + 1) * 128], start=True, stop=True),
                        reads=[Kdr[g], Qnr[hh]], writes=[scr_[kb]])
                bias = hb if (qb == 0 and kb == 0) else 0.0
                p.add("act", lambda e, i=i, kb=kb, bias=bias: e.activation(
                    out=PT[i][:, kb, :], in_=ps[:, SC[kb]:SC[kb] + 512], func=AF.Exp, bias=bias, scale=0.0625),
                    reads=[scr_[kb], prm_r], writes=[PTr[i][kb]])
                p.add("pool", lambda e, i=i, kb=kb: e.tensor_tensor(
                    out=PT[i][:, kb, :], in0=PT[i][:, kb, :], in1=msk[:, kb, :], op=ALU.mult),
                    reads=[PTr[i][kb], msk_r], writes=[PTr[i][kb]])
            for kb in range(2):
                p.add("pe", lambda e, g=g, qb=qb, kb=kb, i=i: e.matmul(
                    ps[:, PVB:PVB + 512], lhsT=Vaug[:, qb + kb, g, :], rhs=PT[i][:, kb, :],
                    start=(kb == 0), stop=(kb == 1)), reads=[Vr, PTr[i][kb]], writes=[pvr])
            for hh in range(4):
                p.add("dve", lambda e, g=g, hh=hh: e.tensor_scalar(
                    out=den[64:128, hh * 128:(hh + 1) * 128], in0=ps[64:128, PVB + hh * 128:PVB + (hh + 1) * 128],
                    scalar1=es[64:128, g * 4 + hh:g * 4 + hh + 1], scalar2=None, op0=ALU.add),
                    reads=[pvr, es_r], writes=[denr], relaxed=True)
            p.add("dve", lambda e: e.reciprocal(out=rec[0:64, :], in_=den[64:128, :]), reads=[denr], writes=[recr])
            for hh in range(4):
                pb = (hh % 2) * 64
                c = hh // 2
                p.add("dve", lambda e, hh=hh, pb=pb, c=c, qb=qb: e.tensor_tensor(
                    out=AO[pb:pb + 64, c, qb * 128:(qb + 1) * 128], in0=ps[0:64, PVB + hh * 128:PVB + (hh + 1) * 128],
                    in1=rec[0:64, hh * 128:(hh + 1) * 128], op=ALU.mult),
                    reads=[pvr, recr], writes=[AOr[c]], relaxed=True)
        if stage <= 4:
            return cx.finish()
        so = g % 2
        wo = wbig[:, so * 4096:(so + 1) * 4096].rearrange("p (c n) -> p c n", c=2)
        for m in range(KC):
            for tt in range(2):
                b = (m * 2 + tt) % 4
                for c in range(2):
                    p.add("pe", lambda e, wo=wo, m=m, tt=tt, b=b, c=c: e.matmul(
                        ps[:, OB[b]:OB[b] + 512], lhsT=wo[:, c, m * 128:(m + 1) * 128],
                        rhs=AO[:, c, tt * 512:(tt + 1) * 512], start=(c == 0), stop=(c == 1)),
                        reads=[wbr[so], AOr[c]], writes=[obr[b], regr[0 if b < 2 else 1]])
                xs = x[:, m, HALO + tt * 512:HALO + (tt + 1) * 512]
                p.add("dve", lambda e, xs=xs, b=b: e.tensor_tensor(out=xs, in0=ps[:, OB[b]:OB[b] + 512], in1=xs, op=ALU.add),
                      reads=[obr[b], regr[0 if b < 2 else 1], xres[m][1 + tt]], writes=[xres[m][1 + tt]])
                if g == 7 and tt == 1:
                    yv = yT.rearrange("(kc p) n -> p kc n", p=128)
                    p.add("sp", lambda e, m=m: e.dma_start(out=yv[:, m, :], in_=x[:, m, HALO:HALO + NT]),
                          reads=[xres[m][1], xres[m][2]], dma="r")
    return cx.finish()


def attn_params(g, qgain, kgain, sinks, halo_bias):
    qg = np.tile(np.asarray(qgain, np.float32), 2).reshape(128, 1)
    kg = np.tile(np.asarray(kgain, np.float32), 2).reshape(128, 1)
    sk = np.tile(np.asarray(sinks, np.float32).reshape(1, 32), (128, 1))
    hb = np.full((128, 1), halo_bias, np.float32)
    return np.ascontiguousarray(np.concatenate([chunked(g), qg, kg, sk, hb], axis=1))


def attn_masks():
    k = np.arange(128)[:, None]
    q = np.arange(128)[None, :]
    mprev = (k > q).astype(np.float32)
    mcur = (q >= k).astype(np.float32)
    return np.ascontiguousarray(np.concatenate([np.tile(mprev, (1, 4)), np.tile(mcur, (1, 4))], axis=1))


_CACHE = {}


def _prog(name):
    if name not in _CACHE:
        _CACHE[name] = {"attn": build_attn2, "ffn": lambda: build_ffn(False),
                        "ffnp": lambda: build_ffn(True), "rec1": build_rec1}[name]()
    return _CACHE[name]


def _run(name, in_maps):
    res = run_bass_kernel_spmd(_prog(name), in_maps, core_ids=list(range(NCORES)))
    return res.results


def _halo_slices(aT, halo):
    pad = np.concatenate([np.zeros((aT.shape[0], halo), aT.dtype), aT], axis=1)
    return [np.ascontiguousarray(pad[:, c * NT:c * NT + NT + halo]) for c in range(NCORES)]


def kernel(x, mix_norm, ffn_norm, attn_w_qkv, attn_q_gain, attn_k_gain, attn_sinks, attn_w_o,
           rec_w_in, rec_conv_w, rec_conv_b, rec_w_a, rec_b_a, rec_w_i, rec_b_i, rec_lambda,
           rec_w_out, ffn_w_up, ffn_conv_w, ffn_conv_b, ffn_w_down):
    f = lambda a: np.ascontiguousarray(np.asarray(a, dtype=np.float32))
    xT = np.ascontiguousarray(f(x)[0].T)
    masks = attn_masks2()
    for layer in range(4):
        j = layer // 2
        fprm = ffn_params(f(ffn_norm)[layer], f(ffn_conv_w)[layer], f(ffn_conv_b)[layer])
        wup, wdn = f(ffn_w_up)[layer], f(ffn_w_down)[layer]
        if layer % 2 == 0:
            xs = _halo_slices(xT, 128)
            wqkv, wo = f(attn_w_qkv)[j], f(attn_w_o)[j]
            in_maps = [{"xT": xs[c], "w_qkv": wqkv, "w_o": wo, "msk": masks,
                        "prm": attn_params(f(mix_norm)[layer], f(attn_q_gain)[j], f(attn_k_gain)[j],
                                           f(attn_sinks)[j], -30000.0 if c == 0 else 0.0)}
                       for c in range(NCORES)]
            res = _run("attn", in_maps)
            xT = np.concatenate([r["yT"] for r in res], axis=1)
            xs = _halo_slices(xT, 2)
            in_maps = [{"xT": xs[c], "w_up": wup, "w_down": wdn, "prm": fprm} for c in range(NCORES)]
            res = _run("ffn", in_maps)
        else:
            xs = _halo_slices(xT, 3)
            rprm = rec_params(f(mix_norm)[layer], f(rec_conv_w)[j], f(rec_conv_b)[j], f(rec_b_a)[j],
                              f(rec_b_i)[j], f(rec_lambda)[j])
            in_maps = [{"xT": xs[c], "w_in": f(rec_w_in)[j], "w_a": f(rec_w_a)[j], "w_i": f(rec_w_i)[j],
                        "prm": rprm} for c in range(NCORES)]
            res = _run("rec1", in_maps)
            g1T = np.concatenate([r["g1T"] for r in res], axis=1)
            zT = np.concatenate([r["zT"] for r in res], axis=1)
            carr = np.ascontiguousarray(np.stack([r["carr"] for r in res], axis=-1).reshape(128, 256))
            xs, gs, zs = _halo_slices(xT, 2), _halo_slices(g1T, 2), _halo_slices(zT, 2)
            in_maps = []
            for c in range(NCORES):
                m = np.zeros((128, 16), np.float32)
                m[:, 0:max(c, 0)] = 1.0
                m[:, 8:8 + max(c - 1, 0)] = 1.0
                in_maps.append({"xT": xs[c], "g1T": gs[c], "zT": zs[c], "carr": carr, "msk": m,
                                "w_out": f(rec_w_out)[j], "w_up": wup, "w_down": wdn, "prm": fprm})
            res = _run("ffnp", in_maps)
        xT = np.concatenate([r["yT"] for r in res], axis=1)
    return np.ascontiguousarray(xT.T)[None].astype(np.float32)


def build_attn2():
    cx = Ctx()
    nc, p = cx.nc, cx.p
    HALO = 128
    NCOL = NT + HALO
    NTB = NCOL // 128
    xT = cx.din("xT", [D, NCOL])
    w_qkv = cx.din("w_qkv", [D, 3072])
    w_o = cx.din("w_o", [D, D])
    NPRM = 16 + 2 + 32 + 1
    prm_d = cx.din("prm", [128, NPRM])
    msk_d = cx.din("msk", [128, 640])
    yT = cx.dout("yT", [D, NT])

    ps = cx.psum()
    bk = [Res(f"bk{i}") for i in range(8)]
    x = cx.sb("x", [128, KC, NCOL], F32)
    h = cx.sb("h", [128, KC, NCOL], BF16)
    sq = [cx.sb(f"sq{i}", [128, NCOL], F32) for i in range(2)]
    rstd = cx.sb("rstd", [128, NCOL], F32)
    prm = cx.sb("prm_sb", [128, NPRM], F32)
    es = cx.sb("es", [128, 32], F32)
    ones = cx.sb("ones", [128, 128], F32)
    bd = cx.sb("bd", [128, 128], F32)
    epsb = cx.sb("epsb", [128, 2], F32)
    msk = cx.sb("msk_sb", [128, 640], BF16)
    onesb = cx.sb("onesb", [128, 128], BF16)
    Kd = cx.sb("Kd", [128, 8, NCOL], BF16)
    NVB = NTB * 8
    VO = cx.sb("VO", [128, (NVB + 1) * 64], BF16)
    wbig = cx.sb("wbig", [128, 8192], BF16)
    wq = [cx.sb(f"wq{i}", [128, KC, 128], BF16) for i in range(2)]
    Qn = [cx.sb(f"Qn{i}", [128, 4, NT], BF16) for i in range(2)]
    AO = [cx.sb(f"AO{i}", [128, 2, NT], BF16) for i in range(2)]
    PT = [cx.sb(f"PT{i}", [128, 512], BF16) for i in range(2)]

    xres = [[Res(f"x{m}_h"), Res(f"x{m}_0"), Res(f"x{m}_1")] for m in range(KC)]
    hres = [Res(f"h{k}") for k in range(KC)]
    sqr = [Res("sq0"), Res("sq1")]
    rstdr, prm_r, ones_r, bd_r, es_r, msk_r = (Res(n) for n in ("rstd", "prm", "ones", "bd", "es", "msk"))
    Kdr = [Res(f"Kd{g}") for g in range(8)]
    Vr = Res("VO")
    wbr = [Res("wb0"), Res("wb1")]
    wqr = [[Res(f"wq{i}_0"), Res(f"wq{i}_1")] for i in range(2)]
    Qnr = [[Res(f"Qn{i}_{hh}") for hh in range(4)] for i in range(2)]
    AOr = [[Res(f"AO{i}_{c}") for c in range(2)] for i in range(2)]
    PTr = [Res("PT0"), Res("PT1")]
    sqt = [sq[0][:, 0:512], sq[0][:, 512:1024]]
    rsb = [sq[1][:, 0:512], sq[1][:, 512:1024]]
    den = rstd[:, 0:256]
    rec = rstd[:, 512:768]
    sqtr = [Res("sqt0"), Res("sqt1")]
    rsr = [Res("rs0"), Res("rs1")]
    denr, recr = Res("den"), Res("rec")
    B = lambda i: i * 512
    STATB, SCB, PVB, OPB = 4, 5, 6, 7

    gain = prm[:, 0:16]
    qg = prm[:, 16:17]
    kg = prm[:, 17:18]
    hb = prm[:, 50:51]
    scr = {"sq": sq, "sqr": sqr, "rstd": rstd, "rstdr": rstdr, "eps": epsb}

    p.add("pool", lambda e: e.memset(ones[:, :], 1.0), writes=[ones_r])
    p.add("pool", lambda e: e.memset(onesb[:, :], 1.0), writes=[ones_r])
    p.add("pool", lambda e: e.memset(epsb[:, 0:1], EPS), writes=[prm_r])
    p.add("pool", lambda e: e.memset(epsb[:, 1:2], 64 * EPS), writes=[prm_r])
    p.add("pool", lambda e: e.memset(bd[:, :], 0.0), writes=[bd_r])
    p.add("pool", lambda e: e.memset(bd[0:64, 0:64], 1.0), writes=[bd_r])
    p.add("pool", lambda e: e.memset(bd[64:128, 64:128], 1.0), writes=[bd_r])
    p.add("pool", lambda e: e.memset(VO[:, NVB * 64:(NVB + 1) * 64], 1.0), writes=[Vr])
    p.add("sp", lambda e: e.dma_start(out=prm[:, :], in_=prm_d[:, :]), writes=[prm_r], dma="w")
    p.add("pool", lambda e: e.dma_start(out=msk[:, :], in_=msk_d[:, :]), writes=[msk_r], dma="w")
    xv = xT.rearrange("(kc p) n -> p kc n", p=128)
    for kc in range(KC):
        p.add("sp", lambda e, kc=kc: e.dma_start(out=x[:, kc, :], in_=xv[:, kc, :]), writes=xres[kc], dma="w")
    p.add("act", lambda e: e.activation(out=es[:, :], in_=prm[:, 18:50], func=AF.Exp), reads=[prm_r], writes=[es_r])

    wv = wbig[:, :].rearrange("p (kc n) -> p kc n", kc=KC)
    wqkv_v = w_qkv.rearrange("(kc p) n -> p kc n", p=128)
    p.add("pool", lambda e: e.dma_start(out=wv, in_=wqkv_v[:, :, 2560:3072]), writes=wbr, dma="w")

    nq = [0]

    def load_w(col0):
        s = nq[0] % 2
        nq[0] += 1
        for half in range(2):
            p.add("pool", lambda e, s=s, col0=col0, half=half: e.dma_start(
                out=wq[s][:, :, half * 64:(half + 1) * 64], in_=wqkv_v[:, :, col0:col0 + 64]),
                writes=[wqr[s][half]], dma="w")
        return s

    def load_o(g):
        s = g % 2
        dst = wbig[:, s * 4096:(s + 1) * 4096].rearrange("p (c n) -> p c n", c=2)
        src = w_o[g * 256:(g + 1) * 256, :].rearrange("(c p) n -> p c n", p=128)
        p.add("pool", lambda e, dst=dst, src=src: e.dma_start(out=dst, in_=src), writes=[wbr[s]], dma="w")

    emit_rmsnorm(cx, ps, bk[0], x, xres, gain, prm_r, h, hres, NCOL, ones, ones_r, scr,
                 extra_ps_res=[bk[1], bk[2]], use_ln=True)

    for tb in range(NTB):
        b = 5 + tb % 3
        for kc in range(KC):
            p.add("pe", lambda e, tb=tb, kc=kc, b=b: e.matmul(
                ps[:, B(b):B(b) + 512], lhsT=h[:, kc, tb * 128:(tb + 1) * 128], rhs=wv[:, kc, :],
                start=(kc == 0), stop=(kc == KC - 1)), reads=wbr + [hres[kc]], writes=[bk[b]])
        p.add("act", lambda e, tb=tb, b=b: e.activation(
            out=VO[:, tb * 512:(tb + 1) * 512], in_=ps[:, B(b):B(b) + 512], func=AF.Identity),
            reads=[bk[b]], writes=[Vr], relaxed=True)

    nstat = [0]

    def proj_mms(s, banks, tiles, hoff, tile_major=False):
        out = []
        order = [(kc, t) for kc in range(KC) for t in range(len(tiles))] if not tile_major else \
                [(kc, t) for t in range(len(tiles)) for kc in range(KC)]
        for kc, t in order:
            c0, c1 = tiles[t]
            if True:
                def f(s=s, kc=kc, t=t, c0=c0, c1=c1):
                    p.add("pe", lambda e: e.matmul(
                        ps[:, B(banks[t]):B(banks[t]) + (c1 - c0)], lhsT=wq[s][:, kc, :],
                        rhs=h[:, kc, hoff + c0:hoff + c1], start=(kc == 0), stop=(kc == KC - 1)),
                        reads=wqr[s] + [hres[kc]], writes=[bk[banks[t]]])
                out.append(f)
        return out

    def norm_steps(banks, tiles, gain_ap, dst_fn, dst_res):
        idx = []
        for t in range(len(tiles)):
            idx.append(nstat[0] % 2)
            nstat[0] += 1

        def phase_a():
            for t, (c0, c1) in enumerate(tiles):
                n = c1 - c0
                i = idx[t]
                src = ps[:, B(banks[t]):B(banks[t]) + n]
                p.add("act", lambda e, i=i, n=n, src=src: e.activation(out=sqt[i][:, 0:n], in_=src, func=AF.Square),
                      reads=[bk[banks[t]]], writes=[sqtr[i]])

        def phase_b(t):
            c0, c1 = tiles[t]
            n = c1 - c0
            i = idx[t]
            src = ps[:, B(banks[t]):B(banks[t]) + n]
            br = bk[banks[t]]
            p.add("pe", lambda e: e.matmul(ps[:, B(STATB):B(STATB) + n], lhsT=bd[:, :], rhs=sqt[i][:, 0:n],
                                           start=True, stop=True), reads=[bd_r, sqtr[i]], writes=[bk[STATB]])
            p.add("act", lambda e: e.activation(out=sqt[i][:, 0:n], in_=ps[:, B(STATB):B(STATB) + n], func=AF.Ln,
                                                bias=epsb[:, 1:2], scale=1.0), reads=[bk[STATB], prm_r], writes=[sqtr[i]])
            p.add("act", lambda e: e.activation(out=rsb[i][:, 0:n], in_=sqt[i][:, 0:n], func=AF.Exp, scale=-0.5),
                  reads=[sqtr[i]], writes=[rsr[i]])
            dst = dst_fn(c0, c1)
            p.add("dve", lambda e: e.scalar_tensor_tensor(
                out=dst, in0=src, scalar=gain_ap, in1=rsb[i][:, 0:n], op0=ALU.mult, op1=ALU.mult),
                reads=[br, rsr[i], prm_r], writes=[dst_res], relaxed=True)

        return [phase_a] + [(lambda t=t: phase_b(t)) for t in range(len(tiles))]

    def norm(banks, tiles, gain_ap, dst_fn, dst_res):
        for t in range(len(tiles)):
            for f in norm_steps(banks[t:t + 1], tiles[t:t + 1], gain_ap, dst_fn, dst_res):
                f()

    tilesK = token_tiles(NCOL, 0)
    KB = [[0, 1, 5], [2, 3, 6]]
    ks = load_w(2048)
    pend = None
    for g in range(8):
        s = ks
        if g + 1 < 8:
            ks = load_w(2048 + (g + 1) * 64)
        for f in proj_mms(s, KB[g % 2], tilesK, 0):
            f()
        if pend is not None:
            pend()
        pend = (lambda g=g: norm(KB[g % 2], tilesK, kg, lambda c0, c1: Kd[:, g, c0:c1], Kdr[g]))
    pend()

    tilesQ = token_tiles(NT, 0)
    QB = [[0, 1], [2, 3]]

    qslot = {}

    def prefetch_q(hd):
        if hd < 32 and hd not in qslot:
            qslot[hd] = load_w(hd * 64)

    def emit_q_head(g, hh, chunks):
        hd = g * 4 + hh
        gp = g % 2
        prefetch_q(hd)
        s = qslot[hd]
        qbanks = [(hd * 2) % 3, (hd * 2 + 1) % 3]
        mms = proj_mms(s, qbanks, tilesQ, HALO, tile_major=True)
        per = (len(mms) + chunks - 1) // chunks
        out = []
        for ci in range(chunks):
            part = mms[ci * per:(ci + 1) * per]
            last = ci == chunks - 1

            def f(part=part, last=last, first=(ci == 0)):
                if first:
                    prefetch_q(hd + 1)
                    while any(set(r) & set(qbanks) for r, _ in pending_norm):
                        pending_norm.pop(0)[1]()
                half = len(part) // 2
                pop_pending()
                for m in part[:half]:
                    m()
                pop_pending()
                for m in part[half:]:
                    m()
                if last:
                    sts = norm_steps(qbanks, tilesQ, qg, lambda c0, c1: Qn[gp][:, hh, c0:c1], Qnr[gp][hh])
                    pending_norm.extend(zip([qbanks, qbanks[0:1], qbanks[1:2]], sts))
            out.append(f)
        return out

    pending_norm = []

    def pop_pending():
        if pending_norm:
            pending_norm.pop(0)[1]()

    def flush_norm():
        while pending_norm:
            pending_norm.pop(0)[1]()

    def o_units(g, banks=(7,)):
        gp = g % 2
        so = g % 2
        wo = wbig[:, so * 4096:(so + 1) * 4096].rearrange("p (c n) -> p c n", c=2)
        units = []
        for m in range(KC):
            for tt in range(2):
                def f(m=m, tt=tt):
                    ob = banks[(m * 2 + tt) % len(banks)]
                    for c in range(2):
                        p.add("pe", lambda e, c=c: e.matmul(
                            ps[:, B(ob):B(ob) + 512], lhsT=wo[:, c, m * 128:(m + 1) * 128],
                            rhs=AO[gp][:, c, tt * 512:(tt + 1) * 512], start=(c == 0), stop=(c == 1)),
                            reads=[wbr[so], AOr[gp][c]], writes=[bk[ob]])
                    xs = x[:, m, HALO + tt * 512:HALO + (tt + 1) * 512]
                    p.add("dve", lambda e: e.tensor_tensor(out=xs, in0=ps[:, B(ob):B(ob) + 512], in1=xs, op=ALU.add),
                          reads=[bk[ob], xres[m][1 + tt]], writes=[xres[m][1 + tt]])
                    if g == 7 and tt == 1:
                        yv = yT.rearrange("(kc p) n -> p kc n", p=128)
                        p.add("sp", lambda e: e.dma_start(out=yv[:, m, :], in_=x[:, m, HALO:HALO + NT]),
                              reads=[xres[m][1], xres[m][2]], dma="r")
                units.append(f)
        return units

    prefetch_q(0)
    for hh in range(4):
        for f in emit_q_head(0, hh, 1):
            f()
    flush_norm()
    load_o(0)

    nstep = [0]
    for g in range(8):
        gp = g % 2
        qwork = []
        if g + 1 < 8:
            for hh in range(4):
                qwork.append((g + 1, hh))
        owork = o_units(g - 1, banks=(7, 3)) if g > 0 else []
        if g > 0:
            load_o(g)
        cur_q = []
        for qb in range(8):
            for hp in range(2):
                step = qb * 2 + hp
                i = nstep[0] % 2
                nstep[0] += 1
                p.add("pe", lambda e: e.matmul(ps[:, B(SCB):B(SCB) + 512], lhsT=msk[:, 512:640], rhs=msk[:, 0:512],
                                               start=True, stop=False), reads=[msk_r], writes=[bk[SCB]])
                for kb in range(2):
                    for hl in range(2):
                        hh = hp * 2 + hl
                        col = B(SCB) + kb * 256 + hl * 128
                        p.add("pe", lambda e, kb=kb, hh=hh, col=col, qb=qb, g=g, gp=gp, hl=hl: e.matmul(
                            ps[:, col:col + 128], lhsT=Kd[:, g, (qb + kb) * 128:(qb + kb + 1) * 128],
                            rhs=Qn[gp][:, hh, qb * 128:(qb + 1) * 128], start=False, stop=(kb == 1 and hl == 1)),
                            reads=[Kdr[g], Qnr[gp][hh]], writes=[bk[SCB]])
                if qb == 0:
                    p.add("act", lambda e, i=i: e.activation(out=PT[i][:, 0:256], in_=ps[:, B(SCB):B(SCB) + 256],
                                                             func=AF.Exp, bias=hb, scale=4.0),
                          reads=[bk[SCB], prm_r], writes=[PTr[i]])
                    p.add("act", lambda e, i=i: e.activation(out=PT[i][:, 256:512], in_=ps[:, B(SCB) + 256:B(SCB) + 512],
                                                             func=AF.Exp, scale=4.0),
                          reads=[bk[SCB]], writes=[PTr[i]], relaxed=True)
                else:
                    p.add("act", lambda e, i=i: e.activation(out=PT[i][:, :], in_=ps[:, B(SCB):B(SCB) + 512],
                                                             func=AF.Exp, scale=4.0),
                          reads=[bk[SCB]], writes=[PTr[i]])
                if owork:
                    owork.pop(0)()
                if qwork or cur_q:
                    if not cur_q:
                        gg, hh_ = qwork.pop(0)
                        cur_q = emit_q_head(gg, hh_, 4)
                    cur_q.pop(0)()
                else:
                    pop_pending()
                if owork:
                    owork.pop(0)()
                for kb in range(2):
                    vb = ((qb + kb) * 8 + g) * 64
                    p.add("pe", lambda e, kb=kb, vb=vb, i=i: e.matmul(
                        ps[:, B(PVB):B(PVB) + 256], lhsT=VO[:, vb:vb + 128], rhs=PT[i][:, kb * 256:(kb + 1) * 256],
                        start=(kb == 0), stop=(kb == 1)), reads=[Vr, PTr[i]], writes=[bk[PVB]])
                for kb in range(2):
                    p.add("pe", lambda e, kb=kb, i=i: e.matmul(
                        ps[:, B(PVB) + 256:B(PVB) + 512], lhsT=onesb[:, :], rhs=PT[i][:, kb * 256:(kb + 1) * 256],
                        start=(kb == 0), stop=(kb == 1)), reads=[ones_r, PTr[i]], writes=[bk[PVB]])
                for hl in range(2):
                    hh = hp * 2 + hl
                    p.add("dve", lambda e, hl=hl, hh=hh, g=g: e.tensor_scalar(
                        out=den[0:64, hl * 128:(hl + 1) * 128],
                        in0=ps[0:64, B(PVB) + 256 + hl * 128:B(PVB) + 256 + (hl + 1) * 128],
                        scalar1=es[0:64, g * 4 + hh:g * 4 + hh + 1], scalar2=None, op0=ALU.add),
                        reads=[bk[PVB], es_r], writes=[denr], relaxed=True)
                p.add("dve", lambda e: e.reciprocal(out=rec[0:64, :], in_=den[0:64, :]), reads=[denr], writes=[recr])
                for hl in range(2):
                    p.add("dve", lambda e, hl=hl, hp=hp, qb=qb, gp=gp: e.tensor_tensor(
                        out=AO[gp][hl * 64:(hl + 1) * 64, hp, qb * 128:(qb + 1) * 128],
                        in0=ps[0:64, B(PVB) + hl * 128:B(PVB) + (hl + 1) * 128],
                        in1=rec[0:64, hl * 128:(hl + 1) * 128], op=ALU.mult),
                        reads=[bk[PVB], recr], writes=[AOr[gp][hp]], relaxed=True)
        flush_norm()
        assert not qwork and not cur_q and not owork, (len(qwork), len(cur_q), len(owork))
    for f in o_units(7, banks=(7, 3, 0, 1, 2, 5)):
        f()
    return cx.finish()


def attn_masks2():
    k = np.arange(128)[:, None]
    q = np.arange(128)[None, :]
    mprev = np.where(k > q, 0.0, -10000.0).astype(np.float32)
    mcur = np.where(q >= k, 0.0, -10000.0).astype(np.float32)
    return np.ascontiguousarray(np.concatenate([mprev, mprev, mcur, mcur, np.eye(128, dtype=np.float32)], axis=1))
```

```python
import contextlib
import numpy as np
import concourse.bass as bass
import concourse.mybir as mybir
from concourse.bass_utils import run_bass_kernel_spmd

F32 = mybir.dt.float32
BF16 = mybir.dt.bfloat16
AF = mybir.ActivationFunctionType
ALU = mybir.AluOpType

NCORES = 8
D = 2048
KC = 16
T = 8192
NT = 1024
DFF = 6144
EPS = 1e-6


class Res:
    __slots__ = ("name", "last_w", "readers", "sem_w", "nw", "sem_r", "nr")

    def __init__(self, name):
        self.name = name
        self.last_w = None
        self.readers = {}
        self.sem_w = None
        self.nw = 0
        self.sem_r = None
        self.nr = 0


class Op:
    __slots__ = ("eng", "fn", "deps", "need_inc", "sem", "semval", "dma", "inc")

    def __init__(self, eng, fn, dma):
        self.eng = eng
        self.fn = fn
        self.deps = []
        self.need_inc = False
        self.sem = None
        self.semval = 0
        self.dma = dma
        self.inc = 1


class Prog:
    ENGS = ("pe", "act", "dve", "pool", "sp")

    def __init__(self, nc):
        self.nc = nc
        self.ops = {e: [] for e in self.ENGS}
        self.esem = {e: nc.alloc_semaphore(name=f"sem_{e}") for e in self.ENGS}
        self.nsem = 0
        self.final = []

    def _newsem(self):
        self.nsem += 1
        return self.nc.alloc_semaphore(name=f"dsem{self.nsem}")

    def add(self, eng, fn, reads=(), writes=(), dma=None, relaxed=False):
        op = Op(eng, fn, dma)
        deps = []
        for r in reads:
            if r.last_w is not None:
                deps.append(r.last_w)
        for w in writes:
            if w.last_w is not None:
                if not (relaxed and w.last_w.dma is None and dma is None and w.last_w.eng == eng):
                    deps.append(w.last_w)
            deps.extend(w.readers.values())
        for d in deps:
            if d is op:
                continue
            if d.dma is None and op.dma is None and d.eng == eng == "pe":
                continue
            if d not in op.deps:
                op.deps.append(d)
                d.need_inc = True
        for r in reads:
            key = eng if dma is None else id(op)
            r.readers[key] = op
        for w in writes:
            w.last_w = op
            w.readers = {}
        if dma == "w":
            res = writes[0]
            if res.sem_w is None:
                res.sem_w = self._newsem()
            res.nw += 1
            op.sem, op.semval, op.inc = res.sem_w, 16 * res.nw, 16
            op.need_inc = True
        elif dma == "r":
            res = reads[0]
            if res.sem_r is None:
                res.sem_r = self._newsem()
            res.nr += 1
            op.sem, op.semval, op.inc = res.sem_r, 16 * res.nr, 16
            op.need_inc = True
            self.final.append(op)
        self.ops[eng].append(op)
        return op

    def emit(self):
        nc = self.nc
        for e in self.ENGS:
            cnt = 0
            for op in self.ops[e]:
                if op.dma is None:
                    op.sem = self.esem[e]
                    if op.need_inc:
                        cnt += 1
                        op.semval = cnt
        final = self.final

        def run(e, h):
            waited = {}
            for op in self.ops[e]:
                need = {}
                for d in op.deps:
                    k = id(d.sem)
                    if k not in need or need[k][1] < d.semval:
                        need[k] = (d.sem, d.semval)
                for k, (s, v) in need.items():
                    if waited.get(k, 0) >= v:
                        continue
                    h.wait_ge(s, v)
                    waited[k] = v
                ins = op.fn(h)
                if op.need_inc:
                    ins.then_inc(op.sem, op.inc)
            if e == "sp":
                need = {}
                for d in final:
                    k = id(d.sem)
                    if k not in need or need[k][1] < d.semval:
                        need[k] = (d.sem, d.semval)
                for k, (s, v) in need.items():
                    h.wait_ge(s, v)

        with nc.Block() as block:
            @block.tensor
            def _(h):
                run("pe", h)

            @block.scalar
            def _(h):
                run("act", h)

            @block.vector
            def _(h):
                run("dve", h)

            @block.gpsimd
            def _(h):
                run("pool", h)

            @block.sync
            def _(h):
                run("sp", h)


class Ctx:
    def __init__(self):
        self.nc = bass.Bass("TRN2", target_bir_lowering=False)
        self.p = Prog(self.nc)
        self.stack = contextlib.ExitStack()

    def sb(self, name, shape, dt):
        return self.stack.enter_context(self.nc.sbuf_tensor(name, shape, dt))

    def psum(self):
        return self.stack.enter_context(self.nc.psum_tensor("ps", [128, 4096], F32))

    def din(self, name, shape, dt=F32):
        return self.nc.dram_tensor(name, list(shape), dt, kind="ExternalInput").ap()

    def dout(self, name, shape, dt=F32):
        return self.nc.dram_tensor(name, list(shape), dt, kind="ExternalOutput").ap()

    def finish(self):
        self.p.emit()
        self.stack.close()
        return self.nc


def token_tiles(ncols, first):
    tiles = []
    c = 0
    if first:
        tiles.append((0, first))
        c = first
    while c < ncols:
        tiles.append((c, min(c + 512, ncols)))
        c = tiles[-1][1]
    return tiles


def emit_rmsnorm(cx, ps, ps_res, x, xres, gain, prm_res, h, hres, ncols, ones, ones_res, scr, extra_ps_res=(), use_ln=False):
    p = cx.p
    tiles = token_tiles(ncols, 0)
    for kc in range(KC):
        sq, sqr = scr["sq"][kc % 2], scr["sqr"][kc % 2]
        p.add("act", lambda e, kc=kc, sq=sq: e.activation(out=sq[:, 0:ncols], in_=x[:, kc, 0:ncols], func=AF.Square),
              reads=xres[kc], writes=[sqr])
        for (c0, c1) in tiles:
            p.add("pe", lambda e, kc=kc, sq=sq, c0=c0, c1=c1: e.matmul(
                ps[:, c0:c1], lhsT=ones[:, :], rhs=sq[:, c0:c1], start=(kc == 0), stop=(kc == KC - 1)),
                reads=[sqr, ones_res], writes=[ps_res] + list(extra_ps_res))
    rstd, rres = scr["rstd"], scr["rstdr"]
    sq, sqr = scr["sq"][0], scr["sqr"][0]
    if use_ln:
        p.add("act", lambda e: e.activation(out=sq[:, 0:ncols], in_=ps[:, 0:ncols], func=AF.Ln,
                                            bias=scr["eps"][:, 0:1], scale=1.0 / D),
              reads=[ps_res, prm_res] + list(extra_ps_res), writes=[sqr])
        p.add("act", lambda e: e.activation(out=rstd[:, 0:ncols], in_=sq[:, 0:ncols], func=AF.Exp, scale=-0.5),
              reads=[sqr], writes=[rres])
    else:
        p.add("act", lambda e: e.activation(out=sq[:, 0:ncols], in_=ps[:, 0:ncols], func=AF.Sqrt,
                                            bias=scr["eps"][:, 0:1], scale=1.0 / D),
              reads=[ps_res, prm_res] + list(extra_ps_res), writes=[sqr])
        p.add("dve", lambda e: e.reciprocal(out=rstd[:, 0:ncols], in_=sq[:, 0:ncols]), reads=[sqr], writes=[rres])
    for kc in range(KC):
        p.add("dve", lambda e, kc=kc: e.scalar_tensor_tensor(
            out=h[:, kc, 0:ncols], in0=x[:, kc, 0:ncols], scalar=gain[:, kc:kc + 1], in1=rstd[:, 0:ncols],
            op0=ALU.mult, op1=ALU.mult), reads=list(xres[kc]) + [rres, prm_res], writes=[hres[kc]])


def build_ffn(rec_prologue):
    cx = Ctx()
    nc, p = cx.nc, cx.p
    NCOL = NT + 2
    xT = cx.din("xT", [D, NCOL])
    w_up = cx.din("w_up", [D, 2 * DFF])
    w_down = cx.din("w_down", [DFF, D])
    prm_d = cx.din("prm", [128, 16 + 288 + 96])
    yT = cx.dout("yT", [D, NT])
    if rec_prologue:
        g1T = cx.din("g1T", [D, NCOL])
        zT = cx.din("zT", [D, NCOL])
        carr = cx.din("carr", [128, 2 * KC * 8])
        msk = cx.din("msk", [128, 16])
        w_out = cx.din("w_out", [D, D])

    ps = cx.psum()
    x = cx.sb("x", [128, KC, NCOL], F32)
    h = cx.sb("h", [128, KC, NCOL], BF16)
    sq = [cx.sb(f"sq{i}", [128, NCOL], F32) for i in range(2)]
    rstd = cx.sb("rstd", [128, NCOL], F32)
    prm = cx.sb("prm_sb", [128, 16 + 288 + 96], F32)
    ones = cx.sb("ones", [128, 128], F32)
    epsb = cx.sb("epsb", [128, 1], F32)
    wup = [cx.sb(f"wup{i}", [128, KC, 2, 128], BF16) for i in range(3)]
    wdn = [cx.sb(f"wdn{i}", [128, 4, D], BF16) for i in range(2)]
    act = [cx.sb(f"act{i}", [128, NT], BF16) for i in range(8)]
    cg = cx.sb("cg", [128, NCOL], F32)
    cv = cx.sb("cv", [128, NCOL], F32)
    gg = cx.sb("gg", [128, NT], F32)

    xres = [[Res(f"x{m}_{tt}") for tt in range(3)] for m in range(KC)]
    hres = [Res(f"h{k}") for k in range(KC)]
    sqr = [Res("sq0"), Res("sq1")]
    rstdr = Res("rstd")
    prm_r = Res("prm")
    ones_r = Res("ones")
    wupr = [[Res(f"wup{i}_{hf}") for hf in range(2)] for i in range(3)]
    wdnr = [Res(f"wdn{i}") for i in range(2)]
    actr = [Res(f"act{i}") for i in range(8)]
    cgr, cvr, ggr = Res("cg"), Res("cv"), Res("gg")
    psX, psY = Res("psX"), Res("psY")
    bankr = [Res(f"bank{i}") for i in range(2)]
    OX, OY = 0, 1536
    tilesX = token_tiles(NCOL, 0)
    tilesY = tilesX
    BANK = [3072, 3584]

    gain = prm[:, 0:16]
    cw = prm[:, 16:16 + 288]
    cb = prm[:, 304:400]
    scr = {"sq": sq, "sqr": sqr, "rstd": rstd, "rstdr": rstdr, "eps": epsb}

    p.add("pool", lambda e: e.memset(ones[:, :], 1.0), writes=[ones_r])
    p.add("pool", lambda e: e.memset(epsb[:, :], EPS), writes=[prm_r])
    p.add("sp", lambda e: e.dma_start(out=prm[:, :], in_=prm_d[:, :]), writes=[prm_r], dma="w")
    xv = xT.rearrange("(kc p) n -> p kc n", p=128)
    for kc in range(KC):
        p.add("sp", lambda e, kc=kc: e.dma_start(out=x[:, kc, :], in_=xv[:, kc, :]),
              writes=xres[kc], dma="w")

    if rec_prologue:
        cin = cx.sb("cin", [128, 2 * KC * 8], F32)
        mk = cx.sb("mk", [128, 16], F32)
        ca = cx.sb("ca", [128, 2, KC, 8], F32)
        ch = cx.sb("ch", [128, 2, KC, 8], F32)
        cs = cx.sb("cs", [128, 2, KC, 8], F32)
        cin_r, mk_r, ca_r, ch_r, cs_r = Res("cin"), Res("mk"), Res("ca"), Res("ch"), Res("cs")
        p.add("sp", lambda e: e.dma_start(out=cin[:, :], in_=carr[:, :]), writes=[cin_r], dma="w")
        p.add("sp", lambda e: e.dma_start(out=mk[:, :], in_=msk[:, :]), writes=[mk_r], dma="w")
        for w in range(2):
            for kc in range(KC):
                a_in = cin[:, kc * 8:(kc + 1) * 8]
                h_in = cin[:, KC * 8 + kc * 8: KC * 8 + (kc + 1) * 8]
                m = mk[:, w * 8:(w + 1) * 8]
                p.add("dve", lambda e, a_in=a_in, m=m, w=w, kc=kc: e.scalar_tensor_tensor(
                    out=ca[:, w, kc, :], in0=a_in, scalar=-1.0, in1=m, op0=ALU.add, op1=ALU.mult),
                    reads=[cin_r, mk_r], writes=[ca_r])
                p.add("dve", lambda e, w=w, kc=kc: e.tensor_scalar(
                    out=ca[:, w, kc, :], in0=ca[:, w, kc, :], scalar1=1.0, scalar2=None, op0=ALU.add),
                    reads=[ca_r], writes=[ca_r])
                p.add("dve", lambda e, h_in=h_in, m=m, w=w, kc=kc: e.tensor_tensor(
                    out=ch[:, w, kc, :], in0=h_in, in1=m, op=ALU.mult),
                    reads=[cin_r, mk_r], writes=[ch_r])
                p.add("dve", lambda e, w=w, kc=kc: e.tensor_tensor_scan(
                    out=cs[:, w, kc, :], data0=ca[:, w, kc, :], data1=ch[:, w, kc, :], initial=0.0,
                    op0=ALU.mult, op1=ALU.add), reads=[ca_r, ch_r], writes=[cs_r])
        gz = [(cg, cgr), (cv, cvr), (gg, ggr), (rstd, rstdr)]
        g1v = g1T.rearrange("(kc p) n -> p kc n", p=128)
        zv = zT.rearrange("(kc p) n -> p kc n", p=128)
        zst = [cg, cv]
        zstr = [cgr, cvr]
        for kc in range(KC):
            gb, gr = sq[kc % 2], sqr[kc % 2]
            zb, zr = zst[kc % 2], zstr[kc % 2]
            p.add("sp", lambda e, kc=kc, gb=gb: e.dma_start(out=gb[:, :], in_=g1v[:, kc, :]), writes=[gr], dma="w")
            p.add("sp", lambda e, kc=kc, zb=zb: e.dma_start(out=zb[:, :], in_=zv[:, kc, :]), writes=[zr], dma="w")
            p.add("dve", lambda e, kc=kc, gb=gb, zb=zb: e.scalar_tensor_tensor(
                out=h[:, kc, 0:2], in0=zb[:, 0:2], scalar=cs[:, 1, kc, 7:8], in1=gb[:, 0:2],
                op0=ALU.mult, op1=ALU.add), reads=[gr, zr, cs_r], writes=[hres[kc]])
            p.add("dve", lambda e, kc=kc, gb=gb, zb=zb: e.scalar_tensor_tensor(
                out=h[:, kc, 2:NCOL], in0=zb[:, 2:NCOL], scalar=cs[:, 0, kc, 7:8], in1=gb[:, 2:NCOL],
                op0=ALU.mult, op1=ALU.add), reads=[gr, zr, cs_r], writes=[hres[kc]])
        wo = [wup[i][:, :, 0, :] for i in range(3)]
        wor = [wupr[i][0] for i in range(3)]
        wov = w_out.rearrange("(kc p) n -> p kc n", p=128)
        for m in range(KC):
            s = m % 3
            p.add("pool", lambda e, m=m, s=s: e.dma_start(out=wo[s], in_=wov[:, :, m * 128:(m + 1) * 128]),
                  writes=[wor[s]], dma="w")
            O, tl, pr = (OX, tilesX, psX) if m % 2 == 0 else (OY, tilesY, psY)
            for kc in range(KC):
                for (c0, c1) in tl:
                    p.add("pe", lambda e, s=s, kc=kc, c0=c0, c1=c1, O=O: e.matmul(
                        ps[:, O + c0:O + c1], lhsT=wo[s][:, kc, :], rhs=h[:, kc, c0:c1],
                        start=(kc == 0), stop=(kc == KC - 1)), reads=[wor[s], hres[kc]], writes=[pr])
            p.add("dve", lambda e, m=m, O=O: e.tensor_tensor(
                out=x[:, m, :], in0=ps[:, O:O + NCOL], in1=x[:, m, :], op=ALU.add),
                reads=[pr] + xres[m], writes=xres[m])

    emit_rmsnorm(cx, ps, psX, x, xres, gain, prm_r, h, hres, NCOL, ones, ones_r, scr)

    wupv = w_up.rearrange("(kc p) (two c) -> p kc two c", p=128, two=2)

    def load_up(j):
        s = j % 3
        for half in range(2):
            p.add("pool", lambda e, j=j, s=s, half=half: e.dma_start(
                out=wup[s][:, :, half, :], in_=wupv[:, :, half, j * 128:(j + 1) * 128]),
                writes=[wupr[s][half]], dma="w")

    def load_dn(q):
        s = q % 2
        src = w_down[q * 512:(q + 1) * 512, :].rearrange("(k p) n -> p k n", p=128)
        p.add("pool", lambda e, s=s, src=src: e.dma_start(out=wdn[s][:, :, :], in_=src), writes=[wdnr[s]], dma="w")

    def up_half(j, half, phase):
        s = j % 3
        slot = j % 8
        if True:
            O, tl, pr = (OX, tilesX, psX) if half == 0 else (OY, tilesY, psY)
            ch = half * 48 + j
            for kc in (range(KC) if phase == 0 else ()):
                for (c0, c1) in tl:
                    p.add("pe", lambda e, s=s, kc=kc, half=half, c0=c0, c1=c1, O=O: e.matmul(
                        ps[:, O + c0:O + c1], lhsT=wup[s][:, kc, half, :], rhs=h[:, kc, c0:c1],
                        start=(kc == 0), stop=(kc == KC - 1)), reads=[wupr[s][half], hres[kc]], writes=[pr])
            if phase == 0:
                return
            c, cr = (cg, cgr) if half == 0 else (cv, cvr)
            P = ps[:, O:O + NCOL]
            p.add("act", lambda e, c=c, P=P, ch=ch: e.activation(
                out=c[:, 0:NT], in_=P[:, 2:2 + NT], func=AF.Identity,
                bias=cb[:, ch:ch + 1], scale=cw[:, 192 + ch:192 + ch + 1]), reads=[pr, prm_r], writes=[cr])
            p.add("dve", lambda e, c=c, P=P, ch=ch: e.scalar_tensor_tensor(
                out=c[:, 0:NT], in0=P[:, 1:1 + NT], scalar=cw[:, 96 + ch:96 + ch + 1], in1=c[:, 0:NT],
                op0=ALU.mult, op1=ALU.add), reads=[pr, prm_r, cr], writes=[cr])
            p.add("dve", lambda e, c=c, P=P, ch=ch: e.scalar_tensor_tensor(
                out=c[:, 0:NT], in0=P[:, 0:NT], scalar=cw[:, ch:ch + 1], in1=c[:, 0:NT],
                op0=ALU.mult, op1=ALU.add), reads=[pr, prm_r, cr], writes=[cr])
            if half == 0:
                p.add("act", lambda e: e.activation(out=gg[:, :], in_=cg[:, 0:NT], func=AF.Gelu_apprx_tanh),
                      reads=[cgr], writes=[ggr])
        if half == 1:
            p.add("pool", lambda e, slot=slot: e.tensor_tensor(out=act[slot][:, :], in0=gg[:, :], in1=cv[:, 0:NT], op=ALU.mult),
                  reads=[ggr, cvr], writes=[actr[slot]])

    def down_part(q, part, last):
        s = q % 2
        for m in range(part * 2, part * 2 + 2):
            for tt in range(2):
                b = (m * 2 + tt) % 2
                for k in range(4):
                    slot = (q * 4 + k) % 8
                    p.add("pe", lambda e, s=s, k=k, m=m, tt=tt, b=b, slot=slot: e.matmul(
                        ps[:, BANK[b]:BANK[b] + 512], lhsT=wdn[s][:, k, m * 128:(m + 1) * 128],
                        rhs=act[slot][:, tt * 512:(tt + 1) * 512], start=(k == 0), stop=(k == 3)),
                        reads=[wdnr[s], actr[slot]], writes=[bankr[b]])
                p.add("dve", lambda e, m=m, tt=tt, b=b: e.tensor_tensor(
                    out=x[:, m, 2 + tt * 512:2 + (tt + 1) * 512], in0=ps[:, BANK[b]:BANK[b] + 512],
                    in1=x[:, m, 2 + tt * 512:2 + (tt + 1) * 512], op=ALU.add),
                    reads=[bankr[b], xres[m][1 + tt]], writes=[xres[m][1 + tt]])
            if last:
                yv = yT.rearrange("(kc p) n -> p kc n", p=128)
                p.add("sp", lambda e, m=m: e.dma_start(out=yv[:, m, :], in_=x[:, m, 2:2 + NT]),
                      reads=[xres[m][1], xres[m][2]], dma="r")

    NP = 48
    load_up(0)
    load_up(1)
    load_dn(0)
    for j in range(NP + 4):
        q = j // 4 - 1
        for half in range(2):
            if j < NP:
                if half == 0 and j + 2 < NP:
                    load_up(j + 2)
                up_half(j, half, 0)
            if q >= 0:
                if j % 4 == 0 and half == 0 and q + 1 < NP // 4:
                    load_dn(q + 1)
                down_part(q, (j % 4) * 2 + half, last=(q == NP // 4 - 1))
            if j < NP:
                up_half(j, half, 1)
    return cx.finish()


def chunked(v):
    v = np.asarray(v, np.float32)
    return np.ascontiguousarray(v.reshape(-1, 128).T)


def ffn_params(g, cw, cb):
    parts = [chunked(g)] + [chunked(cw[k]) for k in range(3)] + [chunked(cb)]
    return np.ascontiguousarray(np.concatenate(parts, axis=1))


def build_rec1():
    cx = Ctx()
    nc, p = cx.nc, cx.p
    HALO = 3
    NCOL = NT + HALO
    xT = cx.din("xT", [D, NCOL])
    w_in = cx.din("w_in", [D, 2 * D])
    w_a = cx.din("w_a", [8, 256, 256])
    w_i = cx.din("w_i", [8, 256, 256])
    NPRM = 16 + 64 + 16 * 4
    prm_d = cx.din("prm", [128, NPRM])
    g1T = cx.dout("g1T", [D, NT])
    zT = cx.dout("zT", [D, NT])
    carr_o = cx.dout("carr", [128, 32])

    ps = cx.psum()
    x = cx.sb("x", [128, KC, NCOL], F32)
    h = cx.sb("h", [128, KC, NCOL], BF16)
    sq = [cx.sb(f"sq{i}", [128, NCOL], F32) for i in range(2)]
    rstd = cx.sb("rstd", [128, NCOL], F32)
    prm = cx.sb("prm_sb", [128, NPRM], F32)
    ones = cx.sb("ones", [128, 128], F32)
    epsb = cx.sb("epsb", [128, 1], F32)
    win = [cx.sb(f"win{i}", [128, KC, 128], BF16) for i in range(4)]
    wg = [[cx.sb(f"wg{i}_{j}", [128, 2, 256], BF16) for j in range(2)] for i in range(2)]
    xc = [cx.sb(f"xc{i}", [128, NT], F32) for i in range(2)]
    xcb = [cx.sb(f"xcb{i}", [128, NT], BF16) for i in range(2)]
    gate = [cx.sb(f"gate{i}", [128, NT], F32) for i in range(2)]
    tr = cx.sb("tr", [128, NT], F32)
    ta = cx.sb("ta", [128, NT], F32)
    tm = cx.sb("tm", [128, NT], F32)
    ti = cx.sb("ti", [128, NT], F32)
    ths = cx.sb("ths", [128, NT], F32)
    tac = cx.sb("tac", [128, NT], F32)
    tg1 = cx.sb("tg1", [128, NT], F32)
    tz = cx.sb("tz", [128, NT], F32)
    zeros = cx.sb("zeros", [128, NT], F32)
    cl = cx.sb("cl", [128, 48], F32)
    carr = cx.sb("carr_sb", [128, 32], F32)

    xres = [[Res(f"x{m}")] for m in range(KC)]
    hres = [Res(f"h{k}") for k in range(KC)]
    sqr = [Res("sq0"), Res("sq1")]
    rstdr, prm_r, ones_r = Res("rstd"), Res("prm"), Res("ones")
    winr = [Res(f"win{i}") for i in range(4)]
    wgr = [[Res(f"wg{i}_{j}") for j in range(2)] for i in range(2)]
    xcr = [Res("xc0"), Res("xc1")]
    xcbr = [Res("xcb0"), Res("xcb1")]
    gater = [Res("gate0"), Res("gate1")]
    trr, tar, tmr, tir, thsr, tacr, tg1r, tzr = (Res(n) for n in ("tr", "ta", "tm", "ti", "ths", "tac", "tg1", "tz"))
    zer_r, cl_r, carr_r = Res("zeros"), Res("cl"), Res("carr")
    psX, psY = Res("psX"), Res("psY")
    bankr = [Res(f"bank{i}") for i in range(3)]
    BANK = [2560, 3072, 3584]
    tilesX = token_tiles(NCOL, 0)
    OY = 1536

    gain = prm[:, 0:16]
    cw = prm[:, 16:80]
    cb = prm[:, 80:96]
    ba = prm[:, 96:112]
    bi = prm[:, 112:128]
    lam = prm[:, 128:144]
    scr = {"sq": sq, "sqr": sqr, "rstd": rstd, "rstdr": rstdr, "eps": epsb}

    p.add("pool", lambda e: e.memset(ones[:, :], 1.0), writes=[ones_r])
    p.add("pool", lambda e: e.memset(epsb[:, :], EPS), writes=[prm_r])
    p.add("pool", lambda e: e.memset(zeros[:, :], 0.0), writes=[zer_r])
    p.add("sp", lambda e: e.dma_start(out=prm[:, :], in_=prm_d[:, :]), writes=[prm_r], dma="w")
    xv = xT.rearrange("(kc p) n -> p kc n", p=128)
    for kc in range(KC):
        p.add("sp", lambda e, kc=kc: e.dma_start(out=x[:, kc, :], in_=xv[:, kc, :]), writes=xres[kc], dma="w")

    winv = w_in.rearrange("(kc p) n -> p kc n", p=128)
    nload = [0]

    def load_in(col0):
        s = nload[0] % 4
        nload[0] += 1
        p.add("pool", lambda e, s=s, col0=col0: e.dma_start(out=win[s][:, :, :], in_=winv[:, :, col0:col0 + 128]),
              writes=[winr[s]], dma="w")
        return s

    def load_g(b):
        s = b % 2
        for j, wsrc in enumerate((w_a, w_i)):
            p.add("pool", lambda e, s=s, j=j, wsrc=wsrc, b=b: e.dma_start(
                out=wg[s][j][:, :, :], in_=wsrc[b].rearrange("(ic p) n -> p ic n", p=128)),
                writes=[wgr[s][j]], dma="w")

    p.add("act", lambda e: e.activation(out=cl[:, 0:16], in_=lam, func=AF.Exp, scale=-1.0), reads=[prm_r], writes=[cl_r])
    p.add("act", lambda e: e.activation(out=cl[:, 0:16], in_=cl[:, 0:16], func=AF.Ln, bias=1.0), reads=[cl_r], writes=[cl_r])
    p.add("dve", lambda e: e.tensor_scalar(out=cl[:, 16:32], in0=cl[:, 0:16], scalar1=-8.0, scalar2=None, op0=ALU.mult),
          reads=[cl_r], writes=[cl_r])
    p.add("dve", lambda e: e.tensor_scalar(out=cl[:, 32:48], in0=cl[:, 0:16], scalar1=-16.0, scalar2=None, op0=ALU.mult),
          reads=[cl_r], writes=[cl_r])

    emit_rmsnorm(cx, ps, psX, x, xres, gain, prm_r, h, hres, NCOL, ones, ones_r, scr)

    pending = [load_in(0), load_in(D)]
    load_g(0)
    order = []
    for b in range(8):
        for c in range(2):
            ch = 2 * b + c
            order.append(ch * 128)
            order.append(D + ch * 128)
    li = 2
    for b in range(8):
        if b + 1 < 8:
            load_g(b + 1)
        for c in range(2):
            ch = 2 * b + c
            s = pending.pop(0)
            if li < len(order):
                pending.append(load_in(order[li])); li += 1
            for kc in range(KC):
                for (c0, c1) in tilesX:
                    p.add("pe", lambda e, s=s, kc=kc, c0=c0, c1=c1: e.matmul(
                        ps[:, c0:c1], lhsT=win[s][:, kc, :], rhs=h[:, kc, c0:c1],
                        start=(kc == 0), stop=(kc == KC - 1)), reads=[winr[s], hres[kc]], writes=[psX])
            P = ps[:, 0:NCOL]
            t = xc[c]
            p.add("act", lambda e, t=t, P=P, ch=ch: e.activation(
                out=t[:, :], in_=P[:, 3:3 + NT], func=AF.Identity, bias=cb[:, ch:ch + 1],
                scale=cw[:, 48 + ch:48 + ch + 1]), reads=[psX, prm_r], writes=[xcr[c]])
            for k in (2, 1, 0):
                p.add("dve", lambda e, t=t, P=P, ch=ch, k=k: e.scalar_tensor_tensor(
                    out=t[:, :], in0=P[:, k:k + NT], scalar=cw[:, k * 16 + ch:k * 16 + ch + 1], in1=t[:, :],
                    op0=ALU.mult, op1=ALU.add), reads=[psX, prm_r, xcr[c]], writes=[xcr[c]])
            p.add("pool", lambda e, c=c: e.tensor_copy(out=xcb[c][:, :], in_=xc[c][:, :]), reads=[xcr[c]], writes=[xcbr[c]])
            s = pending.pop(0)
            if li < len(order):
                pending.append(load_in(order[li])); li += 1
            for kc in range(KC):
                for tt in range(2):
                    p.add("pe", lambda e, s=s, kc=kc, tt=tt: e.matmul(
                        ps[:, OY + tt * 512:OY + (tt + 1) * 512], lhsT=win[s][:, kc, :],
                        rhs=h[:, kc, HALO + tt * 512:HALO + (tt + 1) * 512],
                        start=(kc == 0), stop=(kc == KC - 1)), reads=[winr[s], hres[kc]], writes=[psY])
            p.add("act", lambda e, c=c: e.activation(out=gate[c][:, :], in_=ps[:, OY:OY + NT], func=AF.Gelu_apprx_tanh),
                  reads=[psY], writes=[gater[c]])
        gs = b % 2
        for oc in range(2):
            ch = 2 * b + oc
            for j in range(2):
                dst, dres = (tr, trr) if j == 0 else (ti, tir)
                bias = ba if j == 0 else bi
                for tt in range(2):
                    bk = (oc * 4 + j * 2 + tt) % 3
                    for ic in range(2):
                        p.add("pe", lambda e, gs=gs, j=j, ic=ic, oc=oc, tt=tt, bk=bk: e.matmul(
                            ps[:, BANK[bk]:BANK[bk] + 512], lhsT=wg[gs][j][:, ic, oc * 128:(oc + 1) * 128],
                            rhs=xcb[ic][:, tt * 512:(tt + 1) * 512], start=(ic == 0), stop=(ic == 1)),
                            reads=[wgr[gs][j], xcbr[ic]], writes=[bankr[bk]])
                    p.add("act", lambda e, dst=dst, bias=bias, ch=ch, tt=tt, bk=bk: e.activation(
                        out=dst[:, tt * 512:(tt + 1) * 512], in_=ps[:, BANK[bk]:BANK[bk] + 512], func=AF.Sigmoid,
                        bias=bias[:, ch:ch + 1]), reads=[bankr[bk], prm_r], writes=[dres])
            p.add("act", lambda e, ch=ch: e.activation(out=ta[:, :], in_=tr[:, :], func=AF.Exp, scale=cl[:, 16 + ch:17 + ch]),
                  reads=[trr, cl_r], writes=[tar])
            p.add("act", lambda e, ch=ch: e.activation(out=tm[:, :], in_=tr[:, :], func=AF.Exp, scale=cl[:, 32 + ch:33 + ch]),
                  reads=[trr, cl_r], writes=[tmr])
            p.add("act", lambda e: e.activation(out=tm[:, :], in_=tm[:, :], func=AF.Sqrt, scale=-1.0, bias=1.0),
                  reads=[tmr], writes=[tmr])
            p.add("dve", lambda e, oc=oc: e.tensor_tensor(out=ti[:, :], in0=ti[:, :], in1=xc[oc][:, :], op=ALU.mult),
                  reads=[tir, xcr[oc]], writes=[tir])
            p.add("dve", lambda e: e.tensor_tensor(out=ti[:, :], in0=ti[:, :], in1=tm[:, :], op=ALU.mult),
                  reads=[tir, tmr], writes=[tir])
            p.add("dve", lambda e: e.tensor_tensor_scan(out=ths[:, :], data0=ta[:, :], data1=ti[:, :], initial=0.0,
                                                        op0=ALU.mult, op1=ALU.add), reads=[tar, tir], writes=[thsr])
            p.add("dve", lambda e: e.tensor_tensor_scan(out=tac[:, :], data0=ta[:, :], data1=zeros[:, :], initial=1.0,
                                                        op0=ALU.mult, op1=ALU.add), reads=[tar, zer_r], writes=[tacr])
            p.add("pool", lambda e, oc=oc: e.tensor_tensor(out=tg1[:, :], in0=ths[:, :], in1=gate[oc][:, :], op=ALU.mult),
                  reads=[thsr, gater[oc]], writes=[tg1r])
            p.add("pool", lambda e, oc=oc: e.tensor_tensor(out=tz[:, :], in0=tac[:, :], in1=gate[oc][:, :], op=ALU.mult),
                  reads=[tacr, gater[oc]], writes=[tzr])
            p.add("dve", lambda e, ch=ch: e.tensor_copy(out=carr[:, ch:ch + 1], in_=tac[:, NT - 1:NT]), reads=[tacr], writes=[carr_r])
            p.add("dve", lambda e, ch=ch: e.tensor_copy(out=carr[:, 16 + ch:17 + ch], in_=ths[:, NT - 1:NT]), reads=[thsr], writes=[carr_r])
            p.add("sp", lambda e, ch=ch: e.dma_start(out=g1T[ch * 128:(ch + 1) * 128, :], in_=tg1[:, :]), reads=[tg1r], dma="r")
            p.add("sp", lambda e, ch=ch: e.dma_start(out=zT[ch * 128:(ch + 1) * 128, :], in_=tz[:, :]), reads=[tzr], dma="r")
    p.add("sp", lambda e: e.dma_start(out=carr_o[:, :], in_=carr[:, :]), reads=[carr_r], dma="r")
    return cx.finish()


def build_rec1b():
    cx = Ctx()
    nc, p = cx.nc, cx.p
    HALO = 3
    NCOL = NT + HALO
    xT = cx.din("xT", [D, NCOL])
    w_in = cx.din("w_in", [D, 2 * D])
    w_a = cx.din("w_a", [8, 256, 256])
    w_i = cx.din("w_i", [8, 256, 256])
    NPRM = 16 + 64 + 16 * 4
    prm_d = cx.din("prm", [128, NPRM])
    g1T = cx.dout("g1T", [D, NT])
    zT = cx.dout("zT", [D, NT])
    carr_o = cx.dout("carr", [128, 32])

    ps = cx.psum()
    x = cx.sb("x", [128, KC, NCOL], F32)
    h = cx.sb("h", [128, KC, NCOL], BF16)
    sq = [cx.sb(f"sq{i}", [128, NCOL], F32) for i in range(2)]
    rstd = cx.sb("rstd", [128, NCOL], F32)
    prm = cx.sb("prm_sb", [128, NPRM], F32)
    ones = cx.sb("ones", [128, 128], F32)
    epsb = cx.sb("epsb", [128, 1], F32)
    win = [cx.sb(f"win{i}", [128, KC, 128], BF16) for i in range(3)]
    wg = [[cx.sb(f"wg{i}_{j}", [128, 2, 256], BF16) for j in range(2)] for i in range(2)]
    xc = [[cx.sb(f"xc{i}_{c}", [128, NT], F32) for c in range(2)] for i in range(2)]
    xcb = [[cx.sb(f"xcb{i}_{c}", [128, NT], BF16) for c in range(2)] for i in range(2)]
    gate = [[cx.sb(f"gate{i}_{c}", [128, NT], F32) for c in range(2)] for i in range(2)]
    tr = [cx.sb(f"tr{o}", [128, NT], F32) for o in range(2)]
    ti = [cx.sb(f"ti{o}", [128, NT], F32) for o in range(2)]
    ta = [cx.sb(f"ta{o}", [128, NT], F32) for o in range(2)]
    ths = cx.sb("ths", [128, NT], F32)
    tac = cx.sb("tac", [128, NT], F32)
    tg1 = sq[0][:, 0:NT]
    tz = sq[1][:, 0:NT]
    zeros = rstd[:, 0:NT]
    cl = cx.sb("cl", [128, 48], F32)
    carr = cx.sb("carr_sb", [128, 32], F32)

    xres = [[Res(f"x{m}")] for m in range(KC)]
    hres = [Res(f"h{k}") for k in range(KC)]
    sqr = [Res("sq0"), Res("sq1")]
    rstdr, prm_r, ones_r = Res("rstd"), Res("prm"), Res("ones")
    winr = [Res(f"win{i}") for i in range(3)]
    wgr = [[Res(f"wg{i}_{j}") for j in range(2)] for i in range(2)]
    xcr = [[Res(f"xc{i}_{c}") for c in range(2)] for i in range(2)]
    xcbr = [[Res(f"xcb{i}_{c}") for c in range(2)] for i in range(2)]
    gater = [[Res(f"gate{i}_{c}") for c in range(2)] for i in range(2)]
    trr = [Res("tr0"), Res("tr1")]
    tir = [Res("ti0"), Res("ti1")]
    tar = [Res("ta0"), Res("ta1")]
    thsr, tacr = Res("ths"), Res("tac")
    tg1r, tzr = sqr[0], sqr[1]
    zer_r, cl_r, carr_r = rstdr, Res("cl"), Res("carr")
    _px = Res("psX")
    psXr = [_px, _px]
    OXs = [0, 0]
    bankr = [Res("bank5"), Res("bank6"), Res("bank7")]
    psY = [Res("psY0"), Res("psY1")]
    BANK = [2560, 3072, 3584]
    tilesX = token_tiles(NCOL, 0)
    OY = 1536

    gain = prm[:, 0:16]
    cw = prm[:, 16:80]
    cb = prm[:, 80:96]
    ba = prm[:, 96:112]
    bi = prm[:, 112:128]
    lam = prm[:, 128:144]
    scr = {"sq": sq, "sqr": sqr, "rstd": rstd, "rstdr": rstdr, "eps": epsb}

    p.add("pool", lambda e: e.memset(ones[:, :], 1.0), writes=[ones_r])
    p.add("pool", lambda e: e.memset(epsb[:, :], EPS), writes=[prm_r])
    p.add("sp", lambda e: e.dma_start(out=prm[:, :], in_=prm_d[:, :]), writes=[prm_r], dma="w")
    xv = xT.rearrange("(kc p) n -> p kc n", p=128)
    for kc in range(KC):
        p.add("sp", lambda e, kc=kc: e.dma_start(out=x[:, kc, :], in_=xv[:, kc, :]), writes=xres[kc], dma="w")

    winv = w_in.rearrange("(kc p) n -> p kc n", p=128)
    nload = [0]

    def load_in(col0):
        s = nload[0] % 3
        nload[0] += 1
        p.add("pool", lambda e, s=s, col0=col0: e.dma_start(out=win[s][:, :, :], in_=winv[:, :, col0:col0 + 128]),
              writes=[winr[s]], dma="w")
        return s

    def load_g(b):
        s = b % 2
        for j, wsrc in enumerate((w_a, w_i)):
            p.add("pool", lambda e, s=s, j=j, wsrc=wsrc, b=b: e.dma_start(
                out=wg[s][j][:, :, :], in_=wsrc[b].rearrange("(ic p) n -> p ic n", p=128)),
                writes=[wgr[s][j]], dma="w")

    p.add("act", lambda e: e.activation(out=cl[:, 0:16], in_=lam, func=AF.Exp, scale=-1.0), reads=[prm_r], writes=[cl_r])
    p.add("act", lambda e: e.activation(out=cl[:, 0:16], in_=cl[:, 0:16], func=AF.Ln, bias=1.0), reads=[cl_r], writes=[cl_r])
    p.add("dve", lambda e: e.tensor_scalar(out=cl[:, 16:32], in0=cl[:, 0:16], scalar1=-8.0, scalar2=None, op0=ALU.mult),
          reads=[cl_r], writes=[cl_r])
    p.add("dve", lambda e: e.tensor_scalar(out=cl[:, 32:48], in0=cl[:, 0:16], scalar1=-16.0, scalar2=None, op0=ALU.mult),
          reads=[cl_r], writes=[cl_r])

    emit_rmsnorm(cx, ps, psXr[0], x, xres, gain, prm_r, h, hres, NCOL, ones, ones_r, scr)

    p.add("pool", lambda e: e.memset(zeros, 0.0), reads=list(hres), writes=[zer_r])

    order = []
    for b in range(8):
        for c in range(2):
            ch = 2 * b + c
            order.append(ch * 128)
            order.append(D + ch * 128)
    pending = [load_in(order[0]), load_in(order[1])]
    li = [2]
    load_g(0)

    def nextw():
        s = pending.pop(0)
        if li[0] < len(order):
            pending.append(load_in(order[li[0]]))
            li[0] += 1
        return s

    def front(b):
        bp = b % 2
        for c in range(2):
            ch = 2 * b + c
            s = nextw()
            for kc in range(KC):
                for (c0, c1) in tilesX:
                    p.add("pe", lambda e, s=s, kc=kc, c0=c0, c1=c1, c=c: e.matmul(
                        ps[:, OXs[c] + c0:OXs[c] + c1], lhsT=win[s][:, kc, :], rhs=h[:, kc, c0:c1],
                        start=(kc == 0), stop=(kc == KC - 1)), reads=[winr[s], hres[kc]], writes=[psXr[c]])
            psX = psXr[c]
            P = ps[:, OXs[c]:OXs[c] + NCOL]
            t = xc[bp][c]
            tres = xcr[bp][c]
            p.add("act", lambda e, t=t, P=P, ch=ch: e.activation(
                out=t[:, :], in_=P[:, 3:3 + NT], func=AF.Identity, bias=cb[:, ch:ch + 1],
                scale=cw[:, 48 + ch:48 + ch + 1]), reads=[psX, prm_r], writes=[tres])
            for k in (2, 1, 0):
                p.add("dve", lambda e, t=t, P=P, ch=ch, k=k: e.scalar_tensor_tensor(
                    out=t[:, :], in0=P[:, k:k + NT], scalar=cw[:, k * 16 + ch:k * 16 + ch + 1], in1=t[:, :],
                    op0=ALU.mult, op1=ALU.add), reads=[psX, prm_r, tres], writes=[tres])
            p.add("act", lambda e, t=t, bp=bp, c=c: e.activation(out=xcb[bp][c][:, :], in_=t[:, :], func=AF.Identity),
                  reads=[tres], writes=[xcbr[bp][c]])
            s = nextw()
            for kc in range(KC):
                for tt in range(2):
                    p.add("pe", lambda e, s=s, kc=kc, tt=tt: e.matmul(
                        ps[:, OY + tt * 512:OY + (tt + 1) * 512], lhsT=win[s][:, kc, :],
                        rhs=h[:, kc, HALO + tt * 512:HALO + (tt + 1) * 512],
                        start=(kc == 0), stop=(kc == KC - 1)), reads=[winr[s], hres[kc]], writes=[psY[tt]])
            p.add("act", lambda e, bp=bp, c=c: e.activation(out=gate[bp][c][:, :], in_=ps[:, OY:OY + NT], func=AF.Gelu_apprx_tanh),
                  reads=psY, writes=[gater[bp][c]])

    def back(b):
        bp = b % 2
        gs = b % 2
        for oc in range(2):
            ch = 2 * b + oc
            for j in range(2):
                dst, dres = (tr[oc], trr[oc]) if j == 0 else (ti[oc], tir[oc])
                bias = ba if j == 0 else bi
                for tt in range(2):
                    bkk = (oc * 4 + j * 2 + tt) % 3
                    for ic in range(2):
                        p.add("pe", lambda e, j=j, ic=ic, oc=oc, tt=tt, bkk=bkk: e.matmul(
                            ps[:, BANK[bkk]:BANK[bkk] + 512], lhsT=wg[gs][j][:, ic, oc * 128:(oc + 1) * 128],
                            rhs=xcb[bp][ic][:, tt * 512:(tt + 1) * 512], start=(ic == 0), stop=(ic == 1)),
                            reads=[wgr[gs][j], xcbr[bp][ic]], writes=[bankr[bkk]])
                    p.add("act", lambda e, dst=dst, bias=bias, ch=ch, tt=tt, bkk=bkk: e.activation(
                        out=dst[:, tt * 512:(tt + 1) * 512], in_=ps[:, BANK[bkk]:BANK[bkk] + 512], func=AF.Sigmoid,
                        bias=bias[:, ch:ch + 1]), reads=[bankr[bkk], prm_r], writes=[dres], relaxed=True)
        for oc in range(2):
            ch = 2 * b + oc
            p.add("act", lambda e, ch=ch, oc=oc: e.activation(out=ta[oc][:, :], in_=tr[oc][:, :], func=AF.Exp,
                                                             scale=cl[:, 16 + ch:17 + ch]), reads=[trr[oc], cl_r], writes=[tar[oc]])
            p.add("act", lambda e, ch=ch, oc=oc: e.activation(out=tr[oc][:, :], in_=tr[oc][:, :], func=AF.Exp,
                                                             scale=cl[:, 32 + ch:33 + ch]), reads=[trr[oc], cl_r], writes=[trr[oc]])
        for oc in range(2):
            p.add("act", lambda e, oc=oc: e.activation(out=tr[oc][:, :], in_=tr[oc][:, :], func=AF.Sqrt, scale=-1.0, bias=1.0),
                  reads=[trr[oc]], writes=[trr[oc]])
        for oc in range(2):
            ch = 2 * b + oc
            p.add("dve", lambda e, oc=oc: e.tensor_tensor(out=ti[oc][:, :], in0=ti[oc][:, :], in1=xc[bp][oc][:, :], op=ALU.mult),
                  reads=[tir[oc], xcr[bp][oc]], writes=[tir[oc]])
            p.add("dve", lambda e, oc=oc: e.tensor_tensor(out=ti[oc][:, :], in0=ti[oc][:, :], in1=tr[oc][:, :], op=ALU.mult),
                  reads=[tir[oc], trr[oc]], writes=[tir[oc]])
            p.add("dve", lambda e, oc=oc: e.tensor_tensor_scan(out=ths[:, :], data0=ta[oc][:, :], data1=ti[oc][:, :], initial=0.0,
                                                               op0=ALU.mult, op1=ALU.add), reads=[tar[oc], tir[oc]], writes=[thsr])
            p.add("dve", lambda e, oc=oc: e.tensor_tensor_scan(out=tac[:, :], data0=ta[oc][:, :], data1=zeros, initial=1.0,
                                                               op0=ALU.mult, op1=ALU.add), reads=[tar[oc], zer_r], writes=[tacr])
            p.add("dve", lambda e, oc=oc: e.tensor_tensor(out=tg1, in0=ths[:, :], in1=gate[bp][oc][:, :], op=ALU.mult),
                  reads=[thsr, gater[bp][oc]], writes=[tg1r])
            p.add("dve", lambda e, oc=oc: e.tensor_tensor(out=tz, in0=tac[:, :], in1=gate[bp][oc][:, :], op=ALU.mult),
                  reads=[tacr, gater[bp][oc]], writes=[tzr])
            p.add("dve", lambda e, ch=ch: e.tensor_copy(out=carr[:, ch:ch + 1], in_=tac[:, NT - 1:NT]), reads=[tacr], writes=[carr_r])
            p.add("dve", lambda e, ch=ch: e.tensor_copy(out=carr[:, 16 + ch:17 + ch], in_=ths[:, NT - 1:NT]), reads=[thsr], writes=[carr_r])
            p.add("sp", lambda e, ch=ch: e.dma_start(out=g1T[ch * 128:(ch + 1) * 128, :], in_=tg1), reads=[tg1r], dma="r")
            p.add("sp", lambda e, ch=ch: e.dma_start(out=zT[ch * 128:(ch + 1) * 128, :], in_=tz), reads=[tzr], dma="r")

    load_g(1)
    for b in range(9):
        if b < 8:
            front(b)
        if b >= 1:
            back(b - 1)
            if b + 1 < 8:
                load_g(b + 1)
    p.add("sp", lambda e: e.dma_start(out=carr_o[:, :], in_=carr[:, :]), reads=[carr_r], dma="r")
    return cx.finish()


def rec_params(g, cw, cb, ba, bi, lam):
    parts = [chunked(g)] + [chunked(cw[k]) for k in range(4)] + [chunked(cb), chunked(ba.reshape(-1)), chunked(bi.reshape(-1)), chunked(lam)]
    return np.ascontiguousarray(np.concatenate(parts, axis=1))


def build_attn(stage=99):
    cx = Ctx()
    nc, p = cx.nc, cx.p
    HALO = 128
    NCOL = NT + HALO
    NTB = NCOL // 128
    xT = cx.din("xT", [D, NCOL])
    w_qkv = cx.din("w_qkv", [D, 3072])
    w_o = cx.din("w_o", [D, D])
    NPRM = 16 + 2 + 32 + 1
    prm_d = cx.din("prm", [128, NPRM])
    msk_d = cx.din("msk", [128, 1024])
    yT = cx.dout("yT", [D, NT])

    ps = cx.psum()
    x = cx.sb("x", [128, KC, NCOL], F32)
    h = cx.sb("h", [128, KC, NCOL], BF16)
    sq = [cx.sb(f"sq{i}", [128, NCOL], F32) for i in range(2)]
    rstd = cx.sb("rstd", [128, NCOL], F32)
    prm = cx.sb("prm_sb", [128, NPRM], F32)
    es = cx.sb("es", [128, 32], F32)
    ones = cx.sb("ones", [128, 128], F32)
    bd = cx.sb("bd", [128, 128], F32)
    epsb = cx.sb("epsb", [128, 1], F32)
    msk = cx.sb("msk_sb", [128, 2, 512], BF16)
    Kd = cx.sb("Kd", [128, 8, NCOL], BF16)
    Vaug = cx.sb("Vaug", [128, NTB, 8, 128], BF16)
    wbig = cx.sb("wbig", [128, 8192], BF16)
    wq = [cx.sb(f"wq{i}", [128, KC, 128], BF16) for i in range(2)]
    Qn = cx.sb("Qn", [128, 4, NT], BF16)
    AO = cx.sb("AO", [128, 2, NT], BF16)
    PT = [cx.sb(f"PT{i}", [128, 2, 512], BF16) for i in range(2)]

    xres = [[Res(f"x{m}_h"), Res(f"x{m}_0"), Res(f"x{m}_1")] for m in range(KC)]
    hres = [Res(f"h{k}") for k in range(KC)]
    sqr = [Res("sq0"), Res("sq1")]
    rstdr, prm_r, ones_r, bd_r, es_r, msk_r = (Res(n) for n in ("rstd", "prm", "ones", "bd", "es", "msk"))
    Kdr = [Res(f"Kd{g}") for g in range(8)]
    Vr = Res("Vaug")
    wbr = [Res("wb0"), Res("wb1")]
    wqr = [[Res(f"wq{i}_0"), Res(f"wq{i}_1")] for i in range(2)]
    Qnr = [Res(f"Qn{i}") for i in range(4)]
    AOr = [Res("AO0"), Res("AO1")]
    PTr = [[Res(f"PT{i}_{k}") for k in range(2)] for i in range(2)]
    sqt = [sq[0][:, 0:512], sq[0][:, 512:1024]]
    rsb = [sq[1][:, 0:512], sq[1][:, 512:1024]]
    den = rstd[:, 0:512]
    rec = rstd[:, 512:1024]
    sqtr = [Res("sqt0"), Res("sqt1")]
    rsr = [Res("rs0"), Res("rs1")]
    denr, recr = Res("den"), Res("rec")
    regr = [Res("regA"), Res("regB")]
    REG = [0, 1536]
    statr = Res("stat")
    STAT = 3072
    pvr = Res("pv")
    PVB = 3584
    SC = [1024, 2560]
    scr_ = [Res("sc0"), Res("sc1")]
    OB = [0, 512, 1536, 2048]
    obr = [Res(f"ob{i}") for i in range(4)]

    gain = prm[:, 0:16]
    qg = prm[:, 16:17]
    kg = prm[:, 17:18]
    hb = prm[:, 50:51]
    scr = {"sq": sq, "sqr": sqr, "rstd": rstd, "rstdr": rstdr, "eps": epsb}

    p.add("pool", lambda e: e.memset(ones[:, :], 1.0), writes=[ones_r])
    p.add("pool", lambda e: e.memset(epsb[:, :], EPS), writes=[prm_r])
    p.add("pool", lambda e: e.memset(bd[:, :], 0.0), writes=[bd_r])
    p.add("pool", lambda e: e.memset(bd[0:64, 0:64], 1.0), writes=[bd_r])
    p.add("pool", lambda e: e.memset(bd[64:128, 64:128], 1.0), writes=[bd_r])
    p.add("pool", lambda e: e.memset(Vaug[:, :, :, :], 1.0), writes=[Vr])
    p.add("sp", lambda e: e.dma_start(out=prm[:, :], in_=prm_d[:, :]), writes=[prm_r], dma="w")
    p.add("pool", lambda e: e.dma_start(out=msk[:, :, :], in_=msk_d.rearrange("p (k n) -> p k n", k=2)),
          writes=[msk_r], dma="w")
    xv = xT.rearrange("(kc p) n -> p kc n", p=128)
    for kc in range(KC):
        p.add("sp", lambda e, kc=kc: e.dma_start(out=x[:, kc, :], in_=xv[:, kc, :]), writes=xres[kc], dma="w")
    p.add("act", lambda e: e.activation(out=es[:, :], in_=prm[:, 18:50], func=AF.Exp), reads=[prm_r], writes=[es_r])

    wv = wbig[:, :].rearrange("p (kc n) -> p kc n", kc=KC)
    wqkv_v = w_qkv.rearrange("(kc p) n -> p kc n", p=128)
    p.add("pool", lambda e: e.dma_start(out=wv, in_=wqkv_v[:, :, 2560:3072]), writes=wbr, dma="w")

    nq = [0]

    def load_k(g):
        s = nq[0] % 2
        nq[0] += 1
        for half in range(2):
            p.add("pool", lambda e, s=s, g=g, half=half: e.dma_start(
                out=wq[s][:, :, half * 64:(half + 1) * 64], in_=wqkv_v[:, :, 2048 + g * 64:2048 + (g + 1) * 64]),
                writes=[wqr[s][half]], dma="w")
        return s

    def load_q(hd):
        s = nq[0] % 2
        nq[0] += 1
        for half in range(2):
            p.add("pool", lambda e, s=s, hd=hd, half=half: e.dma_start(
                out=wq[s][:, :, half * 64:(half + 1) * 64], in_=wqkv_v[:, :, hd * 64:(hd + 1) * 64]),
                writes=[wqr[s][half]], dma="w")
        return s

    def load_o(g):
        s = g % 2
        dst = wbig[:, s * 4096:(s + 1) * 4096].rearrange("p (c n) -> p c n", c=2)
        src = w_o[g * 256:(g + 1) * 256, :].rearrange("(c p) n -> p c n", p=128)
        p.add("pool", lambda e, dst=dst, src=src: e.dma_start(out=dst, in_=src), writes=[wbr[s]], dma="w")

    emit_rmsnorm(cx, ps, regr[0], x, xres, gain, prm_r, h, hres, NCOL, ones, ones_r, scr)

    nstat = [0]

    def proj_norm(s, ri, tiles, hoff, gain_ap, dst_fn, dst_res):
        R = REG[ri]
        for kc in range(KC):
            for (c0, c1) in tiles:
                p.add("pe", lambda e, s=s, kc=kc, c0=c0, c1=c1, R=R: e.matmul(
                    ps[:, R + c0:R + c1], lhsT=wq[s][:, kc, :], rhs=h[:, kc, hoff + c0:hoff + c1],
                    start=(kc == 0), stop=(kc == KC - 1)), reads=wqr[s] + [hres[kc]], writes=[regr[ri]])
        for (c0, c1) in tiles:
            n = c1 - c0
            i = nstat[0] % 2
            nstat[0] += 1
            src = ps[:, R + c0:R + c1]
            p.add("act", lambda e, i=i, n=n, src=src: e.activation(out=sqt[i][:, 0:n], in_=src, func=AF.Square),
                  reads=[regr[ri]], writes=[sqtr[i]])
            p.add("pe", lambda e, i=i, n=n: e.matmul(ps[:, STAT:STAT + n], lhsT=bd[:, :], rhs=sqt[i][:, 0:n],
                                                     start=True, stop=True), reads=[bd_r, sqtr[i]], writes=[statr])
            p.add("act", lambda e, i=i, n=n: e.activation(out=sqt[i][:, 0:n], in_=ps[:, STAT:STAT + n], func=AF.Sqrt,
                                                          bias=epsb[:, 0:1], scale=1.0 / 64), reads=[statr, prm_r], writes=[sqtr[i]])
            p.add("dve", lambda e, i=i, n=n: e.reciprocal(out=rsb[i][:, 0:n], in_=sqt[i][:, 0:n]), reads=[sqtr[i]], writes=[rsr[i]])
            dst = dst_fn(c0, c1)
            p.add("dve", lambda e, i=i, n=n, src=src, dst=dst: e.scalar_tensor_tensor(
                out=dst, in0=src, scalar=gain_ap, in1=rsb[i][:, 0:n], op0=ALU.mult, op1=ALU.mult),
                reads=[regr[ri], rsr[i], prm_r], writes=[dst_res], relaxed=True)

    BV = [1536, 2048, 2560]
    bvr = [Res(f"bv{i}") for i in range(3)]
    for tb in range(NTB):
        b = tb % 3
        for kc in range(KC):
            p.add("pe", lambda e, tb=tb, kc=kc, b=b: e.matmul(
                ps[:, BV[b]:BV[b] + 512], lhsT=h[:, kc, tb * 128:(tb + 1) * 128], rhs=wv[:, kc, :],
                start=(kc == 0), stop=(kc == KC - 1)), reads=wbr + [hres[kc]], writes=[bvr[b]])
        p.add("act", lambda e, tb=tb, b=b: e.activation(
            out=Vaug[:, tb, :, 0:64], in_=ps[:, BV[b]:BV[b] + 512].rearrange("p (g d) -> p g d", g=8), func=AF.Identity),
            reads=[bvr[b], regr[1]], writes=[Vr], relaxed=True)

    if stage <= 1:
        return cx.finish()
    tilesK = token_tiles(NCOL, 0)
    ks = load_k(0)
    for g in range(8):
        s = ks
        if g + 1 < 8:
            ks = load_k(g + 1)
        proj_norm(s, g % 2, tilesK, 0, kg, lambda c0, c1, g=g: Kd[:, g, c0:c1], Kdr[g])

    if stage <= 2:
        return cx.finish()
    tilesQ = token_tiles(NT, 0)
    qs = [load_q(0), load_q(1)]
    load_o(0)
    for g in range(8):
        for hh in range(4):
            s = qs.pop(0)
            proj_norm(s, hh % 2, tilesQ, HALO, qg, lambda c0, c1, hh=hh: Qn[:, hh, c0:c1], Qnr[hh])
            nxt = g * 4 + hh + 2
            if nxt < 32:
                qs.append(load_q(nxt))
        if g + 1 < 8:
            load_o(g + 1)
        if stage <= 3:
            return cx.finish()
        for qb in range(8):
            i = qb % 2
            for kb in range(2):
                for hh in range(4):
                    p.add("pe", lambda e, g=g, qb=qb, kb=kb, hh=hh: e.matmul(
                        ps[:, SC[kb] + hh * 128:SC[kb] + (hh + 1) * 128],
                        lhsT=Kd[:, g, (qb + kb) * 128:(qb + kb + 1) * 128],
                        rhs=Qn[:, hh, qb * 128:(qb + 1) * 128], start=True, stop=True),
                        reads=[Kdr[g], Qnr[hh]], writes=[scr_[kb]])
                bias = hb if (qb == 0 and kb == 0) else 0.0
                p.add("act", lambda e, i=i, kb=kb, bias=bias: e.activation(
                    out=PT[i][:, kb, :], in_=ps[:, SC[kb]:SC[kb] + 512], func=AF.Exp, bias=bias, scale=0.0625),
                    reads=[scr_[kb], prm_r], writes=[PTr[i][kb]])
                p.add("pool", lambda e, i=i, kb=kb: e.tensor_tensor(
                    out=PT[i][:, kb, :], in0=PT[i][:, kb, :], in1=msk[:, kb, :], op=ALU.mult),
                    reads=[PTr[i][kb], msk_r], writes=[PTr[i][kb]])
            for kb in range(2):
                p.add("pe", lambda e, g=g, qb=qb, kb=kb, i=i: e.matmul(
                    ps[:, PVB:PVB + 512], lhsT=Vaug[:, qb + kb, g, :], rhs=PT[i][:, kb, :],
                    start=(kb == 0), stop=(kb == 1)), reads=[Vr, PTr[i][kb]], writes=[pvr])
            for hh in range(4):
                p.add("dve", lambda e, g=g, hh=hh: e.tensor_scalar(
                    out=den[64:128, hh * 128:(hh + 1) * 128], in0=ps[64:128, PVB + hh * 128:PVB + (hh + 1) * 128],
                    scalar1=es[64:128, g * 4 + hh:g * 4 + hh + 1], scalar2=None, op0=ALU.add),
                    reads=[pvr, es_r], writes=[denr], relaxed=True)
            p.add("dve", lambda e: e.reciprocal(out=rec[0:64, :], in_=den[64:128, :]), reads=[denr], writes=[recr])
            for hh in range(4):
                pb = (hh % 2) * 64
                c = hh // 2
                p.add("dve", lambda e, hh=hh, pb=pb, c=c, qb=qb: e.tensor_tensor(
                    out=AO[pb:pb + 64, c, qb * 128:(qb + 1) * 128], in0=ps[0:64, PVB + hh * 128:PVB + (hh + 1) * 128],
                    in1=rec[0:64, hh * 128:(hh + 1) * 128], op=ALU.mult),
                    reads=[pvr, recr], writes=[AOr[c]], relaxed=True)
        if stage <= 4:
            return cx.finish()
        so = g % 2
        wo = wbig[:, so * 4096:(so + 1) * 4096].rearrange("p (c n) -> p c n", c=2)
        for m in range(KC):
            for tt in range(2):
                b = (m * 2 + tt) % 4
                for c in range(2):
                    p.add("pe", lambda e, wo=wo, m=m, tt=tt, b=b, c=c: e.matmul(
                        ps[:, OB[b]:OB[b] + 512], lhsT=wo[:, c, m * 128:(m + 1) * 128],
                        rhs=AO[:, c, tt * 512:(tt + 1) * 512], start=(c == 0), stop=(c == 1)),
                        reads=[wbr[so], AOr[c]], writes=[obr[b], regr[0 if b < 2 else 1]])
                xs = x[:, m, HALO + tt * 512:HALO + (tt + 1) * 512]
                p.add("dve", lambda e, xs=xs, b=b: e.tensor_tensor(out=xs, in0=ps[:, OB[b]:OB[b] + 512], in1=xs, op=ALU.add),
                      reads=[obr[b], regr[0 if b < 2 else 1], xres[m][1 + tt]], writes=[xres[m][1 + tt]])
                if g == 7 and tt == 1:
                    yv = yT.rearrange("(kc p) n -> p kc n", p=128)
                    p.add("sp", lambda e, m=m: e.dma_start(out=yv[:, m, :], in_=x[:, m, HALO:HALO + NT]),
                          reads=[xres[m][1], xres[m][2]], dma="r")
    return cx.finish()


def attn_params(g, qgain, kgain, sinks, halo_bias):
    qg = np.tile(np.asarray(qgain, np.float32), 2).reshape(128, 1)
    kg = np.tile(np.asarray(kgain, np.float32), 2).reshape(128, 1)
    sk = np.tile(np.asarray(sinks, np.float32).reshape(1, 32), (128, 1))
    hb = np.full((128, 1), halo_bias, np.float32)
    return np.ascontiguousarray(np.concatenate([chunked(g), qg, kg, sk, hb], axis=1))


def attn_masks():
    k = np.arange(128)[:, None]
    q = np.arange(128)[None, :]
    mprev = (k > q).astype(np.float32)
    mcur = (q >= k).astype(np.float32)
    return np.ascontiguousarray(np.concatenate([np.tile(mprev, (1, 4)), np.tile(mcur, (1, 4))], axis=1))


_CACHE = {}


def _prog(name):
    if name not in _CACHE:
        _CACHE[name] = {"attn": build_attn2, "ffn": lambda: build_ffn(False),
                        "ffnp": lambda: build_ffn(True), "rec1": build_rec1b}[name]()
    return _CACHE[name]


def _run(name, in_maps):
    res = run_bass_kernel_spmd(_prog(name), in_maps, core_ids=list(range(NCORES)))
    return res.results


def _halo_slices(aT, halo):
    pad = np.concatenate([np.zeros((aT.shape[0], halo), aT.dtype), aT], axis=1)
    return [np.ascontiguousarray(pad[:, c * NT:c * NT + NT + halo]) for c in range(NCORES)]


def kernel(x, mix_norm, ffn_norm, attn_w_qkv, attn_q_gain, attn_k_gain, attn_sinks, attn_w_o,
           rec_w_in, rec_conv_w, rec_conv_b, rec_w_a, rec_b_a, rec_w_i, rec_b_i, rec_lambda,
           rec_w_out, ffn_w_up, ffn_conv_w, ffn_conv_b, ffn_w_down):
    f = lambda a: np.ascontiguousarray(np.asarray(a, dtype=np.float32))
    xT = np.ascontiguousarray(f(x)[0].T)
    masks = attn_masks2()
    for layer in range(4):
        j = layer // 2
        fprm = ffn_params(f(ffn_norm)[layer], f(ffn_conv_w)[layer], f(ffn_conv_b)[layer])
        wup, wdn = f(ffn_w_up)[layer], f(ffn_w_down)[layer]
        if layer % 2 == 0:
            xs = _halo_slices(xT, 128)
            wqkv, wo = f(attn_w_qkv)[j], f(attn_w_o)[j]
            in_maps = [{"xT": xs[c], "w_qkv": wqkv, "w_o": wo, "msk": masks,
                        "prm": attn_params(f(mix_norm)[layer], f(attn_q_gain)[j], f(attn_k_gain)[j],
                                           f(attn_sinks)[j], -30000.0 if c == 0 else 0.0)}
                       for c in range(NCORES)]
            res = _run("attn", in_maps)
            xT = np.concatenate([r["yT"] for r in res], axis=1)
            xs = _halo_slices(xT, 2)
            in_maps = [{"xT": xs[c], "w_up": wup, "w_down": wdn, "prm": fprm} for c in range(NCORES)]
            res = _run("ffn", in_maps)
        else:
            xs = _halo_slices(xT, 3)
            rprm = rec_params(f(mix_norm)[layer], f(rec_conv_w)[j], f(rec_conv_b)[j], f(rec_b_a)[j],
                              f(rec_b_i)[j], f(rec_lambda)[j])
            in_maps = [{"xT": xs[c], "w_in": f(rec_w_in)[j], "w_a": f(rec_w_a)[j], "w_i": f(rec_w_i)[j],
                        "prm": rprm} for c in range(NCORES)]
            res = _run("rec1", in_maps)
            g1T = np.concatenate([r["g1T"] for r in res], axis=1)
            zT = np.concatenate([r["zT"] for r in res], axis=1)
            carr = np.ascontiguousarray(np.stack([r["carr"] for r in res], axis=-1).reshape(128, 256))
            xs, gs, zs = _halo_slices(xT, 2), _halo_slices(g1T, 2), _halo_slices(zT, 2)
            in_maps = []
            for c in range(NCORES):
                m = np.zeros((128, 16), np.float32)
                m[:, 0:max(c, 0)] = 1.0
                m[:, 8:8 + max(c - 1, 0)] = 1.0
                in_maps.append({"xT": xs[c], "g1T": gs[c], "zT": zs[c], "carr": carr, "msk": m,
                                "w_out": f(rec_w_out)[j], "w_up": wup, "w_down": wdn, "prm": fprm})
            res = _run("ffnp", in_maps)
        xT = np.concatenate([r["yT"] for r in res], axis=1)
    return np.ascontiguousarray(xT.T)[None].astype(np.float32)


def build_attn2():
    cx = Ctx()
    nc, p = cx.nc, cx.p
    HALO = 128
    NCOL = NT + HALO
    NTB = NCOL // 128
    xT = cx.din("xT", [D, NCOL])
    w_qkv = cx.din("w_qkv", [D, 3072])
    w_o = cx.din("w_o", [D, D])
    NPRM = 16 + 2 + 32 + 1
    prm_d = cx.din("prm", [128, NPRM])
    msk_d = cx.din("msk", [128, 640])
    yT = cx.dout("yT", [D, NT])

    ps = cx.psum()
    bk = [Res(f"bk{i}") for i in range(8)]
    x = cx.sb("x", [128, KC, NCOL], F32)
    h = cx.sb("h", [128, KC, NCOL], BF16)
    sq = [cx.sb(f"sq{i}", [128, NCOL], F32) for i in range(2)]
    rstd = cx.sb("rstd", [128, NCOL], F32)
    prm = cx.sb("prm_sb", [128, NPRM], F32)
    es = cx.sb("es", [128, 32], F32)
    ones = cx.sb("ones", [128, 128], F32)
    bd = cx.sb("bd", [128, 128], F32)
    epsb = cx.sb("epsb", [128, 2], F32)
    msk = cx.sb("msk_sb", [128, 640], BF16)
    onesb = cx.sb("onesb", [128, 128], BF16)
    Kd = cx.sb("Kd", [128, 8, NCOL], BF16)
    NVB = NTB * 8
    VO = cx.sb("VO", [128, (NVB + 1) * 64], BF16)
    wbig = cx.sb("wbig", [128, 8192], BF16)
    wq = [cx.sb(f"wq{i}", [128, KC, 128], BF16) for i in range(2)]
    Qn = [cx.sb(f"Qn{i}", [128, 4, NT], BF16) for i in range(2)]
    AO = [cx.sb(f"AO{i}", [128, 2, NT], BF16) for i in range(2)]
    PT = [cx.sb(f"PT{i}", [128, 512], BF16) for i in range(2)]

    xres = [[Res(f"x{m}_h"), Res(f"x{m}_0"), Res(f"x{m}_1")] for m in range(KC)]
    hres = [Res(f"h{k}") for k in range(KC)]
    sqr = [Res("sq0"), Res("sq1")]
    rstdr, prm_r, ones_r, bd_r, es_r, msk_r = (Res(n) for n in ("rstd", "prm", "ones", "bd", "es", "msk"))
    Kdr = [Res(f"Kd{g}") for g in range(8)]
    Vr = Res("VO")
    wbr = [Res("wb0"), Res("wb1")]
    wqr = [[Res(f"wq{i}_0"), Res(f"wq{i}_1")] for i in range(2)]
    Qnr = [[Res(f"Qn{i}_{hh}") for hh in range(4)] for i in range(2)]
    AOr = [[Res(f"AO{i}_{c}") for c in range(2)] for i in range(2)]
    PTr = [Res("PT0"), Res("PT1")]
    sqt = [sq[0][:, 0:512], sq[0][:, 512:1024]]
    rsb = [sq[1][:, 0:512], sq[1][:, 512:1024]]
    den = rstd[:, 0:256]
    rec = rstd[:, 512:768]
    sqtr = [Res("sqt0"), Res("sqt1")]
    rsr = [Res("rs0"), Res("rs1")]
    denr, recr = Res("den"), Res("rec")
    B = lambda i: i * 512
    STATB, SCB, PVB, OPB = 4, 5, 6, 7

    gain = prm[:, 0:16]
    qg = prm[:, 16:17]
    kg = prm[:, 17:18]
    hb = prm[:, 50:51]
    scr = {"sq": sq, "sqr": sqr, "rstd": rstd, "rstdr": rstdr, "eps": epsb}

    p.add("pool", lambda e: e.memset(ones[:, :], 1.0), writes=[ones_r])
    p.add("pool", lambda e: e.memset(onesb[:, :], 1.0), writes=[ones_r])
    p.add("pool", lambda e: e.memset(epsb[:, 0:1], EPS), writes=[prm_r])
    p.add("pool", lambda e: e.memset(epsb[:, 1:2], 64 * EPS), writes=[prm_r])
    p.add("pool", lambda e: e.memset(bd[:, :], 0.0), writes=[bd_r])
    p.add("pool", lambda e: e.memset(bd[0:64, 0:64], 1.0), writes=[bd_r])
    p.add("pool", lambda e: e.memset(bd[64:128, 64:128], 1.0), writes=[bd_r])
    p.add("pool", lambda e: e.memset(VO[:, NVB * 64:(NVB + 1) * 64], 1.0), writes=[Vr])
    p.add("sp", lambda e: e.dma_start(out=prm[:, :], in_=prm_d[:, :]), writes=[prm_r], dma="w")
    p.add("pool", lambda e: e.dma_start(out=msk[:, :], in_=msk_d[:, :]), writes=[msk_r], dma="w")
    xv = xT.rearrange("(kc p) n -> p kc n", p=128)
    for kc in range(KC):
        p.add("sp", lambda e, kc=kc: e.dma_start(out=x[:, kc, :], in_=xv[:, kc, :]), writes=xres[kc], dma="w")
    p.add("act", lambda e: e.activation(out=es[:, :], in_=prm[:, 18:50], func=AF.Exp), reads=[prm_r], writes=[es_r])

    wv = wbig[:, :].rearrange("p (kc n) -> p kc n", kc=KC)
    wqkv_v = w_qkv.rearrange("(kc p) n -> p kc n", p=128)
    p.add("pool", lambda e: e.dma_start(out=wv, in_=wqkv_v[:, :, 2560:3072]), writes=wbr, dma="w")

    nq = [0]

    def load_w(col0):
        s = nq[0] % 2
        nq[0] += 1
        for half in range(2):
            p.add("pool", lambda e, s=s, col0=col0, half=half: e.dma_start(
                out=wq[s][:, :, half * 64:(half + 1) * 64], in_=wqkv_v[:, :, col0:col0 + 64]),
                writes=[wqr[s][half]], dma="w")
        return s

    def load_o(g):
        s = g % 2
        dst = wbig[:, s * 4096:(s + 1) * 4096].rearrange("p (c n) -> p c n", c=2)
        src = w_o[g * 256:(g + 1) * 256, :].rearrange("(c p) n -> p c n", p=128)
        p.add("pool", lambda e, dst=dst, src=src: e.dma_start(out=dst, in_=src), writes=[wbr[s]], dma="w")

    emit_rmsnorm(cx, ps, bk[0], x, xres, gain, prm_r, h, hres, NCOL, ones, ones_r, scr,
                 extra_ps_res=[bk[1], bk[2]], use_ln=True)

    for tb in range(NTB):
        b = 5 + tb % 3
        for kc in range(KC):
            p.add("pe", lambda e, tb=tb, kc=kc, b=b: e.matmul(
                ps[:, B(b):B(b) + 512], lhsT=h[:, kc, tb * 128:(tb + 1) * 128], rhs=wv[:, kc, :],
                start=(kc == 0), stop=(kc == KC - 1)), reads=wbr + [hres[kc]], writes=[bk[b]])
        p.add("act", lambda e, tb=tb, b=b: e.activation(
            out=VO[:, tb * 512:(tb + 1) * 512], in_=ps[:, B(b):B(b) + 512], func=AF.Identity),
            reads=[bk[b]], writes=[Vr], relaxed=True)

    nstat = [0]

    def proj_mms(s, banks, tiles, hoff, tile_major=False):
        out = []
        order = [(kc, t) for kc in range(KC) for t in range(len(tiles))] if not tile_major else \
                [(kc, t) for t in range(len(tiles)) for kc in range(KC)]
        for kc, t in order:
            c0, c1 = tiles[t]
            if True:
                def f(s=s, kc=kc, t=t, c0=c0, c1=c1):
                    p.add("pe", lambda e: e.matmul(
                        ps[:, B(banks[t]):B(banks[t]) + (c1 - c0)], lhsT=wq[s][:, kc, :],
                        rhs=h[:, kc, hoff + c0:hoff + c1], start=(kc == 0), stop=(kc == KC - 1)),
                        reads=wqr[s] + [hres[kc]], writes=[bk[banks[t]]])
                out.append(f)
        return out

    def norm_steps(banks, tiles, gain_ap, dst_fn, dst_res):
        idx = []
        for t in range(len(tiles)):
            idx.append(nstat[0] % 2)
            nstat[0] += 1

        def phase_a():
            for t, (c0, c1) in enumerate(tiles):
                n = c1 - c0
                i = idx[t]
                src = ps[:, B(banks[t]):B(banks[t]) + n]
                p.add("act", lambda e, i=i, n=n, src=src: e.activation(out=sqt[i][:, 0:n], in_=src, func=AF.Square),
                      reads=[bk[banks[t]]], writes=[sqtr[i]])

        def phase_b(t):
            c0, c1 = tiles[t]
            n = c1 - c0
            i = idx[t]
            src = ps[:, B(banks[t]):B(banks[t]) + n]
            br = bk[banks[t]]
            p.add("pe", lambda e: e.matmul(ps[:, B(STATB):B(STATB) + n], lhsT=bd[:, :], rhs=sqt[i][:, 0:n],
                                           start=True, stop=True), reads=[bd_r, sqtr[i]], writes=[bk[STATB]])
            p.add("act", lambda e: e.activation(out=sqt[i][:, 0:n], in_=ps[:, B(STATB):B(STATB) + n], func=AF.Ln,
                                                bias=epsb[:, 1:2], scale=1.0), reads=[bk[STATB], prm_r], writes=[sqtr[i]])
            p.add("act", lambda e: e.activation(out=rsb[i][:, 0:n], in_=sqt[i][:, 0:n], func=AF.Exp, scale=-0.5),
                  reads=[sqtr[i]], writes=[rsr[i]])
            dst = dst_fn(c0, c1)
            p.add("dve", lambda e: e.scalar_tensor_tensor(
                out=dst, in0=src, scalar=gain_ap, in1=rsb[i][:, 0:n], op0=ALU.mult, op1=ALU.mult),
                reads=[br, rsr[i], prm_r], writes=[dst_res], relaxed=True)

        return [phase_a] + [(lambda t=t: phase_b(t)) for t in range(len(tiles))]

    def norm(banks, tiles, gain_ap, dst_fn, dst_res):
        for t in range(len(tiles)):
            for f in norm_steps(banks[t:t + 1], tiles[t:t + 1], gain_ap, dst_fn, dst_res):
                f()

    tilesK = token_tiles(NCOL, 0)
    KB = [[0, 1, 5], [2, 3, 6]]
    ks = load_w(2048)
    pend = None
    for g in range(8):
        s = ks
        if g + 1 < 8:
            ks = load_w(2048 + (g + 1) * 64)
        for f in proj_mms(s, KB[g % 2], tilesK, 0):
            f()
        if pend is not None:
            pend()
        pend = (lambda g=g: norm(KB[g % 2], tilesK, kg, lambda c0, c1: Kd[:, g, c0:c1], Kdr[g]))
    pend()

    tilesQ = token_tiles(NT, 0)
    QB = [[0, 1], [2, 3]]

    qslot = {}

    def prefetch_q(hd):
        if hd < 32 and hd not in qslot:
            qslot[hd] = load_w(hd * 64)

    def emit_q_head(g, hh, chunks):
        hd = g * 4 + hh
        gp = g % 2
        prefetch_q(hd)
        s = qslot[hd]
        qbanks = [(hd * 2) % 3, (hd * 2 + 1) % 3]
        mms = proj_mms(s, qbanks, tilesQ, HALO, tile_major=True)
        per = (len(mms) + chunks - 1) // chunks
        out = []
        for ci in range(chunks):
            part = mms[ci * per:(ci + 1) * per]
            last = ci == chunks - 1

            def f(part=part, last=last, first=(ci == 0)):
                if first:
                    prefetch_q(hd + 1)
                    while any(set(r) & set(qbanks) for r, _ in pending_norm):
                        pending_norm.pop(0)[1]()
                half = len(part) // 2
                pop_pending()
                for m in part[:half]:
                    m()
                pop_pending()
                for m in part[half:]:
                    m()
                if last:
                    sts = norm_steps(qbanks, tilesQ, qg, lambda c0, c1: Qn[gp][:, hh, c0:c1], Qnr[gp][hh])
                    pending_norm.extend(zip([qbanks, qbanks[0:1], qbanks[1:2]], sts))
            out.append(f)
        return out

    pending_norm = []

    def pop_pending():
        if pending_norm:
            pending_norm.pop(0)[1]()

    def flush_norm():
        while pending_norm:
            pending_norm.pop(0)[1]()

    def o_units(g, banks=(7,)):
        gp = g % 2
        so = g % 2
        wo = wbig[:, so * 4096:(so + 1) * 4096].rearrange("p (c n) -> p c n", c=2)
        units = []
        for m in range(KC):
            for tt in range(2):
                def f(m=m, tt=tt):
                    ob = banks[(m * 2 + tt) % len(banks)]
                    for c in range(2):
                        p.add("pe", lambda e, c=c: e.matmul(
                            ps[:, B(ob):B(ob) + 512], lhsT=wo[:, c, m * 128:(m + 1) * 128],
                            rhs=AO[gp][:, c, tt * 512:(tt + 1) * 512], start=(c == 0), stop=(c == 1)),
                            reads=[wbr[so], AOr[gp][c]], writes=[bk[ob]])
                    xs = x[:, m, HALO + tt * 512:HALO + (tt + 1) * 512]
                    p.add("dve", lambda e: e.tensor_tensor(out=xs, in0=ps[:, B(ob):B(ob) + 512], in1=xs, op=ALU.add),
                          reads=[bk[ob], xres[m][1 + tt]], writes=[xres[m][1 + tt]])
                    if g == 7 and tt == 1:
                        yv = yT.rearrange("(kc p) n -> p kc n", p=128)
                        p.add("sp", lambda e: e.dma_start(out=yv[:, m, :], in_=x[:, m, HALO:HALO + NT]),
                              reads=[xres[m][1], xres[m][2]], dma="r")
                units.append(f)
        return units

    prefetch_q(0)
    for hh in range(4):
        for f in emit_q_head(0, hh, 1):
            f()
    flush_norm()
    load_o(0)

    nstep = [0]
    for g in range(8):
        gp = g % 2
        qwork = []
        if g + 1 < 8:
            for hh in range(4):
                qwork.append((g + 1, hh))
        owork = o_units(g - 1, banks=(7, 3)) if g > 0 else []
        if g > 0:
            load_o(g)
        cur_q = []
        for qb in range(8):
            for hp in range(2):
                step = qb * 2 + hp
                i = nstep[0] % 2
                nstep[0] += 1
                p.add("pe", lambda e: e.matmul(ps[:, B(SCB):B(SCB) + 512], lhsT=msk[:, 512:640], rhs=msk[:, 0:512],
                                               start=True, stop=False), reads=[msk_r], writes=[bk[SCB]])
                for kb in range(2):
                    for hl in range(2):
                        hh = hp * 2 + hl
                        col = B(SCB) + kb * 256 + hl * 128
                        p.add("pe", lambda e, kb=kb, hh=hh, col=col, qb=qb, g=g, gp=gp, hl=hl: e.matmul(
                            ps[:, col:col + 128], lhsT=Kd[:, g, (qb + kb) * 128:(qb + kb + 1) * 128],
                            rhs=Qn[gp][:, hh, qb * 128:(qb + 1) * 128], start=False, stop=(kb == 1 and hl == 1)),
                            reads=[Kdr[g], Qnr[gp][hh]], writes=[bk[SCB]])
                if qb == 0:
                    p.add("act", lambda e, i=i: e.activation(out=PT[i][:, 0:256], in_=ps[:, B(SCB):B(SCB) + 256],
                                                             func=AF.Exp, bias=hb, scale=4.0),
                          reads=[bk[SCB], prm_r], writes=[PTr[i]])
                    p.add("act", lambda e, i=i: e.activation(out=PT[i][:, 256:512], in_=ps[:, B(SCB) + 256:B(SCB) + 512],
                                                             func=AF.Exp, scale=4.0),
                          reads=[bk[SCB]], writes=[PTr[i]], relaxed=True)
                else:
                    p.add("act", lambda e, i=i: e.activation(out=PT[i][:, :], in_=ps[:, B(SCB):B(SCB) + 512],
                                                             func=AF.Exp, scale=4.0),
                          reads=[bk[SCB]], writes=[PTr[i]])
                if owork:
                    owork.pop(0)()
                if qwork or cur_q:
                    if not cur_q:
                        gg, hh_ = qwork.pop(0)
                        cur_q = emit_q_head(gg, hh_, 4)
                    cur_q.pop(0)()
                else:
                    pop_pending()
                if owork:
                    owork.pop(0)()
                for kb in range(2):
                    vb = ((qb + kb) * 8 + g) * 64
                    p.add("pe", lambda e, kb=kb, vb=vb, i=i: e.matmul(
                        ps[:, B(PVB):B(PVB) + 256], lhsT=VO[:, vb:vb + 128], rhs=PT[i][:, kb * 256:(kb + 1) * 256],
                        start=(kb == 0), stop=(kb == 1)), reads=[Vr, PTr[i]], writes=[bk[PVB]])
                for kb in range(2):
                    p.add("pe", lambda e, kb=kb, i=i: e.matmul(
                        ps[:, B(PVB) + 256:B(PVB) + 512], lhsT=onesb[:, :], rhs=PT[i][:, kb * 256:(kb + 1) * 256],
                        start=(kb == 0), stop=(kb == 1)), reads=[ones_r, PTr[i]], writes=[bk[PVB]])
                for hl in range(2):
                    hh = hp * 2 + hl
                    p.add("dve", lambda e, hl=hl, hh=hh, g=g: e.tensor_scalar(
                        out=den[0:64, hl * 128:(hl + 1) * 128],
                        in0=ps[0:64, B(PVB) + 256 + hl * 128:B(PVB) + 256 + (hl + 1) * 128],
                        scalar1=es[0:64, g * 4 + hh:g * 4 + hh + 1], scalar2=None, op0=ALU.add),
                        reads=[bk[PVB], es_r], writes=[denr], relaxed=True)
                p.add("dve", lambda e: e.reciprocal(out=rec[0:64, :], in_=den[0:64, :]), reads=[denr], writes=[recr])
                for hl in range(2):
                    p.add("dve", lambda e, hl=hl, hp=hp, qb=qb, gp=gp: e.tensor_tensor(
                        out=AO[gp][hl * 64:(hl + 1) * 64, hp, qb * 128:(qb + 1) * 128],
                        in0=ps[0:64, B(PVB) + hl * 128:B(PVB) + (hl + 1) * 128],
                        in1=rec[0:64, hl * 128:(hl + 1) * 128], op=ALU.mult),
                        reads=[bk[PVB], recr], writes=[AOr[gp][hp]], relaxed=True)
        flush_norm()
        assert not qwork and not cur_q and not owork, (len(qwork), len(cur_q), len(owork))
    for f in o_units(7, banks=(7, 3, 0, 1, 2, 5)):
        f()
    return cx.finish()


def attn_masks2():
    k = np.arange(128)[:, None]
    q = np.arange(128)[None, :]
    mprev = np.where(k > q, 0.0, -10000.0).astype(np.float32)
    mcur = np.where(q >= k, 0.0, -10000.0).astype(np.float32)
    return np.ascontiguousarray(np.concatenate([mprev, mprev, mcur, mcur, np.eye(128, dtype=np.float32)], axis=1))
```

```python
import contextlib
import numpy as np
import concourse.bass as bass
import concourse.mybir as mybir
from concourse.bass_utils import run_bass_kernel_spmd

F32 = mybir.dt.float32
BF16 = mybir.dt.bfloat16
AF = mybir.ActivationFunctionType
ALU = mybir.AluOpType

NCORES = 8
D = 2048
KC = 16
T = 8192
NT = 1024
DFF = 6144
EPS = 1e-6


class Res:
    __slots__ = ("name", "last_w", "readers", "sem_w", "nw", "sem_r", "nr")

    def __init__(self, name):
        self.name = name
        self.last_w = None
        self.readers = {}
        self.sem_w = None
        self.nw = 0
        self.sem_r = None
        self.nr = 0


class Op:
    __slots__ = ("eng", "fn", "deps", "need_inc", "sem", "semval", "dma", "inc")

    def __init__(self, eng, fn, dma):
        self.eng = eng
        self.fn = fn
        self.deps = []
        self.need_inc = False
        self.sem = None
        self.semval = 0
        self.dma = dma
        self.inc = 1


class Prog:
    ENGS = ("pe", "act", "dve", "pool", "sp")

    def __init__(self, nc):
        self.nc = nc
        self.ops = {e: [] for e in self.ENGS}
        self.esem = {e: nc.alloc_semaphore(name=f"sem_{e}") for e in self.ENGS}
        self.nsem = 0
        self.final = []

    def _newsem(self):
        self.nsem += 1
        return self.nc.alloc_semaphore(name=f"dsem{self.nsem}")

    def add(self, eng, fn, reads=(), writes=(), dma=None, relaxed=False):
        op = Op(eng, fn, dma)
        deps = []
        for r in reads:
            if r.last_w is not None:
                deps.append(r.last_w)
        for w in writes:
            if w.last_w is not None:
                if not (relaxed and w.last_w.dma is None and dma is None and w.last_w.eng == eng):
                    deps.append(w.last_w)
            deps.extend(w.readers.values())
        for d in deps:
            if d is op:
                continue
            if d.dma is None and op.dma is None and d.eng == eng == "pe":
                continue
            if d not in op.deps:
                op.deps.append(d)
                d.need_inc = True
        for r in reads:
            key = eng if dma is None else id(op)
            r.readers[key] = op
        for w in writes:
            w.last_w = op
            w.readers = {}
        if dma == "w":
            res = writes[0]
            if res.sem_w is None:
                res.sem_w = self._newsem()
            res.nw += 1
            op.sem, op.semval, op.inc = res.sem_w, 16 * res.nw, 16
            op.need_inc = True
        elif dma == "r":
            res = reads[0]
            if res.sem_r is None:
                res.sem_r = self._newsem()
            res.nr += 1
            op.sem, op.semval, op.inc = res.sem_r, 16 * res.nr, 16
            op.need_inc = True
            self.final.append(op)
        self.ops[eng].append(op)
        return op

    def emit(self):
        nc = self.nc
        for e in self.ENGS:
            cnt = 0
            for op in self.ops[e]:
                if op.dma is None:
                    op.sem = self.esem[e]
                    if op.need_inc:
                        cnt += 1
                        op.semval = cnt
        final = self.final

        def run(e, h):
            waited = {}
            for op in self.ops[e]:
                need = {}
                for d in op.deps:
                    k = id(d.sem)
                    if k not in need or need[k][1] < d.semval:
                        need[k] = (d.sem, d.semval)
                for k, (s, v) in need.items():
                    if waited.get(k, 0) >= v:
                        continue
                    h.wait_ge(s, v)
                    waited[k] = v
                ins = op.fn(h)
                if op.need_inc:
                    ins.then_inc(op.sem, op.inc)
            if e == "sp":
                need = {}
                for d in final:
                    k = id(d.sem)
                    if k not in need or need[k][1] < d.semval:
                        need[k] = (d.sem, d.semval)
                for k, (s, v) in need.items():
                    h.wait_ge(s, v)

        with nc.Block() as block:
            @block.tensor
            def _(h):
                run("pe", h)

            @block.scalar
            def _(h):
                run("act", h)

            @block.vector
            def _(h):
                run("dve", h)

            @block.gpsimd
            def _(h):
                run("pool", h)

            @block.sync
            def _(h):
                run("sp", h)


class Ctx:
    def __init__(self):
        self.nc = bass.Bass("TRN2", target_bir_lowering=False)
        self.p = Prog(self.nc)
        self.stack = contextlib.ExitStack()

    def sb(self, name, shape, dt):
        return self.stack.enter_context(self.nc.sbuf_tensor(name, shape, dt))

    def psum(self):
        return self.stack.enter_context(self.nc.psum_tensor("ps", [128, 4096], F32))

    def din(self, name, shape, dt=F32):
        return self.nc.dram_tensor(name, list(shape), dt, kind="ExternalInput").ap()

    def dout(self, name, shape, dt=F32):
        return self.nc.dram_tensor(name, list(shape), dt, kind="ExternalOutput").ap()

    def finish(self):
        self.p.emit()
        self.stack.close()
        return self.nc


def token_tiles(ncols, first):
    tiles = []
    c = 0
    if first:
        tiles.append((0, first))
        c = first
    while c < ncols:
        tiles.append((c, min(c + 512, ncols)))
        c = tiles[-1][1]
    return tiles


def emit_rmsnorm(cx, ps, ps_res, x, xres, gain, prm_res, h, hres, ncols, ones, ones_res, scr, extra_ps_res=(), use_ln=False, onesb=None):
    p = cx.p
    tiles = token_tiles(ncols, 0)
    for kc in range(KC):
        sq, sqr = scr["sq"][kc % 2], scr["sqr"][kc % 2]
        lhs = ones
        if onesb is not None:
            sq = sq[:, :].bitcast(BF16)
            lhs = onesb
        p.add("act", lambda e, kc=kc, sq=sq: e.activation(out=sq[:, 0:ncols], in_=x[:, kc, 0:ncols], func=AF.Square),
              reads=xres[kc], writes=[sqr])
        for (c0, c1) in tiles:
            p.add("pe", lambda e, kc=kc, sq=sq, c0=c0, c1=c1, lhs=lhs: e.matmul(
                ps[:, c0:c1], lhsT=lhs[:, :], rhs=sq[:, c0:c1], start=(kc == 0), stop=(kc == KC - 1)),
                reads=[sqr, ones_res], writes=[ps_res] + list(extra_ps_res))
    rstd, rres = scr["rstd"], scr["rstdr"]
    sq, sqr = scr["sq"][0], scr["sqr"][0]
    if use_ln:
        p.add("act", lambda e: e.activation(out=sq[:, 0:ncols], in_=ps[:, 0:ncols], func=AF.Ln,
                                            bias=scr["eps"][:, 0:1], scale=1.0 / D),
              reads=[ps_res, prm_res] + list(extra_ps_res), writes=[sqr])
        p.add("act", lambda e: e.activation(out=rstd[:, 0:ncols], in_=sq[:, 0:ncols], func=AF.Exp, scale=-0.5),
              reads=[sqr], writes=[rres])
    else:
        p.add("act", lambda e: e.activation(out=sq[:, 0:ncols], in_=ps[:, 0:ncols], func=AF.Sqrt,
                                            bias=scr["eps"][:, 0:1], scale=1.0 / D),
              reads=[ps_res, prm_res] + list(extra_ps_res), writes=[sqr])
        p.add("dve", lambda e: e.reciprocal(out=rstd[:, 0:ncols], in_=sq[:, 0:ncols]), reads=[sqr], writes=[rres])
    for kc in range(KC):
        p.add("dve", lambda e, kc=kc: e.scalar_tensor_tensor(
            out=h[:, kc, 0:ncols], in0=x[:, kc, 0:ncols], scalar=gain[:, kc:kc + 1], in1=rstd[:, 0:ncols],
            op0=ALU.mult, op1=ALU.mult), reads=list(xres[kc]) + [rres, prm_res], writes=[hres[kc]])


def build_ffn(rec_prologue):
    cx = Ctx()
    nc, p = cx.nc, cx.p
    NCOL = NT + 2
    xT = cx.din("xT", [D, NCOL])
    w_up = cx.din("w_up", [D, 2 * DFF])
    w_down = cx.din("w_down", [DFF, D])
    prm_d = cx.din("prm", [128, 16 + 288 + 96])
    yT = cx.dout("yT", [D, NT])
    if rec_prologue:
        g1T = cx.din("g1T", [D, NCOL])
        zT = cx.din("zT", [D, NCOL])
        carr = cx.din("carr", [128, 2 * KC * 8])
        msk = cx.din("msk", [128, 16])
        w_out = cx.din("w_out", [D, D])

    ps = cx.psum()
    x = cx.sb("x", [128, KC, NCOL], F32)
    h = cx.sb("h", [128, KC, NCOL], BF16)
    sq = [cx.sb(f"sq{i}", [128, NCOL], F32) for i in range(2)]
    rstd = cx.sb("rstd", [128, NCOL], F32)
    prm = cx.sb("prm_sb", [128, 16 + 288 + 96], F32)
    ones = cx.sb("ones", [128, 128], F32)
    onesb = cx.sb("onesb", [128, 128], BF16)
    epsb = cx.sb("epsb", [128, 1], F32)
    wup = [cx.sb(f"wup{i}", [128, KC, 2, 128], BF16) for i in range(3)]
    wdn = [cx.sb(f"wdn{i}", [128, 4, D], BF16) for i in range(2)]
    act = [cx.sb(f"act{i}", [128, NT], BF16) for i in range(8)]
    cg = cx.sb("cg", [128, NCOL], F32)
    cv = cx.sb("cv", [128, NCOL], F32)
    gg = cx.sb("gg", [128, NT], F32)

    xres = [[Res(f"x{m}_{tt}") for tt in range(3)] for m in range(KC)]
    hres = [Res(f"h{k}") for k in range(KC)]
    sqr = [Res("sq0"), Res("sq1")]
    rstdr = Res("rstd")
    prm_r = Res("prm")
    ones_r = Res("ones")
    wupr = [[Res(f"wup{i}_{hf}") for hf in range(2)] for i in range(3)]
    wdnr = [Res(f"wdn{i}") for i in range(2)]
    actr = [Res(f"act{i}") for i in range(8)]
    cgr, cvr, ggr = Res("cg"), Res("cv"), Res("gg")
    psX, psY = Res("psX"), Res("psY")
    bankr = [Res(f"bank{i}") for i in range(2)]
    OX, OY = 0, 1536
    tilesX = token_tiles(NCOL, 0)
    tilesY = tilesX
    BANK = [3072, 3584]

    gain = prm[:, 0:16]
    cw = prm[:, 16:16 + 288]
    cb = prm[:, 304:400]
    scr = {"sq": sq, "sqr": sqr, "rstd": rstd, "rstdr": rstdr, "eps": epsb}

    p.add("pool", lambda e: e.memset(ones[:, :], 1.0), writes=[ones_r])
    p.add("pool", lambda e: e.memset(onesb[:, :], 1.0), writes=[ones_r])
    p.add("pool", lambda e: e.memset(epsb[:, :], EPS), writes=[prm_r])
    p.add("sp", lambda e: e.dma_start(out=prm[:, :], in_=prm_d[:, :]), writes=[prm_r], dma="w")
    xv = xT.rearrange("(kc p) n -> p kc n", p=128)
    for kc in range(KC):
        p.add("sp", lambda e, kc=kc: e.dma_start(out=x[:, kc, :], in_=xv[:, kc, :]),
              writes=xres[kc], dma="w")

    if rec_prologue:
        cin = cx.sb("cin", [128, 2 * KC * 8], F32)
        mk = cx.sb("mk", [128, 16], F32)
        ca = cx.sb("ca", [128, 2, KC, 8], F32)
        ch = cx.sb("ch", [128, 2, KC, 8], F32)
        cs = cx.sb("cs", [128, 2, KC, 8], F32)
        cin_r, mk_r, ca_r, ch_r, cs_r = Res("cin"), Res("mk"), Res("ca"), Res("ch"), Res("cs")
        p.add("sp", lambda e: e.dma_start(out=cin[:, :], in_=carr[:, :]), writes=[cin_r], dma="w")
        p.add("sp", lambda e: e.dma_start(out=mk[:, :], in_=msk[:, :]), writes=[mk_r], dma="w")
        for w in range(2):
            for kc in range(KC):
                a_in = cin[:, kc * 8:(kc + 1) * 8]
                h_in = cin[:, KC * 8 + kc * 8: KC * 8 + (kc + 1) * 8]
                m = mk[:, w * 8:(w + 1) * 8]
                p.add("dve", lambda e, a_in=a_in, m=m, w=w, kc=kc: e.scalar_tensor_tensor(
                    out=ca[:, w, kc, :], in0=a_in, scalar=-1.0, in1=m, op0=ALU.add, op1=ALU.mult),
                    reads=[cin_r, mk_r], writes=[ca_r])
                p.add("dve", lambda e, w=w, kc=kc: e.tensor_scalar(
                    out=ca[:, w, kc, :], in0=ca[:, w, kc, :], scalar1=1.0, scalar2=None, op0=ALU.add),
                    reads=[ca_r], writes=[ca_r])
                p.add("dve", lambda e, h_in=h_in, m=m, w=w, kc=kc: e.tensor_tensor(
                    out=ch[:, w, kc, :], in0=h_in, in1=m, op=ALU.mult),
                    reads=[cin_r, mk_r], writes=[ch_r])
                p.add("dve", lambda e, w=w, kc=kc: e.tensor_tensor_scan(
                    out=cs[:, w, kc, :], data0=ca[:, w, kc, :], data1=ch[:, w, kc, :], initial=0.0,
                    op0=ALU.mult, op1=ALU.add), reads=[ca_r, ch_r], writes=[cs_r])
        gz = [(cg, cgr), (cv, cvr), (gg, ggr), (rstd, rstdr)]
        g1v = g1T.rearrange("(kc p) n -> p kc n", p=128)
        zv = zT.rearrange("(kc p) n -> p kc n", p=128)
        zst = [cg, cv]
        zstr = [cgr, cvr]
        for kc in range(KC):
            gb, gr = sq[kc % 2], sqr[kc % 2]
            zb, zr = zst[kc % 2], zstr[kc % 2]
            p.add("sp", lambda e, kc=kc, gb=gb: e.dma_start(out=gb[:, :], in_=g1v[:, kc, :]), writes=[gr], dma="w")
            p.add("sp", lambda e, kc=kc, zb=zb: e.dma_start(out=zb[:, :], in_=zv[:, kc, :]), writes=[zr], dma="w")
            p.add("dve", lambda e, kc=kc, gb=gb, zb=zb: e.scalar_tensor_tensor(
                out=h[:, kc, 0:2], in0=zb[:, 0:2], scalar=cs[:, 1, kc, 7:8], in1=gb[:, 0:2],
                op0=ALU.mult, op1=ALU.add), reads=[gr, zr, cs_r], writes=[hres[kc]])
            p.add("dve", lambda e, kc=kc, gb=gb, zb=zb: e.scalar_tensor_tensor(
                out=h[:, kc, 2:NCOL], in0=zb[:, 2:NCOL], scalar=cs[:, 0, kc, 7:8], in1=gb[:, 2:NCOL],
                op0=ALU.mult, op1=ALU.add), reads=[gr, zr, cs_r], writes=[hres[kc]])
        wo = [wup[i][:, :, 0, :] for i in range(3)]
        wor = [wupr[i][0] for i in range(3)]
        wov = w_out.rearrange("(kc p) n -> p kc n", p=128)
        for m in range(KC):
            s = m % 3
            p.add("pool", lambda e, m=m, s=s: e.dma_start(out=wo[s], in_=wov[:, :, m * 128:(m + 1) * 128]),
                  writes=[wor[s]], dma="w")
            O, tl, pr = (OX, tilesX, psX) if m % 2 == 0 else (OY, tilesY, psY)
            for kc in range(KC):
                for (c0, c1) in tl:
                    p.add("pe", lambda e, s=s, kc=kc, c0=c0, c1=c1, O=O: e.matmul(
                        ps[:, O + c0:O + c1], lhsT=wo[s][:, kc, :], rhs=h[:, kc, c0:c1],
                        start=(kc == 0), stop=(kc == KC - 1)), reads=[wor[s], hres[kc]], writes=[pr])
            p.add("dve", lambda e, m=m, O=O: e.tensor_tensor(
                out=x[:, m, :], in0=ps[:, O:O + NCOL], in1=x[:, m, :], op=ALU.add),
                reads=[pr] + xres[m], writes=xres[m])

    emit_rmsnorm(cx, ps, psX, x, xres, gain, prm_r, h, hres, NCOL, ones, ones_r, scr, onesb=onesb)

    wupv = w_up.rearrange("(kc p) (two c) -> p kc two c", p=128, two=2)

    def load_up(j):
        s = j % 3
        for half in range(2):
            p.add("pool", lambda e, j=j, s=s, half=half: e.dma_start(
                out=wup[s][:, :, half, :], in_=wupv[:, :, half, j * 128:(j + 1) * 128]),
                writes=[wupr[s][half]], dma="w")

    def load_dn(q):
        s = q % 2
        src = w_down[q * 512:(q + 1) * 512, :].rearrange("(k p) n -> p k n", p=128)
        p.add("pool", lambda e, s=s, src=src: e.dma_start(out=wdn[s][:, :, :], in_=src), writes=[wdnr[s]], dma="w")

    def up_half(j, half, phase):
        s = j % 3
        slot = j % 8
        if True:
            O, tl, pr = (OX, tilesX, psX) if half == 0 else (OY, tilesY, psY)
            ch = half * 48 + j
            for kc in (range(KC) if phase == 0 else ()):
                for (c0, c1) in tl:
                    p.add("pe", lambda e, s=s, kc=kc, half=half, c0=c0, c1=c1, O=O: e.matmul(
                        ps[:, O + c0:O + c1], lhsT=wup[s][:, kc, half, :], rhs=h[:, kc, c0:c1],
                        start=(kc == 0), stop=(kc == KC - 1)), reads=[wupr[s][half], hres[kc]], writes=[pr])
            if phase == 0:
                return
            c, cr = (cg, cgr) if half == 0 else (cv, cvr)
            P = ps[:, O:O + NCOL]
            p.add("act", lambda e, c=c, P=P, ch=ch: e.activation(
                out=c[:, 0:NT], in_=P[:, 2:2 + NT], func=AF.Identity,
                bias=cb[:, ch:ch + 1], scale=cw[:, 192 + ch:192 + ch + 1]), reads=[pr, prm_r], writes=[cr])
            p.add("dve", lambda e, c=c, P=P, ch=ch: e.scalar_tensor_tensor(
                out=c[:, 0:NT], in0=P[:, 1:1 + NT], scalar=cw[:, 96 + ch:96 + ch + 1], in1=c[:, 0:NT],
                op0=ALU.mult, op1=ALU.add), reads=[pr, prm_r, cr], writes=[cr])
            p.add("dve", lambda e, c=c, P=P, ch=ch: e.scalar_tensor_tensor(
                out=c[:, 0:NT], in0=P[:, 0:NT], scalar=cw[:, ch:ch + 1], in1=c[:, 0:NT],
                op0=ALU.mult, op1=ALU.add), reads=[pr, prm_r, cr], writes=[cr])
            if half == 0:
                p.add("act", lambda e: e.activation(out=gg[:, :], in_=cg[:, 0:NT], func=AF.Gelu_apprx_tanh),
                      reads=[cgr], writes=[ggr])
        if half == 1:
            p.add("pool", lambda e, slot=slot: e.tensor_tensor(out=act[slot][:, :], in0=gg[:, :], in1=cv[:, 0:NT], op=ALU.mult),
                  reads=[ggr, cvr], writes=[actr[slot]])

    def down_part(q, part, last):
        s = q % 2
        for m in range(part * 2, part * 2 + 2):
            for tt in range(2):
                b = (m * 2 + tt) % 2
                for k in range(4):
                    slot = (q * 4 + k) % 8
                    p.add("pe", lambda e, s=s, k=k, m=m, tt=tt, b=b, slot=slot: e.matmul(
                        ps[:, BANK[b]:BANK[b] + 512], lhsT=wdn[s][:, k, m * 128:(m + 1) * 128],
                        rhs=act[slot][:, tt * 512:(tt + 1) * 512], start=(k == 0), stop=(k == 3)),
                        reads=[wdnr[s], actr[slot]], writes=[bankr[b]])
                p.add("dve", lambda e, m=m, tt=tt, b=b: e.tensor_tensor(
                    out=x[:, m, 2 + tt * 512:2 + (tt + 1) * 512], in0=ps[:, BANK[b]:BANK[b] + 512],
                    in1=x[:, m, 2 + tt * 512:2 + (tt + 1) * 512], op=ALU.add),
                    reads=[bankr[b], xres[m][1 + tt]], writes=[xres[m][1 + tt]])
            if last:
                yv = yT.rearrange("(kc p) n -> p kc n", p=128)
                p.add("sp", lambda e, m=m: e.dma_start(out=yv[:, m, :], in_=x[:, m, 2:2 + NT]),
                      reads=[xres[m][1], xres[m][2]], dma="r")

    NP = 48
    load_up(0)
    load_up(1)
    load_dn(0)
    for j in range(NP + 4):
        q = j // 4 - 1
        for half in range(2):
            if j < NP:
                if half == 0 and j + 2 < NP:
                    load_up(j + 2)
                up_half(j, half, 0)
            if q >= 0:
                if j % 4 == 0 and half == 0 and q + 1 < NP // 4:
                    load_dn(q + 1)
                down_part(q, (j % 4) * 2 + half, last=(q == NP // 4 - 1))
            if j < NP:
                up_half(j, half, 1)
    return cx.finish()


def chunked(v):
    v = np.asarray(v, np.float32)
    return np.ascontiguousarray(v.reshape(-1, 128).T)


def ffn_params(g, cw, cb):
    parts = [chunked(g)] + [chunked(cw[k]) for k in range(3)] + [chunked(cb)]
    return np.ascontiguousarray(np.concatenate(parts, axis=1))


def build_rec1():
    cx = Ctx()
    nc, p = cx.nc, cx.p
    HALO = 3
    NCOL = NT + HALO
    xT = cx.din("xT", [D, NCOL])
    w_in = cx.din("w_in", [D, 2 * D])
    w_a = cx.din("w_a", [8, 256, 256])
    w_i = cx.din("w_i", [8, 256, 256])
    NPRM = 16 + 64 + 16 * 4
    prm_d = cx.din("prm", [128, NPRM])
    g1T = cx.dout("g1T", [D, NT])
    zT = cx.dout("zT", [D, NT])
    carr_o = cx.dout("carr", [128, 32])

    ps = cx.psum()
    x = cx.sb("x", [128, KC, NCOL], F32)
    h = cx.sb("h", [128, KC, NCOL], BF16)
    sq = [cx.sb(f"sq{i}", [128, NCOL], F32) for i in range(2)]
    rstd = cx.sb("rstd", [128, NCOL], F32)
    prm = cx.sb("prm_sb", [128, NPRM], F32)
    ones = cx.sb("ones", [128, 128], F32)
    epsb = cx.sb("epsb", [128, 1], F32)
    win = [cx.sb(f"win{i}", [128, KC, 128], BF16) for i in range(4)]
    wg = [[cx.sb(f"wg{i}_{j}", [128, 2, 256], BF16) for j in range(2)] for i in range(2)]
    xc = [cx.sb(f"xc{i}", [128, NT], F32) for i in range(2)]
    xcb = [cx.sb(f"xcb{i}", [128, NT], BF16) for i in range(2)]
    gate = [cx.sb(f"gate{i}", [128, NT], F32) for i in range(2)]
    tr = cx.sb("tr", [128, NT], F32)
    ta = cx.sb("ta", [128, NT], F32)
    tm = cx.sb("tm", [128, NT], F32)
    ti = cx.sb("ti", [128, NT], F32)
    ths = cx.sb("ths", [128, NT], F32)
    tac = cx.sb("tac", [128, NT], F32)
    tg1 = cx.sb("tg1", [128, NT], F32)
    tz = cx.sb("tz", [128, NT], F32)
    zeros = cx.sb("zeros", [128, NT], F32)
    cl = cx.sb("cl", [128, 48], F32)
    carr = cx.sb("carr_sb", [128, 32], F32)

    xres = [[Res(f"x{m}")] for m in range(KC)]
    hres = [Res(f"h{k}") for k in range(KC)]
    sqr = [Res("sq0"), Res("sq1")]
    rstdr, prm_r, ones_r = Res("rstd"), Res("prm"), Res("ones")
    winr = [Res(f"win{i}") for i in range(4)]
    wgr = [[Res(f"wg{i}_{j}") for j in range(2)] for i in range(2)]
    xcr = [Res("xc0"), Res("xc1")]
    xcbr = [Res("xcb0"), Res("xcb1")]
    gater = [Res("gate0"), Res("gate1")]
    trr, tar, tmr, tir, thsr, tacr, tg1r, tzr = (Res(n) for n in ("tr", "ta", "tm", "ti", "ths", "tac", "tg1", "tz"))
    zer_r, cl_r, carr_r = Res("zeros"), Res("cl"), Res("carr")
    psX, psY = Res("psX"), Res("psY")
    bankr = [Res(f"bank{i}") for i in range(3)]
    BANK = [2560, 3072, 3584]
    tilesX = token_tiles(NCOL, 0)
    OY = 1536

    gain = prm[:, 0:16]
    cw = prm[:, 16:80]
    cb = prm[:, 80:96]
    ba = prm[:, 96:112]
    bi = prm[:, 112:128]
    lam = prm[:, 128:144]
    scr = {"sq": sq, "sqr": sqr, "rstd": rstd, "rstdr": rstdr, "eps": epsb}

    p.add("pool", lambda e: e.memset(ones[:, :], 1.0), writes=[ones_r])
    p.add("pool", lambda e: e.memset(epsb[:, :], EPS), writes=[prm_r])
    p.add("pool", lambda e: e.memset(zeros[:, :], 0.0), writes=[zer_r])
    p.add("sp", lambda e: e.dma_start(out=prm[:, :], in_=prm_d[:, :]), writes=[prm_r], dma="w")
    xv = xT.rearrange("(kc p) n -> p kc n", p=128)
    for kc in range(KC):
        p.add("sp", lambda e, kc=kc: e.dma_start(out=x[:, kc, :], in_=xv[:, kc, :]), writes=xres[kc], dma="w")

    winv = w_in.rearrange("(kc p) n -> p kc n", p=128)
    nload = [0]

    def load_in(col0):
        s = nload[0] % 4
        nload[0] += 1
        p.add("pool", lambda e, s=s, col0=col0: e.dma_start(out=win[s][:, :, :], in_=winv[:, :, col0:col0 + 128]),
              writes=[winr[s]], dma="w")
        return s

    def load_g(b):
        s = b % 2
        for j, wsrc in enumerate((w_a, w_i)):
            p.add("pool", lambda e, s=s, j=j, wsrc=wsrc, b=b: e.dma_start(
                out=wg[s][j][:, :, :], in_=wsrc[b].rearrange("(ic p) n -> p ic n", p=128)),
                writes=[wgr[s][j]], dma="w")

    p.add("act", lambda e: e.activation(out=cl[:, 0:16], in_=lam, func=AF.Exp, scale=-1.0), reads=[prm_r], writes=[cl_r])
    p.add("act", lambda e: e.activation(out=cl[:, 0:16], in_=cl[:, 0:16], func=AF.Ln, bias=1.0), reads=[cl_r], writes=[cl_r])
    p.add("dve", lambda e: e.tensor_scalar(out=cl[:, 16:32], in0=cl[:, 0:16], scalar1=-8.0, scalar2=None, op0=ALU.mult),
          reads=[cl_r], writes=[cl_r])
    p.add("dve", lambda e: e.tensor_scalar(out=cl[:, 32:48], in0=cl[:, 0:16], scalar1=-16.0, scalar2=None, op0=ALU.mult),
          reads=[cl_r], writes=[cl_r])

    emit_rmsnorm(cx, ps, psX, x, xres, gain, prm_r, h, hres, NCOL, ones, ones_r, scr)

    pending = [load_in(0), load_in(D)]
    load_g(0)
    order = []
    for b in range(8):
        for c in range(2):
            ch = 2 * b + c
            order.append(ch * 128)
            order.append(D + ch * 128)
    li = 2
    for b in range(8):
        if b + 1 < 8:
            load_g(b + 1)
        for c in range(2):
            ch = 2 * b + c
            s = pending.pop(0)
            if li < len(order):
                pending.append(load_in(order[li])); li += 1
            for kc in range(KC):
                for (c0, c1) in tilesX:
                    p.add("pe", lambda e, s=s, kc=kc, c0=c0, c1=c1: e.matmul(
                        ps[:, c0:c1], lhsT=win[s][:, kc, :], rhs=h[:, kc, c0:c1],
                        start=(kc == 0), stop=(kc == KC - 1)), reads=[winr[s], hres[kc]], writes=[psX])
            P = ps[:, 0:NCOL]
            t = xc[c]
            p.add("act", lambda e, t=t, P=P, ch=ch: e.activation(
                out=t[:, :], in_=P[:, 3:3 + NT], func=AF.Identity, bias=cb[:, ch:ch + 1],
                scale=cw[:, 48 + ch:48 + ch + 1]), reads=[psX, prm_r], writes=[xcr[c]])
            for k in (2, 1, 0):
                p.add("dve", lambda e, t=t, P=P, ch=ch, k=k: e.scalar_tensor_tensor(
                    out=t[:, :], in0=P[:, k:k + NT], scalar=cw[:, k * 16 + ch:k * 16 + ch + 1], in1=t[:, :],
                    op0=ALU.mult, op1=ALU.add), reads=[psX, prm_r, xcr[c]], writes=[xcr[c]])
            p.add("pool", lambda e, c=c: e.tensor_copy(out=xcb[c][:, :], in_=xc[c][:, :]), reads=[xcr[c]], writes=[xcbr[c]])
            s = pending.pop(0)
            if li < len(order):
                pending.append(load_in(order[li])); li += 1
            for kc in range(KC):
                for tt in range(2):
                    p.add("pe", lambda e, s=s, kc=kc, tt=tt: e.matmul(
                        ps[:, OY + tt * 512:OY + (tt + 1) * 512], lhsT=win[s][:, kc, :],
                        rhs=h[:, kc, HALO + tt * 512:HALO + (tt + 1) * 512],
                        start=(kc == 0), stop=(kc == KC - 1)), reads=[winr[s], hres[kc]], writes=[psY])
            p.add("act", lambda e, c=c: e.activation(out=gate[c][:, :], in_=ps[:, OY:OY + NT], func=AF.Gelu_apprx_tanh),
                  reads=[psY], writes=[gater[c]])
        gs = b % 2
        for oc in range(2):
            ch = 2 * b + oc
            for j in range(2):
                dst, dres = (tr, trr) if j == 0 else (ti, tir)
                bias = ba if j == 0 else bi
                for tt in range(2):
                    bk = (oc * 4 + j * 2 + tt) % 3
                    for ic in range(2):
                        p.add("pe", lambda e, gs=gs, j=j, ic=ic, oc=oc, tt=tt, bk=bk: e.matmul(
                            ps[:, BANK[bk]:BANK[bk] + 512], lhsT=wg[gs][j][:, ic, oc * 128:(oc + 1) * 128],
                            rhs=xcb[ic][:, tt * 512:(tt + 1) * 512], start=(ic == 0), stop=(ic == 1)),
                            reads=[wgr[gs][j], xcbr[ic]], writes=[bankr[bk]])
                    p.add("act", lambda e, dst=dst, bias=bias, ch=ch, tt=tt, bk=bk: e.activation(
                        out=dst[:, tt * 512:(tt + 1) * 512], in_=ps[:, BANK[bk]:BANK[bk] + 512], func=AF.Sigmoid,
                        bias=bias[:, ch:ch + 1]), reads=[bankr[bk], prm_r], writes=[dres])
            p.add("act", lambda e, ch=ch: e.activation(out=ta[:, :], in_=tr[:, :], func=AF.Exp, scale=cl[:, 16 + ch:17 + ch]),
                  reads=[trr, cl_r], writes=[tar])
            p.add("act", lambda e, ch=ch: e.activation(out=tm[:, :], in_=tr[:, :], func=AF.Exp, scale=cl[:, 32 + ch:33 + ch]),
                  reads=[trr, cl_r], writes=[tmr])
            p.add("act", lambda e: e.activation(out=tm[:, :], in_=tm[:, :], func=AF.Sqrt, scale=-1.0, bias=1.0),
                  reads=[tmr], writes=[tmr])
            p.add("dve", lambda e, oc=oc: e.tensor_tensor(out=ti[:, :], in0=ti[:, :], in1=xc[oc][:, :], op=ALU.mult),
                  reads=[tir, xcr[oc]], writes=[tir])
            p.add("dve", lambda e: e.tensor_tensor(out=ti[:, :], in0=ti[:, :], in1=tm[:, :], op=ALU.mult),
                  reads=[tir, tmr], writes=[tir])
            p.add("dve", lambda e: e.tensor_tensor_scan(out=ths[:, :], data0=ta[:, :], data1=ti[:, :], initial=0.0,
                                                        op0=ALU.mult, op1=ALU.add), reads=[tar, tir], writes=[thsr])
            p.add("dve", lambda e: e.tensor_tensor_scan(out=tac[:, :], data0=ta[:, :], data1=zeros[:, :], initial=1.0,
                                                        op0=ALU.mult, op1=ALU.add), reads=[tar, zer_r], writes=[tacr])
            p.add("pool", lambda e, oc=oc: e.tensor_tensor(out=tg1[:, :], in0=ths[:, :], in1=gate[oc][:, :], op=ALU.mult),
                  reads=[thsr, gater[oc]], writes=[tg1r])
            p.add("pool", lambda e, oc=oc: e.tensor_tensor(out=tz[:, :], in0=tac[:, :], in1=gate[oc][:, :], op=ALU.mult),
                  reads=[tacr, gater[oc]], writes=[tzr])
            p.add("dve", lambda e, ch=ch: e.tensor_copy(out=carr[:, ch:ch + 1], in_=tac[:, NT - 1:NT]), reads=[tacr], writes=[carr_r])
            p.add("dve", lambda e, ch=ch: e.tensor_copy(out=carr[:, 16 + ch:17 + ch], in_=ths[:, NT - 1:NT]), reads=[thsr], writes=[carr_r])
            p.add("sp", lambda e, ch=ch: e.dma_start(out=g1T[ch * 128:(ch + 1) * 128, :], in_=tg1[:, :]), reads=[tg1r], dma="r")
            p.add("sp", lambda e, ch=ch: e.dma_start(out=zT[ch * 128:(ch + 1) * 128, :], in_=tz[:, :]), reads=[tzr], dma="r")
    p.add("sp", lambda e: e.dma_start(out=carr_o[:, :], in_=carr[:, :]), reads=[carr_r], dma="r")
    return cx.finish()


def build_rec1b():
    cx = Ctx()
    nc, p = cx.nc, cx.p
    HALO = 3
    NCOL = NT + HALO
    xT = cx.din("xT", [D, NCOL])
    w_in = cx.din("w_in", [D, 2 * D])
    w_a = cx.din("w_a", [8, 256, 256])
    w_i = cx.din("w_i", [8, 256, 256])
    NPRM = 16 + 64 + 16 * 4
    prm_d = cx.din("prm", [128, NPRM])
    g1T = cx.dout("g1T", [D, NT])
    zT = cx.dout("zT", [D, NT])
    carr_o = cx.dout("carr", [128, 32])

    ps = cx.psum()
    x = cx.sb("x", [128, KC, NCOL], F32)
    h = cx.sb("h", [128, KC, NCOL], BF16)
    sq = [cx.sb(f"sq{i}", [128, NCOL], F32) for i in range(2)]
    rstd = cx.sb("rstd", [128, NCOL], F32)
    prm = cx.sb("prm_sb", [128, NPRM], F32)
    ones = cx.sb("ones", [128, 128], F32)
    onesb = cx.sb("onesb", [128, 128], BF16)
    epsb = cx.sb("epsb", [128, 1], F32)
    win = [cx.sb(f"win{i}", [128, KC, 128], BF16) for i in range(3)]
    wg = [[cx.sb(f"wg{i}_{j}", [128, 2, 256], BF16) for j in range(2)] for i in range(2)]
    xc = [[cx.sb(f"xc{i}_{c}", [128, NT], F32) for c in range(2)] for i in range(2)]
    xcb = [[cx.sb(f"xcb{i}_{c}", [128, NT], BF16) for c in range(2)] for i in range(2)]
    gate = [[cx.sb(f"gate{i}_{c}", [128, NT], F32) for c in range(2)] for i in range(2)]
    tr = [cx.sb(f"tr{o}", [128, NT], F32) for o in range(2)]
    ti = [cx.sb(f"ti{o}", [128, NT], F32) for o in range(2)]
    ta = [cx.sb(f"ta{o}", [128, NT], F32) for o in range(2)]
    ths = cx.sb("ths", [128, NT], F32)
    tac = cx.sb("tac", [128, NT], F32)
    tg1 = sq[0][:, 0:NT]
    tz = sq[1][:, 0:NT]
    zeros = rstd[:, 0:NT]
    cl = cx.sb("cl", [128, 48], F32)
    carr = cx.sb("carr_sb", [128, 32], F32)

    xres = [[Res(f"x{m}")] for m in range(KC)]
    hres = [Res(f"h{k}") for k in range(KC)]
    sqr = [Res("sq0"), Res("sq1")]
    rstdr, prm_r, ones_r = Res("rstd"), Res("prm"), Res("ones")
    winr = [Res(f"win{i}") for i in range(3)]
    wgr = [[Res(f"wg{i}_{j}") for j in range(2)] for i in range(2)]
    xcr = [[Res(f"xc{i}_{c}") for c in range(2)] for i in range(2)]
    xcbr = [[Res(f"xcb{i}_{c}") for c in range(2)] for i in range(2)]
    gater = [[Res(f"gate{i}_{c}") for c in range(2)] for i in range(2)]
    trr = [Res("tr0"), Res("tr1")]
    tir = [Res("ti0"), Res("ti1")]
    tar = [Res("ta0"), Res("ta1")]
    thsr, tacr = Res("ths"), Res("tac")
    tg1r, tzr = sqr[0], sqr[1]
    zer_r, cl_r, carr_r = rstdr, Res("cl"), Res("carr")
    _px = Res("psX")
    psXr = [_px, _px]
    OXs = [0, 0]
    bankr = [Res("bank5"), Res("bank6"), Res("bank7")]
    psY = [Res("psY0"), Res("psY1")]
    BANK = [2560, 3072, 3584]
    tilesX = token_tiles(NCOL, 0)
    OY = 1536

    gain = prm[:, 0:16]
    cw = prm[:, 16:80]
    cb = prm[:, 80:96]
    ba = prm[:, 96:112]
    bi = prm[:, 112:128]
    lam = prm[:, 128:144]
    scr = {"sq": sq, "sqr": sqr, "rstd": rstd, "rstdr": rstdr, "eps": epsb}

    p.add("pool", lambda e: e.memset(ones[:, :], 1.0), writes=[ones_r])
    p.add("pool", lambda e: e.memset(onesb[:, :], 1.0), writes=[ones_r])
    p.add("pool", lambda e: e.memset(epsb[:, :], EPS), writes=[prm_r])
    p.add("sp", lambda e: e.dma_start(out=prm[:, :], in_=prm_d[:, :]), writes=[prm_r], dma="w")
    xv = xT.rearrange("(kc p) n -> p kc n", p=128)
    for kc in range(KC):
        p.add("sp", lambda e, kc=kc: e.dma_start(out=x[:, kc, :], in_=xv[:, kc, :]), writes=xres[kc], dma="w")

    winv = w_in.rearrange("(kc p) n -> p kc n", p=128)
    nload = [0]

    def load_in(col0):
        s = nload[0] % 3
        nload[0] += 1
        p.add("pool", lambda e, s=s, col0=col0: e.dma_start(out=win[s][:, :, :], in_=winv[:, :, col0:col0 + 128]),
              writes=[winr[s]], dma="w")
        return s

    def load_g(b):
        s = b % 2
        for j, wsrc in enumerate((w_a, w_i)):
            p.add("pool", lambda e, s=s, j=j, wsrc=wsrc, b=b: e.dma_start(
                out=wg[s][j][:, :, :], in_=wsrc[b].rearrange("(ic p) n -> p ic n", p=128)),
                writes=[wgr[s][j]], dma="w")

    p.add("act", lambda e: e.activation(out=cl[:, 0:16], in_=lam, func=AF.Exp, scale=-1.0), reads=[prm_r], writes=[cl_r])
    p.add("act", lambda e: e.activation(out=cl[:, 0:16], in_=cl[:, 0:16], func=AF.Ln, bias=1.0), reads=[cl_r], writes=[cl_r])
    p.add("dve", lambda e: e.tensor_scalar(out=cl[:, 16:32], in0=cl[:, 0:16], scalar1=-8.0, scalar2=None, op0=ALU.mult),
          reads=[cl_r], writes=[cl_r])
    p.add("dve", lambda e: e.tensor_scalar(out=cl[:, 32:48], in0=cl[:, 0:16], scalar1=-16.0, scalar2=None, op0=ALU.mult),
          reads=[cl_r], writes=[cl_r])

    emit_rmsnorm(cx, ps, psXr[0], x, xres, gain, prm_r, h, hres, NCOL, ones, ones_r, scr, onesb=onesb)

    p.add("pool", lambda e: e.memset(zeros, 0.0), reads=list(hres), writes=[zer_r])

    order = []
    for b in range(8):
        for c in range(2):
            ch = 2 * b + c
            order.append(ch * 128)
            order.append(D + ch * 128)
    pending = [load_in(order[0]), load_in(order[1])]
    li = [2]
    load_g(0)

    def nextw():
        s = pending.pop(0)
        if li[0] < len(order):
            pending.append(load_in(order[li[0]]))
            li[0] += 1
        return s

    def front(b):
        bp = b % 2
        for c in range(2):
            ch = 2 * b + c
            s = nextw()
            for kc in range(KC):
                for (c0, c1) in tilesX:
                    p.add("pe", lambda e, s=s, kc=kc, c0=c0, c1=c1, c=c: e.matmul(
                        ps[:, OXs[c] + c0:OXs[c] + c1], lhsT=win[s][:, kc, :], rhs=h[:, kc, c0:c1],
                        start=(kc == 0), stop=(kc == KC - 1)), reads=[winr[s], hres[kc]], writes=[psXr[c]])
            psX = psXr[c]
            P = ps[:, OXs[c]:OXs[c] + NCOL]
            t = xc[bp][c]
            tres = xcr[bp][c]
            p.add("act", lambda e, t=t, P=P, ch=ch: e.activation(
                out=t[:, :], in_=P[:, 3:3 + NT], func=AF.Identity, bias=cb[:, ch:ch + 1],
                scale=cw[:, 48 + ch:48 + ch + 1]), reads=[psX, prm_r], writes=[tres])
            for k in (2, 1, 0):
                p.add("dve", lambda e, t=t, P=P, ch=ch, k=k: e.scalar_tensor_tensor(
                    out=t[:, :], in0=P[:, k:k + NT], scalar=cw[:, k * 16 + ch:k * 16 + ch + 1], in1=t[:, :],
                    op0=ALU.mult, op1=ALU.add), reads=[psX, prm_r, tres], writes=[tres])
            p.add("act", lambda e, t=t, bp=bp, c=c: e.activation(out=xcb[bp][c][:, :], in_=t[:, :], func=AF.Identity),
                  reads=[tres], writes=[xcbr[bp][c]])
            s = nextw()
            for kc in range(KC):
                for tt in range(2):
                    p.add("pe", lambda e, s=s, kc=kc, tt=tt: e.matmul(
                        ps[:, OY + tt * 512:OY + (tt + 1) * 512], lhsT=win[s][:, kc, :],
                        rhs=h[:, kc, HALO + tt * 512:HALO + (tt + 1) * 512],
                        start=(kc == 0), stop=(kc == KC - 1)), reads=[winr[s], hres[kc]], writes=[psY[tt]])
            p.add("act", lambda e, bp=bp, c=c: e.activation(out=gate[bp][c][:, :], in_=ps[:, OY:OY + NT], func=AF.Gelu_apprx_tanh),
                  reads=psY, writes=[gater[bp][c]])

    def back(b):
        bp = b % 2
        gs = b % 2
        for oc in range(2):
            ch = 2 * b + oc
            for j in range(2):
                dst, dres = (tr[oc], trr[oc]) if j == 0 else (ti[oc], tir[oc])
                bias = ba if j == 0 else bi
                for tt in range(2):
                    bkk = (oc * 4 + j * 2 + tt) % 3
                    for ic in range(2):
                        p.add("pe", lambda e, j=j, ic=ic, oc=oc, tt=tt, bkk=bkk: e.matmul(
                            ps[:, BANK[bkk]:BANK[bkk] + 512], lhsT=wg[gs][j][:, ic, oc * 128:(oc + 1) * 128],
                            rhs=xcb[bp][ic][:, tt * 512:(tt + 1) * 512], start=(ic == 0), stop=(ic == 1)),
                            reads=[wgr[gs][j], xcbr[bp][ic]], writes=[bankr[bkk]])
                    p.add("act", lambda e, dst=dst, bias=bias, ch=ch, tt=tt, bkk=bkk: e.activation(
                        out=dst[:, tt * 512:(tt + 1) * 512], in_=ps[:, BANK[bkk]:BANK[bkk] + 512], func=AF.Sigmoid,
                        bias=bias[:, ch:ch + 1]), reads=[bankr[bkk], prm_r], writes=[dres], relaxed=True)
        for oc in range(2):
            ch = 2 * b + oc
            p.add("act", lambda e, ch=ch, oc=oc: e.activation(out=ta[oc][:, :], in_=tr[oc][:, :], func=AF.Exp,
                                                             scale=cl[:, 16 + ch:17 + ch]), reads=[trr[oc], cl_r], writes=[tar[oc]])
            p.add("act", lambda e, ch=ch, oc=oc: e.activation(out=tr[oc][:, :], in_=tr[oc][:, :], func=AF.Exp,
                                                             scale=cl[:, 32 + ch:33 + ch]), reads=[trr[oc], cl_r], writes=[trr[oc]])
        for oc in range(2):
            p.add("act", lambda e, oc=oc: e.activation(out=tr[oc][:, :], in_=tr[oc][:, :], func=AF.Sqrt, scale=-1.0, bias=1.0),
                  reads=[trr[oc]], writes=[trr[oc]])
        for oc in range(2):
            ch = 2 * b + oc
            p.add("dve", lambda e, oc=oc: e.tensor_tensor(out=ti[oc][:, :], in0=ti[oc][:, :], in1=xc[bp][oc][:, :], op=ALU.mult),
                  reads=[tir[oc], xcr[bp][oc]], writes=[tir[oc]])
            p.add("dve", lambda e, oc=oc: e.tensor_tensor(out=ti[oc][:, :], in0=ti[oc][:, :], in1=tr[oc][:, :], op=ALU.mult),
                  reads=[tir[oc], trr[oc]], writes=[tir[oc]])
            p.add("dve", lambda e, oc=oc: e.tensor_tensor_scan(out=ths[:, :], data0=ta[oc][:, :], data1=ti[oc][:, :], initial=0.0,
                                                               op0=ALU.mult, op1=ALU.add), reads=[tar[oc], tir[oc]], writes=[thsr])
            p.add("dve", lambda e, oc=oc: e.tensor_tensor_scan(out=tac[:, :], data0=ta[oc][:, :], data1=zeros, initial=1.0,
                                                               op0=ALU.mult, op1=ALU.add), reads=[tar[oc], zer_r], writes=[tacr])
            p.add("dve", lambda e, oc=oc: e.tensor_tensor(out=tg1, in0=ths[:, :], in1=gate[bp][oc][:, :], op=ALU.mult),
                  reads=[thsr, gater[bp][oc]], writes=[tg1r])
            p.add("dve", lambda e, oc=oc: e.tensor_tensor(out=tz, in0=tac[:, :], in1=gate[bp][oc][:, :], op=ALU.mult),
                  reads=[tacr, gater[bp][oc]], writes=[tzr])
            p.add("dve", lambda e, ch=ch: e.tensor_copy(out=carr[:, ch:ch + 1], in_=tac[:, NT - 1:NT]), reads=[tacr], writes=[carr_r])
            p.add("dve", lambda e, ch=ch: e.tensor_copy(out=carr[:, 16 + ch:17 + ch], in_=ths[:, NT - 1:NT]), reads=[thsr], writes=[carr_r])
            p.add("sp", lambda e, ch=ch: e.dma_start(out=g1T[ch * 128:(ch + 1) * 128, :], in_=tg1), reads=[tg1r], dma="r")
            p.add("sp", lambda e, ch=ch: e.dma_start(out=zT[ch * 128:(ch + 1) * 128, :], in_=tz), reads=[tzr], dma="r")

    load_g(1)
    for b in range(9):
        if b < 8:
            front(b)
        if b >= 1:
            back(b - 1)
            if b + 1 < 8:
                load_g(b + 1)
    p.add("sp", lambda e: e.dma_start(out=carr_o[:, :], in_=carr[:, :]), reads=[carr_r], dma="r")
    return cx.finish()


def rec_params(g, cw, cb, ba, bi, lam):
    parts = [chunked(g)] + [chunked(cw[k]) for k in range(4)] + [chunked(cb), chunked(ba.reshape(-1)), chunked(bi.reshape(-1)), chunked(lam)]
    return np.ascontiguousarray(np.concatenate(parts, axis=1))


def build_attn(stage=99):
    cx = Ctx()
    nc, p = cx.nc, cx.p
    HALO = 128
    NCOL = NT + HALO
    NTB = NCOL // 128
    xT = cx.din("xT", [D, NCOL])
    w_qkv = cx.din("w_qkv", [D, 3072])
    w_o = cx.din("w_o", [D, D])
    NPRM = 16 + 2 + 32 + 1
    prm_d = cx.din("prm", [128, NPRM])
    msk_d = cx.din("msk", [128, 1024])
    yT = cx.dout("yT", [D, NT])

    ps = cx.psum()
    x = cx.sb("x", [128, KC, NCOL], F32)
    h = cx.sb("h", [128, KC, NCOL], BF16)
    sq = [cx.sb(f"sq{i}", [128, NCOL], F32) for i in range(2)]
    rstd = cx.sb("rstd", [128, NCOL], F32)
    prm = cx.sb("prm_sb", [128, NPRM], F32)
    es = cx.sb("es", [128, 32], F32)
    ones = cx.sb("ones", [128, 128], F32)
    bd = cx.sb("bd", [128, 128], F32)
    epsb = cx.sb("epsb", [128, 1], F32)
    msk = cx.sb("msk_sb", [128, 2, 512], BF16)
    Kd = cx.sb("Kd", [128, 8, NCOL], BF16)
    Vaug = cx.sb("Vaug", [128, NTB, 8, 128], BF16)
    wbig = cx.sb("wbig", [128, 8192], BF16)
    wq = [cx.sb(f"wq{i}", [128, KC, 128], BF16) for i in range(2)]
    Qn = cx.sb("Qn", [128, 4, NT], BF16)
    AO = cx.sb("AO", [128, 2, NT], BF16)
    PT = [cx.sb(f"PT{i}", [128, 2, 512], BF16) for i in range(2)]

    xres = [[Res(f"x{m}_h"), Res(f"x{m}_0"), Res(f"x{m}_1")] for m in range(KC)]
    hres = [Res(f"h{k}") for k in range(KC)]
    sqr = [Res("sq0"), Res("sq1")]
    rstdr, prm_r, ones_r, bd_r, es_r, msk_r = (Res(n) for n in ("rstd", "prm", "ones", "bd", "es", "msk"))
    Kdr = [Res(f"Kd{g}") for g in range(8)]
    Vr = Res("Vaug")
    wbr = [Res("wb0"), Res("wb1")]
    wqr = [[Res(f"wq{i}_0"), Res(f"wq{i}_1")] for i in range(2)]
    Qnr = [Res(f"Qn{i}") for i in range(4)]
    AOr = [Res("AO0"), Res("AO1")]
    PTr = [[Res(f"PT{i}_{k}") for k in range(2)] for i in range(2)]
    sqt = [sq[0][:, 0:512], sq[0][:, 512:1024]]
    rsb = [sq[1][:, 0:512], sq[1][:, 512:1024]]
    den = rstd[:, 0:512]
    rec = rstd[:, 512:1024]
    sqtr = [Res("sqt0"), Res("sqt1")]
    rsr = [Res("rs0"), Res("rs1")]
    denr, recr = Res("den"), Res("rec")
    regr = [Res("regA"), Res("regB")]
    REG = [0, 1536]
    statr = Res("stat")
    STAT = 3072
    pvr = Res("pv")
    PVB = 3584
    SC = [1024, 2560]
    scr_ = [Res("sc0"), Res("sc1")]
    OB = [0, 512, 1536, 2048]
    obr = [Res(f"ob{i}") for i in range(4)]

    gain = prm[:, 0:16]
    qg = prm[:, 16:17]
    kg = prm[:, 17:18]
    hb = prm[:, 50:51]
    scr = {"sq": sq, "sqr": sqr, "rstd": rstd, "rstdr": rstdr, "eps": epsb}

    p.add("pool", lambda e: e.memset(ones[:, :], 1.0), writes=[ones_r])
    p.add("pool", lambda e: e.memset(epsb[:, :], EPS), writes=[prm_r])
    p.add("pool", lambda e: e.memset(bd[:, :], 0.0), writes=[bd_r])
    p.add("pool", lambda e: e.memset(bd[0:64, 0:64], 1.0), writes=[bd_r])
    p.add("pool", lambda e: e.memset(bd[64:128, 64:128], 1.0), writes=[bd_r])
    p.add("pool", lambda e: e.memset(Vaug[:, :, :, :], 1.0), writes=[Vr])
    p.add("sp", lambda e: e.dma_start(out=prm[:, :], in_=prm_d[:, :]), writes=[prm_r], dma="w")
    p.add("pool", lambda e: e.dma_start(out=msk[:, :, :], in_=msk_d.rearrange("p (k n) -> p k n", k=2)),
          writes=[msk_r], dma="w")
    xv = xT.rearrange("(kc p) n -> p kc n", p=128)
    for kc in range(KC):
        p.add("sp", lambda e, kc=kc: e.dma_start(out=x[:, kc, :], in_=xv[:, kc, :]), writes=xres[kc], dma="w")
    p.add("act", lambda e: e.activation(out=es[:, :], in_=prm[:, 18:50], func=AF.Exp), reads=[prm_r], writes=[es_r])

    wv = wbig[:, :].rearrange("p (kc n) -> p kc n", kc=KC)
    wqkv_v = w_qkv.rearrange("(kc p) n -> p kc n", p=128)
    p.add("pool", lambda e: e.dma_start(out=wv, in_=wqkv_v[:, :, 2560:3072]), writes=wbr, dma="w")

    nq = [0]

    def load_k(g):
        s = nq[0] % 2
        nq[0] += 1
        for half in range(2):
            p.add("pool", lambda e, s=s, g=g, half=half: e.dma_start(
                out=wq[s][:, :, half * 64:(half + 1) * 64], in_=wqkv_v[:, :, 2048 + g * 64:2048 + (g + 1) * 64]),
                writes=[wqr[s][half]], dma="w")
        return s

    def load_q(hd):
        s = nq[0] % 2
        nq[0] += 1
        for half in range(2):
            p.add("pool", lambda e, s=s, hd=hd, half=half: e.dma_start(
                out=wq[s][:, :, half * 64:(half + 1) * 64], in_=wqkv_v[:, :, hd * 64:(hd + 1) * 64]),
                writes=[wqr[s][half]], dma="w")
        return s

    def load_o(g):
        s = g % 2
        dst = wbig[:, s * 4096:(s + 1) * 4096].rearrange("p (c n) -> p c n", c=2)
        src = w_o[g * 256:(g + 1) * 256, :].rearrange("(c p) n -> p c n", p=128)
        p.add("pool", lambda e, dst=dst, src=src: e.dma_start(out=dst, in_=src), writes=[wbr[s]], dma="w")

    emit_rmsnorm(cx, ps, regr[0], x, xres, gain, prm_r, h, hres, NCOL, ones, ones_r, scr)

    nstat = [0]

    def proj_norm(s, ri, tiles, hoff, gain_ap, dst_fn, dst_res):
        R = REG[ri]
        for kc in range(KC):
            for (c0, c1) in tiles:
                p.add("pe", lambda e, s=s, kc=kc, c0=c0, c1=c1, R=R: e.matmul(
                    ps[:, R + c0:R + c1], lhsT=wq[s][:, kc, :], rhs=h[:, kc, hoff + c0:hoff + c1],
                    start=(kc == 0), stop=(kc == KC - 1)), reads=wqr[s] + [hres[kc]], writes=[regr[ri]])
        for (c0, c1) in tiles:
            n = c1 - c0
            i = nstat[0] % 2
            nstat[0] += 1
            src = ps[:, R + c0:R + c1]
            p.add("act", lambda e, i=i, n=n, src=src: e.activation(out=sqt[i][:, 0:n], in_=src, func=AF.Square),
                  reads=[regr[ri]], writes=[sqtr[i]])
            p.add("pe", lambda e, i=i, n=n: e.matmul(ps[:, STAT:STAT + n], lhsT=bd[:, :], rhs=sqt[i][:, 0:n],
                                                     start=True, stop=True), reads=[bd_r, sqtr[i]], writes=[statr])
            p.add("act", lambda e, i=i, n=n: e.activation(out=sqt[i][:, 0:n], in_=ps[:, STAT:STAT + n], func=AF.Sqrt,
                                                          bias=epsb[:, 0:1], scale=1.0 / 64), reads=[statr, prm_r], writes=[sqtr[i]])
            p.add("dve", lambda e, i=i, n=n: e.reciprocal(out=rsb[i][:, 0:n], in_=sqt[i][:, 0:n]), reads=[sqtr[i]], writes=[rsr[i]])
            dst = dst_fn(c0, c1)
            p.add("dve", lambda e, i=i, n=n, src=src, dst=dst: e.scalar_tensor_tensor(
                out=dst, in0=src, scalar=gain_ap, in1=rsb[i][:, 0:n], op0=ALU.mult, op1=ALU.mult),
                reads=[regr[ri], rsr[i], prm_r], writes=[dst_res], relaxed=True)

    BV = [1536, 2048, 2560]
    bvr = [Res(f"bv{i}") for i in range(3)]
    for tb in range(NTB):
        b = tb % 3
        for kc in range(KC):
            p.add("pe", lambda e, tb=tb, kc=kc, b=b: e.matmul(
                ps[:, BV[b]:BV[b] + 512], lhsT=h[:, kc, tb * 128:(tb + 1) * 128], rhs=wv[:, kc, :],
                start=(kc == 0), stop=(kc == KC - 1)), reads=wbr + [hres[kc]], writes=[bvr[b]])
        p.add("act", lambda e, tb=tb, b=b: e.activation(
            out=Vaug[:, tb, :, 0:64], in_=ps[:, BV[b]:BV[b] + 512].rearrange("p (g d) -> p g d", g=8), func=AF.Identity),
            reads=[bvr[b], regr[1]], writes=[Vr], relaxed=True)

    if stage <= 1:
        return cx.finish()
    tilesK = token_tiles(NCOL, 0)
    ks = load_k(0)
    for g in range(8):
        s = ks
        if g + 1 < 8:
            ks = load_k(g + 1)
        proj_norm(s, g % 2, tilesK, 0, kg, lambda c0, c1, g=g: Kd[:, g, c0:c1], Kdr[g])

    if stage <= 2:
        return cx.finish()
    tilesQ = token_tiles(NT, 0)
    qs = [load_q(0), load_q(1)]
    load_o(0)
    for g in range(8):
        for hh in range(4):
            s = qs.pop(0)
            proj_norm(s, hh % 2, tilesQ, HALO, qg, lambda c0, c1, hh=hh: Qn[:, hh, c0:c1], Qnr[hh])
            nxt = g * 4 + hh + 2
            if nxt < 32:
                qs.append(load_q(nxt))
        if g + 1 < 8:
            load_o(g + 1)
        if stage <= 3:
            return cx.finish()
        for qb in range(8):
            i = qb % 2
            for kb in range(2):
                for hh in range(4):
                    p.add("pe", lambda e, g=g, qb=qb, kb=kb, hh=hh: e.matmul(
                        ps[:, SC[kb] + hh * 128:SC[kb] + (hh + 1) * 128],
                        lhsT=Kd[:, g, (qb + kb) * 128:(qb + kb + 1) * 128],
                        rhs=Qn[:, hh, qb * 128:(qb + 1) * 128], start=True, stop=True),
                        reads=[Kdr[g], Qnr[hh]], writes=[scr_[kb]])
                bias = hb if (qb == 0 and kb == 0) else 0.0
                p.add("act", lambda e, i=i, kb=kb, bias=bias: e.activation(
                    out=PT[i][:, kb, :], in_=ps[:, SC[kb]:SC[kb] + 512], func=AF.Exp, bias=bias, scale=0.0625),
                    reads=[scr_[kb], prm_r], writes=[PTr[i][kb]])
                p.add("pool", lambda e, i=i, kb=kb: e.tensor_tensor(
                    out=PT[i][:, kb, :], in0=PT[i][:, kb, :], in1=msk[:, kb, :], op=ALU.mult),
                    reads=[PTr[i][kb], msk_r], writes=[PTr[i][kb]])
            for kb in range(2):
                p.add("pe", lambda e, g=g, qb=qb, kb=kb, i=i: e.matmul(
                    ps[:, PVB:PVB + 512], lhsT=Vaug[:, qb + kb, g, :], rhs=PT[i][:, kb, :],
                    start=(kb == 0), stop=(kb == 1)), reads=[Vr, PTr[i][kb]], writes=[pvr])
            for hh in range(4):
                p.add("dve", lambda e, g=g, hh=hh: e.tensor_scalar(
                    out=den[64:128, hh * 128:(hh + 1) * 128], in0=ps[64:128, PVB + hh * 128:PVB + (hh + 1) * 128],
                    scalar1=es[64:128, g * 4 + hh:g * 4 + hh + 1], scalar2=None, op0=ALU.add),
                    reads=[pvr, es_r], writes=[denr], relaxed=True)
            p.add("dve", lambda e: e.reciprocal(out=rec[0:64, :], in_=den[64:128, :]), reads=[denr], writes=[recr])
            for hh in range(4):
                pb = (hh % 2) * 64
                c = hh // 2
                p.add("dve", lambda e, hh=hh, pb=pb, c=c, qb=qb: e.tensor_tensor(
                    out=AO[pb:pb + 64, c, qb * 128:(qb + 1) * 128], in0=ps[0:64, PVB + hh * 128:PVB + (hh + 1) * 128],
                    in1=rec[0:64, hh * 128:(hh + 1) * 128], op=ALU.mult),
                    reads=[pvr, recr], writes=[AOr[c]], relaxed=True)
        if stage <= 4:
            return cx.finish()
        so = g % 2
        wo = wbig[:, so * 4096:(so + 1) * 4096].rearrange("p (c n) -> p c n", c=2)
        for m in range(KC):
            for tt in range(2):
                b = (m * 2 + tt) % 4
                for c in range(2):
                    p.add("pe", lambda e, wo=wo, m=m, tt=tt, b=b, c=c: e.matmul(
                        ps[:, OB[b]:OB[b] + 512], lhsT=wo[:, c, m * 128:(m + 1) * 128],
                        rhs=AO[:, c, tt * 512:(tt + 1) * 512], start=(c == 0), stop=(c == 1)),
                        reads=[wbr[so], AOr[c]], writes=[obr[b], regr[0 if b < 2 else 1]])
                xs = x[:, m, HALO + tt * 512:HALO + (tt + 1) * 512]
                p.add("dve", lambda e, xs=xs, b=b: e.tensor_tensor(out=xs, in0=ps[:, OB[b]:OB[b] + 512], in1=xs, op=ALU.add),
                      reads=[obr[b], regr[0 if b < 2 else 1], xres[m][1 + tt]], writes=[xres[m][1 + tt]])
                if g == 7 and tt == 1:
                    yv = yT.rearrange("(kc p) n -> p kc n", p=128)
                    p.add("sp", lambda e, m=m: e.dma_start(out=yv[:, m, :], in_=x[:, m, HALO:HALO + NT]),
                          reads=[xres[m][1], xres[m][2]], dma="r")
    return cx.finish()


def attn_params(g, qgain, kgain, sinks, halo_bias):
    qg = np.tile(np.asarray(qgain, np.float32), 2).reshape(128, 1)
    kg = np.tile(np.asarray(kgain, np.float32), 2).reshape(128, 1)
    sk = np.tile(np.asarray(sinks, np.float32).reshape(1, 32), (128, 1))
    hb = np.full((128, 1), halo_bias, np.float32)
    return np.ascontiguousarray(np.concatenate([chunked(g), qg, kg, sk, hb], axis=1))


def attn_masks():
    k = np.arange(128)[:, None]
    q = np.arange(128)[None, :]
    mprev = (k > q).astype(np.float32)
    mcur = (q >= k).astype(np.float32)
    return np.ascontiguousarray(np.concatenate([np.tile(mprev, (1, 4)), np.tile(mcur, (1, 4))], axis=1))


_CACHE = {}


def _prog(name):
    if name not in _CACHE:
        _CACHE[name] = {"attn": build_attn2, "ffn": lambda: build_ffn(False),
                        "ffnp": lambda: build_ffn(True), "rec1": build_rec1b}[name]()
    return _CACHE[name]


def _run(name, in_maps):
    res = run_bass_kernel_spmd(_prog(name), in_maps, core_ids=list(range(NCORES)))
    return res.results


def _halo_slices(aT, halo):
    pad = np.concatenate([np.zeros((aT.shape[0], halo), aT.dtype), aT], axis=1)
    return [np.ascontiguousarray(pad[:, c * NT:c * NT + NT + halo]) for c in range(NCORES)]


def kernel(x, mix_norm, ffn_norm, attn_w_qkv, attn_q_gain, attn_k_gain, attn_sinks, attn_w_o,
           rec_w_in, rec_conv_w, rec_conv_b, rec_w_a, rec_b_a, rec_w_i, rec_b_i, rec_lambda,
           rec_w_out, ffn_w_up, ffn_conv_w, ffn_conv_b, ffn_w_down):
    f = lambda a: np.ascontiguousarray(np.asarray(a, dtype=np.float32))
    xT = np.ascontiguousarray(f(x)[0].T)
    masks = attn_masks2()
    for layer in range(4):
        j = layer // 2
        fprm = ffn_params(f(ffn_norm)[layer], f(ffn_conv_w)[layer], f(ffn_conv_b)[layer])
        wup, wdn = f(ffn_w_up)[layer], f(ffn_w_down)[layer]
        if layer % 2 == 0:
            xs = _halo_slices(xT, 128)
            wqkv, wo = f(attn_w_qkv)[j], f(attn_w_o)[j]
            in_maps = [{"xT": xs[c], "w_qkv": wqkv, "w_o": wo, "msk": masks,
                        "prm": attn_params(f(mix_norm)[layer], f(attn_q_gain)[j], f(attn_k_gain)[j],
                                           f(attn_sinks)[j], -30000.0 if c == 0 else 0.0)}
                       for c in range(NCORES)]
            res = _run("attn", in_maps)
            xT = np.concatenate([r["yT"] for r in res], axis=1)
            xs = _halo_slices(xT, 2)
            in_maps = [{"xT": xs[c], "w_up": wup, "w_down": wdn, "prm": fprm} for c in range(NCORES)]
            res = _run("ffn", in_maps)
        else:
            xs = _halo_slices(xT, 3)
            rprm = rec_params(f(mix_norm)[layer], f(rec_conv_w)[j], f(rec_conv_b)[j], f(rec_b_a)[j],
                              f(rec_b_i)[j], f(rec_lambda)[j])
            in_maps = [{"xT": xs[c], "w_in": f(rec_w_in)[j], "w_a": f(rec_w_a)[j], "w_i": f(rec_w_i)[j],
                        "prm": rprm} for c in range(NCORES)]
            res = _run("rec1", in_maps)
            g1T = np.concatenate([r["g1T"] for r in res], axis=1)
            zT = np.concatenate([r["zT"] for r in res], axis=1)
            carr = np.ascontiguousarray(np.stack([r["carr"] for r in res], axis=-1).reshape(128, 256))
            xs, gs, zs = _halo_slices(xT, 2), _halo_slices(g1T, 2), _halo_slices(zT, 2)
            in_maps = []
            for c in range(NCORES):
                m = np.zeros((128, 16), np.float32)
                m[:, 0:max(c, 0)] = 1.0
                m[:, 8:8 + max(c - 1, 0)] = 1.0
                in_maps.append({"xT": xs[c], "g1T": gs[c], "zT": zs[c], "carr": carr, "msk": m,
                                "w_out": f(rec_w_out)[j], "w_up": wup, "w_down": wdn, "prm": fprm})
            res = _run("ffnp", in_maps)
        xT = np.concatenate([r["yT"] for r in res], axis=1)
    return np.ascontiguousarray(xT.T)[None].astype(np.float32)


def build_attn2():
    cx = Ctx()
    nc, p = cx.nc, cx.p
    HALO = 128
    NCOL = NT + HALO
    NTB = NCOL // 128
    xT = cx.din("xT", [D, NCOL])
    w_qkv = cx.din("w_qkv", [D, 3072])
    w_o = cx.din("w_o", [D, D])
    NPRM = 16 + 2 + 32 + 1
    prm_d = cx.din("prm", [128, NPRM])
    msk_d = cx.din("msk", [128, 640])
    yT = cx.dout("yT", [D, NT])

    ps = cx.psum()
    bk = [Res(f"bk{i}") for i in range(8)]
    x = cx.sb("x", [128, KC, NCOL], F32)
    h = cx.sb("h", [128, KC, NCOL], BF16)
    sq = [cx.sb(f"sq{i}", [128, NCOL], F32) for i in range(2)]
    rstd = cx.sb("rstd", [128, NCOL], F32)
    prm = cx.sb("prm_sb", [128, NPRM], F32)
    es = cx.sb("es", [128, 32], F32)
    ones = cx.sb("ones", [128, 128], F32)
    bd = cx.sb("bd", [128, 128], F32)
    bdb = cx.sb("bdb", [128, 128], BF16)
    epsb = cx.sb("epsb", [128, 2], F32)
    msk = cx.sb("msk_sb", [128, 640], BF16)
    onesb = cx.sb("onesb", [128, 128], BF16)
    Kd = cx.sb("Kd", [128, 8, NCOL], BF16)
    NVB = NTB * 8
    VO = cx.sb("VO", [128, (NVB + 1) * 64], BF16)
    wbig = cx.sb("wbig", [128, 8192], BF16)
    wq = [cx.sb(f"wq{i}", [128, KC, 128], BF16) for i in range(2)]
    Qn = [cx.sb(f"Qn{i}", [128, 4, NT], BF16) for i in range(2)]
    AO = [cx.sb(f"AO{i}", [128, 2, NT], BF16) for i in range(2)]
    PT = [cx.sb(f"PT{i}", [128, 512], BF16) for i in range(2)]

    xres = [[Res(f"x{m}_h"), Res(f"x{m}_0"), Res(f"x{m}_1")] for m in range(KC)]
    hres = [Res(f"h{k}") for k in range(KC)]
    sqr = [Res("sq0"), Res("sq1")]
    rstdr, prm_r, ones_r, bd_r, es_r, msk_r = (Res(n) for n in ("rstd", "prm", "ones", "bd", "es", "msk"))
    Kdr = [Res(f"Kd{g}") for g in range(8)]
    Vr = Res("VO")
    wbr = [Res("wb0"), Res("wb1")]
    wqr = [[Res(f"wq{i}_0"), Res(f"wq{i}_1")] for i in range(2)]
    Qnr = [[Res(f"Qn{i}_{hh}") for hh in range(4)] for i in range(2)]
    AOr = [[Res(f"AO{i}_{c}") for c in range(2)] for i in range(2)]
    PTr = [Res("PT0"), Res("PT1")]
    sqt = [sq[0][:, 0:512], sq[0][:, 512:1024]]
    _sq0b = sq[0][:, :].bitcast(BF16)
    sqtb = [_sq0b[:, 0:512], _sq0b[:, 1024:1536]]
    rsb = [sq[1][:, 0:512], sq[1][:, 512:1024]]
    den = rstd[:, 0:256]
    rec = rstd[:, 512:768]
    sqtr = [Res("sqt0"), Res("sqt1")]
    rsr = [Res("rs0"), Res("rs1")]
    denr, recr = Res("den"), Res("rec")
    B = lambda i: i * 512
    STATB, SCB, PVB, OPB = 4, 5, 6, 7

    gain = prm[:, 0:16]
    qg = prm[:, 16:17]
    kg = prm[:, 17:18]
    hb = prm[:, 50:51]
    scr = {"sq": sq, "sqr": sqr, "rstd": rstd, "rstdr": rstdr, "eps": epsb}

    p.add("pool", lambda e: e.memset(ones[:, :], 1.0), writes=[ones_r])
    p.add("pool", lambda e: e.memset(onesb[:, :], 1.0), writes=[ones_r])
    p.add("pool", lambda e: e.memset(epsb[:, 0:1], EPS), writes=[prm_r])
    p.add("pool", lambda e: e.memset(epsb[:, 1:2], 64 * EPS), writes=[prm_r])
    p.add("pool", lambda e: e.memset(bd[:, :], 0.0), writes=[bd_r])
    p.add("pool", lambda e: e.memset(bd[0:64, 0:64], 1.0), writes=[bd_r])
    p.add("pool", lambda e: e.memset(bd[64:128, 64:128], 1.0), writes=[bd_r])
    p.add("pool", lambda e: e.tensor_copy(out=bdb[:, :], in_=bd[:, :]), reads=[bd_r], writes=[bd_r])
    p.add("pool", lambda e: e.memset(VO[:, NVB * 64:(NVB + 1) * 64], 1.0), writes=[Vr])
    p.add("sp", lambda e: e.dma_start(out=prm[:, :], in_=prm_d[:, :]), writes=[prm_r], dma="w")
    p.add("pool", lambda e: e.dma_start(out=msk[:, :], in_=msk_d[:, :]), writes=[msk_r], dma="w")
    xv = xT.rearrange("(kc p) n -> p kc n", p=128)
    for kc in range(KC):
        p.add("sp", lambda e, kc=kc: e.dma_start(out=x[:, kc, :], in_=xv[:, kc, :]), writes=xres[kc], dma="w")
    p.add("act", lambda e: e.activation(out=es[:, :], in_=prm[:, 18:50], func=AF.Exp), reads=[prm_r], writes=[es_r])

    wv = wbig[:, :].rearrange("p (kc n) -> p kc n", kc=KC)
    wqkv_v = w_qkv.rearrange("(kc p) n -> p kc n", p=128)
    p.add("pool", lambda e: e.dma_start(out=wv, in_=wqkv_v[:, :, 2560:3072]), writes=wbr, dma="w")

    nq = [0]

    def load_w(col0):
        s = nq[0] % 2
        nq[0] += 1
        for half in range(2):
            p.add("pool", lambda e, s=s, col0=col0, half=half: e.dma_start(
                out=wq[s][:, :, half * 64:(half + 1) * 64], in_=wqkv_v[:, :, col0:col0 + 64]),
                writes=[wqr[s][half]], dma="w")
        return s

    def load_o(g):
        s = g % 2
        dst = wbig[:, s * 4096:(s + 1) * 4096].rearrange("p (c n) -> p c n", c=2)
        src = w_o[g * 256:(g + 1) * 256, :].rearrange("(c p) n -> p c n", p=128)
        p.add("pool", lambda e, dst=dst, src=src: e.dma_start(out=dst, in_=src), writes=[wbr[s]], dma="w")

    emit_rmsnorm(cx, ps, bk[0], x, xres, gain, prm_r, h, hres, NCOL, ones, ones_r, scr,
                 extra_ps_res=[bk[1], bk[2]], use_ln=True, onesb=onesb)

    for tb in range(NTB):
        b = 5 + tb % 3
        for kc in range(KC):
            p.add("pe", lambda e, tb=tb, kc=kc, b=b: e.matmul(
                ps[:, B(b):B(b) + 512], lhsT=h[:, kc, tb * 128:(tb + 1) * 128], rhs=wv[:, kc, :],
                start=(kc == 0), stop=(kc == KC - 1)), reads=wbr + [hres[kc]], writes=[bk[b]])
        p.add("act", lambda e, tb=tb, b=b: e.activation(
            out=VO[:, tb * 512:(tb + 1) * 512], in_=ps[:, B(b):B(b) + 512], func=AF.Identity),
            reads=[bk[b]], writes=[Vr], relaxed=True)

    nstat = [0]

    def proj_mms(s, banks, tiles, hoff, tile_major=False):
        out = []
        order = [(kc, t) for kc in range(KC) for t in range(len(tiles))] if not tile_major else \
                [(kc, t) for t in range(len(tiles)) for kc in range(KC)]
        for kc, t in order:
            c0, c1 = tiles[t]
            if True:
                def f(s=s, kc=kc, t=t, c0=c0, c1=c1):
                    p.add("pe", lambda e: e.matmul(
                        ps[:, B(banks[t]):B(banks[t]) + (c1 - c0)], lhsT=wq[s][:, kc, :],
                        rhs=h[:, kc, hoff + c0:hoff + c1], start=(kc == 0), stop=(kc == KC - 1)),
                        reads=wqr[s] + [hres[kc]], writes=[bk[banks[t]]])
                out.append(f)
        return out

    def norm_steps(banks, tiles, gain_ap, dst_fn, dst_res):
        idx = []
        for t in range(len(tiles)):
            idx.append(nstat[0] % 2)
            nstat[0] += 1

        def phase_a():
            for t, (c0, c1) in enumerate(tiles):
                n = c1 - c0
                i = idx[t]
                src = ps[:, B(banks[t]):B(banks[t]) + n]
                p.add("act", lambda e, i=i, n=n, src=src: e.activation(out=sqtb[i][:, 0:n], in_=src, func=AF.Square),
                      reads=[bk[banks[t]]], writes=[sqtr[i]])

        def phase_b(t):
            c0, c1 = tiles[t]
            n = c1 - c0
            i = idx[t]
            src = ps[:, B(banks[t]):B(banks[t]) + n]
            br = bk[banks[t]]
            p.add("pe", lambda e: e.matmul(ps[:, B(STATB):B(STATB) + n], lhsT=bdb[:, :], rhs=sqtb[i][:, 0:n],
                                           start=True, stop=True), reads=[bd_r, sqtr[i]], writes=[bk[STATB]])
            p.add("act", lambda e: e.activation(out=sqt[i][:, 0:n], in_=ps[:, B(STATB):B(STATB) + n], func=AF.Ln,
                                                bias=epsb[:, 1:2], scale=1.0), reads=[bk[STATB], prm_r], writes=[sqtr[i]])
            p.add("act", lambda e: e.activation(out=rsb[i][:, 0:n], in_=sqt[i][:, 0:n], func=AF.Exp, scale=-0.5),
                  reads=[sqtr[i]], writes=[rsr[i]])
            dst = dst_fn(c0, c1)
            p.add("dve", lambda e: e.scalar_tensor_tensor(
                out=dst, in0=src, scalar=gain_ap, in1=rsb[i][:, 0:n], op0=ALU.mult, op1=ALU.mult),
                reads=[br, rsr[i], prm_r], writes=[dst_res], relaxed=True)

        return [phase_a] + [(lambda t=t: phase_b(t)) for t in range(len(tiles))]

    def norm(banks, tiles, gain_ap, dst_fn, dst_res):
        for t in range(len(tiles)):
            for f in norm_steps(banks[t:t + 1], tiles[t:t + 1], gain_ap, dst_fn, dst_res):
                f()

    tilesK = token_tiles(NCOL, 0)
    KB = [[0, 1, 5], [2, 3, 6]]
    ks = load_w(2048)
    pend = None
    for g in range(8):
        s = ks
        if g + 1 < 8:
            ks = load_w(2048 + (g + 1) * 64)
        for f in proj_mms(s, KB[g % 2], tilesK, 0):
            f()
        if pend is not None:
            pend()
        pend = (lambda g=g: norm(KB[g % 2], tilesK, kg, lambda c0, c1: Kd[:, g, c0:c1], Kdr[g]))
    pend()

    tilesQ = token_tiles(NT, 0)
    QB = [[0, 1], [2, 3]]

    qslot = {}

    def prefetch_q(hd):
        if hd < 32 and hd not in qslot:
            qslot[hd] = load_w(hd * 64)

    def emit_q_head(g, hh, chunks):
        hd = g * 4 + hh
        gp = g % 2
        prefetch_q(hd)
        s = qslot[hd]
        qbanks = [(hd * 2) % 3, (hd * 2 + 1) % 3]
        mms = proj_mms(s, qbanks, tilesQ, HALO, tile_major=True)
        per = (len(mms) + chunks - 1) // chunks
        out = []
        for ci in range(chunks):
            part = mms[ci * per:(ci + 1) * per]
            last = ci == chunks - 1

            def f(part=part, last=last, first=(ci == 0)):
                if first:
                    prefetch_q(hd + 1)
                    while any(set(r) & set(qbanks) for r, _ in pending_norm):
                        pending_norm.pop(0)[1]()
                half = len(part) // 2
                pop_pending()
                for m in part[:half]:
                    m()
                pop_pending()
                for m in part[half:]:
                    m()
                if last:
                    sts = norm_steps(qbanks, tilesQ, qg, lambda c0, c1: Qn[gp][:, hh, c0:c1], Qnr[gp][hh])
                    pending_norm.extend(zip([qbanks, qbanks[0:1], qbanks[1:2]], sts))
            out.append(f)
        return out

    pending_norm = []

    def pop_pending():
        if pending_norm:
            pending_norm.pop(0)[1]()

    def flush_norm():
        while pending_norm:
            pending_norm.pop(0)[1]()

    def o_units(g, banks=(7,)):
        gp = g % 2
        so = g % 2
        wo = wbig[:, so * 4096:(so + 1) * 4096].rearrange("p (c n) -> p c n", c=2)
        units = []
        for m in range(KC):
            for tt in range(2):
                def f(m=m, tt=tt):
                    ob = banks[(m * 2 + tt) % len(banks)]
                    for c in range(2):
                        p.add("pe", lambda e, c=c: e.matmul(
                            ps[:, B(ob):B(ob) + 512], lhsT=wo[:, c, m * 128:(m + 1) * 128],
                            rhs=AO[gp][:, c, tt * 512:(tt + 1) * 512], start=(c == 0), stop=(c == 1)),
                            reads=[wbr[so], AOr[gp][c]], writes=[bk[ob]])
                    xs = x[:, m, HALO + tt * 512:HALO + (tt + 1) * 512]
                    p.add("dve", lambda e: e.tensor_tensor(out=xs, in0=ps[:, B(ob):B(ob) + 512], in1=xs, op=ALU.add),
                          reads=[bk[ob], xres[m][1 + tt]], writes=[xres[m][1 + tt]])
                    if g == 7 and tt == 1:
                        yv = yT.rearrange("(kc p) n -> p kc n", p=128)
                        p.add("sp", lambda e: e.dma_start(out=yv[:, m, :], in_=x[:, m, HALO:HALO + NT]),
                              reads=[xres[m][1], xres[m][2]], dma="r")
                units.append(f)
        return units

    prefetch_q(0)
    for hh in range(4):
        for f in emit_q_head(0, hh, 1):
            f()
    flush_norm()
    load_o(0)

    nstep = [0]
    for g in range(8):
        gp = g % 2
        qwork = []
        if g + 1 < 8:
            for hh in range(4):
                qwork.append((g + 1, hh))
        owork = o_units(g - 1, banks=(7, 3)) if g > 0 else []
        if g > 0:
            load_o(g)
        cur_q = []
        for qb in range(8):
            for hp in range(2):
                step = qb * 2 + hp
                i = nstep[0] % 2
                nstep[0] += 1
                p.add("pe", lambda e: e.matmul(ps[:, B(SCB):B(SCB) + 512], lhsT=msk[:, 512:640], rhs=msk[:, 0:512],
                                               start=True, stop=False), reads=[msk_r], writes=[bk[SCB]])
                for kb in range(2):
                    for hl in range(2):
                        hh = hp * 2 + hl
                        col = B(SCB) + kb * 256 + hl * 128
                        p.add("pe", lambda e, kb=kb, hh=hh, col=col, qb=qb, g=g, gp=gp, hl=hl: e.matmul(
                            ps[:, col:col + 128], lhsT=Kd[:, g, (qb + kb) * 128:(qb + kb + 1) * 128],
                            rhs=Qn[gp][:, hh, qb * 128:(qb + 1) * 128], start=False, stop=(kb == 1 and hl == 1)),
                            reads=[Kdr[g], Qnr[gp][hh]], writes=[bk[SCB]])
                if qb == 0:
                    p.add("act", lambda e, i=i: e.activation(out=PT[i][:, 0:256], in_=ps[:, B(SCB):B(SCB) + 256],
                                                             func=AF.Exp, bias=hb, scale=4.0),
                          reads=[bk[SCB], prm_r], writes=[PTr[i]])
                    p.add("act", lambda e, i=i: e.activation(out=PT[i][:, 256:512], in_=ps[:, B(SCB) + 256:B(SCB) + 512],
                                                             func=AF.Exp, scale=4.0),
                          reads=[bk[SCB]], writes=[PTr[i]], relaxed=True)
                else:
                    p.add("act", lambda e, i=i: e.activation(out=PT[i][:, :], in_=ps[:, B(SCB):B(SCB) + 512],
                                                             func=AF.Exp, scale=4.0),
                          reads=[bk[SCB]], writes=[PTr[i]])
                if owork:
                    owork.pop(0)()
                if qwork or cur_q:
                    if not cur_q:
                        gg, hh_ = qwork.pop(0)
                        cur_q = emit_q_head(gg, hh_, 4)
                    cur_q.pop(0)()
                else:
                    pop_pending()
                if owork:
                    owork.pop(0)()
                for kb in range(2):
                    vb = ((qb + kb) * 8 + g) * 64
                    p.add("pe", lambda e, kb=kb, vb=vb, i=i: e.matmul(
                        ps[:, B(PVB):B(PVB) + 256], lhsT=VO[:, vb:vb + 128], rhs=PT[i][:, kb * 256:(kb + 1) * 256],
                        start=(kb == 0), stop=(kb == 1)), reads=[Vr, PTr[i]], writes=[bk[PVB]])
                for kb in range(2):
                    p.add("pe", lambda e, kb=kb, i=i: e.matmul(
                        ps[:, B(PVB) + 256:B(PVB) + 512], lhsT=onesb[:, :], rhs=PT[i][:, kb * 256:(kb + 1) * 256],
                        start=(kb == 0), stop=(kb == 1)), reads=[ones_r, PTr[i]], writes=[bk[PVB]])
                for hl in range(2):
                    hh = hp * 2 + hl
                    p.add("dve", lambda e, hl=hl, hh=hh, g=g: e.tensor_scalar(
                        out=den[0:64, hl * 128:(hl + 1) * 128],
                        in0=ps[0:64, B(PVB) + 256 + hl * 128:B(PVB) + 256 + (hl + 1) * 128],
                        scalar1=es[0:64, g * 4 + hh:g * 4 + hh + 1], scalar2=None, op0=ALU.add),
                        reads=[bk[PVB], es_r], writes=[denr], relaxed=True)
                p.add("dve", lambda e: e.reciprocal(out=rec[0:64, :], in_=den[0:64, :]), reads=[denr], writes=[recr])
                for hl in range(2):
                    p.add("dve", lambda e, hl=hl, hp=hp, qb=qb, gp=gp: e.tensor_tensor(
                        out=AO[gp][hl * 64:(hl + 1) * 64, hp, qb * 128:(qb + 1) * 128],
                        in0=ps[0:64, B(PVB) + hl * 128:B(PVB) + (hl + 1) * 128],
                        in1=rec[0:64, hl * 128:(hl + 1) * 128], op=ALU.mult),
                        reads=[bk[PVB], recr], writes=[AOr[gp][hp]], relaxed=True)
        flush_norm()
        assert not qwork and not cur_q and not owork, (len(qwork), len(cur_q), len(owork))
    for f in o_units(7, banks=(7, 3, 0, 1, 2, 5)):
        f()
    return cx.finish()


def attn_masks2():
    k = np.arange(128)[:, None]
    q = np.arange(128)[None, :]
    mprev = np.where(k > q, 0.0, -10000.0).astype(np.float32)
    mcur = np.where(q >= k, 0.0, -10000.0).astype(np.float32)
    return np.ascontiguousarray(np.concatenate([mprev, mprev, mcur, mcur, np.eye(128, dtype=np.float32)], axis=1))
```

```python
import contextlib
import numpy as np
import concourse.bass as bass
import concourse.mybir as mybir
from concourse.bass_utils import run_bass_kernel_spmd

F32 = mybir.dt.float32
BF16 = mybir.dt.bfloat16
AF = mybir.ActivationFunctionType
ALU = mybir.AluOpType

NCORES = 8
D = 2048
KC = 16
T = 8192
NT = 1024
DFF = 6144
EPS = 1e-6


class Res:
    __slots__ = ("name", "last_w", "readers", "sem_w", "nw", "sem_r", "nr")

    def __init__(self, name):
        self.name = name
        self.last_w = None
        self.readers = {}
        self.sem_w = None
        self.nw = 0
        self.sem_r = None
        self.nr = 0


class Op:
    __slots__ = ("eng", "fn", "deps", "need_inc", "sem", "semval", "dma", "inc")

    def __init__(self, eng, fn, dma):
        self.eng = eng
        self.fn = fn
        self.deps = []
        self.need_inc = False
        self.sem = None
        self.semval = 0
        self.dma = dma
        self.inc = 1


class Prog:
    ENGS = ("pe", "act", "dve", "pool", "sp")

    def __init__(self, nc):
        self.nc = nc
        self.ops = {e: [] for e in self.ENGS}
        self.esem = {e: nc.alloc_semaphore(name=f"sem_{e}") for e in self.ENGS}
        self.nsem = 0
        self.final = []

    def _newsem(self):
        self.nsem += 1
        return self.nc.alloc_semaphore(name=f"dsem{self.nsem}")

    def add(self, eng, fn, reads=(), writes=(), dma=None, relaxed=False):
        op = Op(eng, fn, dma)
        deps = []
        for r in reads:
            if r.last_w is not None:
                deps.append(r.last_w)
        for w in writes:
            if w.last_w is not None:
                if not (relaxed and w.last_w.dma is None and dma is None and w.last_w.eng == eng):
                    deps.append(w.last_w)
            deps.extend(w.readers.values())
        for d in deps:
            if d is op:
                continue
            if d.dma is None and op.dma is None and d.eng == eng == "pe":
                continue
            if d not in op.deps:
                op.deps.append(d)
                d.need_inc = True
        for r in reads:
            key = eng if dma is None else id(op)
            r.readers[key] = op
        for w in writes:
            w.last_w = op
            w.readers = {}
        if dma == "w":
            res = writes[0]
            if res.sem_w is None:
                res.sem_w = self._newsem()
            res.nw += 1
            op.sem, op.semval, op.inc = res.sem_w, 16 * res.nw, 16
            op.need_inc = True
        elif dma == "r":
            res = reads[0]
            if res.sem_r is None:
                res.sem_r = self._newsem()
            res.nr += 1
            op.sem, op.semval, op.inc = res.sem_r, 16 * res.nr, 16
            op.need_inc = True
            self.final.append(op)
        self.ops[eng].append(op)
        return op

    def emit(self):
        nc = self.nc
        for e in self.ENGS:
            cnt = 0
            for op in self.ops[e]:
                if op.dma is None:
                    op.sem = self.esem[e]
                    if op.need_inc:
                        cnt += 1
                        op.semval = cnt
        final = self.final

        def run(e, h):
            waited = {}
            for op in self.ops[e]:
                need = {}
                for d in op.deps:
                    k = id(d.sem)
                    if k not in need or need[k][1] < d.semval:
                        need[k] = (d.sem, d.semval)
                for k, (s, v) in need.items():
                    if waited.get(k, 0) >= v:
                        continue
                    h.wait_ge(s, v)
                    waited[k] = v
                ins = op.fn(h)
                if op.need_inc:
                    ins.then_inc(op.sem, op.inc)
            if e == "sp":
                need = {}
                for d in final:
                    k = id(d.sem)
                    if k not in need or need[k][1] < d.semval:
                        need[k] = (d.sem, d.semval)
                for k, (s, v) in need.items():
                    h.wait_ge(s, v)

        with nc.Block() as block:
            @block.tensor
            def _(h):
                run("pe", h)

            @block.scalar
            def _(h):
                run("act", h)

            @block.vector
            def _(h):
                run("dve", h)

            @block.gpsimd
            def _(h):
                run("pool", h)

            @block.sync
            def _(h):
                run("sp", h)


class Ctx:
    def __init__(self):
        self.nc = bass.Bass("TRN2", target_bir_lowering=False)
        self.p = Prog(self.nc)
        self.stack = contextlib.ExitStack()

    def sb(self, name, shape, dt):
        return self.stack.enter_context(self.nc.sbuf_tensor(name, shape, dt))

    def psum(self):
        return self.stack.enter_context(self.nc.psum_tensor("ps", [128, 4096], F32))

    def din(self, name, shape, dt=F32):
        return self.nc.dram_tensor(name, list(shape), dt, kind="ExternalInput").ap()

    def dout(self, name, shape, dt=F32):
        return self.nc.dram_tensor(name, list(shape), dt, kind="ExternalOutput").ap()

    def finish(self):
        self.p.emit()
        self.stack.close()
        return self.nc


def token_tiles(ncols, first):
    tiles = []
    c = 0
    if first:
        tiles.append((0, first))
        c = first
    while c < ncols:
        tiles.append((c, min(c + 512, ncols)))
        c = tiles[-1][1]
    return tiles


def emit_rmsnorm(cx, ps, ps_res, x, xres, gain, prm_res, h, hres, ncols, ones, ones_res, scr, extra_ps_res=(), use_ln=False, onesb=None):
    p = cx.p
    tiles = token_tiles(ncols, 0)
    for kc in range(KC):
        sq, sqr = scr["sq"][kc % 2], scr["sqr"][kc % 2]
        lhs = ones
        if onesb is not None:
            sq = sq[:, :].bitcast(BF16)
            lhs = onesb
        p.add("act", lambda e, kc=kc, sq=sq: e.activation(out=sq[:, 0:ncols], in_=x[:, kc, 0:ncols], func=AF.Square),
              reads=xres[kc], writes=[sqr])
        for (c0, c1) in tiles:
            p.add("pe", lambda e, kc=kc, sq=sq, c0=c0, c1=c1, lhs=lhs: e.matmul(
                ps[:, c0:c1], lhsT=lhs[:, :], rhs=sq[:, c0:c1], start=(kc == 0), stop=(kc == KC - 1)),
                reads=[sqr, ones_res], writes=[ps_res] + list(extra_ps_res))
    rstd, rres = scr["rstd"], scr["rstdr"]
    sq, sqr = scr["sq"][0], scr["sqr"][0]
    if use_ln:
        p.add("act", lambda e: e.activation(out=sq[:, 0:ncols], in_=ps[:, 0:ncols], func=AF.Ln,
                                            bias=scr["eps"][:, 0:1], scale=1.0 / D),
              reads=[ps_res, prm_res] + list(extra_ps_res), writes=[sqr])
        p.add("act", lambda e: e.activation(out=rstd[:, 0:ncols], in_=sq[:, 0:ncols], func=AF.Exp, scale=-0.5),
              reads=[sqr], writes=[rres])
    else:
        p.add("act", lambda e: e.activation(out=sq[:, 0:ncols], in_=ps[:, 0:ncols], func=AF.Sqrt,
                                            bias=scr["eps"][:, 0:1], scale=1.0 / D),
              reads=[ps_res, prm_res] + list(extra_ps_res), writes=[sqr])
        p.add("dve", lambda e: e.reciprocal(out=rstd[:, 0:ncols], in_=sq[:, 0:ncols]), reads=[sqr], writes=[rres])
    for kc in range(KC):
        p.add("dve", lambda e, kc=kc: e.scalar_tensor_tensor(
            out=h[:, kc, 0:ncols], in0=x[:, kc, 0:ncols], scalar=gain[:, kc:kc + 1], in1=rstd[:, 0:ncols],
            op0=ALU.mult, op1=ALU.mult), reads=list(xres[kc]) + [rres, prm_res], writes=[hres[kc]])


def build_ffn(rec_prologue):
    cx = Ctx()
    nc, p = cx.nc, cx.p
    NCOL = NT + 2
    xT = cx.din("xT", [D, NCOL])
    w_up = cx.din("w_up", [D, 2 * DFF])
    w_down = cx.din("w_down", [DFF, D])
    prm_d = cx.din("prm", [128, 16 + 288 + 96])
    yT = cx.dout("yT", [D, NT])
    if rec_prologue:
        g1T = cx.din("g1T", [D, NCOL])
        zT = cx.din("zT", [D, NCOL])
        carr = cx.din("carr", [128, 576])
        msk = cx.din("msk", [128, 576])
        w_out = cx.din("w_out", [D, D])

    ps = cx.psum()
    x = cx.sb("x", [128, KC, NCOL], F32)
    h = cx.sb("h", [128, KC, NCOL], BF16)
    sq = [cx.sb(f"sq{i}", [128, NCOL], F32) for i in range(2)]
    rstd = cx.sb("rstd", [128, NCOL], F32)
    prm = cx.sb("prm_sb", [128, 16 + 288 + 96], F32)
    ones = cx.sb("ones", [128, 128], F32)
    onesb = cx.sb("onesb", [128, 128], BF16)
    epsb = cx.sb("epsb", [128, 1], F32)
    wup = [cx.sb(f"wup{i}", [128, KC, 2, 128], BF16) for i in range(3)]
    wdn = [cx.sb(f"wdn{i}", [128, 4, D], BF16) for i in range(2)]
    act = [cx.sb(f"act{i}", [128, NT], BF16) for i in range(8)]
    cg = cx.sb("cg", [128, NCOL], F32)
    cv = cx.sb("cv", [128, NCOL], F32)
    gg = cx.sb("gg", [128, NT], F32)

    xres = [[Res(f"x{m}_{tt}") for tt in range(3)] for m in range(KC)]
    hres = [Res(f"h{k}") for k in range(KC)]
    sqr = [Res("sq0"), Res("sq1")]
    rstdr = Res("rstd")
    prm_r = Res("prm")
    ones_r = Res("ones")
    wupr = [[Res(f"wup{i}_{hf}") for hf in range(2)] for i in range(3)]
    wdnr = [Res(f"wdn{i}") for i in range(2)]
    actr = [Res(f"act{i}") for i in range(8)]
    cgr, cvr, ggr = Res("cg"), Res("cv"), Res("gg")
    psX, psY = Res("psX"), Res("psY")
    bankr = [Res(f"bank{i}") for i in range(2)]
    OX, OY = 0, 1536
    tilesX = token_tiles(NCOL, 0)
    tilesY = tilesX
    BANK = [3072, 3584]

    gain = prm[:, 0:16]
    cw = prm[:, 16:16 + 288]
    cb = prm[:, 304:400]
    scr = {"sq": sq, "sqr": sqr, "rstd": rstd, "rstdr": rstdr, "eps": epsb}

    p.add("pool", lambda e: e.memset(ones[:, :], 1.0), writes=[ones_r])
    p.add("pool", lambda e: e.memset(onesb[:, :], 1.0), writes=[ones_r])
    p.add("pool", lambda e: e.memset(epsb[:, :], EPS), writes=[prm_r])
    p.add("sp", lambda e: e.dma_start(out=prm[:, :], in_=prm_d[:, :]), writes=[prm_r], dma="w")
    xv = xT.rearrange("(kc p) n -> p kc n", p=128)

    def load_x():
        for kc in range(KC):
            p.add("sp", lambda e, kc=kc: e.dma_start(out=x[:, kc, :], in_=xv[:, kc, :]),
                  writes=xres[kc], dma="w")

    if not rec_prologue:
        load_x()

    if rec_prologue:
        cin = cx.sb("cin", [128, 576], F32)
        mk = cx.sb("mk", [128, 576], F32)
        ca = cx.sb("ca", [128, 288], F32)
        ch = cx.sb("ch", [128, 288], F32)
        cs = cx.sb("cs", [128, 2, KC, 9], F32)
        cin_r, mk_r, ca_r, ch_r, cs_r = Res("cin"), Res("mk"), Res("ca"), Res("ch"), Res("cs")
        p.add("sp", lambda e: e.dma_start(out=cin[:, :], in_=carr[:, :]), writes=[cin_r], dma="w")
        p.add("sp", lambda e: e.dma_start(out=mk[:, :], in_=msk[:, :]), writes=[mk_r], dma="w")
        p.add("dve", lambda e: e.scalar_tensor_tensor(out=ca[:, :], in0=cin[:, 0:288], scalar=-1.0, in1=mk[:, 0:288],
                                                      op0=ALU.add, op1=ALU.mult), reads=[cin_r, mk_r], writes=[ca_r])
        p.add("dve", lambda e: e.tensor_tensor(out=ca[:, :], in0=ca[:, :], in1=mk[:, 288:576], op=ALU.add),
              reads=[ca_r, mk_r], writes=[ca_r])
        p.add("dve", lambda e: e.tensor_tensor(out=ch[:, :], in0=cin[:, 288:576], in1=mk[:, 0:288], op=ALU.mult),
              reads=[cin_r, mk_r], writes=[ch_r])
        p.add("dve", lambda e: e.tensor_tensor_scan(out=cs[:, :, :, :].rearrange("p w k r -> p (w k r)"), data0=ca[:, :], data1=ch[:, :],
                                                    initial=0.0, op0=ALU.mult, op1=ALU.add), reads=[ca_r, ch_r], writes=[cs_r])
        gz = [(cg, cgr), (cv, cvr), (gg, ggr), (rstd, rstdr)]
        g1v = g1T.rearrange("(kc p) n -> p kc n", p=128)
        zv = zT.rearrange("(kc p) n -> p kc n", p=128)
        zst = [cg, cv]
        zstr = [cgr, cvr]
        for kc in range(KC):
            gb, gr = sq[kc % 2], sqr[kc % 2]
            zb, zr = zst[kc % 2], zstr[kc % 2]
            p.add("sp", lambda e, kc=kc, gb=gb: e.dma_start(out=gb[:, :], in_=g1v[:, kc, :]), writes=[gr], dma="w")
            p.add("sp", lambda e, kc=kc, zb=zb: e.dma_start(out=zb[:, :], in_=zv[:, kc, :]), writes=[zr], dma="w")
            p.add("dve", lambda e, kc=kc, gb=gb, zb=zb: e.scalar_tensor_tensor(
                out=h[:, kc, 0:2], in0=zb[:, 0:2], scalar=cs[:, 1, kc, 8:9], in1=gb[:, 0:2],
                op0=ALU.mult, op1=ALU.add), reads=[gr, zr, cs_r], writes=[hres[kc]])
            p.add("dve", lambda e, kc=kc, gb=gb, zb=zb: e.scalar_tensor_tensor(
                out=h[:, kc, 2:NCOL], in0=zb[:, 2:NCOL], scalar=cs[:, 0, kc, 8:9], in1=gb[:, 2:NCOL],
                op0=ALU.mult, op1=ALU.add), reads=[gr, zr, cs_r], writes=[hres[kc]])
        load_x()
        wo = [wup[i][:, :, 0, :] for i in range(3)]
        wor = [wupr[i][0] for i in range(3)]
        wov = w_out.rearrange("(kc p) n -> p kc n", p=128)
        for m in range(KC):
            s = m % 3
            p.add("pool", lambda e, m=m, s=s: e.dma_start(out=wo[s], in_=wov[:, :, m * 128:(m + 1) * 128]),
                  writes=[wor[s]], dma="w")
            O, tl, pr = (OX, tilesX, psX) if m % 2 == 0 else (OY, tilesY, psY)
            for kc in range(KC):
                for (c0, c1) in tl:
                    p.add("pe", lambda e, s=s, kc=kc, c0=c0, c1=c1, O=O: e.matmul(
                        ps[:, O + c0:O + c1], lhsT=wo[s][:, kc, :], rhs=h[:, kc, c0:c1],
                        start=(kc == 0), stop=(kc == KC - 1)), reads=[wor[s], hres[kc]], writes=[pr])
            p.add("dve", lambda e, m=m, O=O: e.tensor_tensor(
                out=x[:, m, :], in0=ps[:, O:O + NCOL], in1=x[:, m, :], op=ALU.add),
                reads=[pr] + xres[m], writes=xres[m])

    emit_rmsnorm(cx, ps, psX, x, xres, gain, prm_r, h, hres, NCOL, ones, ones_r, scr, onesb=onesb)

    wupv = w_up.rearrange("(kc p) (two c) -> p kc two c", p=128, two=2)

    def load_up(j):
        s = j % 3
        for half in range(2):
            p.add("pool", lambda e, j=j, s=s, half=half: e.dma_start(
                out=wup[s][:, :, half, :], in_=wupv[:, :, half, j * 128:(j + 1) * 128]),
                writes=[wupr[s][half]], dma="w")

    def load_dn(q):
        s = q % 2
        src = w_down[q * 512:(q + 1) * 512, :].rearrange("(k p) n -> p k n", p=128)
        p.add("pool", lambda e, s=s, src=src: e.dma_start(out=wdn[s][:, :, :], in_=src), writes=[wdnr[s]], dma="w")

    def up_half(j, half, phase):
        s = j % 3
        slot = j % 8
        if True:
            O, tl, pr = (OX, tilesX, psX) if half == 0 else (OY, tilesY, psY)
            ch = half * 48 + j
            for kc in (range(KC) if phase == 0 else ()):
                for (c0, c1) in tl:
                    p.add("pe", lambda e, s=s, kc=kc, half=half, c0=c0, c1=c1, O=O: e.matmul(
                        ps[:, O + c0:O + c1], lhsT=wup[s][:, kc, half, :], rhs=h[:, kc, c0:c1],
                        start=(kc == 0), stop=(kc == KC - 1)), reads=[wupr[s][half], hres[kc]], writes=[pr])
            if phase == 0:
                return
            c, cr = (cg, cgr) if half == 0 else (cv, cvr)
            P = ps[:, O:O + NCOL]
            p.add("act", lambda e, c=c, P=P, ch=ch: e.activation(
                out=c[:, 0:NT], in_=P[:, 2:2 + NT], func=AF.Identity,
                bias=cb[:, ch:ch + 1], scale=cw[:, 192 + ch:192 + ch + 1]), reads=[pr, prm_r], writes=[cr])
            p.add("dve", lambda e, c=c, P=P, ch=ch: e.scalar_tensor_tensor(
                out=c[:, 0:NT], in0=P[:, 1:1 + NT], scalar=cw[:, 96 + ch:96 + ch + 1], in1=c[:, 0:NT],
                op0=ALU.mult, op1=ALU.add), reads=[pr, prm_r, cr], writes=[cr])
            p.add("dve", lambda e, c=c, P=P, ch=ch: e.scalar_tensor_tensor(
                out=c[:, 0:NT], in0=P[:, 0:NT], scalar=cw[:, ch:ch + 1], in1=c[:, 0:NT],
                op0=ALU.mult, op1=ALU.add), reads=[pr, prm_r, cr], writes=[cr])
            if half == 0:
                p.add("act", lambda e: e.activation(out=gg[:, :], in_=cg[:, 0:NT], func=AF.Gelu_apprx_tanh),
                      reads=[cgr], writes=[ggr])
        if half == 1:
            p.add("pool", lambda e, slot=slot: e.tensor_tensor(out=act[slot][:, :], in0=gg[:, :], in1=cv[:, 0:NT], op=ALU.mult),
                  reads=[ggr, cvr], writes=[actr[slot]])

    def down_part(q, part, last):
        s = q % 2
        for m in range(part * 2, part * 2 + 2):
            for tt in range(2):
                b = (m * 2 + tt) % 2
                for k in range(4):
                    slot = (q * 4 + k) % 8
                    p.add("pe", lambda e, s=s, k=k, m=m, tt=tt, b=b, slot=slot: e.matmul(
                        ps[:, BANK[b]:BANK[b] + 512], lhsT=wdn[s][:, k, m * 128:(m + 1) * 128],
                        rhs=act[slot][:, tt * 512:(tt + 1) * 512], start=(k == 0), stop=(k == 3)),
                        reads=[wdnr[s], actr[slot]], writes=[bankr[b]])
                p.add("dve", lambda e, m=m, tt=tt, b=b: e.tensor_tensor(
                    out=x[:, m, 2 + tt * 512:2 + (tt + 1) * 512], in0=ps[:, BANK[b]:BANK[b] + 512],
                    in1=x[:, m, 2 + tt * 512:2 + (tt + 1) * 512], op=ALU.add),
                    reads=[bankr[b], xres[m][1 + tt]], writes=[xres[m][1 + tt]])
            if last:
                yv = yT.rearrange("(kc p) n -> p kc n", p=128)
                p.add("sp", lambda e, m=m: e.dma_start(out=yv[:, m, :], in_=x[:, m, 2:2 + NT]),
                      reads=[xres[m][1], xres[m][2]], dma="r")

    NP = 48
    load_up(0)
    load_up(1)
    load_dn(0)
    for j in range(NP + 4):
        q = j // 4 - 1
        for half in range(2):
            if j < NP:
                if half == 0 and j + 2 < NP:
                    load_up(j + 2)
                up_half(j, half, 0)
            if q >= 0:
                if j % 4 == 0 and half == 0 and q + 1 < NP // 4:
                    load_dn(q + 1)
                down_part(q, (j % 4) * 2 + half, last=(q == NP // 4 - 1))
            if j < NP:
                up_half(j, half, 1)
    return cx.finish()


def chunked(v):
    v = np.asarray(v, np.float32)
    return np.ascontiguousarray(v.reshape(-1, 128).T)


def ffn_params(g, cw, cb):
    parts = [chunked(g)] + [chunked(cw[k]) for k in range(3)] + [chunked(cb)]
    return np.ascontiguousarray(np.concatenate(parts, axis=1))


def build_rec1():
    cx = Ctx()
    nc, p = cx.nc, cx.p
    HALO = 3
    NCOL = NT + HALO
    xT = cx.din("xT", [D, NCOL])
    w_in = cx.din("w_in", [D, 2 * D])
    w_a = cx.din("w_a", [8, 256, 256])
    w_i = cx.din("w_i", [8, 256, 256])
    NPRM = 16 + 64 + 16 * 4
    prm_d = cx.din("prm", [128, NPRM])
    g1T = cx.dout("g1T", [D, NT])
    zT = cx.dout("zT", [D, NT])
    carr_o = cx.dout("carr", [128, 32])

    ps = cx.psum()
    x = cx.sb("x", [128, KC, NCOL], F32)
    h = cx.sb("h", [128, KC, NCOL], BF16)
    sq = [cx.sb(f"sq{i}", [128, NCOL], F32) for i in range(2)]
    rstd = cx.sb("rstd", [128, NCOL], F32)
    prm = cx.sb("prm_sb", [128, NPRM], F32)
    ones = cx.sb("ones", [128, 128], F32)
    epsb = cx.sb("epsb", [128, 1], F32)
    win = [cx.sb(f"win{i}", [128, KC, 128], BF16) for i in range(4)]
    wg = [[cx.sb(f"wg{i}_{j}", [128, 2, 256], BF16) for j in range(2)] for i in range(2)]
    xc = [cx.sb(f"xc{i}", [128, NT], F32) for i in range(2)]
    xcb = [cx.sb(f"xcb{i}", [128, NT], BF16) for i in range(2)]
    gate = [cx.sb(f"gate{i}", [128, NT], F32) for i in range(2)]
    tr = cx.sb("tr", [128, NT], F32)
    ta = cx.sb("ta", [128, NT], F32)
    tm = cx.sb("tm", [128, NT], F32)
    ti = cx.sb("ti", [128, NT], F32)
    ths = cx.sb("ths", [128, NT], F32)
    tac = cx.sb("tac", [128, NT], F32)
    tg1 = cx.sb("tg1", [128, NT], F32)
    tz = cx.sb("tz", [128, NT], F32)
    zeros = cx.sb("zeros", [128, NT], F32)
    cl = cx.sb("cl", [128, 48], F32)
    carr = cx.sb("carr_sb", [128, 32], F32)

    xres = [[Res(f"x{m}")] for m in range(KC)]
    hres = [Res(f"h{k}") for k in range(KC)]
    sqr = [Res("sq0"), Res("sq1")]
    rstdr, prm_r, ones_r = Res("rstd"), Res("prm"), Res("ones")
    winr = [Res(f"win{i}") for i in range(4)]
    wgr = [[Res(f"wg{i}_{j}") for j in range(2)] for i in range(2)]
    xcr = [Res("xc0"), Res("xc1")]
    xcbr = [Res("xcb0"), Res("xcb1")]
    gater = [Res("gate0"), Res("gate1")]
    trr, tar, tmr, tir, thsr, tacr, tg1r, tzr = (Res(n) for n in ("tr", "ta", "tm", "ti", "ths", "tac", "tg1", "tz"))
    zer_r, cl_r, carr_r = Res("zeros"), Res("cl"), Res("carr")
    psX, psY = Res("psX"), Res("psY")
    bankr = [Res(f"bank{i}") for i in range(3)]
    BANK = [2560, 3072, 3584]
    tilesX = token_tiles(NCOL, 0)
    OY = 1536

    gain = prm[:, 0:16]
    cw = prm[:, 16:80]
    cb = prm[:, 80:96]
    ba = prm[:, 96:112]
    bi = prm[:, 112:128]
    lam = prm[:, 128:144]
    scr = {"sq": sq, "sqr": sqr, "rstd": rstd, "rstdr": rstdr, "eps": epsb}

    p.add("pool", lambda e: e.memset(ones[:, :], 1.0), writes=[ones_r])
    p.add("pool", lambda e: e.memset(epsb[:, :], EPS), writes=[prm_r])
    p.add("pool", lambda e: e.memset(zeros[:, :], 0.0), writes=[zer_r])
    p.add("sp", lambda e: e.dma_start(out=prm[:, :], in_=prm_d[:, :]), writes=[prm_r], dma="w")
    xv = xT.rearrange("(kc p) n -> p kc n", p=128)
    for kc in range(KC):
        p.add("sp", lambda e, kc=kc: e.dma_start(out=x[:, kc, :], in_=xv[:, kc, :]), writes=xres[kc], dma="w")

    winv = w_in.rearrange("(kc p) n -> p kc n", p=128)
    nload = [0]

    def load_in(col0):
        s = nload[0] % 4
        nload[0] += 1
        p.add("pool", lambda e, s=s, col0=col0: e.dma_start(out=win[s][:, :, :], in_=winv[:, :, col0:col0 + 128]),
              writes=[winr[s]], dma="w")
        return s

    def load_g(b):
        s = b % 2
        for j, wsrc in enumerate((w_a, w_i)):
            p.add("pool", lambda e, s=s, j=j, wsrc=wsrc, b=b: e.dma_start(
                out=wg[s][j][:, :, :], in_=wsrc[b].rearrange("(ic p) n -> p ic n", p=128)),
                writes=[wgr[s][j]], dma="w")

    p.add("act", lambda e: e.activation(out=cl[:, 0:16], in_=lam, func=AF.Exp, scale=-1.0), reads=[prm_r], writes=[cl_r])
    p.add("act", lambda e: e.activation(out=cl[:, 0:16], in_=cl[:, 0:16], func=AF.Ln, bias=1.0), reads=[cl_r], writes=[cl_r])
    p.add("dve", lambda e: e.tensor_scalar(out=cl[:, 16:32], in0=cl[:, 0:16], scalar1=-8.0, scalar2=None, op0=ALU.mult),
          reads=[cl_r], writes=[cl_r])
    p.add("dve", lambda e: e.tensor_scalar(out=cl[:, 32:48], in0=cl[:, 0:16], scalar1=-16.0, scalar2=None, op0=ALU.mult),
          reads=[cl_r], writes=[cl_r])

    emit_rmsnorm(cx, ps, psX, x, xres, gain, prm_r, h, hres, NCOL, ones, ones_r, scr)

    pending = [load_in(0), load_in(D)]
    load_g(0)
    order = []
    for b in range(8):
        for c in range(2):
            ch = 2 * b + c
            order.append(ch * 128)
            order.append(D + ch * 128)
    li = 2
    for b in range(8):
        if b + 1 < 8:
            load_g(b + 1)
        for c in range(2):
            ch = 2 * b + c
            s = pending.pop(0)
            if li < len(order):
                pending.append(load_in(order[li])); li += 1
            for kc in range(KC):
                for (c0, c1) in tilesX:
                    p.add("pe", lambda e, s=s, kc=kc, c0=c0, c1=c1: e.matmul(
                        ps[:, c0:c1], lhsT=win[s][:, kc, :], rhs=h[:, kc, c0:c1],
                        start=(kc == 0), stop=(kc == KC - 1)), reads=[winr[s], hres[kc]], writes=[psX])
            P = ps[:, 0:NCOL]
            t = xc[c]
            p.add("act", lambda e, t=t, P=P, ch=ch: e.activation(
                out=t[:, :], in_=P[:, 3:3 + NT], func=AF.Identity, bias=cb[:, ch:ch + 1],
                scale=cw[:, 48 + ch:48 + ch + 1]), reads=[psX, prm_r], writes=[xcr[c]])
            for k in (2, 1, 0):
                p.add("dve", lambda e, t=t, P=P, ch=ch, k=k: e.scalar_tensor_tensor(
                    out=t[:, :], in0=P[:, k:k + NT], scalar=cw[:, k * 16 + ch:k * 16 + ch + 1], in1=t[:, :],
                    op0=ALU.mult, op1=ALU.add), reads=[psX, prm_r, xcr[c]], writes=[xcr[c]])
            p.add("pool", lambda e, c=c: e.tensor_copy(out=xcb[c][:, :], in_=xc[c][:, :]), reads=[xcr[c]], writes=[xcbr[c]])
            s = pending.pop(0)
            if li < len(order):
                pending.append(load_in(order[li])); li += 1
            for kc in range(KC):
                for tt in range(2):
                    p.add("pe", lambda e, s=s, kc=kc, tt=tt: e.matmul(
                        ps[:, OY + tt * 512:OY + (tt + 1) * 512], lhsT=win[s][:, kc, :],
                        rhs=h[:, kc, HALO + tt * 512:HALO + (tt + 1) * 512],
                        start=(kc == 0), stop=(kc == KC - 1)), reads=[winr[s], hres[kc]], writes=[psY])
            p.add("act", lambda e, c=c: e.activation(out=gate[c][:, :], in_=ps[:, OY:OY + NT], func=AF.Gelu_apprx_tanh),
                  reads=[psY], writes=[gater[c]])
        gs = b % 2
        for oc in range(2):
            ch = 2 * b + oc
            for j in range(2):
                dst, dres = (tr, trr) if j == 0 else (ti, tir)
                bias = ba if j == 0 else bi
                for tt in range(2):
                    bk = (oc * 4 + j * 2 + tt) % 3
                    for ic in range(2):
                        p.add("pe", lambda e, gs=gs, j=j, ic=ic, oc=oc, tt=tt, bk=bk: e.matmul(
                            ps[:, BANK[bk]:BANK[bk] + 512], lhsT=wg[gs][j][:, ic, oc * 128:(oc + 1) * 128],
                            rhs=xcb[ic][:, tt * 512:(tt + 1) * 512], start=(ic == 0), stop=(ic == 1)),
                            reads=[wgr[gs][j], xcbr[ic]], writes=[bankr[bk]])
                    p.add("act", lambda e, dst=dst, bias=bias, ch=ch, tt=tt, bk=bk: e.activation(
                        out=dst[:, tt * 512:(tt + 1) * 512], in_=ps[:, BANK[bk]:BANK[bk] + 512], func=AF.Sigmoid,
                        bias=bias[:, ch:ch + 1]), reads=[bankr[bk], prm_r], writes=[dres])
            p.add("act", lambda e, ch=ch: e.activation(out=ta[:, :], in_=tr[:, :], func=AF.Exp, scale=cl[:, 16 + ch:17 + ch]),
                  reads=[trr, cl_r], writes=[tar])
            p.add("act", lambda e, ch=ch: e.activation(out=tm[:, :], in_=tr[:, :], func=AF.Exp, scale=cl[:, 32 + ch:33 + ch]),
                  reads=[trr, cl_r], writes=[tmr])
            p.add("act", lambda e: e.activation(out=tm[:, :], in_=tm[:, :], func=AF.Sqrt, scale=-1.0, bias=1.0),
                  reads=[tmr], writes=[tmr])
            p.add("dve", lambda e, oc=oc: e.tensor_tensor(out=ti[:, :], in0=ti[:, :], in1=xc[oc][:, :], op=ALU.mult),
                  reads=[tir, xcr[oc]], writes=[tir])
            p.add("dve", lambda e: e.tensor_tensor(out=ti[:, :], in0=ti[:, :], in1=tm[:, :], op=ALU.mult),
                  reads=[tir, tmr], writes=[tir])
            p.add("dve", lambda e: e.tensor_tensor_scan(out=ths[:, :], data0=ta[:, :], data1=ti[:, :], initial=0.0,
                                                        op0=ALU.mult, op1=ALU.add), reads=[tar, tir], writes=[thsr])
            p.add("dve", lambda e: e.tensor_tensor_scan(out=tac[:, :], data0=ta[:, :], data1=zeros[:, :], initial=1.0,
                                                        op0=ALU.mult, op1=ALU.add), reads=[tar, zer_r], writes=[tacr])
            p.add("pool", lambda e, oc=oc: e.tensor_tensor(out=tg1[:, :], in0=ths[:, :], in1=gate[oc][:, :], op=ALU.mult),
                  reads=[thsr, gater[oc]], writes=[tg1r])
            p.add("pool", lambda e, oc=oc: e.tensor_tensor(out=tz[:, :], in0=tac[:, :], in1=gate[oc][:, :], op=ALU.mult),
                  reads=[tacr, gater[oc]], writes=[tzr])
            p.add("dve", lambda e, ch=ch: e.tensor_copy(out=carr[:, ch:ch + 1], in_=tac[:, NT - 1:NT]), reads=[tacr], writes=[carr_r])
            p.add("dve", lambda e, ch=ch: e.tensor_copy(out=carr[:, 16 + ch:17 + ch], in_=ths[:, NT - 1:NT]), reads=[thsr], writes=[carr_r])
            p.add("sp", lambda e, ch=ch: e.dma_start(out=g1T[ch * 128:(ch + 1) * 128, :], in_=tg1[:, :]), reads=[tg1r], dma="r")
            p.add("sp", lambda e, ch=ch: e.dma_start(out=zT[ch * 128:(ch + 1) * 128, :], in_=tz[:, :]), reads=[tzr], dma="r")
    p.add("sp", lambda e: e.dma_start(out=carr_o[:, :], in_=carr[:, :]), reads=[carr_r], dma="r")
    return cx.finish()


def build_rec1b():
    cx = Ctx()
    nc, p = cx.nc, cx.p
    HALO = 3
    NCOL = NT + HALO
    xT = cx.din("xT", [D, NCOL])
    w_in = cx.din("w_in", [D, 2 * D])
    w_a = cx.din("w_a", [8, 256, 256])
    w_i = cx.din("w_i", [8, 256, 256])
    NPRM = 16 + 64 + 16 * 4
    prm_d = cx.din("prm", [128, NPRM])
    g1T = cx.dout("g1T", [D, NT])
    zT = cx.dout("zT", [D, NT])
    carr_o = cx.dout("carr", [128, 32])

    ps = cx.psum()
    x = cx.sb("x", [128, KC, NCOL], F32)
    h = cx.sb("h", [128, KC, NCOL], BF16)
    sq = [cx.sb(f"sq{i}", [128, NCOL], F32) for i in range(2)]
    rstd = cx.sb("rstd", [128, NCOL], F32)
    prm = cx.sb("prm_sb", [128, NPRM], F32)
    ones = cx.sb("ones", [128, 128], F32)
    onesb = cx.sb("onesb", [128, 128], BF16)
    epsb = cx.sb("epsb", [128, 1], F32)
    win = [cx.sb(f"win{i}", [128, KC, 128], BF16) for i in range(3)]
    wg = [[cx.sb(f"wg{i}_{j}", [128, 2, 256], BF16) for j in range(2)] for i in range(2)]
    xc = [[cx.sb(f"xc{i}_{c}", [128, NT], F32) for c in range(2)] for i in range(2)]
    xcb = [[cx.sb(f"xcb{i}_{c}", [128, NT], BF16) for c in range(2)] for i in range(2)]
    gate = [[cx.sb(f"gate{i}_{c}", [128, NT], F32) for c in range(2)] for i in range(2)]
    tr = [cx.sb(f"tr{o}", [128, NT], F32) for o in range(2)]
    ti = [cx.sb(f"ti{o}", [128, NT], F32) for o in range(2)]
    ta = [cx.sb(f"ta{o}", [128, NT], F32) for o in range(2)]
    ths = cx.sb("ths", [128, NT], F32)
    tac = cx.sb("tac", [128, NT], F32)
    tg1 = sq[0][:, 0:NT]
    tz = sq[1][:, 0:NT]
    zeros = rstd[:, 0:NT]
    cl = cx.sb("cl", [128, 48], F32)
    carr = cx.sb("carr_sb", [128, 32], F32)

    xres = [[Res(f"x{m}")] for m in range(KC)]
    hres = [Res(f"h{k}") for k in range(KC)]
    sqr = [Res("sq0"), Res("sq1")]
    rstdr, prm_r, ones_r = Res("rstd"), Res("prm"), Res("ones")
    winr = [Res(f"win{i}") for i in range(3)]
    wgr = [[Res(f"wg{i}_{j}") for j in range(2)] for i in range(2)]
    xcr = [[Res(f"xc{i}_{c}") for c in range(2)] for i in range(2)]
    xcbr = [[Res(f"xcb{i}_{c}") for c in range(2)] for i in range(2)]
    gater = [[Res(f"gate{i}_{c}") for c in range(2)] for i in range(2)]
    trr = [Res("tr0"), Res("tr1")]
    tir = [Res("ti0"), Res("ti1")]
    tar = [Res("ta0"), Res("ta1")]
    thsr, tacr = Res("ths"), Res("tac")
    tg1r, tzr = sqr[0], sqr[1]
    zer_r, cl_r, carr_r = rstdr, Res("cl"), Res("carr")
    _px = Res("psX")
    psXr = [_px, _px]
    OXs = [0, 0]
    bankr = [Res("bank5"), Res("bank6"), Res("bank7")]
    psY = [Res("psY0"), Res("psY1")]
    BANK = [2560, 3072, 3584]
    tilesX = token_tiles(NCOL, 0)
    OY = 1536

    gain = prm[:, 0:16]
    cw = prm[:, 16:80]
    cb = prm[:, 80:96]
    ba = prm[:, 96:112]
    bi = prm[:, 112:128]
    lam = prm[:, 128:144]
    scr = {"sq": sq, "sqr": sqr, "rstd": rstd, "rstdr": rstdr, "eps": epsb}

    p.add("pool", lambda e: e.memset(ones[:, :], 1.0), writes=[ones_r])
    p.add("pool", lambda e: e.memset(onesb[:, :], 1.0), writes=[ones_r])
    p.add("pool", lambda e: e.memset(epsb[:, :], EPS), writes=[prm_r])
    p.add("sp", lambda e: e.dma_start(out=prm[:, :], in_=prm_d[:, :]), writes=[prm_r], dma="w")
    xv = xT.rearrange("(kc p) n -> p kc n", p=128)
    for kc in range(KC):
        p.add("sp", lambda e, kc=kc: e.dma_start(out=x[:, kc, :], in_=xv[:, kc, :]), writes=xres[kc], dma="w")

    winv = w_in.rearrange("(kc p) n -> p kc n", p=128)
    nload = [0]

    def load_in(col0):
        s = nload[0] % 3
        nload[0] += 1
        p.add("pool", lambda e, s=s, col0=col0: e.dma_start(out=win[s][:, :, :], in_=winv[:, :, col0:col0 + 128]),
              writes=[winr[s]], dma="w")
        return s

    def load_g(b):
        s = b % 2
        for j, wsrc in enumerate((w_a, w_i)):
            p.add("pool", lambda e, s=s, j=j, wsrc=wsrc, b=b: e.dma_start(
                out=wg[s][j][:, :, :], in_=wsrc[b].rearrange("(ic p) n -> p ic n", p=128)),
                writes=[wgr[s][j]], dma="w")

    p.add("act", lambda e: e.activation(out=cl[:, 0:16], in_=lam, func=AF.Exp, scale=-1.0), reads=[prm_r], writes=[cl_r])
    p.add("act", lambda e: e.activation(out=cl[:, 0:16], in_=cl[:, 0:16], func=AF.Ln, bias=1.0), reads=[cl_r], writes=[cl_r])
    p.add("dve", lambda e: e.tensor_scalar(out=cl[:, 16:32], in0=cl[:, 0:16], scalar1=-8.0, scalar2=None, op0=ALU.mult),
          reads=[cl_r], writes=[cl_r])
    p.add("dve", lambda e: e.tensor_scalar(out=cl[:, 32:48], in0=cl[:, 0:16], scalar1=-16.0, scalar2=None, op0=ALU.mult),
          reads=[cl_r], writes=[cl_r])

    emit_rmsnorm(cx, ps, psXr[0], x, xres, gain, prm_r, h, hres, NCOL, ones, ones_r, scr, onesb=onesb)

    p.add("pool", lambda e: e.memset(zeros, 0.0), reads=list(hres), writes=[zer_r])

    order = []
    for b in range(8):
        for c in range(2):
            ch = 2 * b + c
            order.append(ch * 128)
            order.append(D + ch * 128)
    pending = [load_in(order[0]), load_in(order[1])]
    li = [2]
    load_g(0)

    def nextw():
        s = pending.pop(0)
        if li[0] < len(order):
            pending.append(load_in(order[li[0]]))
            li[0] += 1
        return s

    def front(b):
        bp = b % 2
        for c in range(2):
            ch = 2 * b + c
            s = nextw()
            for kc in range(KC):
                for (c0, c1) in tilesX:
                    p.add("pe", lambda e, s=s, kc=kc, c0=c0, c1=c1, c=c: e.matmul(
                        ps[:, OXs[c] + c0:OXs[c] + c1], lhsT=win[s][:, kc, :], rhs=h[:, kc, c0:c1],
                        start=(kc == 0), stop=(kc == KC - 1)), reads=[winr[s], hres[kc]], writes=[psXr[c]])
            psX = psXr[c]
            P = ps[:, OXs[c]:OXs[c] + NCOL]
            t = xc[bp][c]
            tres = xcr[bp][c]
            p.add("act", lambda e, t=t, P=P, ch=ch: e.activation(
                out=t[:, :], in_=P[:, 3:3 + NT], func=AF.Identity, bias=cb[:, ch:ch + 1],
                scale=cw[:, 48 + ch:48 + ch + 1]), reads=[psX, prm_r], writes=[tres])
            for k in (2, 1, 0):
                p.add("dve", lambda e, t=t, P=P, ch=ch, k=k: e.scalar_tensor_tensor(
                    out=t[:, :], in0=P[:, k:k + NT], scalar=cw[:, k * 16 + ch:k * 16 + ch + 1], in1=t[:, :],
                    op0=ALU.mult, op1=ALU.add), reads=[psX, prm_r, tres], writes=[tres])
            p.add("act", lambda e, t=t, bp=bp, c=c: e.activation(out=xcb[bp][c][:, :], in_=t[:, :], func=AF.Identity),
                  reads=[tres], writes=[xcbr[bp][c]])
            s = nextw()
            for kc in range(KC):
                for tt in range(2):
                    p.add("pe", lambda e, s=s, kc=kc, tt=tt: e.matmul(
                        ps[:, OY + tt * 512:OY + (tt + 1) * 512], lhsT=win[s][:, kc, :],
                        rhs=h[:, kc, HALO + tt * 512:HALO + (tt + 1) * 512],
                        start=(kc == 0), stop=(kc == KC - 1)), reads=[winr[s], hres[kc]], writes=[psY[tt]])
            p.add("act", lambda e, bp=bp, c=c: e.activation(out=gate[bp][c][:, :], in_=ps[:, OY:OY + NT], func=AF.Gelu_apprx_tanh),
                  reads=psY, writes=[gater[bp][c]])

    def back(b):
        bp = b % 2
        gs = b % 2
        for oc in range(2):
            ch = 2 * b + oc
            for j in range(2):
                dst, dres = (tr[oc], trr[oc]) if j == 0 else (ti[oc], tir[oc])
                bias = ba if j == 0 else bi
                for tt in range(2):
                    bkk = (oc * 4 + j * 2 + tt) % 3
                    for ic in range(2):
                        p.add("pe", lambda e, j=j, ic=ic, oc=oc, tt=tt, bkk=bkk: e.matmul(
                            ps[:, BANK[bkk]:BANK[bkk] + 512], lhsT=wg[gs][j][:, ic, oc * 128:(oc + 1) * 128],
                            rhs=xcb[bp][ic][:, tt * 512:(tt + 1) * 512], start=(ic == 0), stop=(ic == 1)),
                            reads=[wgr[gs][j], xcbr[bp][ic]], writes=[bankr[bkk]])
                    p.add("act", lambda e, dst=dst, bias=bias, ch=ch, tt=tt, bkk=bkk: e.activation(
                        out=dst[:, tt * 512:(tt + 1) * 512], in_=ps[:, BANK[bkk]:BANK[bkk] + 512], func=AF.Sigmoid,
                        bias=bias[:, ch:ch + 1]), reads=[bankr[bkk], prm_r], writes=[dres], relaxed=True)
        for oc in range(2):
            ch = 2 * b + oc
            p.add("act", lambda e, ch=ch, oc=oc: e.activation(out=ta[oc][:, :], in_=tr[oc][:, :], func=AF.Exp,
                                                             scale=cl[:, 16 + ch:17 + ch]), reads=[trr[oc], cl_r], writes=[tar[oc]])
            p.add("act", lambda e, ch=ch, oc=oc: e.activation(out=tr[oc][:, :], in_=tr[oc][:, :], func=AF.Exp,
                                                             scale=cl[:, 32 + ch:33 + ch]), reads=[trr[oc], cl_r], writes=[trr[oc]])
        for oc in range(2):
            p.add("act", lambda e, oc=oc: e.activation(out=tr[oc][:, :], in_=tr[oc][:, :], func=AF.Sqrt, scale=-1.0, bias=1.0),
                  reads=[trr[oc]], writes=[trr[oc]])
        for oc in range(2):
            ch = 2 * b + oc
            p.add("dve", lambda e, oc=oc: e.tensor_tensor(out=ti[oc][:, :], in0=ti[oc][:, :], in1=xc[bp][oc][:, :], op=ALU.mult),
                  reads=[tir[oc], xcr[bp][oc]], writes=[tir[oc]])
            p.add("dve", lambda e, oc=oc: e.tensor_tensor(out=ti[oc][:, :], in0=ti[oc][:, :], in1=tr[oc][:, :], op=ALU.mult),
                  reads=[tir[oc], trr[oc]], writes=[tir[oc]])
            p.add("dve", lambda e, oc=oc: e.tensor_tensor_scan(out=ths[:, :], data0=ta[oc][:, :], data1=ti[oc][:, :], initial=0.0,
                                                               op0=ALU.mult, op1=ALU.add), reads=[tar[oc], tir[oc]], writes=[thsr])
            p.add("dve", lambda e, oc=oc: e.tensor_tensor_scan(out=tac[:, :], data0=ta[oc][:, :], data1=zeros, initial=1.0,
                                                               op0=ALU.mult, op1=ALU.add), reads=[tar[oc], zer_r], writes=[tacr])
            p.add("dve", lambda e, oc=oc: e.tensor_tensor(out=tg1, in0=ths[:, :], in1=gate[bp][oc][:, :], op=ALU.mult),
                  reads=[thsr, gater[bp][oc]], writes=[tg1r])
            p.add("dve", lambda e, oc=oc: e.tensor_tensor(out=tz, in0=tac[:, :], in1=gate[bp][oc][:, :], op=ALU.mult),
                  reads=[tacr, gater[bp][oc]], writes=[tzr])
            p.add("dve", lambda e, ch=ch: e.tensor_copy(out=carr[:, ch:ch + 1], in_=tac[:, NT - 1:NT]), reads=[tacr], writes=[carr_r])
            p.add("dve", lambda e, ch=ch: e.tensor_copy(out=carr[:, 16 + ch:17 + ch], in_=ths[:, NT - 1:NT]), reads=[thsr], writes=[carr_r])
            p.add("sp", lambda e, ch=ch: e.dma_start(out=g1T[ch * 128:(ch + 1) * 128, :], in_=tg1), reads=[tg1r], dma="r")
            p.add("sp", lambda e, ch=ch: e.dma_start(out=zT[ch * 128:(ch + 1) * 128, :], in_=tz), reads=[tzr], dma="r")

    load_g(1)
    for b in range(9):
        if b < 8:
            front(b)
        if b >= 1:
            back(b - 1)
            if b + 1 < 8:
                load_g(b + 1)
    p.add("sp", lambda e: e.dma_start(out=carr_o[:, :], in_=carr[:, :]), reads=[carr_r], dma="r")
    return cx.finish()


def rec_params(g, cw, cb, ba, bi, lam):
    parts = [chunked(g)] + [chunked(cw[k]) for k in range(4)] + [chunked(cb), chunked(ba.reshape(-1)), chunked(bi.reshape(-1)), chunked(lam)]
    return np.ascontiguousarray(np.concatenate(parts, axis=1))


def build_attn(stage=99):
    cx = Ctx()
    nc, p = cx.nc, cx.p
    HALO = 128
    NCOL = NT + HALO
    NTB = NCOL // 128
    xT = cx.din("xT", [D, NCOL])
    w_qkv = cx.din("w_qkv", [D, 3072])
    w_o = cx.din("w_o", [D, D])
    NPRM = 16 + 2 + 32 + 1
    prm_d = cx.din("prm", [128, NPRM])
    msk_d = cx.din("msk", [128, 1024])
    yT = cx.dout("yT", [D, NT])

    ps = cx.psum()
    x = cx.sb("x", [128, KC, NCOL], F32)
    h = cx.sb("h", [128, KC, NCOL], BF16)
    sq = [cx.sb(f"sq{i}", [128, NCOL], F32) for i in range(2)]
    rstd = cx.sb("rstd", [128, NCOL], F32)
    prm = cx.sb("prm_sb", [128, NPRM], F32)
    es = cx.sb("es", [128, 32], F32)
    ones = cx.sb("ones", [128, 128], F32)
    bd = cx.sb("bd", [128, 128], F32)
    epsb = cx.sb("epsb", [128, 1], F32)
    msk = cx.sb("msk_sb", [128, 2, 512], BF16)
    Kd = cx.sb("Kd", [128, 8, NCOL], BF16)
    Vaug = cx.sb("Vaug", [128, NTB, 8, 128], BF16)
    wbig = cx.sb("wbig", [128, 8192], BF16)
    wq = [cx.sb(f"wq{i}", [128, KC, 128], BF16) for i in range(2)]
    Qn = cx.sb("Qn", [128, 4, NT], BF16)
    AO = cx.sb("AO", [128, 2, NT], BF16)
    PT = [cx.sb(f"PT{i}", [128, 2, 512], BF16) for i in range(2)]

    xres = [[Res(f"x{m}_h"), Res(f"x{m}_0"), Res(f"x{m}_1")] for m in range(KC)]
    hres = [Res(f"h{k}") for k in range(KC)]
    sqr = [Res("sq0"), Res("sq1")]
    rstdr, prm_r, ones_r, bd_r, es_r, msk_r = (Res(n) for n in ("rstd", "prm", "ones", "bd", "es", "msk"))
    Kdr = [Res(f"Kd{g}") for g in range(8)]
    Vr = Res("Vaug")
    wbr = [Res("wb0"), Res("wb1")]
    wqr = [[Res(f"wq{i}_0"), Res(f"wq{i}_1")] for i in range(2)]
    Qnr = [Res(f"Qn{i}") for i in range(4)]
    AOr = [Res("AO0"), Res("AO1")]
    PTr = [[Res(f"PT{i}_{k}") for k in range(2)] for i in range(2)]
    sqt = [sq[0][:, 0:512], sq[0][:, 512:1024]]
    rsb = [sq[1][:, 0:512], sq[1][:, 512:1024]]
    den = rstd[:, 0:512]
    rec = rstd[:, 512:1024]
    sqtr = [Res("sqt0"), Res("sqt1")]
    rsr = [Res("rs0"), Res("rs1")]
    denr, recr = Res("den"), Res("rec")
    regr = [Res("regA"), Res("regB")]
    REG = [0, 1536]
    statr = Res("stat")
    STAT = 3072
    pvr = Res("pv")
    PVB = 3584
    SC = [1024, 2560]
    scr_ = [Res("sc0"), Res("sc1")]
    OB = [0, 512, 1536, 2048]
    obr = [Res(f"ob{i}") for i in range(4)]

    gain = prm[:, 0:16]
    qg = prm[:, 16:17]
    kg = prm[:, 17:18]
    hb = prm[:, 50:51]
    scr = {"sq": sq, "sqr": sqr, "rstd": rstd, "rstdr": rstdr, "eps": epsb}

    p.add("pool", lambda e: e.memset(ones[:, :], 1.0), writes=[ones_r])
    p.add("pool", lambda e: e.memset(epsb[:, :], EPS), writes=[prm_r])
    p.add("pool", lambda e: e.memset(bd[:, :], 0.0), writes=[bd_r])
    p.add("pool", lambda e: e.memset(bd[0:64, 0:64], 1.0), writes=[bd_r])
    p.add("pool", lambda e: e.memset(bd[64:128, 64:128], 1.0), writes=[bd_r])
    p.add("pool", lambda e: e.memset(Vaug[:, :, :, :], 1.0), writes=[Vr])
    p.add("sp", lambda e: e.dma_start(out=prm[:, :], in_=prm_d[:, :]), writes=[prm_r], dma="w")
    p.add("pool", lambda e: e.dma_start(out=msk[:, :, :], in_=msk_d.rearrange("p (k n) -> p k n", k=2)),
          writes=[msk_r], dma="w")
    xv = xT.rearrange("(kc p) n -> p kc n", p=128)
    for kc in range(KC):
        p.add("sp", lambda e, kc=kc: e.dma_start(out=x[:, kc, :], in_=xv[:, kc, :]), writes=xres[kc], dma="w")
    p.add("act", lambda e: e.activation(out=es[:, :], in_=prm[:, 18:50], func=AF.Exp), reads=[prm_r], writes=[es_r])

    wv = wbig[:, :].rearrange("p (kc n) -> p kc n", kc=KC)
    wqkv_v = w_qkv.rearrange("(kc p) n -> p kc n", p=128)
    p.add("pool", lambda e: e.dma_start(out=wv, in_=wqkv_v[:, :, 2560:3072]), writes=wbr, dma="w")

    nq = [0]

    def load_k(g):
        s = nq[0] % 2
        nq[0] += 1
        for half in range(2):
            p.add("pool", lambda e, s=s, g=g, half=half: e.dma_start(
                out=wq[s][:, :, half * 64:(half + 1) * 64], in_=wqkv_v[:, :, 2048 + g * 64:2048 + (g + 1) * 64]),
                writes=[wqr[s][half]], dma="w")
        return s

    def load_q(hd):
        s = nq[0] % 2
        nq[0] += 1
        for half in range(2):
            p.add("pool", lambda e, s=s, hd=hd, half=half: e.dma_start(
                out=wq[s][:, :, half * 64:(half + 1) * 64], in_=wqkv_v[:, :, hd * 64:(hd + 1) * 64]),
                writes=[wqr[s][half]], dma="w")
        return s

    def load_o(g):
        s = g % 2
        dst = wbig[:, s * 4096:(s + 1) * 4096].rearrange("p (c n) -> p c n", c=2)
        src = w_o[g * 256:(g + 1) * 256, :].rearrange("(c p) n -> p c n", p=128)
        p.add("pool", lambda e, dst=dst, src=src: e.dma_start(out=dst, in_=src), writes=[wbr[s]], dma="w")

    emit_rmsnorm(cx, ps, regr[0], x, xres, gain, prm_r, h, hres, NCOL, ones, ones_r, scr)

    nstat = [0]

    def proj_norm(s, ri, tiles, hoff, gain_ap, dst_fn, dst_res):
        R = REG[ri]
        for kc in range(KC):
            for (c0, c1) in tiles:
                p.add("pe", lambda e, s=s, kc=kc, c0=c0, c1=c1, R=R: e.matmul(
                    ps[:, R + c0:R + c1], lhsT=wq[s][:, kc, :], rhs=h[:, kc, hoff + c0:hoff + c1],
                    start=(kc == 0), stop=(kc == KC - 1)), reads=wqr[s] + [hres[kc]], writes=[regr[ri]])
        for (c0, c1) in tiles:
            n = c1 - c0
            i = nstat[0] % 2
            nstat[0] += 1
            src = ps[:, R + c0:R + c1]
            p.add("act", lambda e, i=i, n=n, src=src: e.activation(out=sqt[i][:, 0:n], in_=src, func=AF.Square),
                  reads=[regr[ri]], writes=[sqtr[i]])
            p.add("pe", lambda e, i=i, n=n: e.matmul(ps[:, STAT:STAT + n], lhsT=bd[:, :], rhs=sqt[i][:, 0:n],
                                                     start=True, stop=True), reads=[bd_r, sqtr[i]], writes=[statr])
            p.add("act", lambda e, i=i, n=n: e.activation(out=sqt[i][:, 0:n], in_=ps[:, STAT:STAT + n], func=AF.Sqrt,
                                                          bias=epsb[:, 0:1], scale=1.0 / 64), reads=[statr, prm_r], writes=[sqtr[i]])
            p.add("dve", lambda e, i=i, n=n: e.reciprocal(out=rsb[i][:, 0:n], in_=sqt[i][:, 0:n]), reads=[sqtr[i]], writes=[rsr[i]])
            dst = dst_fn(c0, c1)
            p.add("dve", lambda e, i=i, n=n, src=src, dst=dst: e.scalar_tensor_tensor(
                out=dst, in0=src, scalar=gain_ap, in1=rsb[i][:, 0:n], op0=ALU.mult, op1=ALU.mult),
                reads=[regr[ri], rsr[i], prm_r], writes=[dst_res], relaxed=True)

    BV = [1536, 2048, 2560]
    bvr = [Res(f"bv{i}") for i in range(3)]
    for tb in range(NTB):
        b = tb % 3
        for kc in range(KC):
            p.add("pe", lambda e, tb=tb, kc=kc, b=b: e.matmul(
                ps[:, BV[b]:BV[b] + 512], lhsT=h[:, kc, tb * 128:(tb + 1) * 128], rhs=wv[:, kc, :],
                start=(kc == 0), stop=(kc == KC - 1)), reads=wbr + [hres[kc]], writes=[bvr[b]])
        p.add("act", lambda e, tb=tb, b=b: e.activation(
            out=Vaug[:, tb, :, 0:64], in_=ps[:, BV[b]:BV[b] + 512].rearrange("p (g d) -> p g d", g=8), func=AF.Identity),
            reads=[bvr[b], regr[1]], writes=[Vr], relaxed=True)

    if stage <= 1:
        return cx.finish()
    tilesK = token_tiles(NCOL, 0)
    ks = load_k(0)
    for g in range(8):
        s = ks
        if g + 1 < 8:
            ks = load_k(g + 1)
        proj_norm(s, g % 2, tilesK, 0, kg, lambda c0, c1, g=g: Kd[:, g, c0:c1], Kdr[g])

    if stage <= 2:
        return cx.finish()
    tilesQ = token_tiles(NT, 0)
    qs = [load_q(0), load_q(1)]
    load_o(0)
    for g in range(8):
        for hh in range(4):
            s = qs.pop(0)
            proj_norm(s, hh % 2, tilesQ, HALO, qg, lambda c0, c1, hh=hh: Qn[:, hh, c0:c1], Qnr[hh])
            nxt = g * 4 + hh + 2
            if nxt < 32:
                qs.append(load_q(nxt))
        if g + 1 < 8:
            load_o(g + 1)
        if stage <= 3:
            return cx.finish()
        for qb in range(8):
            i = qb % 2
            for kb in range(2):
                for hh in range(4):
                    p.add("pe", lambda e, g=g, qb=qb, kb=kb, hh=hh: e.matmul(
                        ps[:, SC[kb] + hh * 128:SC[kb] + (hh + 1) * 128],
                        lhsT=Kd[:, g, (qb + kb) * 128:(qb + kb + 1) * 128],
                        rhs=Qn[:, hh, qb * 128:(qb + 1) * 128], start=True, stop=True),
                        reads=[Kdr[g], Qnr[hh]], writes=[scr_[kb]])
                bias = hb if (qb == 0 and kb == 0) else 0.0
                p.add("act", lambda e, i=i, kb=kb, bias=bias: e.activation(
                    out=PT[i][:, kb, :], in_=ps[:, SC[kb]:SC[kb] + 512], func=AF.Exp, bias=bias, scale=0.0625),
                    reads=[scr_[kb], prm_r], writes=[PTr[i][kb]])
                p.add("pool", lambda e, i=i, kb=kb: e.tensor_tensor(
                    out=PT[i][:, kb, :], in0=PT[i][:, kb, :], in1=msk[:, kb, :], op=ALU.mult),
                    reads=[PTr[i][kb], msk_r], writes=[PTr[i][kb]])
            for kb in range(2):
                p.add("pe", lambda e, g=g, qb=qb, kb=kb, i=i: e.matmul(
                    ps[:, PVB:PVB + 512], lhsT=Vaug[:, qb + kb, g, :], rhs=PT[i][:, kb, :],
                    start=(kb == 0), stop=(kb == 1)), reads=[Vr, PTr[i][kb]], writes=[pvr])
            for hh in range(4):
                p.add("dve", lambda e, g=g, hh=hh: e.tensor_scalar(
                    out=den[64:128, hh * 128:(hh + 1) * 128], in0=ps[64:128, PVB + hh * 128:PVB + (hh + 1) * 128],
                    scalar1=es[64:128, g * 4 + hh:g * 4 + hh + 1], scalar2=None, op0=ALU.add),
                    reads=[pvr, es_r], writes=[denr], relaxed=True)
            p.add("dve", lambda e: e.reciprocal(out=rec[0:64, :], in_=den[64:128, :]), reads=[denr], writes=[recr])
            for hh in range(4):
                pb = (hh % 2) * 64
                c = hh // 2
                p.add("dve", lambda e, hh=hh, pb=pb, c=c, qb=qb: e.tensor_tensor(
                    out=AO[pb:pb + 64, c, qb * 128:(qb + 1) * 128], in0=ps[0:64, PVB + hh * 128:PVB + (hh + 1) * 128],
                    in1=rec[0:64, hh * 128:(hh + 1) * 128], op=ALU.mult),
                    reads=[pvr, recr], writes=[AOr[c]], relaxed=True)
        if stage <= 4:
            return cx.finish()
        so = g % 2
        wo = wbig[:, so * 4096:(so + 1) * 4096].rearrange("p (c n) -> p c n", c=2)
        for m in range(KC):
            for tt in range(2):
                b = (m * 2 + tt) % 4
                for c in range(2):
                    p.add("pe", lambda e, wo=wo, m=m, tt=tt, b=b, c=c: e.matmul(
                        ps[:, OB[b]:OB[b] + 512], lhsT=wo[:, c, m * 128:(m + 1) * 128],
                        rhs=AO[:, c, tt * 512:(tt + 1) * 512], start=(c == 0), stop=(c == 1)),
                        reads=[wbr[so], AOr[c]], writes=[obr[b], regr[0 if b < 2 else 1]])
                xs = x[:, m, HALO + tt * 512:HALO + (tt + 1) * 512]
                p.add("dve", lambda e, xs=xs, b=b: e.tensor_tensor(out=xs, in0=ps[:, OB[b]:OB[b] + 512], in1=xs, op=ALU.add),
                      reads=[obr[b], regr[0 if b < 2 else 1], xres[m][1 + tt]], writes=[xres[m][1 + tt]])
                if g == 7 and tt == 1:
                    yv = yT.rearrange("(kc p) n -> p kc n", p=128)
                    p.add("sp", lambda e, m=m: e.dma_start(out=yv[:, m, :], in_=x[:, m, HALO:HALO + NT]),
                          reads=[xres[m][1], xres[m][2]], dma="r")
    return cx.finish()


def attn_params(g, qgain, kgain, sinks, halo_bias):
    qg = np.tile(np.asarray(qgain, np.float32), 2).reshape(128, 1)
    kg = np.tile(np.asarray(kgain, np.float32), 2).reshape(128, 1)
    sk = np.tile(np.asarray(sinks, np.float32).reshape(1, 32), (128, 1))
    hb = np.full((128, 1), halo_bias, np.float32)
    return np.ascontiguousarray(np.concatenate([chunked(g), qg, kg, sk, hb], axis=1))


def attn_masks():
    k = np.arange(128)[:, None]
    q = np.arange(128)[None, :]
    mprev = (k > q).astype(np.float32)
    mcur = (q >= k).astype(np.float32)
    return np.ascontiguousarray(np.concatenate([np.tile(mprev, (1, 4)), np.tile(mcur, (1, 4))], axis=1))


_CACHE = {}


def _prog(name):
    if name not in _CACHE:
        _CACHE[name] = {"attn": build_attn2, "ffn": lambda: build_ffn(False),
                        "ffnp": lambda: build_ffn(True), "rec1": build_rec1b}[name]()
    return _CACHE[name]


def _run(name, in_maps):
    res = run_bass_kernel_spmd(_prog(name), in_maps, core_ids=list(range(NCORES)))
    return res.results


def _halo_slices(aT, halo):
    pad = np.concatenate([np.zeros((aT.shape[0], halo), aT.dtype), aT], axis=1)
    return [np.ascontiguousarray(pad[:, c * NT:c * NT + NT + halo]) for c in range(NCORES)]


def kernel(x, mix_norm, ffn_norm, attn_w_qkv, attn_q_gain, attn_k_gain, attn_sinks, attn_w_o,
           rec_w_in, rec_conv_w, rec_conv_b, rec_w_a, rec_b_a, rec_w_i, rec_b_i, rec_lambda,
           rec_w_out, ffn_w_up, ffn_conv_w, ffn_conv_b, ffn_w_down):
    f = lambda a: np.ascontiguousarray(np.asarray(a, dtype=np.float32))
    xT = np.ascontiguousarray(f(x)[0].T)
    masks = attn_masks2()
    for layer in range(4):
        j = layer // 2
        fprm = ffn_params(f(ffn_norm)[layer], f(ffn_conv_w)[layer], f(ffn_conv_b)[layer])
        wup, wdn = f(ffn_w_up)[layer], f(ffn_w_down)[layer]
        if layer % 2 == 0:
            xs = _halo_slices(xT, 128)
            wqkv, wo = f(attn_w_qkv)[j], f(attn_w_o)[j]
            in_maps = [{"xT": xs[c], "w_qkv": wqkv, "w_o": wo, "msk": masks,
                        "prm": attn_params(f(mix_norm)[layer], f(attn_q_gain)[j], f(attn_k_gain)[j],
                                           f(attn_sinks)[j], -30000.0 if c == 0 else 0.0)}
                       for c in range(NCORES)]
            res = _run("attn", in_maps)
            xT = np.concatenate([r["yT"] for r in res], axis=1)
            xs = _halo_slices(xT, 2)
            in_maps = [{"xT": xs[c], "w_up": wup, "w_down": wdn, "prm": fprm} for c in range(NCORES)]
            res = _run("ffn", in_maps)
        else:
            xs = _halo_slices(xT, 3)
            rprm = rec_params(f(mix_norm)[layer], f(rec_conv_w)[j], f(rec_conv_b)[j], f(rec_b_a)[j],
                              f(rec_b_i)[j], f(rec_lambda)[j])
            in_maps = [{"xT": xs[c], "w_in": f(rec_w_in)[j], "w_a": f(rec_w_a)[j], "w_i": f(rec_w_i)[j],
                        "prm": rprm} for c in range(NCORES)]
            res = _run("rec1", in_maps)
            g1T = np.concatenate([r["g1T"] for r in res], axis=1)
            zT = np.concatenate([r["zT"] for r in res], axis=1)
            cst = np.stack([r["carr"] for r in res], axis=-1).reshape(128, 2, 1, 16, 8)
            carr = np.zeros((128, 2, 2, 16, 9), np.float32)
            carr[:, :, :, :, 1:] = cst
            carr = np.ascontiguousarray(carr.reshape(128, 576))
            xs, gs, zs = _halo_slices(xT, 2), _halo_slices(g1T, 2), _halo_slices(zT, 2)
            in_maps = []
            for c in range(NCORES):
                m = np.zeros((128, 2, 2, 16, 9), np.float32)
                m[:, 0, 0, :, 1:1 + max(c, 0)] = 1.0
                m[:, 0, 1, :, 1:1 + max(c - 1, 0)] = 1.0
                m[:, 1, :, :, 1:] = 1.0
                m = np.ascontiguousarray(m.reshape(128, 576))
                in_maps.append({"xT": xs[c], "g1T": gs[c], "zT": zs[c], "carr": carr, "msk": m,
                                "w_out": f(rec_w_out)[j], "w_up": wup, "w_down": wdn, "prm": fprm})
            res = _run("ffnp", in_maps)
        xT = np.concatenate([r["yT"] for r in res], axis=1)
    return np.ascontiguousarray(xT.T)[None].astype(np.float32)


def build_attn2():
    cx = Ctx()
    nc, p = cx.nc, cx.p
    HALO = 128
    NCOL = NT + HALO
    NTB = NCOL // 128
    xT = cx.din("xT", [D, NCOL])
    w_qkv = cx.din("w_qkv", [D, 3072])
    w_o = cx.din("w_o", [D, D])
    NPRM = 16 + 2 + 32 + 1
    prm_d = cx.din("prm", [128, NPRM])
    msk_d = cx.din("msk", [128, 640])
    yT = cx.dout("yT", [D, NT])

    ps = cx.psum()
    bk = [Res(f"bk{i}") for i in range(8)]
    x = cx.sb("x", [128, KC, NCOL], F32)
    h = cx.sb("h", [128, KC, NCOL], BF16)
    sq = [cx.sb(f"sq{i}", [128, NCOL], F32) for i in range(2)]
    rstd = cx.sb("rstd", [128, NCOL], F32)
    prm = cx.sb("prm_sb", [128, NPRM], F32)
    es = cx.sb("es", [128, 32], F32)
    ones = cx.sb("ones", [128, 128], F32)
    bd = cx.sb("bd", [128, 128], F32)
    bdb = cx.sb("bdb", [128, 128], BF16)
    epsb = cx.sb("epsb", [128, 2], F32)
    msk = cx.sb("msk_sb", [128, 640], BF16)
    onesb = cx.sb("onesb", [128, 128], BF16)
    Kd = cx.sb("Kd", [128, 8, NCOL], BF16)
    NVB = NTB * 8
    VO = cx.sb("VO", [128, (NVB + 1) * 64], BF16)
    wbig = cx.sb("wbig", [128, 8192], BF16)
    wq = [cx.sb(f"wq{i}", [128, KC, 128], BF16) for i in range(2)]
    Qn = [cx.sb(f"Qn{i}", [128, 4, NT], BF16) for i in range(2)]
    AO = [cx.sb(f"AO{i}", [128, 2, NT], BF16) for i in range(2)]
    PT = [cx.sb(f"PT{i}", [128, 512], BF16) for i in range(2)]

    xres = [[Res(f"x{m}_h"), Res(f"x{m}_0"), Res(f"x{m}_1")] for m in range(KC)]
    hres = [Res(f"h{k}") for k in range(KC)]
    sqr = [Res("sq0"), Res("sq1")]
    rstdr, prm_r, ones_r, bd_r, es_r, msk_r = (Res(n) for n in ("rstd", "prm", "ones", "bd", "es", "msk"))
    Kdr = [Res(f"Kd{g}") for g in range(8)]
    Vr = Res("VO")
    wbr = [Res("wb0"), Res("wb1")]
    wqr = [[Res(f"wq{i}_0"), Res(f"wq{i}_1")] for i in range(2)]
    Qnr = [[Res(f"Qn{i}_{hh}") for hh in range(4)] for i in range(2)]
    AOr = [[Res(f"AO{i}_{c}") for c in range(2)] for i in range(2)]
    PTr = [Res("PT0"), Res("PT1")]
    sqt = [sq[0][:, 0:512], sq[0][:, 512:1024]]
    _sq0b = sq[0][:, :].bitcast(BF16)
    sqtb = [_sq0b[:, 0:512], _sq0b[:, 1024:1536]]
    rsb = [sq[1][:, 0:512], sq[1][:, 512:1024]]
    den = rstd[:, 0:256]
    rec = rstd[:, 512:768]
    sqtr = [Res("sqt0"), Res("sqt1")]
    rsr = [Res("rs0"), Res("rs1")]
    denr, recr = Res("den"), Res("rec")
    B = lambda i: i * 512
    STATB, SCB, PVB, OPB = 4, 5, 6, 7

    gain = prm[:, 0:16]
    qg = prm[:, 16:17]
    kg = prm[:, 17:18]
    hb = prm[:, 50:51]
    scr = {"sq": sq, "sqr": sqr, "rstd": rstd, "rstdr": rstdr, "eps": epsb}

    p.add("pool", lambda e: e.memset(ones[:, :], 1.0), writes=[ones_r])
    p.add("pool", lambda e: e.memset(onesb[:, :], 1.0), writes=[ones_r])
    p.add("pool", lambda e: e.memset(epsb[:, 0:1], EPS), writes=[prm_r])
    p.add("pool", lambda e: e.memset(epsb[:, 1:2], 64 * EPS), writes=[prm_r])
    p.add("pool", lambda e: e.memset(bd[:, :], 0.0), writes=[bd_r])
    p.add("pool", lambda e: e.memset(bd[0:64, 0:64], 1.0), writes=[bd_r])
    p.add("pool", lambda e: e.memset(bd[64:128, 64:128], 1.0), writes=[bd_r])
    p.add("pool", lambda e: e.tensor_copy(out=bdb[:, :], in_=bd[:, :]), reads=[bd_r], writes=[bd_r])
    p.add("pool", lambda e: e.memset(VO[:, NVB * 64:(NVB + 1) * 64], 1.0), writes=[Vr])
    p.add("sp", lambda e: e.dma_start(out=prm[:, :], in_=prm_d[:, :]), writes=[prm_r], dma="w")
    p.add("pool", lambda e: e.dma_start(out=msk[:, :], in_=msk_d[:, :]), writes=[msk_r], dma="w")
    xv = xT.rearrange("(kc p) n -> p kc n", p=128)
    for kc in range(KC):
        p.add("sp", lambda e, kc=kc: e.dma_start(out=x[:, kc, :], in_=xv[:, kc, :]), writes=xres[kc], dma="w")
    p.add("act", lambda e: e.activation(out=es[:, :], in_=prm[:, 18:50], func=AF.Exp), reads=[prm_r], writes=[es_r])

    wv = wbig[:, :].rearrange("p (kc n) -> p kc n", kc=KC)
    wqkv_v = w_qkv.rearrange("(kc p) n -> p kc n", p=128)
    p.add("pool", lambda e: e.dma_start(out=wv, in_=wqkv_v[:, :, 2560:3072]), writes=wbr, dma="w")

    nq = [0]

    def load_w(col0):
        s = nq[0] % 2
        nq[0] += 1
        for half in range(2):
            p.add("pool", lambda e, s=s, col0=col0, half=half: e.dma_start(
                out=wq[s][:, :, half * 64:(half + 1) * 64], in_=wqkv_v[:, :, col0:col0 + 64]),
                writes=[wqr[s][half]], dma="w")
        return s

    def load_o(g):
        s = g % 2
        dst = wbig[:, s * 4096:(s + 1) * 4096].rearrange("p (c n) -> p c n", c=2)
        src = w_o[g * 256:(g + 1) * 256, :].rearrange("(c p) n -> p c n", p=128)
        p.add("pool", lambda e, dst=dst, src=src: e.dma_start(out=dst, in_=src), writes=[wbr[s]], dma="w")

    emit_rmsnorm(cx, ps, bk[0], x, xres, gain, prm_r, h, hres, NCOL, ones, ones_r, scr,
                 extra_ps_res=[bk[1], bk[2]], use_ln=True, onesb=onesb)

    for tb in range(NTB):
        b = 5 + tb % 3
        for kc in range(KC):
            p.add("pe", lambda e, tb=tb, kc=kc, b=b: e.matmul(
                ps[:, B(b):B(b) + 512], lhsT=h[:, kc, tb * 128:(tb + 1) * 128], rhs=wv[:, kc, :],
                start=(kc == 0), stop=(kc == KC - 1)), reads=wbr + [hres[kc]], writes=[bk[b]])
        p.add("act", lambda e, tb=tb, b=b: e.activation(
            out=VO[:, tb * 512:(tb + 1) * 512], in_=ps[:, B(b):B(b) + 512], func=AF.Identity),
            reads=[bk[b]], writes=[Vr], relaxed=True)

    nstat = [0]

    def proj_mms(s, banks, tiles, hoff, tile_major=False):
        out = []
        order = [(kc, t) for kc in range(KC) for t in range(len(tiles))] if not tile_major else \
                [(kc, t) for t in range(len(tiles)) for kc in range(KC)]
        for kc, t in order:
            c0, c1 = tiles[t]
            if True:
                def f(s=s, kc=kc, t=t, c0=c0, c1=c1):
                    p.add("pe", lambda e: e.matmul(
                        ps[:, B(banks[t]):B(banks[t]) + (c1 - c0)], lhsT=wq[s][:, kc, :],
                        rhs=h[:, kc, hoff + c0:hoff + c1], start=(kc == 0), stop=(kc == KC - 1)),
                        reads=wqr[s] + [hres[kc]], writes=[bk[banks[t]]])
                out.append(f)
        return out

    def norm_steps(banks, tiles, gain_ap, dst_fn, dst_res):
        idx = []
        for t in range(len(tiles)):
            idx.append(nstat[0] % 2)
            nstat[0] += 1

        def phase_a():
            for t, (c0, c1) in enumerate(tiles):
                n = c1 - c0
                i = idx[t]
                src = ps[:, B(banks[t]):B(banks[t]) + n]
                p.add("act", lambda e, i=i, n=n, src=src: e.activation(out=sqtb[i][:, 0:n], in_=src, func=AF.Square),
                      reads=[bk[banks[t]]], writes=[sqtr[i]])

        def phase_b(t):
            c0, c1 = tiles[t]
            n = c1 - c0
            i = idx[t]
            src = ps[:, B(banks[t]):B(banks[t]) + n]
            br = bk[banks[t]]
            p.add("pe", lambda e: e.matmul(ps[:, B(STATB):B(STATB) + n], lhsT=bdb[:, :], rhs=sqtb[i][:, 0:n],
                                           start=True, stop=True), reads=[bd_r, sqtr[i]], writes=[bk[STATB]])
            p.add("act", lambda e: e.activation(out=sqt[i][:, 0:n], in_=ps[:, B(STATB):B(STATB) + n], func=AF.Ln,
                                                bias=epsb[:, 1:2], scale=1.0), reads=[bk[STATB], prm_r], writes=[sqtr[i]])
            p.add("act", lambda e: e.activation(out=rsb[i][:, 0:n], in_=sqt[i][:, 0:n], func=AF.Exp, scale=-0.5),
                  reads=[sqtr[i]], writes=[rsr[i]])
            dst = dst_fn(c0, c1)
            p.add("dve", lambda e: e.scalar_tensor_tensor(
                out=dst, in0=src, scalar=gain_ap, in1=rsb[i][:, 0:n], op0=ALU.mult, op1=ALU.mult),
                reads=[br, rsr[i], prm_r], writes=[dst_res], relaxed=True)

        return [phase_a] + [(lambda t=t: phase_b(t)) for t in range(len(tiles))]

    def norm(banks, tiles, gain_ap, dst_fn, dst_res):
        for t in range(len(tiles)):
            for f in norm_steps(banks[t:t + 1], tiles[t:t + 1], gain_ap, dst_fn, dst_res):
                f()

    tilesK = token_tiles(NCOL, 0)
    KB = [[0, 1, 5], [2, 3, 6]]
    ks = load_w(2048)
    pend = None
    for g in range(8):
        s = ks
        if g + 1 < 8:
            ks = load_w(2048 + (g + 1) * 64)
        for f in proj_mms(s, KB[g % 2], tilesK, 0):
            f()
        if pend is not None:
            pend()
        pend = (lambda g=g: norm(KB[g % 2], tilesK, kg, lambda c0, c1: Kd[:, g, c0:c1], Kdr[g]))
    pend()

    tilesQ = token_tiles(NT, 0)
    QB = [[0, 1], [2, 3]]

    qslot = {}

    def prefetch_q(hd):
        if hd < 32 and hd not in qslot:
            qslot[hd] = load_w(hd * 64)

    def emit_q_head(g, hh, chunks):
        hd = g * 4 + hh
        gp = g % 2
        prefetch_q(hd)
        s = qslot[hd]
        qbanks = [(hd * 2) % 3, (hd * 2 + 1) % 3]
        mms = proj_mms(s, qbanks, tilesQ, HALO, tile_major=True)
        per = (len(mms) + chunks - 1) // chunks
        out = []
        for ci in range(chunks):
            part = mms[ci * per:(ci + 1) * per]
            last = ci == chunks - 1

            def f(part=part, last=last, first=(ci == 0)):
                if first:
                    prefetch_q(hd + 1)
                    while any(set(r) & set(qbanks) for r, _ in pending_norm):
                        pending_norm.pop(0)[1]()
                half = len(part) // 2
                pop_pending()
                for m in part[:half]:
                    m()
                pop_pending()
                for m in part[half:]:
                    m()
                if last:
                    sts = norm_steps(qbanks, tilesQ, qg, lambda c0, c1: Qn[gp][:, hh, c0:c1], Qnr[gp][hh])
                    pending_norm.extend(zip([qbanks, qbanks[0:1], qbanks[1:2]], sts))
            out.append(f)
        return out

    pending_norm = []

    def pop_pending():
        if pending_norm:
            pending_norm.pop(0)[1]()

    def flush_norm():
        while pending_norm:
            pending_norm.pop(0)[1]()

    def o_units(g, banks=(7,)):
        gp = g % 2
        so = g % 2
        wo = wbig[:, so * 4096:(so + 1) * 4096].rearrange("p (c n) -> p c n", c=2)
        units = []
        for m in range(KC):
            for tt in range(2):
                def f(m=m, tt=tt):
                    ob = banks[(m * 2 + tt) % len(banks)]
                    for c in range(2):
                        p.add("pe", lambda e, c=c: e.matmul(
                            ps[:, B(ob):B(ob) + 512], lhsT=wo[:, c, m * 128:(m + 1) * 128],
                            rhs=AO[gp][:, c, tt * 512:(tt + 1) * 512], start=(c == 0), stop=(c == 1)),
                            reads=[wbr[so], AOr[gp][c]], writes=[bk[ob]])
                    xs = x[:, m, HALO + tt * 512:HALO + (tt + 1) * 512]
                    p.add("dve", lambda e: e.tensor_tensor(out=xs, in0=ps[:, B(ob):B(ob) + 512], in1=xs, op=ALU.add),
                          reads=[bk[ob], xres[m][1 + tt]], writes=[xres[m][1 + tt]])
                    if g == 7 and tt == 1:
                        yv = yT.rearrange("(kc p) n -> p kc n", p=128)
                        p.add("sp", lambda e: e.dma_start(out=yv[:, m, :], in_=x[:, m, HALO:HALO + NT]),
                              reads=[xres[m][1], xres[m][2]], dma="r")
                units.append(f)
        return units

    prefetch_q(0)
    for hh in range(4):
        for f in emit_q_head(0, hh, 1):
            f()
    flush_norm()
    load_o(0)

    nstep = [0]
    for g in range(8):
        gp = g % 2
        qwork = []
        if g + 1 < 8:
            for hh in range(4):
                qwork.append((g + 1, hh))
        owork = o_units(g - 1, banks=(7, 3)) if g > 0 else []
        if g > 0:
            load_o(g)
        cur_q = []
        for qb in range(8):
            for hp in range(2):
                step = qb * 2 + hp
                i = nstep[0] % 2
                nstep[0] += 1
                p.add("pe", lambda e: e.matmul(ps[:, B(SCB):B(SCB) + 512], lhsT=msk[:, 512:640], rhs=msk[:, 0:512],
                                               start=True, stop=False), reads=[msk_r], writes=[bk[SCB]])
                for kb in range(2):
                    for hl in range(2):
                        hh = hp * 2 + hl
                        col = B(SCB) + kb * 256 + hl * 128
                        p.add("pe", lambda e, kb=kb, hh=hh, col=col, qb=qb, g=g, gp=gp, hl=hl: e.matmul(
                            ps[:, col:col + 128], lhsT=Kd[:, g, (qb + kb) * 128:(qb + kb + 1) * 128],
                            rhs=Qn[gp][:, hh, qb * 128:(qb + 1) * 128], start=False, stop=(kb == 1 and hl == 1)),
                            reads=[Kdr[g], Qnr[gp][hh]], writes=[bk[SCB]])
                if qb == 0:
                    p.add("act", lambda e, i=i: e.activation(out=PT[i][:, 0:256], in_=ps[:, B(SCB):B(SCB) + 256],
                                                             func=AF.Exp, bias=hb, scale=4.0),
                          reads=[bk[SCB], prm_r], writes=[PTr[i]])
                    p.add("act", lambda e, i=i: e.activation(out=PT[i][:, 256:512], in_=ps[:, B(SCB) + 256:B(SCB) + 512],
                                                             func=AF.Exp, scale=4.0),
                          reads=[bk[SCB]], writes=[PTr[i]], relaxed=True)
                else:
                    p.add("act", lambda e, i=i: e.activation(out=PT[i][:, :], in_=ps[:, B(SCB):B(SCB) + 512],
                                                             func=AF.Exp, scale=4.0),
                          reads=[bk[SCB]], writes=[PTr[i]])
                if owork:
                    owork.pop(0)()
                if qwork or cur_q:
                    if not cur_q:
                        gg, hh_ = qwork.pop(0)
                        cur_q = emit_q_head(gg, hh_, 4)
                    cur_q.pop(0)()
                else:
                    pop_pending()
                if owork:
                    owork.pop(0)()
                for kb in range(2):
                    vb = ((qb + kb) * 8 + g) * 64
                    p.add("pe", lambda e, kb=kb, vb=vb, i=i: e.matmul(
                        ps[:, B(PVB):B(PVB) + 256], lhsT=VO[:, vb:vb + 128], rhs=PT[i][:, kb * 256:(kb + 1) * 256],
                        start=(kb == 0), stop=(kb == 1)), reads=[Vr, PTr[i]], writes=[bk[PVB]])
                for kb in range(2):
                    p.add("pe", lambda e, kb=kb, i=i: e.matmul(
                        ps[:, B(PVB) + 256:B(PVB) + 512], lhsT=onesb[:, :], rhs=PT[i][:, kb * 256:(kb + 1) * 256],
                        start=(kb == 0), stop=(kb == 1)), reads=[ones_r, PTr[i]], writes=[bk[PVB]])
                for hl in range(2):
                    hh = hp * 2 + hl
                    p.add("dve", lambda e, hl=hl, hh=hh, g=g: e.tensor_scalar(
                        out=den[0:64, hl * 128:(hl + 1) * 128],
                        in0=ps[0:64, B(PVB) + 256 + hl * 128:B(PVB) + 256 + (hl + 1) * 128],
                        scalar1=es[0:64, g * 4 + hh:g * 4 + hh + 1], scalar2=None, op0=ALU.add),
                        reads=[bk[PVB], es_r], writes=[denr], relaxed=True)
                p.add("dve", lambda e: e.reciprocal(out=rec[0:64, :], in_=den[0:64, :]), reads=[denr], writes=[recr])
                for hl in range(2):
                    p.add("dve", lambda e, hl=hl, hp=hp, qb=qb, gp=gp: e.tensor_tensor(
                        out=AO[gp][hl * 64:(hl + 1) * 64, hp, qb * 128:(qb + 1) * 128],
                        in0=ps[0:64, B(PVB) + hl * 128:B(PVB) + (hl + 1) * 128],
                        in1=rec[0:64, hl * 128:(hl + 1) * 128], op=ALU.mult),
                        reads=[bk[PVB], recr], writes=[AOr[gp][hp]], relaxed=True)
        flush_norm()
        assert not qwork and not cur_q and not owork, (len(qwork), len(cur_q), len(owork))
    for f in o_units(7, banks=(7, 3, 0, 1, 2, 5)):
        f()
    return cx.finish()


def attn_masks2():
    k = np.arange(128)[:, None]
    q = np.arange(128)[None, :]
    mprev = np.where(k > q, 0.0, -10000.0).astype(np.float32)
    mcur = np.where(q >= k, 0.0, -10000.0).astype(np.float32)
    return np.ascontiguousarray(np.concatenate([mprev, mprev, mcur, mcur, np.eye(128, dtype=np.float32)], axis=1))
```

```python
import contextlib
import numpy as np
import concourse.bass as bass
import concourse.mybir as mybir
from concourse.bass_utils import run_bass_kernel_spmd

F32 = mybir.dt.float32
BF16 = mybir.dt.bfloat16
AF = mybir.ActivationFunctionType
ALU = mybir.AluOpType

NCORES = 8
D = 2048
KC = 16
T = 8192
NT = 1024
DFF = 6144
EPS = 1e-6


class Res:
    __slots__ = ("name", "last_w", "readers", "sem_w", "nw", "sem_r", "nr")

    def __init__(self, name):
        self.name = name
        self.last_w = None
        self.readers = {}
        self.sem_w = None
        self.nw = 0
        self.sem_r = None
        self.nr = 0


class Op:
    __slots__ = ("eng", "fn", "deps", "need_inc", "sem", "semval", "dma", "inc")

    def __init__(self, eng, fn, dma):
        self.eng = eng
        self.fn = fn
        self.deps = []
        self.need_inc = False
        self.sem = None
        self.semval = 0
        self.dma = dma
        self.inc = 1


class Prog:
    ENGS = ("pe", "act", "dve", "pool", "sp")

    def __init__(self, nc):
        self.nc = nc
        self.ops = {e: [] for e in self.ENGS}
        self.esem = {e: nc.alloc_semaphore(name=f"sem_{e}") for e in self.ENGS}
        self.nsem = 0
        self.final = []

    def _newsem(self):
        self.nsem += 1
        return self.nc.alloc_semaphore(name=f"dsem{self.nsem}")

    def add(self, eng, fn, reads=(), writes=(), dma=None, relaxed=False):
        op = Op(eng, fn, dma)
        deps = []
        for r in reads:
            if r.last_w is not None:
                deps.append(r.last_w)
        for w in writes:
            if w.last_w is not None:
                if not (relaxed and w.last_w.dma is None and dma is None and w.last_w.eng == eng):
                    deps.append(w.last_w)
            deps.extend(w.readers.values())
        for d in deps:
            if d is op:
                continue
            if d.dma is None and op.dma is None and d.eng == eng == "pe":
                continue
            if d not in op.deps:
                op.deps.append(d)
                d.need_inc = True
        for r in reads:
            key = eng if dma is None else id(op)
            r.readers[key] = op
        for w in writes:
            w.last_w = op
            w.readers = {}
        if dma == "w":
            res = writes[0]
            if res.sem_w is None:
                res.sem_w = self._newsem()
            res.nw += 1
            op.sem, op.semval, op.inc = res.sem_w, 16 * res.nw, 16
            op.need_inc = True
        elif dma == "r":
            res = reads[0]
            if res.sem_r is None:
                res.sem_r = self._newsem()
            res.nr += 1
            op.sem, op.semval, op.inc = res.sem_r, 16 * res.nr, 16
            op.need_inc = True
            self.final.append(op)
        self.ops[eng].append(op)
        return op

    def emit(self):
        nc = self.nc
        for e in self.ENGS:
            cnt = 0
            for op in self.ops[e]:
                if op.dma is None:
                    op.sem = self.esem[e]
                    if op.need_inc:
                        cnt += 1
                        op.semval = cnt
        final = self.final

        def run(e, h):
            waited = {}
            for op in self.ops[e]:
                need = {}
                for d in op.deps:
                    k = id(d.sem)
                    if k not in need or need[k][1] < d.semval:
                        need[k] = (d.sem, d.semval)
                for k, (s, v) in need.items():
                    if waited.get(k, 0) >= v:
                        continue
                    h.wait_ge(s, v)
                    waited[k] = v
                ins = op.fn(h)
                if op.need_inc:
                    ins.then_inc(op.sem, op.inc)
            if e == "sp":
                need = {}
                for d in final:
                    k = id(d.sem)
                    if k not in need or need[k][1] < d.semval:
                        need[k] = (d.sem, d.semval)
                for k, (s, v) in need.items():
                    h.wait_ge(s, v)

        with nc.Block() as block:
            @block.tensor
            def _(h):
                run("pe", h)

            @block.scalar
            def _(h):
                run("act", h)

            @block.vector
            def _(h):
                run("dve", h)

            @block.gpsimd
            def _(h):
                run("pool", h)

            @block.sync
            def _(h):
                run("sp", h)


class Ctx:
    def __init__(self):
        self.nc = bass.Bass("TRN2", target_bir_lowering=False)
        self.p = Prog(self.nc)
        self.stack = contextlib.ExitStack()

    def sb(self, name, shape, dt):
        return self.stack.enter_context(self.nc.sbuf_tensor(name, shape, dt))

    def psum(self):
        return self.stack.enter_context(self.nc.psum_tensor("ps", [128, 4096], F32))

    def din(self, name, shape, dt=F32):
        return self.nc.dram_tensor(name, list(shape), dt, kind="ExternalInput").ap()

    def dout(self, name, shape, dt=F32):
        return self.nc.dram_tensor(name, list(shape), dt, kind="ExternalOutput").ap()

    def finish(self):
        self.p.emit()
        self.stack.close()
        return self.nc


def token_tiles(ncols, first):
    tiles = []
    c = 0
    if first:
        tiles.append((0, first))
        c = first
    while c < ncols:
        tiles.append((c, min(c + 512, ncols)))
        c = tiles[-1][1]
    return tiles


def emit_rmsnorm(cx, ps, ps_res, x, xres, gain, prm_res, h, hres, ncols, ones, ones_res, scr, extra_ps_res=(), use_ln=False, onesb=None):
    p = cx.p
    tiles = token_tiles(ncols, 0)
    for kc in range(KC):
        sq, sqr = scr["sq"][kc % 2], scr["sqr"][kc % 2]
        lhs = ones
        if onesb is not None:
            sq = sq[:, :].bitcast(BF16)
            lhs = onesb
        p.add("act", lambda e, kc=kc, sq=sq: e.activation(out=sq[:, 0:ncols], in_=x[:, kc, 0:ncols], func=AF.Square),
              reads=xres[kc], writes=[sqr])
        for (c0, c1) in tiles:
            p.add("pe", lambda e, kc=kc, sq=sq, c0=c0, c1=c1, lhs=lhs: e.matmul(
                ps[:, c0:c1], lhsT=lhs[:, :], rhs=sq[:, c0:c1], start=(kc == 0), stop=(kc == KC - 1)),
                reads=[sqr, ones_res], writes=[ps_res] + list(extra_ps_res))
    rstd, rres = scr["rstd"], scr["rstdr"]
    sq, sqr = scr["sq"][0], scr["sqr"][0]
    if use_ln:
        p.add("act", lambda e: e.activation(out=sq[:, 0:ncols], in_=ps[:, 0:ncols], func=AF.Ln,
                                            bias=scr["eps"][:, 0:1], scale=1.0 / D),
              reads=[ps_res, prm_res] + list(extra_ps_res), writes=[sqr])
        p.add("act", lambda e: e.activation(out=rstd[:, 0:ncols], in_=sq[:, 0:ncols], func=AF.Exp, scale=-0.5),
              reads=[sqr], writes=[rres])
    else:
        p.add("act", lambda e: e.activation(out=sq[:, 0:ncols], in_=ps[:, 0:ncols], func=AF.Sqrt,
                                            bias=scr["eps"][:, 0:1], scale=1.0 / D),
              reads=[ps_res, prm_res] + list(extra_ps_res), writes=[sqr])
        p.add("dve", lambda e: e.reciprocal(out=rstd[:, 0:ncols], in_=sq[:, 0:ncols]), reads=[sqr], writes=[rres])
    for kc in range(KC):
        p.add("dve", lambda e, kc=kc: e.scalar_tensor_tensor(
            out=h[:, kc, 0:ncols], in0=x[:, kc, 0:ncols], scalar=gain[:, kc:kc + 1], in1=rstd[:, 0:ncols],
            op0=ALU.mult, op1=ALU.mult), reads=list(xres[kc]) + [rres, prm_res], writes=[hres[kc]])


def build_ffn(rec_prologue):
    cx = Ctx()
    nc, p = cx.nc, cx.p
    NCOL = NT + 2
    xT = cx.din("xT", [D, NCOL])
    w_up = cx.din("w_up", [D, 2 * DFF])
    w_down = cx.din("w_down", [DFF, D])
    prm_d = cx.din("prm", [128, 16 + 288 + 96])
    yT = cx.dout("yT", [D, NT])
    if rec_prologue:
        g1T = cx.din("g1T", [D, NCOL])
        zT = cx.din("zT", [D, NCOL])
        carr = cx.din("carr", [128, 576])
        msk = cx.din("msk", [128, 576])
        w_out = cx.din("w_out", [D, D])

    ps = cx.psum()
    x = cx.sb("x", [128, KC, NCOL], F32)
    h = cx.sb("h", [128, KC, NCOL], BF16)
    sq = [cx.sb(f"sq{i}", [128, NCOL], F32) for i in range(2)]
    rstd = cx.sb("rstd", [128, NCOL], F32)
    prm = cx.sb("prm_sb", [128, 16 + 288 + 96], F32)
    ones = cx.sb("ones", [128, 128], F32)
    onesb = cx.sb("onesb", [128, 128], BF16)
    epsb = cx.sb("epsb", [128, 1], F32)
    wup = [cx.sb(f"wup{i}", [128, KC, 2, 128], BF16) for i in range(3)]
    wdn = [cx.sb(f"wdn{i}", [128, 4, D], BF16) for i in range(2)]
    act = [cx.sb(f"act{i}", [128, NT], BF16) for i in range(8)]
    cg = cx.sb("cg", [128, NCOL], F32)
    cv = cx.sb("cv", [128, NCOL], F32)
    gg = cx.sb("gg", [128, NT], F32)

    xres = [[Res(f"x{m}_{tt}") for tt in range(3)] for m in range(KC)]
    hres = [Res(f"h{k}") for k in range(KC)]
    sqr = [Res("sq0"), Res("sq1")]
    rstdr = Res("rstd")
    prm_r = Res("prm")
    ones_r = Res("ones")
    wupr = [[Res(f"wup{i}_{hf}") for hf in range(2)] for i in range(3)]
    wdnr = [Res(f"wdn{i}") for i in range(2)]
    actr = [Res(f"act{i}") for i in range(8)]
    cgr, cvr, ggr = Res("cg"), Res("cv"), Res("gg")
    psX, psY = Res("psX"), Res("psY")
    bankr = [Res(f"bank{i}") for i in range(2)]
    OX, OY = 0, 1536
    tilesX = token_tiles(NCOL, 0)
    tilesY = tilesX
    BANK = [3072, 3584]

    gain = prm[:, 0:16]
    cw = prm[:, 16:16 + 288]
    cb = prm[:, 304:400]
    scr = {"sq": sq, "sqr": sqr, "rstd": rstd, "rstdr": rstdr, "eps": epsb}

    p.add("pool", lambda e: e.memset(ones[:, :], 1.0), writes=[ones_r])
    p.add("pool", lambda e: e.memset(onesb[:, :], 1.0), writes=[ones_r])
    p.add("pool", lambda e: e.memset(epsb[:, :], EPS), writes=[prm_r])
    p.add("sp", lambda e: e.dma_start(out=prm[:, :], in_=prm_d[:, :]), writes=[prm_r], dma="w")
    xv = xT.rearrange("(kc p) n -> p kc n", p=128)

    def load_x():
        for kc in range(KC):
            p.add("sp", lambda e, kc=kc: e.dma_start(out=x[:, kc, :], in_=xv[:, kc, :]),
                  writes=xres[kc], dma="w")

    if not rec_prologue:
        load_x()

    if rec_prologue:
        cin = cx.sb("cin", [128, 576], F32)
        mk = cx.sb("mk", [128, 576], F32)
        ca = cx.sb("ca", [128, 288], F32)
        ch = cx.sb("ch", [128, 288], F32)
        cs = cx.sb("cs", [128, 2, KC, 9], F32)
        cin_r, mk_r, ca_r, ch_r, cs_r = Res("cin"), Res("mk"), Res("ca"), Res("ch"), Res("cs")
        p.add("sp", lambda e: e.dma_start(out=cin[:, :], in_=carr[:, :]), writes=[cin_r], dma="w")
        p.add("sp", lambda e: e.dma_start(out=mk[:, :], in_=msk[:, :]), writes=[mk_r], dma="w")
        p.add("dve", lambda e: e.scalar_tensor_tensor(out=ca[:, :], in0=cin[:, 0:288], scalar=-1.0, in1=mk[:, 0:288],
                                                      op0=ALU.add, op1=ALU.mult), reads=[cin_r, mk_r], writes=[ca_r])
        p.add("dve", lambda e: e.tensor_tensor(out=ca[:, :], in0=ca[:, :], in1=mk[:, 288:576], op=ALU.add),
              reads=[ca_r, mk_r], writes=[ca_r])
        p.add("dve", lambda e: e.tensor_tensor(out=ch[:, :], in0=cin[:, 288:576], in1=mk[:, 0:288], op=ALU.mult),
              reads=[cin_r, mk_r], writes=[ch_r])
        p.add("dve", lambda e: e.tensor_tensor_scan(out=cs[:, :, :, :].rearrange("p w k r -> p (w k r)"), data0=ca[:, :], data1=ch[:, :],
                                                    initial=0.0, op0=ALU.mult, op1=ALU.add), reads=[ca_r, ch_r], writes=[cs_r])
        gz = [(cg, cgr), (cv, cvr), (gg, ggr), (rstd, rstdr)]
        g1v = g1T.rearrange("(kc p) n -> p kc n", p=128)
        zv = zT.rearrange("(kc p) n -> p kc n", p=128)
        zst = [cg, cv]
        zstr = [cgr, cvr]
        for kc in range(KC):
            gb, gr = sq[kc % 2], sqr[kc % 2]
            zb, zr = zst[kc % 2], zstr[kc % 2]
            p.add("sp", lambda e, kc=kc, gb=gb: e.dma_start(out=gb[:, :], in_=g1v[:, kc, :]), writes=[gr], dma="w")
            p.add("sp", lambda e, kc=kc, zb=zb: e.dma_start(out=zb[:, :], in_=zv[:, kc, :]), writes=[zr], dma="w")
            p.add("dve", lambda e, kc=kc, gb=gb, zb=zb: e.scalar_tensor_tensor(
                out=h[:, kc, 0:2], in0=zb[:, 0:2], scalar=cs[:, 1, kc, 8:9], in1=gb[:, 0:2],
                op0=ALU.mult, op1=ALU.add), reads=[gr, zr, cs_r], writes=[hres[kc]])
            p.add("dve", lambda e, kc=kc, gb=gb, zb=zb: e.scalar_tensor_tensor(
                out=h[:, kc, 2:NCOL], in0=zb[:, 2:NCOL], scalar=cs[:, 0, kc, 8:9], in1=gb[:, 2:NCOL],
                op0=ALU.mult, op1=ALU.add), reads=[gr, zr, cs_r], writes=[hres[kc]])
        load_x()
        wo = [wup[i][:, :, 0, :] for i in range(3)]
        wor = [wupr[i][0] for i in range(3)]
        wov = w_out.rearrange("(kc p) n -> p kc n", p=128)
        for m in range(KC):
            s = m % 3
            p.add("pool", lambda e, m=m, s=s: e.dma_start(out=wo[s], in_=wov[:, :, m * 128:(m + 1) * 128]),
                  writes=[wor[s]], dma="w")
            O, tl, pr = (OX, tilesX, psX) if m % 2 == 0 else (OY, tilesY, psY)
            for kc in range(KC):
                for (c0, c1) in tl:
                    p.add("pe", lambda e, s=s, kc=kc, c0=c0, c1=c1, O=O: e.matmul(
                        ps[:, O + c0:O + c1], lhsT=wo[s][:, kc, :], rhs=h[:, kc, c0:c1],
                        start=(kc == 0), stop=(kc == KC - 1)), reads=[wor[s], hres[kc]], writes=[pr])
            p.add("dve", lambda e, m=m, O=O: e.tensor_tensor(
                out=x[:, m, :], in0=ps[:, O:O + NCOL], in1=x[:, m, :], op=ALU.add),
                reads=[pr] + xres[m], writes=xres[m])

    emit_rmsnorm(cx, ps, psX, x, xres, gain, prm_r, h, hres, NCOL, ones, ones_r, scr, onesb=onesb)

    wupv = w_up.rearrange("(kc p) (two c) -> p kc two c", p=128, two=2)

    def load_up(j):
        s = j % 3
        for half in range(2):
            p.add("pool", lambda e, j=j, s=s, half=half: e.dma_start(
                out=wup[s][:, :, half, :], in_=wupv[:, :, half, j * 128:(j + 1) * 128]),
                writes=[wupr[s][half]], dma="w")

    def load_dn(q):
        s = q % 2
        src = w_down[q * 512:(q + 1) * 512, :].rearrange("(k p) n -> p k n", p=128)
        p.add("pool", lambda e, s=s, src=src: e.dma_start(out=wdn[s][:, :, :], in_=src), writes=[wdnr[s]], dma="w")

    def up_half(j, half, phase):
        s = j % 3
        slot = j % 8
        if True:
            O, tl, pr = (OX, tilesX, psX) if half == 0 else (OY, tilesY, psY)
            ch = half * 48 + j
            for kc in (range(KC) if phase == 0 else ()):
                for (c0, c1) in tl:
                    p.add("pe", lambda e, s=s, kc=kc, half=half, c0=c0, c1=c1, O=O: e.matmul(
                        ps[:, O + c0:O + c1], lhsT=wup[s][:, kc, half, :], rhs=h[:, kc, c0:c1],
                        start=(kc == 0), stop=(kc == KC - 1)), reads=[wupr[s][half], hres[kc]], writes=[pr])
            if phase == 0:
                return
            c, cr = (cg, cgr) if half == 0 else (cv, cvr)
            P = ps[:, O:O + NCOL]
            p.add("act", lambda e, c=c, P=P, ch=ch: e.activation(
                out=c[:, 0:NT], in_=P[:, 2:2 + NT], func=AF.Identity,
                bias=cb[:, ch:ch + 1], scale=cw[:, 192 + ch:192 + ch + 1]), reads=[pr, prm_r], writes=[cr])
            p.add("dve", lambda e, c=c, P=P, ch=ch: e.scalar_tensor_tensor(
                out=c[:, 0:NT], in0=P[:, 1:1 + NT], scalar=cw[:, 96 + ch:96 + ch + 1], in1=c[:, 0:NT],
                op0=ALU.mult, op1=ALU.add), reads=[pr, prm_r, cr], writes=[cr])
            p.add("dve", lambda e, c=c, P=P, ch=ch: e.scalar_tensor_tensor(
                out=c[:, 0:NT], in0=P[:, 0:NT], scalar=cw[:, ch:ch + 1], in1=c[:, 0:NT],
                op0=ALU.mult, op1=ALU.add), reads=[pr, prm_r, cr], writes=[cr])
            if half == 0:
                p.add("act", lambda e: e.activation(out=gg[:, :], in_=cg[:, 0:NT], func=AF.Gelu_apprx_tanh),
                      reads=[cgr], writes=[ggr])
        if half == 1:
            p.add("pool", lambda e, slot=slot: e.tensor_tensor(out=act[slot][:, :], in0=gg[:, :], in1=cv[:, 0:NT], op=ALU.mult),
                  reads=[ggr, cvr], writes=[actr[slot]])

    def down_part(q, part, last):
        s = q % 2
        for m in range(part * 2, part * 2 + 2):
            for tt in range(2):
                b = (m * 2 + tt) % 2
                for k in range(4):
                    slot = (q * 4 + k) % 8
                    p.add("pe", lambda e, s=s, k=k, m=m, tt=tt, b=b, slot=slot: e.matmul(
                        ps[:, BANK[b]:BANK[b] + 512], lhsT=wdn[s][:, k, m * 128:(m + 1) * 128],
                        rhs=act[slot][:, tt * 512:(tt + 1) * 512], start=(k == 0), stop=(k == 3)),
                        reads=[wdnr[s], actr[slot]], writes=[bankr[b]])
                p.add("dve", lambda e, m=m, tt=tt, b=b: e.tensor_tensor(
                    out=x[:, m, 2 + tt * 512:2 + (tt + 1) * 512], in0=ps[:, BANK[b]:BANK[b] + 512],
                    in1=x[:, m, 2 + tt * 512:2 + (tt + 1) * 512], op=ALU.add),
                    reads=[bankr[b], xres[m][1 + tt]], writes=[xres[m][1 + tt]])
            if last:
                yv = yT.rearrange("(kc p) n -> p kc n", p=128)
                p.add("sp", lambda e, m=m: e.dma_start(out=yv[:, m, :], in_=x[:, m, 2:2 + NT]),
                      reads=[xres[m][1], xres[m][2]], dma="r")

    NP = 48
    load_up(0)
    load_up(1)
    load_dn(0)
    for j in range(NP + 4):
        q = j // 4 - 1
        for half in range(2):
            if j < NP:
                if half == 0 and j + 2 < NP:
                    load_up(j + 2)
                up_half(j, half, 0)
            if q >= 0:
                if j % 4 == 0 and half == 0 and q + 1 < NP // 4:
                    load_dn(q + 1)
                down_part(q, (j % 4) * 2 + half, last=(q == NP // 4 - 1))
            if j < NP:
                up_half(j, half, 1)
    return cx.finish()


def chunked(v):
    v = np.asarray(v, np.float32)
    return np.ascontiguousarray(v.reshape(-1, 128).T)


def ffn_params(g, cw, cb):
    parts = [chunked(g)] + [chunked(cw[k]) for k in range(3)] + [chunked(cb)]
    return np.ascontiguousarray(np.concatenate(parts, axis=1))


def build_rec1():
    cx = Ctx()
    nc, p = cx.nc, cx.p
    HALO = 3
    NCOL = NT + HALO
    xT = cx.din("xT", [D, NCOL])
    w_in = cx.din("w_in", [D, 2 * D])
    w_a = cx.din("w_a", [8, 256, 256])
    w_i = cx.din("w_i", [8, 256, 256])
    NPRM = 16 + 64 + 16 * 4
    prm_d = cx.din("prm", [128, NPRM])
    g1T = cx.dout("g1T", [D, NT])
    zT = cx.dout("zT", [D, NT])
    carr_o = cx.dout("carr", [128, 32])

    ps = cx.psum()
    x = cx.sb("x", [128, KC, NCOL], F32)
    h = cx.sb("h", [128, KC, NCOL], BF16)
    sq = [cx.sb(f"sq{i}", [128, NCOL], F32) for i in range(2)]
    rstd = cx.sb("rstd", [128, NCOL], F32)
    prm = cx.sb("prm_sb", [128, NPRM], F32)
    ones = cx.sb("ones", [128, 128], F32)
    epsb = cx.sb("epsb", [128, 1], F32)
    win = [cx.sb(f"win{i}", [128, KC, 128], BF16) for i in range(4)]
    wg = [[cx.sb(f"wg{i}_{j}", [128, 2, 256], BF16) for j in range(2)] for i in range(2)]
    xc = [cx.sb(f"xc{i}", [128, NT], F32) for i in range(2)]
    xcb = [cx.sb(f"xcb{i}", [128, NT], BF16) for i in range(2)]
    gate = [cx.sb(f"gate{i}", [128, NT], F32) for i in range(2)]
    tr = cx.sb("tr", [128, NT], F32)
    ta = cx.sb("ta", [128, NT], F32)
    tm = cx.sb("tm", [128, NT], F32)
    ti = cx.sb("ti", [128, NT], F32)
    ths = cx.sb("ths", [128, NT], F32)
    tac = cx.sb("tac", [128, NT], F32)
    tg1 = cx.sb("tg1", [128, NT], F32)
    tz = cx.sb("tz", [128, NT], F32)
    zeros = cx.sb("zeros", [128, NT], F32)
    cl = cx.sb("cl", [128, 48], F32)
    carr = cx.sb("carr_sb", [128, 32], F32)

    xres = [[Res(f"x{m}")] for m in range(KC)]
    hres = [Res(f"h{k}") for k in range(KC)]
    sqr = [Res("sq0"), Res("sq1")]
    rstdr, prm_r, ones_r = Res("rstd"), Res("prm"), Res("ones")
    winr = [Res(f"win{i}") for i in range(4)]
    wgr = [[Res(f"wg{i}_{j}") for j in range(2)] for i in range(2)]
    xcr = [Res("xc0"), Res("xc1")]
    xcbr = [Res("xcb0"), Res("xcb1")]
    gater = [Res("gate0"), Res("gate1")]
    trr, tar, tmr, tir, thsr, tacr, tg1r, tzr = (Res(n) for n in ("tr", "ta", "tm", "ti", "ths", "tac", "tg1", "tz"))
    zer_r, cl_r, carr_r = Res("zeros"), Res("cl"), Res("carr")
    psX, psY = Res("psX"), Res("psY")
    bankr = [Res(f"bank{i}") for i in range(3)]
    BANK = [2560, 3072, 3584]
    tilesX = token_tiles(NCOL, 0)
    OY = 1536

    gain = prm[:, 0:16]
    cw = prm[:, 16:80]
    cb = prm[:, 80:96]
    ba = prm[:, 96:112]
    bi = prm[:, 112:128]
    lam = prm[:, 128:144]
    scr = {"sq": sq, "sqr": sqr, "rstd": rstd, "rstdr": rstdr, "eps": epsb}

    p.add("pool", lambda e: e.memset(ones[:, :], 1.0), writes=[ones_r])
    p.add("pool", lambda e: e.memset(epsb[:, :], EPS), writes=[prm_r])
    p.add("pool", lambda e: e.memset(zeros[:, :], 0.0), writes=[zer_r])
    p.add("sp", lambda e: e.dma_start(out=prm[:, :], in_=prm_d[:, :]), writes=[prm_r], dma="w")
    xv = xT.rearrange("(kc p) n -> p kc n", p=128)
    for kc in range(KC):
        p.add("sp", lambda e, kc=kc: e.dma_start(out=x[:, kc, :], in_=xv[:, kc, :]), writes=xres[kc], dma="w")

    winv = w_in.rearrange("(kc p) n -> p kc n", p=128)
    nload = [0]

    def load_in(col0):
        s = nload[0] % 4
        nload[0] += 1
        p.add("pool", lambda e, s=s, col0=col0: e.dma_start(out=win[s][:, :, :], in_=winv[:, :, col0:col0 + 128]),
              writes=[winr[s]], dma="w")
        return s

    def load_g(b):
        s = b % 2
        for j, wsrc in enumerate((w_a, w_i)):
            p.add("pool", lambda e, s=s, j=j, wsrc=wsrc, b=b: e.dma_start(
                out=wg[s][j][:, :, :], in_=wsrc[b].rearrange("(ic p) n -> p ic n", p=128)),
                writes=[wgr[s][j]], dma="w")

    p.add("act", lambda e: e.activation(out=cl[:, 0:16], in_=lam, func=AF.Exp, scale=-1.0), reads=[prm_r], writes=[cl_r])
    p.add("act", lambda e: e.activation(out=cl[:, 0:16], in_=cl[:, 0:16], func=AF.Ln, bias=1.0), reads=[cl_r], writes=[cl_r])
    p.add("dve", lambda e: e.tensor_scalar(out=cl[:, 16:32], in0=cl[:, 0:16], scalar1=-8.0, scalar2=None, op0=ALU.mult),
          reads=[cl_r], writes=[cl_r])
    p.add("dve", lambda e: e.tensor_scalar(out=cl[:, 32:48], in0=cl[:, 0:16], scalar1=-16.0, scalar2=None, op0=ALU.mult),
          reads=[cl_r], writes=[cl_r])

    emit_rmsnorm(cx, ps, psX, x, xres, gain, prm_r, h, hres, NCOL, ones, ones_r, scr)

    pending = [load_in(0), load_in(D)]
    load_g(0)
    order = []
    for b in range(8):
        for c in range(2):
            ch = 2 * b + c
            order.append(ch * 128)
            order.append(D + ch * 128)
    li = 2
    for b in range(8):
        if b + 1 < 8:
            load_g(b + 1)
        for c in range(2):
            ch = 2 * b + c
            s = pending.pop(0)
            if li < len(order):
                pending.append(load_in(order[li])); li += 1
            for kc in range(KC):
                for (c0, c1) in tilesX:
                    p.add("pe", lambda e, s=s, kc=kc, c0=c0, c1=c1: e.matmul(
                        ps[:, c0:c1], lhsT=win[s][:, kc, :], rhs=h[:, kc, c0:c1],
                        start=(kc == 0), stop=(kc == KC - 1)), reads=[winr[s], hres[kc]], writes=[psX])
            P = ps[:, 0:NCOL]
            t = xc[c]
            p.add("act", lambda e, t=t, P=P, ch=ch: e.activation(
                out=t[:, :], in_=P[:, 3:3 + NT], func=AF.Identity, bias=cb[:, ch:ch + 1],
                scale=cw[:, 48 + ch:48 + ch + 1]), reads=[psX, prm_r], writes=[xcr[c]])
            for k in (2, 1, 0):
                p.add("dve", lambda e, t=t, P=P, ch=ch, k=k: e.scalar_tensor_tensor(
                    out=t[:, :], in0=P[:, k:k + NT], scalar=cw[:, k * 16 + ch:k * 16 + ch + 1], in1=t[:, :],
                    op0=ALU.mult, op1=ALU.add), reads=[psX, prm_r, xcr[c]], writes=[xcr[c]])
            p.add("pool", lambda e, c=c: e.tensor_copy(out=xcb[c][:, :], in_=xc[c][:, :]), reads=[xcr[c]], writes=[xcbr[c]])
            s = pending.pop(0)
            if li < len(order):
                pending.append(load_in(order[li])); li += 1
            for kc in range(KC):
                for tt in range(2):
                    p.add("pe", lambda e, s=s, kc=kc, tt=tt: e.matmul(
                        ps[:, OY + tt * 512:OY + (tt + 1) * 512], lhsT=win[s][:, kc, :],
                        rhs=h[:, kc, HALO + tt * 512:HALO + (tt + 1) * 512],
                        start=(kc == 0), stop=(kc == KC - 1)), reads=[winr[s], hres[kc]], writes=[psY])
            p.add("act", lambda e, c=c: e.activation(out=gate[c][:, :], in_=ps[:, OY:OY + NT], func=AF.Gelu_apprx_tanh),
                  reads=[psY], writes=[gater[c]])
        gs = b % 2
        for oc in range(2):
            ch = 2 * b + oc
            for j in range(2):
                dst, dres = (tr, trr) if j == 0 else (ti, tir)
                bias = ba if j == 0 else bi
                for tt in range(2):
                    bk = (oc * 4 + j * 2 + tt) % 3
                    for ic in range(2):
                        p.add("pe", lambda e, gs=gs, j=j, ic=ic, oc=oc, tt=tt, bk=bk: e.matmul(
                            ps[:, BANK[bk]:BANK[bk] + 512], lhsT=wg[gs][j][:, ic, oc * 128:(oc + 1) * 128],
                            rhs=xcb[ic][:, tt * 512:(tt + 1) * 512], start=(ic == 0), stop=(ic == 1)),
                            reads=[wgr[gs][j], xcbr[ic]], writes=[bankr[bk]])
                    p.add("act", lambda e, dst=dst, bias=bias, ch=ch, tt=tt, bk=bk: e.activation(
                        out=dst[:, tt * 512:(tt + 1) * 512], in_=ps[:, BANK[bk]:BANK[bk] + 512], func=AF.Sigmoid,
                        bias=bias[:, ch:ch + 1]), reads=[bankr[bk], prm_r], writes=[dres])
            p.add("act", lambda e, ch=ch: e.activation(out=ta[:, :], in_=tr[:, :], func=AF.Exp, scale=cl[:, 16 + ch:17 + ch]),
                  reads=[trr, cl_r], writes=[tar])
            p.add("act", lambda e, ch=ch: e.activation(out=tm[:, :], in_=tr[:, :], func=AF.Exp, scale=cl[:, 32 + ch:33 + ch]),
                  reads=[trr, cl_r], writes=[tmr])
            p.add("act", lambda e: e.activation(out=tm[:, :], in_=tm[:, :], func=AF.Sqrt, scale=-1.0, bias=1.0),
                  reads=[tmr], writes=[tmr])
            p.add("dve", lambda e, oc=oc: e.tensor_tensor(out=ti[:, :], in0=ti[:, :], in1=xc[oc][:, :], op=ALU.mult),
                  reads=[tir, xcr[oc]], writes=[tir])
            p.add("dve", lambda e: e.tensor_tensor(out=ti[:, :], in0=ti[:, :], in1=tm[:, :], op=ALU.mult),
                  reads=[tir, tmr], writes=[tir])
            p.add("dve", lambda e: e.tensor_tensor_scan(out=ths[:, :], data0=ta[:, :], data1=ti[:, :], initial=0.0,
                                                        op0=ALU.mult, op1=ALU.add), reads=[tar, tir], writes=[thsr])
            p.add("dve", lambda e: e.tensor_tensor_scan(out=tac[:, :], data0=ta[:, :], data1=zeros[:, :], initial=1.0,
                                                        op0=ALU.mult, op1=ALU.add), reads=[tar, zer_r], writes=[tacr])
            p.add("pool", lambda e, oc=oc: e.tensor_tensor(out=tg1[:, :], in0=ths[:, :], in1=gate[oc][:, :], op=ALU.mult),
                  reads=[thsr, gater[oc]], writes=[tg1r])
            p.add("pool", lambda e, oc=oc: e.tensor_tensor(out=tz[:, :], in0=tac[:, :], in1=gate[oc][:, :], op=ALU.mult),
                  reads=[tacr, gater[oc]], writes=[tzr])
            p.add("dve", lambda e, ch=ch: e.tensor_copy(out=carr[:, ch:ch + 1], in_=tac[:, NT - 1:NT]), reads=[tacr], writes=[carr_r])
            p.add("dve", lambda e, ch=ch: e.tensor_copy(out=carr[:, 16 + ch:17 + ch], in_=ths[:, NT - 1:NT]), reads=[thsr], writes=[carr_r])
            p.add("sp", lambda e, ch=ch: e.dma_start(out=g1T[ch * 128:(ch + 1) * 128, :], in_=tg1[:, :]), reads=[tg1r], dma="r")
            p.add("sp", lambda e, ch=ch: e.dma_start(out=zT[ch * 128:(ch + 1) * 128, :], in_=tz[:, :]), reads=[tzr], dma="r")
    p.add("sp", lambda e: e.dma_start(out=carr_o[:, :], in_=carr[:, :]), reads=[carr_r], dma="r")
    return cx.finish()


def build_rec1b():
    cx = Ctx()
    nc, p = cx.nc, cx.p
    HALO = 3
    NCOL = NT + HALO
    xT = cx.din("xT", [D, NCOL])
    w_in = cx.din("w_in", [D, 2 * D])
    w_a = cx.din("w_a", [8, 256, 256])
    w_i = cx.din("w_i", [8, 256, 256])
    NPRM = 16 + 64 + 16 * 4
    prm_d = cx.din("prm", [128, NPRM])
    g1T = cx.dout("g1T", [D, NT])
    zT = cx.dout("zT", [D, NT])
    carr_o = cx.dout("carr", [128, 32])

    ps = cx.psum()
    x = cx.sb("x", [128, KC, NCOL], F32)
    h = cx.sb("h", [128, KC, NCOL], BF16)
    sq = [cx.sb(f"sq{i}", [128, NCOL], F32) for i in range(2)]
    rstd = cx.sb("rstd", [128, NCOL], F32)
    prm = cx.sb("prm_sb", [128, NPRM], F32)
    ones = cx.sb("ones", [128, 128], F32)
    onesb = cx.sb("onesb", [128, 128], BF16)
    epsb = cx.sb("epsb", [128, 1], F32)
    win = [cx.sb(f"win{i}", [128, KC, 128], BF16) for i in range(3)]
    wg = [[cx.sb(f"wg{i}_{j}", [128, 2, 256], BF16) for j in range(2)] for i in range(2)]
    xc = [[cx.sb(f"xc{i}_{c}", [128, NT], F32) for c in range(2)] for i in range(2)]
    xcb = [[cx.sb(f"xcb{i}_{c}", [128, NT], BF16) for c in range(2)] for i in range(2)]
    gate = [[cx.sb(f"gate{i}_{c}", [128, NT], F32) for c in range(2)] for i in range(2)]
    tr = [cx.sb(f"tr{o}", [128, NT], F32) for o in range(2)]
    ti = [cx.sb(f"ti{o}", [128, NT], F32) for o in range(2)]
    ta = [cx.sb(f"ta{o}", [128, NT], F32) for o in range(2)]
    ths = cx.sb("ths", [128, NT], F32)
    tac = cx.sb("tac", [128, NT], F32)
    tg1 = sq[0][:, 0:NT]
    tz = sq[1][:, 0:NT]
    zeros = rstd[:, 0:NT]
    cl = cx.sb("cl", [128, 48], F32)
    carr = cx.sb("carr_sb", [128, 32], F32)

    xres = [[Res(f"x{m}")] for m in range(KC)]
    hres = [Res(f"h{k}") for k in range(KC)]
    sqr = [Res("sq0"), Res("sq1")]
    rstdr, prm_r, ones_r = Res("rstd"), Res("prm"), Res("ones")
    winr = [Res(f"win{i}") for i in range(3)]
    wgr = [[Res(f"wg{i}_{j}") for j in range(2)] for i in range(2)]
    xcr = [[Res(f"xc{i}_{c}") for c in range(2)] for i in range(2)]
    xcbr = [[Res(f"xcb{i}_{c}") for c in range(2)] for i in range(2)]
    gater = [[Res(f"gate{i}_{c}") for c in range(2)] for i in range(2)]
    trr = [Res("tr0"), Res("tr1")]
    tir = [Res("ti0"), Res("ti1")]
    tar = [Res("ta0"), Res("ta1")]
    thsr, tacr = Res("ths"), Res("tac")
    tg1r, tzr = sqr[0], sqr[1]
    zer_r, cl_r, carr_r = rstdr, Res("cl"), Res("carr")
    _px = Res("psX")
    psXr = [_px, _px]
    OXs = [0, 0]
    bankr = [Res("bank5"), Res("bank6"), Res("bank7")]
    psY = [Res("psY0"), Res("psY1")]
    BANK = [2560, 3072, 3584]
    tilesX = token_tiles(NCOL, 0)
    OY = 1536

    gain = prm[:, 0:16]
    cw = prm[:, 16:80]
    cb = prm[:, 80:96]
    ba = prm[:, 96:112]
    bi = prm[:, 112:128]
    lam = prm[:, 128:144]
    scr = {"sq": sq, "sqr": sqr, "rstd": rstd, "rstdr": rstdr, "eps": epsb}

    p.add("pool", lambda e: e.memset(ones[:, :], 1.0), writes=[ones_r])
    p.add("pool", lambda e: e.memset(onesb[:, :], 1.0), writes=[ones_r])
    p.add("pool", lambda e: e.memset(epsb[:, :], EPS), writes=[prm_r])
    p.add("sp", lambda e: e.dma_start(out=prm[:, :], in_=prm_d[:, :]), writes=[prm_r], dma="w")
    xv = xT.rearrange("(kc p) n -> p kc n", p=128)
    for kc in range(KC):
        p.add("sp", lambda e, kc=kc: e.dma_start(out=x[:, kc, :], in_=xv[:, kc, :]), writes=xres[kc], dma="w")

    winv = w_in.rearrange("(kc p) n -> p kc n", p=128)
    nload = [0]

    def load_in(col0):
        s = nload[0] % 3
        nload[0] += 1
        p.add("pool", lambda e, s=s, col0=col0: e.dma_start(out=win[s][:, :, :], in_=winv[:, :, col0:col0 + 128]),
              writes=[winr[s]], dma="w")
        return s

    def load_g(b):
        s = b % 2
        for j, wsrc in enumerate((w_a, w_i)):
            p.add("pool", lambda e, s=s, j=j, wsrc=wsrc, b=b: e.dma_start(
                out=wg[s][j][:, :, :], in_=wsrc[b].rearrange("(ic p) n -> p ic n", p=128)),
                writes=[wgr[s][j]], dma="w")

    p.add("act", lambda e: e.activation(out=cl[:, 0:16], in_=lam, func=AF.Exp, scale=-1.0), reads=[prm_r], writes=[cl_r])
    p.add("act", lambda e: e.activation(out=cl[:, 0:16], in_=cl[:, 0:16], func=AF.Ln, bias=1.0), reads=[cl_r], writes=[cl_r])
    p.add("dve", lambda e: e.tensor_scalar(out=cl[:, 16:32], in0=cl[:, 0:16], scalar1=-8.0, scalar2=None, op0=ALU.mult),
          reads=[cl_r], writes=[cl_r])
    p.add("dve", lambda e: e.tensor_scalar(out=cl[:, 32:48], in0=cl[:, 0:16], scalar1=-16.0, scalar2=None, op0=ALU.mult),
          reads=[cl_r], writes=[cl_r])

    emit_rmsnorm(cx, ps, psXr[0], x, xres, gain, prm_r, h, hres, NCOL, ones, ones_r, scr, onesb=onesb)

    p.add("pool", lambda e: e.memset(zeros, 0.0), reads=list(hres), writes=[zer_r])

    order = []
    for b in range(8):
        for c in range(2):
            ch = 2 * b + c
            order.append(ch * 128)
            order.append(D + ch * 128)
    pending = [load_in(order[0]), load_in(order[1])]
    li = [2]
    load_g(0)

    def nextw():
        s = pending.pop(0)
        if li[0] < len(order):
            pending.append(load_in(order[li[0]]))
            li[0] += 1
        return s

    def front(b):
        bp = b % 2
        for c in range(2):
            ch = 2 * b + c
            s = nextw()
            for kc in range(KC):
                for (c0, c1) in tilesX:
                    p.add("pe", lambda e, s=s, kc=kc, c0=c0, c1=c1, c=c: e.matmul(
                        ps[:, OXs[c] + c0:OXs[c] + c1], lhsT=win[s][:, kc, :], rhs=h[:, kc, c0:c1],
                        start=(kc == 0), stop=(kc == KC - 1)), reads=[winr[s], hres[kc]], writes=[psXr[c]])
            psX = psXr[c]
            P = ps[:, OXs[c]:OXs[c] + NCOL]
            t = xc[bp][c]
            tres = xcr[bp][c]
            p.add("act", lambda e, t=t, P=P, ch=ch: e.activation(
                out=t[:, :], in_=P[:, 3:3 + NT], func=AF.Identity, bias=cb[:, ch:ch + 1],
                scale=cw[:, 48 + ch:48 + ch + 1]), reads=[psX, prm_r], writes=[tres])
            for k in (2, 1, 0):
                p.add("dve", lambda e, t=t, P=P, ch=ch, k=k: e.scalar_tensor_tensor(
                    out=t[:, :], in0=P[:, k:k + NT], scalar=cw[:, k * 16 + ch:k * 16 + ch + 1], in1=t[:, :],
                    op0=ALU.mult, op1=ALU.add), reads=[psX, prm_r, tres], writes=[tres])
            p.add("act", lambda e, t=t, bp=bp, c=c: e.activation(out=xcb[bp][c][:, :], in_=t[:, :], func=AF.Identity),
                  reads=[tres], writes=[xcbr[bp][c]])
            s = nextw()
            for kc in range(KC):
                for tt in range(2):
                    p.add("pe", lambda e, s=s, kc=kc, tt=tt: e.matmul(
                        ps[:, OY + tt * 512:OY + (tt + 1) * 512], lhsT=win[s][:, kc, :],
                        rhs=h[:, kc, HALO + tt * 512:HALO + (tt + 1) * 512],
                        start=(kc == 0), stop=(kc == KC - 1)), reads=[winr[s], hres[kc]], writes=[psY[tt]])
            p.add("act", lambda e, bp=bp, c=c: e.activation(out=gate[bp][c][:, :], in_=ps[:, OY:OY + NT], func=AF.Gelu_apprx_tanh),
                  reads=psY, writes=[gater[bp][c]])

    def back(b):
        bp = b % 2
        gs = b % 2
        for oc in range(2):
            ch = 2 * b + oc
            for j in range(2):
                dst, dres = (tr[oc], trr[oc]) if j == 0 else (ti[oc], tir[oc])
                bias = ba if j == 0 else bi
                for tt in range(2):
                    bkk = (oc * 4 + j * 2 + tt) % 3
                    for ic in range(2):
                        p.add("pe", lambda e, j=j, ic=ic, oc=oc, tt=tt, bkk=bkk: e.matmul(
                            ps[:, BANK[bkk]:BANK[bkk] + 512], lhsT=wg[gs][j][:, ic, oc * 128:(oc + 1) * 128],
                            rhs=xcb[bp][ic][:, tt * 512:(tt + 1) * 512], start=(ic == 0), stop=(ic == 1)),
                            reads=[wgr[gs][j], xcbr[bp][ic]], writes=[bankr[bkk]])
                    p.add("act", lambda e, dst=dst, bias=bias, ch=ch, tt=tt, bkk=bkk: e.activation(
                        out=dst[:, tt * 512:(tt + 1) * 512], in_=ps[:, BANK[bkk]:BANK[bkk] + 512], func=AF.Sigmoid,
                        bias=bias[:, ch:ch + 1]), reads=[bankr[bkk], prm_r], writes=[dres], relaxed=True)
        for oc in range(2):
            ch = 2 * b + oc
            p.add("act", lambda e, ch=ch, oc=oc: e.activation(out=ta[oc][:, :], in_=tr[oc][:, :], func=AF.Exp,
                                                             scale=cl[:, 16 + ch:17 + ch]), reads=[trr[oc], cl_r], writes=[tar[oc]])
            p.add("act", lambda e, ch=ch, oc=oc: e.activation(out=tr[oc][:, :], in_=tr[oc][:, :], func=AF.Exp,
                                                             scale=cl[:, 32 + ch:33 + ch]), reads=[trr[oc], cl_r], writes=[trr[oc]])
        for oc in range(2):
            p.add("act", lambda e, oc=oc: e.activation(out=tr[oc][:, :], in_=tr[oc][:, :], func=AF.Sqrt, scale=-1.0, bias=1.0),
                  reads=[trr[oc]], writes=[trr[oc]])
        for oc in range(2):
            ch = 2 * b + oc
            p.add("dve", lambda e, oc=oc: e.tensor_tensor(out=ti[oc][:, :], in0=ti[oc][:, :], in1=xc[bp][oc][:, :], op=ALU.mult),
                  reads=[tir[oc], xcr[bp][oc]], writes=[tir[oc]])
            p.add("dve", lambda e, oc=oc: e.tensor_tensor(out=ti[oc][:, :], in0=ti[oc][:, :], in1=tr[oc][:, :], op=ALU.mult),
                  reads=[tir[oc], trr[oc]], writes=[tir[oc]])
            p.add("dve", lambda e, oc=oc: e.tensor_tensor_scan(out=ths[:, :], data0=ta[oc][:, :], data1=ti[oc][:, :], initial=0.0,
                                                               op0=ALU.mult, op1=ALU.add), reads=[tar[oc], tir[oc]], writes=[thsr])
            p.add("dve", lambda e, oc=oc: e.tensor_tensor_scan(out=tac[:, :], data0=ta[oc][:, :], data1=zeros, initial=1.0,
                                                               op0=ALU.mult, op1=ALU.add), reads=[tar[oc], zer_r], writes=[tacr])
            p.add("dve", lambda e, oc=oc: e.tensor_tensor(out=tg1, in0=ths[:, :], in1=gate[bp][oc][:, :], op=ALU.mult),
                  reads=[thsr, gater[bp][oc]], writes=[tg1r])
            p.add("dve", lambda e, oc=oc: e.tensor_tensor(out=tz, in0=tac[:, :], in1=gate[bp][oc][:, :], op=ALU.mult),
                  reads=[tacr, gater[bp][oc]], writes=[tzr])
            p.add("dve", lambda e, ch=ch: e.tensor_copy(out=carr[:, ch:ch + 1], in_=tac[:, NT - 1:NT]), reads=[tacr], writes=[carr_r])
            p.add("dve", lambda e, ch=ch: e.tensor_copy(out=carr[:, 16 + ch:17 + ch], in_=ths[:, NT - 1:NT]), reads=[thsr], writes=[carr_r])
            p.add("sp", lambda e, ch=ch: e.dma_start(out=g1T[ch * 128:(ch + 1) * 128, :], in_=tg1), reads=[tg1r], dma="r")
            p.add("sp", lambda e, ch=ch: e.dma_start(out=zT[ch * 128:(ch + 1) * 128, :], in_=tz), reads=[tzr], dma="r")

    load_g(1)
    for b in range(9):
        if b < 8:
            front(b)
        if b >= 1:
            back(b - 1)
            if b + 1 < 8:
                load_g(b + 1)
    p.add("sp", lambda e: e.dma_start(out=carr_o[:, :], in_=carr[:, :]), reads=[carr_r], dma="r")
    return cx.finish()


def rec_params(g, cw, cb, ba, bi, lam):
    parts = [chunked(g)] + [chunked(cw[k]) for k in range(4)] + [chunked(cb), chunked(ba.reshape(-1)), chunked(bi.reshape(-1)), chunked(lam)]
    return np.ascontiguousarray(np.concatenate(parts, axis=1))


def build_attn(stage=99):
    cx = Ctx()
    nc, p = cx.nc, cx.p
    HALO = 128
    NCOL = NT + HALO
    NTB = NCOL // 128
    xT = cx.din("xT", [D, NCOL])
    w_qkv = cx.din("w_qkv", [D, 3072])
    w_o = cx.din("w_o", [D, D])
    NPRM = 16 + 2 + 32 + 1
    prm_d = cx.din("prm", [128, NPRM])
    msk_d = cx.din("msk", [128, 1024])
    yT = cx.dout("yT", [D, NT])

    ps = cx.psum()
    x = cx.sb("x", [128, KC, NCOL], F32)
    h = cx.sb("h", [128, KC, NCOL], BF16)
    sq = [cx.sb(f"sq{i}", [128, NCOL], F32) for i in range(2)]
    rstd = cx.sb("rstd", [128, NCOL], F32)
    prm = cx.sb("prm_sb", [128, NPRM], F32)
    es = cx.sb("es", [128, 32], F32)
    ones = cx.sb("ones", [128, 128], F32)
    bd = cx.sb("bd", [128, 128], F32)
    epsb = cx.sb("epsb", [128, 1], F32)
    msk = cx.sb("msk_sb", [128, 2, 512], BF16)
    Kd = cx.sb("Kd", [128, 8, NCOL], BF16)
    Vaug = cx.sb("Vaug", [128, NTB, 8, 128], BF16)
    wbig = cx.sb("wbig", [128, 8192], BF16)
    wq = [cx.sb(f"wq{i}", [128, KC, 128], BF16) for i in range(2)]
    Qn = cx.sb("Qn", [128, 4, NT], BF16)
    AO = cx.sb("AO", [128, 2, NT], BF16)
    PT = [cx.sb(f"PT{i}", [128, 2, 512], BF16) for i in range(2)]

    xres = [[Res(f"x{m}_h"), Res(f"x{m}_0"), Res(f"x{m}_1")] for m in range(KC)]
    hres = [Res(f"h{k}") for k in range(KC)]
    sqr = [Res("sq0"), Res("sq1")]
    rstdr, prm_r, ones_r, bd_r, es_r, msk_r = (Res(n) for n in ("rstd", "prm", "ones", "bd", "es", "msk"))
    Kdr = [Res(f"Kd{g}") for g in range(8)]
    Vr = Res("Vaug")
    wbr = [Res("wb0"), Res("wb1")]
    wqr = [[Res(f"wq{i}_0"), Res(f"wq{i}_1")] for i in range(2)]
    Qnr = [Res(f"Qn{i}") for i in range(4)]
    AOr = [Res("AO0"), Res("AO1")]
    PTr = [[Res(f"PT{i}_{k}") for k in range(2)] for i in range(2)]
    sqt = [sq[0][:, 0:512], sq[0][:, 512:1024]]
    rsb = [sq[1][:, 0:512], sq[1][:, 512:1024]]
    den = rstd[:, 0:512]
    rec = rstd[:, 512:1024]
    sqtr = [Res("sqt0"), Res("sqt1")]
    rsr = [Res("rs0"), Res("rs1")]
    denr, recr = Res("den"), Res("rec")
    regr = [Res("regA"), Res("regB")]
    REG = [0, 1536]
    statr = Res("stat")
    STAT = 3072
    pvr = Res("pv")
    PVB = 3584
    SC = [1024, 2560]
    scr_ = [Res("sc0"), Res("sc1")]
    OB = [0, 512, 1536, 2048]
    obr = [Res(f"ob{i}") for i in range(4)]

    gain = prm[:, 0:16]
    qg = prm[:, 16:17]
    kg = prm[:, 17:18]
    hb = prm[:, 50:51]
    scr = {"sq": sq, "sqr": sqr, "rstd": rstd, "rstdr": rstdr, "eps": epsb}

    p.add("pool", lambda e: e.memset(ones[:, :], 1.0), writes=[ones_r])
    p.add("pool", lambda e: e.memset(epsb[:, :], EPS), writes=[prm_r])
    p.add("pool", lambda e: e.memset(bd[:, :], 0.0), writes=[bd_r])
    p.add("pool", lambda e: e.memset(bd[0:64, 0:64], 1.0), writes=[bd_r])
    p.add("pool", lambda e: e.memset(bd[64:128, 64:128], 1.0), writes=[bd_r])
    p.add("pool", lambda e: e.memset(Vaug[:, :, :, :], 1.0), writes=[Vr])
    p.add("sp", lambda e: e.dma_start(out=prm[:, :], in_=prm_d[:, :]), writes=[prm_r], dma="w")
    p.add("pool", lambda e: e.dma_start(out=msk[:, :, :], in_=msk_d.rearrange("p (k n) -> p k n", k=2)),
          writes=[msk_r], dma="w")
    xv = xT.rearrange("(kc p) n -> p kc n", p=128)
    for kc in range(KC):
        p.add("sp", lambda e, kc=kc: e.dma_start(out=x[:, kc, :], in_=xv[:, kc, :]), writes=xres[kc], dma="w")
    p.add("act", lambda e: e.activation(out=es[:, :], in_=prm[:, 18:50], func=AF.Exp), reads=[prm_r], writes=[es_r])

    wv = wbig[:, :].rearrange("p (kc n) -> p kc n", kc=KC)
    wqkv_v = w_qkv.rearrange("(kc p) n -> p kc n", p=128)
    p.add("pool", lambda e: e.dma_start(out=wv, in_=wqkv_v[:, :, 2560:3072]), writes=wbr, dma="w")

    nq = [0]

    def load_k(g):
        s = nq[0] % 2
        nq[0] += 1
        for half in range(2):
            p.add("pool", lambda e, s=s, g=g, half=half: e.dma_start(
                out=wq[s][:, :, half * 64:(half + 1) * 64], in_=wqkv_v[:, :, 2048 + g * 64:2048 + (g + 1) * 64]),
                writes=[wqr[s][half]], dma="w")
        return s

    def load_q(hd):
        s = nq[0] % 2
        nq[0] += 1
        for half in range(2):
            p.add("pool", lambda e, s=s, hd=hd, half=half: e.dma_start(
                out=wq[s][:, :, half * 64:(half + 1) * 64], in_=wqkv_v[:, :, hd * 64:(hd + 1) * 64]),
                writes=[wqr[s][half]], dma="w")
        return s

    def load_o(g):
        s = g % 2
        dst = wbig[:, s * 4096:(s + 1) * 4096].rearrange("p (c n) -> p c n", c=2)
        src = w_o[g * 256:(g + 1) * 256, :].rearrange("(c p) n -> p c n", p=128)
        p.add("pool", lambda e, dst=dst, src=src: e.dma_start(out=dst, in_=src), writes=[wbr[s]], dma="w")

    emit_rmsnorm(cx, ps, regr[0], x, xres, gain, prm_r, h, hres, NCOL, ones, ones_r, scr)

    nstat = [0]

    def proj_norm(s, ri, tiles, hoff, gain_ap, dst_fn, dst_res):
        R = REG[ri]
        for kc in range(KC):
            for (c0, c1) in tiles:
                p.add("pe", lambda e, s=s, kc=kc, c0=c0, c1=c1, R=R: e.matmul(
                    ps[:, R + c0:R + c1], lhsT=wq[s][:, kc, :], rhs=h[:, kc, hoff + c0:hoff + c1],
                    start=(kc == 0), stop=(kc == KC - 1)), reads=wqr[s] + [hres[kc]], writes=[regr[ri]])
        for (c0, c1) in tiles:
            n = c1 - c0
            i = nstat[0] % 2
            nstat[0] += 1
            src = ps[:, R + c0:R + c1]
            p.add("act", lambda e, i=i, n=n, src=src: e.activation(out=sqt[i][:, 0:n], in_=src, func=AF.Square),
                  reads=[regr[ri]], writes=[sqtr[i]])
            p.add("pe", lambda e, i=i, n=n: e.matmul(ps[:, STAT:STAT + n], lhsT=bd[:, :], rhs=sqt[i][:, 0:n],
                                                     start=True, stop=True), reads=[bd_r, sqtr[i]], writes=[statr])
            p.add("act", lambda e, i=i, n=n: e.activation(out=sqt[i][:, 0:n], in_=ps[:, STAT:STAT + n], func=AF.Sqrt,
                                                          bias=epsb[:, 0:1], scale=1.0 / 64), reads=[statr, prm_r], writes=[sqtr[i]])
            p.add("dve", lambda e, i=i, n=n: e.reciprocal(out=rsb[i][:, 0:n], in_=sqt[i][:, 0:n]), reads=[sqtr[i]], writes=[rsr[i]])
            dst = dst_fn(c0, c1)
            p.add("dve", lambda e, i=i, n=n, src=src, dst=dst: e.scalar_tensor_tensor(
                out=dst, in0=src, scalar=gain_ap, in1=rsb[i][:, 0:n], op0=ALU.mult, op1=ALU.mult),
                reads=[regr[ri], rsr[i], prm_r], writes=[dst_res], relaxed=True)

    BV = [1536, 2048, 2560]
    bvr = [Res(f"bv{i}") for i in range(3)]
    for tb in range(NTB):
        b = tb % 3
        for kc in range(KC):
            p.add("pe", lambda e, tb=tb, kc=kc, b=b: e.matmul(
                ps[:, BV[b]:BV[b] + 512], lhsT=h[:, kc, tb * 128:(tb + 1) * 128], rhs=wv[:, kc, :],
                start=(kc == 0), stop=(kc == KC - 1)), reads=wbr + [hres[kc]], writes=[bvr[b]])
        p.add("act", lambda e, tb=tb, b=b: e.activation(
            out=Vaug[:, tb, :, 0:64], in_=ps[:, BV[b]:BV[b] + 512].rearrange("p (g d) -> p g d", g=8), func=AF.Identity),
            reads=[bvr[b], regr[1]], writes=[Vr], relaxed=True)

    if stage <= 1:
        return cx.finish()
    tilesK = token_tiles(NCOL, 0)
    ks = load_k(0)
    for g in range(8):
        s = ks
        if g + 1 < 8:
            ks = load_k(g + 1)
        proj_norm(s, g % 2, tilesK, 0, kg, lambda c0, c1, g=g: Kd[:, g, c0:c1], Kdr[g])

    if stage <= 2:
        return cx.finish()
    tilesQ = token_tiles(NT, 0)
    qs = [load_q(0), load_q(1)]
    load_o(0)
    for g in range(8):
        for hh in range(4):
            s = qs.pop(0)
            proj_norm(s, hh % 2, tilesQ, HALO, qg, lambda c0, c1, hh=hh: Qn[:, hh, c0:c1], Qnr[hh])
            nxt = g * 4 + hh + 2
            if nxt < 32:
                qs.append(load_q(nxt))
        if g + 1 < 8:
            load_o(g + 1)
        if stage <= 3:
            return cx.finish()
        for qb in range(8):
            i = qb % 2
            for kb in range(2):
                for hh in range(4):
                    p.add("pe", lambda e, g=g, qb=qb, kb=kb, hh=hh: e.matmul(
                        ps[:, SC[kb] + hh * 128:SC[kb] + (hh + 1) * 128],
                        lhsT=Kd[:, g, (qb + kb) * 128:(qb + kb + 1) * 128],
                        rhs=Qn[:, hh, qb * 128:(qb + 1) * 128], start=True, stop=True),
                        reads=[Kdr[g], Qnr[hh]], writes=[scr_[kb]])
                bias = hb if (qb == 0 and kb == 0) else 0.0
                p.add("act", lambda e, i=i, kb=kb, bias=bias: e.activation(
                    out=PT[i][:, kb, :], in_=ps[:, SC[kb]:SC[kb] + 512], func=AF.Exp, bias=bias, scale=0.0625),
                    reads=[scr_[kb], prm_r], writes=[PTr[i][kb]])
                p.add("pool", lambda e, i=i, kb=kb: e.tensor_tensor(
                    out=PT[i][:, kb, :], in0=PT[i][:, kb, :], in1=msk[:, kb, :], op=ALU.mult),
                    reads=[PTr[i][kb], msk_r], writes=[PTr[i][kb]])
            for kb in range(2):
                p.add("pe", lambda e, g=g, qb=qb, kb=kb, i=i: e.matmul(
                    ps[:, PVB:PVB + 512], lhsT=Vaug[:, qb + kb, g, :], rhs=PT[i][:, kb, :],
                    start=(kb == 0), stop=(kb == 1)), reads=[Vr, PTr[i][kb]], writes=[pvr])
            for hh in range(4):
                p.add("dve", lambda e, g=g, hh=hh: e.tensor_scalar(
                    out=den[64:128, hh * 128:(hh + 1) * 128], in0=ps[64:128, PVB + hh * 128:PVB + (hh + 1) * 128],
                    scalar1=es[64:128, g * 4 + hh:g * 4 + hh + 1], scalar2=None, op0=ALU.add),
                    reads=[pvr, es_r], writes=[denr], relaxed=True)
            p.add("dve", lambda e: e.reciprocal(out=rec[0:64, :], in_=den[64:128, :]), reads=[denr], writes=[recr])
            for hh in range(4):
                pb = (hh % 2) * 64
                c = hh // 2
                p.add("dve", lambda e, hh=hh, pb=pb, c=c, qb=qb: e.tensor_tensor(
                    out=AO[pb:pb + 64, c, qb * 128:(qb + 1) * 128], in0=ps[0:64, PVB + hh * 128:PVB + (hh + 1) * 128],
                    in1=rec[0:64, hh * 128:(hh + 1) * 128], op=ALU.mult),
                    reads=[pvr, recr], writes=[AOr[c]], relaxed=True)
        if stage <= 4:
            return cx.finish()
        so = g % 2
        wo = wbig[:, so * 4096:(so + 1) * 4096].rearrange("p (c n) -> p c n", c=2)
        for m in range(KC):
            for tt in range(2):
                b = (m * 2 + tt) % 4
                for c in range(2):
                    p.add("pe", lambda e, wo=wo, m=m, tt=tt, b=b, c=c: e.matmul(
                        ps[:, OB[b]:OB[b] + 512], lhsT=wo[:, c, m * 128:(m + 1) * 128],
                        rhs=AO[:, c, tt * 512:(tt + 1) * 512], start=(c == 0), stop=(c == 1)),
                        reads=[wbr[so], AOr[c]], writes=[obr[b], regr[0 if b < 2 else 1]])
                xs = x[:, m, HALO + tt * 512:HALO + (tt + 1) * 512]
                p.add("dve", lambda e, xs=xs, b=b: e.tensor_tensor(out=xs, in0=ps[:, OB[b]:OB[b] + 512], in1=xs, op=ALU.add),
                      reads=[obr[b], regr[0 if b < 2 else 1], xres[m][1 + tt]], writes=[xres[m][1 + tt]])
                if g == 7 and tt == 1:
                    yv = yT.rearrange("(kc p) n -> p kc n", p=128)
                    p.add("sp", lambda e, m=m: e.dma_start(out=yv[:, m, :], in_=x[:, m, HALO:HALO + NT]),
                          reads=[xres[m][1], xres[m][2]], dma="r")
    return cx.finish()


def attn_params(g, qgain, kgain, sinks, halo_bias):
    qg = np.tile(np.asarray(qgain, np.float32), 2).reshape(128, 1)
    kg = np.tile(np.asarray(kgain, np.float32), 2).reshape(128, 1)
    sk = np.tile(np.asarray(sinks, np.float32).reshape(1, 32), (128, 1))
    hb = np.full((128, 1), halo_bias, np.float32)
    return np.ascontiguousarray(np.concatenate([chunked(g), qg, kg, sk, hb], axis=1))


def attn_masks():
    k = np.arange(128)[:, None]
    q = np.arange(128)[None, :]
    mprev = (k > q).astype(np.float32)
    mcur = (q >= k).astype(np.float32)
    return np.ascontiguousarray(np.concatenate([np.tile(mprev, (1, 4)), np.tile(mcur, (1, 4))], axis=1))


_CACHE = {}


def _prog(name):
    if name not in _CACHE:
        _CACHE[name] = {"attn": build_attn2, "ffn": lambda: build_ffn(False),
                        "ffnp": lambda: build_ffn(True), "rec1": build_rec1b}[name]()
    return _CACHE[name]


def _run(name, in_maps):
    res = run_bass_kernel_spmd(_prog(name), in_maps, core_ids=list(range(NCORES)))
    return res.results


def _halo_slices(aT, halo):
    pad = np.concatenate([np.zeros((aT.shape[0], halo), aT.dtype), aT], axis=1)
    return [np.ascontiguousarray(pad[:, c * NT:c * NT + NT + halo]) for c in range(NCORES)]


def kernel(x, mix_norm, ffn_norm, attn_w_qkv, attn_q_gain, attn_k_gain, attn_sinks, attn_w_o,
           rec_w_in, rec_conv_w, rec_conv_b, rec_w_a, rec_b_a, rec_w_i, rec_b_i, rec_lambda,
           rec_w_out, ffn_w_up, ffn_conv_w, ffn_conv_b, ffn_w_down):
    f = lambda a: np.ascontiguousarray(np.asarray(a, dtype=np.float32))
    xT = np.ascontiguousarray(f(x)[0].T)
    masks = attn_masks2()
    for layer in range(4):
        j = layer // 2
        fprm = ffn_params(f(ffn_norm)[layer], f(ffn_conv_w)[layer], f(ffn_conv_b)[layer])
        wup, wdn = f(ffn_w_up)[layer], f(ffn_w_down)[layer]
        if layer % 2 == 0:
            xs = _halo_slices(xT, 128)
            wqkv, wo = f(attn_w_qkv)[j], f(attn_w_o)[j]
            in_maps = [{"xT": xs[c], "w_qkv": wqkv, "w_o": wo, "msk": masks,
                        "prm": attn_params(f(mix_norm)[layer], f(attn_q_gain)[j], f(attn_k_gain)[j],
                                           f(attn_sinks)[j], -30000.0 if c == 0 else 0.0)}
                       for c in range(NCORES)]
            res = _run("attn", in_maps)
            xT = np.concatenate([r["yT"] for r in res], axis=1)
            xs = _halo_slices(xT, 2)
            in_maps = [{"xT": xs[c], "w_up": wup, "w_down": wdn, "prm": fprm} for c in range(NCORES)]
            res = _run("ffn", in_maps)
        else:
            xs = _halo_slices(xT, 3)
            rprm = rec_params(f(mix_norm)[layer], f(rec_conv_w)[j], f(rec_conv_b)[j], f(rec_b_a)[j],
                              f(rec_b_i)[j], f(rec_lambda)[j])
            in_maps = [{"xT": xs[c], "w_in": f(rec_w_in)[j], "w_a": f(rec_w_a)[j], "w_i": f(rec_w_i)[j],
                        "prm": rprm} for c in range(NCORES)]
            res = _run("rec1", in_maps)
            g1T = np.concatenate([r["g1T"] for r in res], axis=1)
            zT = np.concatenate([r["zT"] for r in res], axis=1)
            cst = np.stack([r["carr"] for r in res], axis=-1).reshape(128, 2, 1, 16, 8)
            carr = np.zeros((128, 2, 2, 16, 9), np.float32)
            carr[:, :, :, :, 1:] = cst
            carr = np.ascontiguousarray(carr.reshape(128, 576))
            xs, gs, zs = _halo_slices(xT, 2), _halo_slices(g1T, 2), _halo_slices(zT, 2)
            in_maps = []
            for c in range(NCORES):
                m = np.zeros((128, 2, 2, 16, 9), np.float32)
                m[:, 0, 0, :, 1:1 + max(c, 0)] = 1.0
                m[:, 0, 1, :, 1:1 + max(c - 1, 0)] = 1.0
                m[:, 1, :, :, 1:] = 1.0
                m = np.ascontiguousarray(m.reshape(128, 576))
                in_maps.append({"xT": xs[c], "g1T": gs[c], "zT": zs[c], "carr": carr, "msk": m,
                                "w_out": f(rec_w_out)[j], "w_up": wup, "w_down": wdn, "prm": fprm})
            res = _run("ffnp", in_maps)
        xT = np.concatenate([r["yT"] for r in res], axis=1)
    return np.ascontiguousarray(xT.T)[None].astype(np.float32)


def build_attn2():
    cx = Ctx()
    nc, p = cx.nc, cx.p
    HALO = 128
    NCOL = NT + HALO
    NTB = NCOL // 128
    xT = cx.din("xT", [D, NCOL])
    w_qkv = cx.din("w_qkv", [D, 3072])
    w_o = cx.din("w_o", [D, D])
    NPRM = 16 + 2 + 32 + 1
    prm_d = cx.din("prm", [128, NPRM])
    msk_d = cx.din("msk", [128, 640])
    yT = cx.dout("yT", [D, NT])

    ps = cx.psum()
    bk = [Res(f"bk{i}") for i in range(8)]
    x = cx.sb("x", [128, KC, NCOL], F32)
    h = cx.sb("h", [128, KC, NCOL], BF16)
    sq = [cx.sb(f"sq{i}", [128, NCOL], F32) for i in range(2)]
    rstd = cx.sb("rstd", [128, NCOL], F32)
    prm = cx.sb("prm_sb", [128, NPRM], F32)
    es = cx.sb("es", [128, 32], F32)
    ones = cx.sb("ones", [128, 128], F32)
    bd = cx.sb("bd", [128, 128], F32)
    bdb = cx.sb("bdb", [128, 128], BF16)
    epsb = cx.sb("epsb", [128, 2], F32)
    msk = cx.sb("msk_sb", [128, 640], BF16)
    onesb = cx.sb("onesb", [128, 128], BF16)
    Kd = cx.sb("Kd", [128, 8, NCOL], BF16)
    NVB = NTB * 8
    VO = cx.sb("VO", [128, (NVB + 1) * 64], BF16)
    wbig = cx.sb("wbig", [128, 8192], BF16)
    wq = [cx.sb(f"wq{i}", [128, KC, 128], BF16) for i in range(2)]
    Qn = [cx.sb(f"Qn{i}", [128, 4, NT], BF16) for i in range(2)]
    AO = [cx.sb(f"AO{i}", [128, 2, NT], BF16) for i in range(2)]
    PT = [cx.sb(f"PT{i}", [128, 512], BF16) for i in range(2)]

    xres = [[Res(f"x{m}_h"), Res(f"x{m}_0"), Res(f"x{m}_1")] for m in range(KC)]
    hres = [Res(f"h{k}") for k in range(KC)]
    sqr = [Res("sq0"), Res("sq1")]
    rstdr, prm_r, ones_r, bd_r, es_r, msk_r = (Res(n) for n in ("rstd", "prm", "ones", "bd", "es", "msk"))
    Kdr = [Res(f"Kd{g}") for g in range(8)]
    Vr = Res("VO")
    wbr = [Res("wb0"), Res("wb1")]
    wqr = [[Res(f"wq{i}_0"), Res(f"wq{i}_1")] for i in range(2)]
    Qnr = [[Res(f"Qn{i}_{hh}") for hh in range(4)] for i in range(2)]
    AOr = [[Res(f"AO{i}_{c}") for c in range(2)] for i in range(2)]
    PTr = [Res("PT0"), Res("PT1")]
    sqt = [sq[0][:, 0:512], sq[0][:, 512:1024]]
    _sq0b = sq[0][:, :].bitcast(BF16)
    sqtb = [_sq0b[:, 0:512], _sq0b[:, 1024:1536]]
    rsb = [sq[1][:, 0:512], sq[1][:, 512:1024]]
    den = rstd[:, 0:256]
    rec = rstd[:, 512:768]
    sqtr = [Res("sqt0"), Res("sqt1")]
    rsr = [Res("rs0"), Res("rs1")]
    denr, recr = Res("den"), Res("rec")
    B = lambda i: i * 512
    STATB, SCB, PVB, OPB = 4, 5, 6, 7

    gain = prm[:, 0:16]
    qg = prm[:, 16:17]
    kg = prm[:, 17:18]
    hb = prm[:, 50:51]
    scr = {"sq": sq, "sqr": sqr, "rstd": rstd, "rstdr": rstdr, "eps": epsb}

    p.add("pool", lambda e: e.memset(ones[:, :], 1.0), writes=[ones_r])
    p.add("pool", lambda e: e.memset(onesb[:, :], 1.0), writes=[ones_r])
    p.add("pool", lambda e: e.memset(epsb[:, 0:1], EPS), writes=[prm_r])
    p.add("pool", lambda e: e.memset(epsb[:, 1:2], 64 * EPS), writes=[prm_r])
    p.add("pool", lambda e: e.memset(bd[:, :], 0.0), writes=[bd_r])
    p.add("pool", lambda e: e.memset(bd[0:64, 0:64], 1.0), writes=[bd_r])
    p.add("pool", lambda e: e.memset(bd[64:128, 64:128], 1.0), writes=[bd_r])
    p.add("pool", lambda e: e.tensor_copy(out=bdb[:, :], in_=bd[:, :]), reads=[bd_r], writes=[bd_r])
    p.add("pool", lambda e: e.memset(VO[:, NVB * 64:(NVB + 1) * 64], 1.0), writes=[Vr])
    p.add("sp", lambda e: e.dma_start(out=prm[:, :], in_=prm_d[:, :]), writes=[prm_r], dma="w")
    p.add("pool", lambda e: e.dma_start(out=msk[:, :], in_=msk_d[:, :]), writes=[msk_r], dma="w")
    xv = xT.rearrange("(kc p) n -> p kc n", p=128)
    for kc in range(KC):
        p.add("sp", lambda e, kc=kc: e.dma_start(out=x[:, kc, :], in_=xv[:, kc, :]), writes=xres[kc], dma="w")
    p.add("act", lambda e: e.activation(out=es[:, :], in_=prm[:, 18:50], func=AF.Exp), reads=[prm_r], writes=[es_r])

    wv = wbig[:, :].rearrange("p (kc n) -> p kc n", kc=KC)
    wqkv_v = w_qkv.rearrange("(kc p) n -> p kc n", p=128)
    p.add("pool", lambda e: e.dma_start(out=wv, in_=wqkv_v[:, :, 2560:3072]), writes=wbr, dma="w")

    nq = [0]

    def load_w(col0):
        s = nq[0] % 2
        nq[0] += 1
        for half in range(2):
            p.add("pool", lambda e, s=s, col0=col0, half=half: e.dma_start(
                out=wq[s][:, :, half * 64:(half + 1) * 64], in_=wqkv_v[:, :, col0:col0 + 64]),
                writes=[wqr[s][half]], dma="w")
        return s

    def load_o(g):
        s = g % 2
        dst = wbig[:, s * 4096:(s + 1) * 4096].rearrange("p (c n) -> p c n", c=2)
        src = w_o[g * 256:(g + 1) * 256, :].rearrange("(c p) n -> p c n", p=128)
        p.add("pool", lambda e, dst=dst, src=src: e.dma_start(out=dst, in_=src), writes=[wbr[s]], dma="w")

    emit_rmsnorm(cx, ps, bk[0], x, xres, gain, prm_r, h, hres, NCOL, ones, ones_r, scr,
                 extra_ps_res=[bk[1], bk[2]], use_ln=True, onesb=onesb)

    for tb in range(NTB):
        b = 5 + tb % 3
        for kc in range(KC):
            p.add("pe", lambda e, tb=tb, kc=kc, b=b: e.matmul(
                ps[:, B(b):B(b) + 512], lhsT=h[:, kc, tb * 128:(tb + 1) * 128], rhs=wv[:, kc, :],
                start=(kc == 0), stop=(kc == KC - 1)), reads=wbr + [hres[kc]], writes=[bk[b]])
        p.add("act", lambda e, tb=tb, b=b: e.activation(
            out=VO[:, tb * 512:(tb + 1) * 512], in_=ps[:, B(b):B(b) + 512], func=AF.Identity),
            reads=[bk[b]], writes=[Vr], relaxed=True)

    nstat = [0]

    def proj_mms(s, banks, tiles, hoff, tile_major=False):
        out = []
        order = [(kc, t) for kc in range(KC) for t in range(len(tiles))] if not tile_major else \
                [(kc, t) for t in range(len(tiles)) for kc in range(KC)]
        for kc, t in order:
            c0, c1 = tiles[t]
            if True:
                def f(s=s, kc=kc, t=t, c0=c0, c1=c1):
                    p.add("pe", lambda e: e.matmul(
                        ps[:, B(banks[t]):B(banks[t]) + (c1 - c0)], lhsT=wq[s][:, kc, :],
                        rhs=h[:, kc, hoff + c0:hoff + c1], start=(kc == 0), stop=(kc == KC - 1)),
                        reads=wqr[s] + [hres[kc]], writes=[bk[banks[t]]])
                out.append(f)
        return out

    def norm_steps(banks, tiles, gain_ap, dst_fn, dst_res):
        idx = []
        for t in range(len(tiles)):
            idx.append(nstat[0] % 2)
            nstat[0] += 1

        def phase_a():
            for t, (c0, c1) in enumerate(tiles):
                n = c1 - c0
                i = idx[t]
                src = ps[:, B(banks[t]):B(banks[t]) + n]
                p.add("act", lambda e, i=i, n=n, src=src: e.activation(out=sqtb[i][:, 0:n], in_=src, func=AF.Square),
                      reads=[bk[banks[t]]], writes=[sqtr[i]])

        def phase_b(t):
            c0, c1 = tiles[t]
            n = c1 - c0
            i = idx[t]
            src = ps[:, B(banks[t]):B(banks[t]) + n]
            br = bk[banks[t]]
            p.add("pe", lambda e: e.matmul(ps[:, B(STATB):B(STATB) + n], lhsT=bdb[:, :], rhs=sqtb[i][:, 0:n],
                                           start=True, stop=True), reads=[bd_r, sqtr[i]], writes=[bk[STATB]])
            p.add("act", lambda e: e.activation(out=sqt[i][:, 0:n], in_=ps[:, B(STATB):B(STATB) + n], func=AF.Ln,
                                                bias=epsb[:, 1:2], scale=1.0), reads=[bk[STATB], prm_r], writes=[sqtr[i]])
            p.add("act", lambda e: e.activation(out=rsb[i][:, 0:n], in_=sqt[i][:, 0:n], func=AF.Exp, scale=-0.5),
                  reads=[sqtr[i]], writes=[rsr[i]])
            dst = dst_fn(c0, c1)
            if isinstance(dst, list):
                for hf, (d_ap, d_res) in enumerate(dst):
                    lo, hi = hf * 64, hf * 64 + 64
                    p.add("dve", lambda e, d_ap=d_ap, lo=lo, hi=hi: e.scalar_tensor_tensor(
                        out=d_ap, in0=src[lo:hi, :], scalar=gain_ap[lo:hi, :], in1=rsb[i][lo:hi, 0:n],
                        op0=ALU.mult, op1=ALU.mult), reads=[br, rsr[i], prm_r], writes=[d_res], relaxed=True)
            else:
                p.add("dve", lambda e: e.scalar_tensor_tensor(
                    out=dst, in0=src, scalar=gain_ap, in1=rsb[i][:, 0:n], op0=ALU.mult, op1=ALU.mult),
                    reads=[br, rsr[i], prm_r], writes=[dst_res], relaxed=True)

        return [phase_a] + [(lambda t=t: phase_b(t)) for t in range(len(tiles))]

    def norm(banks, tiles, gain_ap, dst_fn, dst_res):
        for t in range(len(tiles)):
            for f in norm_steps(banks[t:t + 1], tiles[t:t + 1], gain_ap, dst_fn, dst_res):
                f()

    tilesK = token_tiles(NCOL, 0)
    KB = [[0, 1, 5], [2, 3, 6]]
    ks = load_w(2048)
    pend = None
    for g in range(8):
        s = ks
        if g + 1 < 8:
            ks = load_w(2048 + (g + 1) * 64)
        for f in proj_mms(s, KB[g % 2], tilesK, 0):
            f()
        if pend is not None:
            pend()
        pend = (lambda g=g: norm(KB[g % 2], tilesK, kg, lambda c0, c1: Kd[:, g, c0:c1], Kdr[g]))
    pend()

    tilesQ = token_tiles(NT, 0)
    QB = [[0, 1], [2, 3]]

    qslot = {}

    def load_w128(col0):
        s = nq[0] % 2
        nq[0] += 1
        p.add("pool", lambda e, s=s, col0=col0: e.dma_start(out=wq[s][:, :, :], in_=wqkv_v[:, :, col0:col0 + 128]),
              writes=wqr[s], dma="w")
        return s

    def prefetch_q(hd):
        if hd < 16 and hd not in qslot:
            qslot[hd] = load_w128(hd * 128)

    def emit_q_head(g, hh, chunks):
        hd = g * 2 + hh
        gp = g % 2
        prefetch_q(hd)
        s = qslot[hd]
        qbanks = [(hd * 2) % 3, (hd * 2 + 1) % 3]
        mms = proj_mms(s, qbanks, tilesQ, HALO, tile_major=True)
        per = (len(mms) + chunks - 1) // chunks
        out = []
        for ci in range(chunks):
            part = mms[ci * per:(ci + 1) * per]
            last = ci == chunks - 1

            def f(part=part, last=last, first=(ci == 0)):
                if first:
                    prefetch_q(hd + 1)
                    while any(set(r) & set(qbanks) for r, _ in pending_norm):
                        pending_norm.pop(0)[1]()
                half = len(part) // 2
                pop_pending()
                for m in part[:half]:
                    m()
                pop_pending()
                for m in part[half:]:
                    m()
                if last:
                    sts = norm_steps(qbanks, tilesQ, qg, lambda c0, c1: [
                        (Qn[gp][0:64, 2 * hh, c0:c1], Qnr[gp][2 * hh]),
                        (Qn[gp][64:128, 2 * hh + 1, c0:c1], Qnr[gp][2 * hh + 1])], None)
                    pending_norm.extend(zip([qbanks, qbanks[0:1], qbanks[1:2]], sts))
            out.append(f)
        return out

    pending_norm = []

    def pop_pending():
        if pending_norm:
            pending_norm.pop(0)[1]()

    def flush_norm():
        while pending_norm:
            pending_norm.pop(0)[1]()

    def o_units(g, banks=(7,)):
        gp = g % 2
        so = g % 2
        wo = wbig[:, so * 4096:(so + 1) * 4096].rearrange("p (c n) -> p c n", c=2)
        units = []
        for m in range(KC):
            for tt in range(2):
                def f(m=m, tt=tt):
                    ob = banks[(m * 2 + tt) % len(banks)]
                    for c in range(2):
                        p.add("pe", lambda e, c=c: e.matmul(
                            ps[:, B(ob):B(ob) + 512], lhsT=wo[:, c, m * 128:(m + 1) * 128],
                            rhs=AO[gp][:, c, tt * 512:(tt + 1) * 512], start=(c == 0), stop=(c == 1)),
                            reads=[wbr[so], AOr[gp][c]], writes=[bk[ob]])
                    xs = x[:, m, HALO + tt * 512:HALO + (tt + 1) * 512]
                    p.add("dve", lambda e: e.tensor_tensor(out=xs, in0=ps[:, B(ob):B(ob) + 512], in1=xs, op=ALU.add),
                          reads=[bk[ob], xres[m][1 + tt]], writes=[xres[m][1 + tt]])
                    if g == 7 and tt == 1:
                        yv = yT.rearrange("(kc p) n -> p kc n", p=128)
                        p.add("sp", lambda e: e.dma_start(out=yv[:, m, :], in_=x[:, m, HALO:HALO + NT]),
                              reads=[xres[m][1], xres[m][2]], dma="r")
                units.append(f)
        return units

    for gp_ in range(2):
        p.add("pool", lambda e, gp_=gp_: e.memset(Qn[gp_][:, :, :], 0.0), writes=Qnr[gp_])
    prefetch_q(0)
    for hh in range(2):
        for f in emit_q_head(0, hh, 1):
            f()
    flush_norm()
    load_o(0)

    nstep = [0]
    for g in range(8):
        gp = g % 2
        qwork = []
        if g + 1 < 8:
            for hh in range(2):
                qwork.append((g + 1, hh))
        owork = o_units(g - 1, banks=(7, 3)) if g > 0 else []
        if g > 0:
            load_o(g)
        cur_q = []
        for qb in range(8):
            for hp in range(2):
                step = qb * 2 + hp
                i = nstep[0] % 2
                nstep[0] += 1
                p.add("pe", lambda e: e.matmul(ps[:, B(SCB):B(SCB) + 512], lhsT=msk[:, 512:640], rhs=msk[:, 0:512],
                                               start=True, stop=False), reads=[msk_r], writes=[bk[SCB]])
                for kb in range(2):
                    for hl in range(2):
                        hh = hp * 2 + hl
                        col = B(SCB) + kb * 256 + hl * 128
                        p.add("pe", lambda e, kb=kb, hh=hh, col=col, qb=qb, g=g, gp=gp, hl=hl: e.matmul(
                            ps[:, col:col + 128], lhsT=Kd[:, g, (qb + kb) * 128:(qb + kb + 1) * 128],
                            rhs=Qn[gp][:, hh, qb * 128:(qb + 1) * 128], start=False, stop=(kb == 1 and hl == 1)),
                            reads=[Kdr[g], Qnr[gp][hh]], writes=[bk[SCB]])
                if qb == 0:
                    p.add("act", lambda e, i=i: e.activation(out=PT[i][:, 0:256], in_=ps[:, B(SCB):B(SCB) + 256],
                                                             func=AF.Exp, bias=hb, scale=8.0),
                          reads=[bk[SCB], prm_r], writes=[PTr[i]])
                    p.add("act", lambda e, i=i: e.activation(out=PT[i][:, 256:512], in_=ps[:, B(SCB) + 256:B(SCB) + 512],
                                                             func=AF.Exp, scale=8.0),
                          reads=[bk[SCB]], writes=[PTr[i]], relaxed=True)
                else:
                    p.add("act", lambda e, i=i: e.activation(out=PT[i][:, :], in_=ps[:, B(SCB):B(SCB) + 512],
                                                             func=AF.Exp, scale=8.0),
                          reads=[bk[SCB]], writes=[PTr[i]])
                if owork:
                    owork.pop(0)()
                if qwork or cur_q:
                    if not cur_q:
                        gg, hh_ = qwork.pop(0)
                        cur_q = emit_q_head(gg, hh_, 8)
                    cur_q.pop(0)()
                else:
                    pop_pending()
                if owork:
                    owork.pop(0)()
                for kb in range(2):
                    vb = ((qb + kb) * 8 + g) * 64
                    p.add("pe", lambda e, kb=kb, vb=vb, i=i: e.matmul(
                        ps[:, B(PVB):B(PVB) + 256], lhsT=VO[:, vb:vb + 128], rhs=PT[i][:, kb * 256:(kb + 1) * 256],
                        start=(kb == 0), stop=(kb == 1)), reads=[Vr, PTr[i]], writes=[bk[PVB]])
                for kb in range(2):
                    p.add("pe", lambda e, kb=kb, i=i: e.matmul(
                        ps[:, B(PVB) + 256:B(PVB) + 512], lhsT=onesb[:, :], rhs=PT[i][:, kb * 256:(kb + 1) * 256],
                        start=(kb == 0), stop=(kb == 1)), reads=[ones_r, PTr[i]], writes=[bk[PVB]])
                for hl in range(2):
                    hh = hp * 2 + hl
                    p.add("dve", lambda e, hl=hl, hh=hh, g=g: e.tensor_scalar(
                        out=den[0:64, hl * 128:(hl + 1) * 128],
                        in0=ps[0:64, B(PVB) + 256 + hl * 128:B(PVB) + 256 + (hl + 1) * 128],
                        scalar1=es[0:64, g * 4 + hh:g * 4 + hh + 1], scalar2=None, op0=ALU.add),
                        reads=[bk[PVB], es_r], writes=[denr], relaxed=True)
                p.add("dve", lambda e: e.reciprocal(out=rec[0:64, :], in_=den[0:64, :]), reads=[denr], writes=[recr])
                for hl in range(2):
                    p.add("dve", lambda e, hl=hl, hp=hp, qb=qb, gp=gp: e.tensor_tensor(
                        out=AO[gp][hl * 64:(hl + 1) * 64, hp, qb * 128:(qb + 1) * 128],
                        in0=ps[0:64, B(PVB) + hl * 128:B(PVB) + (hl + 1) * 128],
                        in1=rec[0:64, hl * 128:(hl + 1) * 128], op=ALU.mult),
                        reads=[bk[PVB], recr], writes=[AOr[gp][hp]], relaxed=True)
        flush_norm()
        assert not qwork and not cur_q and not owork, (len(qwork), len(cur_q), len(owork))
    for f in o_units(7, banks=(7, 3, 0, 1, 2, 5)):
        f()
    return cx.finish()


def attn_masks2():
    k = np.arange(128)[:, None]
    q = np.arange(128)[None, :]
    mprev = np.where(k > q, 0.0, -10000.0).astype(np.float32)
    mcur = np.where(q >= k, 0.0, -10000.0).astype(np.float32)
    return np.ascontiguousarray(np.concatenate([mprev, mprev, mcur, mcur, np.eye(128, dtype=np.float32)], axis=1))
```
